# Optimizing a Trainium2 kernel written in Bass

```python
import math
import jax, jax.numpy as jnp
from jax import lax
import numpy as np

D_MODEL = 2048
BATCH = 2
SEQ = 4096
DEPTH = 1
DEC_BATCH = 8
DEC_SEQ = 4
PAST_LEN = 16384
PAGE_SIZE = 128

HEAD_DIM = 128
N_MIX_HEADS = D_MODEL // HEAD_DIM
GMLP_GROUPS = N_MIX_HEADS // 4
GMLP_GW = HEAD_DIM
GMLP_W = GMLP_GROUPS * GMLP_GW
DIL_WINDOWS = (128, 512, 2048)
DIL_RATES = (1, 4, 16)
N_DIL = len(DIL_WINDOWS)
H_PER = (N_MIX_HEADS - GMLP_GROUPS) // N_DIL
N_ATTN_HEADS = N_DIL * H_PER
ATTN_W = N_ATTN_HEADS * HEAD_DIM
D_IN = 3 * ATTN_W + 2 * GMLP_W
D_OUT_MIX = H_PER * HEAD_DIM + GMLP_W
CHUNK = 128
QBLK = 128
D_FF = 4 * D_MODEL
NUM_BUCKETS = 32
MAX_EXACT = NUM_BUCKETS // 2
REL_MAX_DIST = 2048
EPS = 1e-6

kernel_name = "hymba_gmlp_dilated_window_decoder_step"


def rms_norm(x, g):
    xf = x.astype(jnp.float32)
    y = xf * lax.rsqrt(jnp.mean(xf * xf, axis=-1, keepdims=True) + EPS)
    return (y * g.astype(jnp.float32)).astype(x.dtype)


def t5_bucket(dist):
    is_small = dist < MAX_EXACT
    d = jnp.maximum(dist, 1).astype(jnp.float32)
    large = MAX_EXACT + (jnp.log(d / MAX_EXACT) / math.log(REL_MAX_DIST / MAX_EXACT)
                         * (NUM_BUCKETS - MAX_EXACT)).astype(jnp.int32)
    large = jnp.minimum(large, NUM_BUCKETS - 1)
    return jnp.where(is_small, dist, large)


def dilated_attn_prompt(q, k, v, bias_tab, dil, nk):
    B, S, H, Dh = q.shape
    span = dil * QBLK
    Sp = -(-S // span) * span
    M = Sp // dil
    nb = M // QBLK

    def to_blocks(a):
        a = jnp.pad(a, ((0, 0), (0, Sp - S), (0, 0), (0, 0)))
        return a.reshape(B, M, dil, H, Dh).transpose(0, 2, 1, 3, 4).reshape(B, dil, nb, QBLK, H, Dh)

    def band(a):
        prev = jnp.pad(a[:, :, :-1], ((0, 0), (0, 0), (1, 0), (0, 0), (0, 0), (0, 0)))
        return jnp.concatenate([prev, a], axis=3)

    qb = to_blocks(q)
    kband = band(to_blocks(k))
    vband = band(to_blocks(v))
    i = jnp.arange(QBLK)[:, None]
    j = jnp.arange(2 * QBLK)[None, :]
    sub = i + QBLK - j
    blk = jnp.arange(nb)[:, None, None]
    valid = (sub >= 0) & (sub <= nk) & (blk * QBLK + j - QBLK >= 0)
    bias = bias_tab[t5_bucket(jnp.clip(sub, 0, nk) * dil)]
    bias = bias.astype(jnp.float32).transpose(2, 0, 1)
    s = jnp.einsum('brnqhd,brnkhd->brnhqk', qb, kband,
                   preferred_element_type=jnp.float32) * (HEAD_DIM ** -0.5) + bias
    s = jnp.where(valid[None, None, :, None], s, -jnp.inf)
    m = jnp.max(s, axis=-1, keepdims=True)
    p = jnp.exp(s - m)
    den = jnp.sum(p, axis=-1, keepdims=True)
    o = jnp.einsum('brnhqk,brnkhd->brnqhd', p / den, vband.astype(jnp.float32))
    lse = (m + jnp.log(den))[..., 0]
    o = o.reshape(B, dil, M, H, Dh).transpose(0, 2, 1, 3, 4).reshape(B, Sp, H, Dh)[:, :S]
    lse = lse.transpose(0, 1, 2, 4, 3).reshape(B, dil, M, H).transpose(0, 2, 1, 3).reshape(B, Sp, H)[:, :S]
    return o, lse


def dilated_attn_sample(q, k_new, v_new, kv_cache, bias_tab, dil, nk):
    L = kv_cache.shape[2]
    T = q.shape[1]
    k_all = jnp.concatenate([kv_cache[:, 0], k_new], axis=1)
    v_all = jnp.concatenate([kv_cache[:, 1], v_new], axis=1)
    steps = jnp.arange(nk + 1)
    idx = L + jnp.arange(T)[:, None] - steps[None, :] * dil
    valid = idx >= 0
    idxc = jnp.maximum(idx, 0)
    kg = k_all[:, idxc]
    vg = v_all[:, idxc]
    bias = bias_tab[t5_bucket(steps * dil)].astype(jnp.float32).T
    s = jnp.einsum('bthd,btkhd->bthk', q, kg,
                   preferred_element_type=jnp.float32) * (HEAD_DIM ** -0.5) + bias
    s = jnp.where(valid[None, :, None, :], s, -jnp.inf)
    m = jnp.max(s, axis=-1, keepdims=True)
    p = jnp.exp(s - m)
    den = jnp.sum(p, axis=-1, keepdims=True)
    o = jnp.einsum('bthk,btkhd->bthd', p / den, vg.astype(jnp.float32))
    return o, (m + jnp.log(den))[..., 0]


def spatial_gate(u, g, w_s, b_s):
    B, S, _ = u.shape
    c = min(S, CHUNK)
    n = S // c
    wm = w_s[:, :c, :c] * jnp.tril(jnp.ones((c, c), w_s.dtype))
    gb = g.reshape(B, n, c, GMLP_GROUPS, GMLP_GW)
    mixed = jnp.einsum('gts,bnsgd->bntgd', wm, gb) + b_s[:, :c].T[None, None, :, :, None]
    return u * mixed.reshape(B, S, GMLP_W)


def decoder_layer(x, caches, norm_mix, w_in, q_norm, k_norm, gmlp_v_norm, gmlp_w, gmlp_b,
                  w_out, norm_ffn, w_up, w_down, rel_bias):
    B, S, _ = x.shape
    h = rms_norm(x, norm_mix)
    z = h @ w_in
    q = rms_norm(z[..., :ATTN_W].reshape(B, S, N_DIL, H_PER, HEAD_DIM), q_norm)
    k = rms_norm(z[..., ATTN_W:2 * ATTN_W].reshape(B, S, N_DIL, H_PER, HEAD_DIM), k_norm)
    v = z[..., 2 * ATTN_W:3 * ATTN_W].reshape(B, S, N_DIL, H_PER, HEAD_DIM)
    u = jax.nn.gelu(z[..., 3 * ATTN_W:3 * ATTN_W + GMLP_W], approximate=False)
    g = rms_norm(jax.nn.gelu(z[..., 3 * ATTN_W + GMLP_W:], approximate=False), gmlp_v_norm)

    outs, lses, new_kv = [], [], []
    for gi in range(N_DIL):
        win, dil = DIL_WINDOWS[gi], DIL_RATES[gi]
        nk = win // dil
        bias_g = rel_bias[:, gi * H_PER:(gi + 1) * H_PER]
        qg, kg, vg = q[:, :, gi], k[:, :, gi], v[:, :, gi]
        if caches is None:
            o, lse = dilated_attn_prompt(qg, kg, vg, bias_g, dil, nk)
            keep = min(win, S)
            new_kv.append(jnp.stack([kg[:, S - keep:], vg[:, S - keep:]], axis=1))
        else:
            o, lse = dilated_attn_sample(qg, kg, vg, caches[gi], bias_g, dil, nk)
            new_kv.append(jnp.stack([kg, vg], axis=1))
        outs.append(o)
        lses.append(lse)
    alpha = jax.nn.softmax(jnp.stack(lses, axis=0), axis=0)
    attn = jnp.einsum('gbsh,gbshd->bshd', alpha, jnp.stack(outs, axis=0))
    attn = attn.reshape(B, S, H_PER * HEAD_DIM).astype(x.dtype)
    gm = spatial_gate(u, g, gmlp_w, gmlp_b)
    x = x + jnp.concatenate([attn, gm], axis=-1) @ w_out
    h2 = rms_norm(x, norm_ffn)
    x = x + jnp.square(jax.nn.relu(h2 @ w_up)) @ w_down
    return x, new_kv, g


def setup_inputs(seed: int = 0) -> dict:
    key = jax.random.key(seed)
    ks = jax.random.split(key, 20)
    f32 = jnp.float32

    def nrm(k, shape, scale):
        return jax.random.normal(k, shape, f32) * scale

    def gain(k, shape):
        return 1.0 + nrm(k, shape, 0.02)

    return {
        "x_prompt": nrm(ks[0], (BATCH, SEQ, D_MODEL), 1.0),
        "x_sample": nrm(ks[1], (DEC_BATCH, DEC_SEQ, D_MODEL), 1.0),
        "cache_kv_w128": nrm(ks[2], (DEPTH, DEC_BATCH, 2, min(DIL_WINDOWS[0], PAST_LEN), H_PER, HEAD_DIM), 1.0),
        "cache_kv_w512": nrm(ks[3], (DEPTH, DEC_BATCH, 2, min(DIL_WINDOWS[1], PAST_LEN), H_PER, HEAD_DIM), 1.0),
        "cache_kv_w2048": nrm(ks[4], (DEPTH, DEC_BATCH, 2, min(DIL_WINDOWS[2], PAST_LEN), H_PER, HEAD_DIM), 1.0),
        "norm_mix": gain(ks[5], (DEPTH, D_MODEL)),
        "w_in": nrm(ks[6], (DEPTH, D_MODEL, D_IN), D_MODEL ** -0.5),
        "q_norm": gain(ks[7], (DEPTH, HEAD_DIM)),
        "k_norm": gain(ks[8], (DEPTH, HEAD_DIM)),
        "rel_bias": nrm(ks[9], (NUM_BUCKETS, N_ATTN_HEADS), 0.5),
        "gmlp_v_norm": gain(ks[10], (DEPTH, GMLP_W)),
        "gmlp_w": nrm(ks[11], (DEPTH, GMLP_GROUPS, CHUNK, CHUNK), CHUNK ** -0.5),
        "gmlp_b": gain(ks[12], (DEPTH, GMLP_GROUPS, CHUNK)),
        "w_out": nrm(ks[13], (DEPTH, D_OUT_MIX, D_MODEL), D_OUT_MIX ** -0.5),
        "norm_ffn": gain(ks[14], (DEPTH, D_MODEL)),
        "w_up": nrm(ks[15], (DEPTH, D_MODEL, D_FF), D_MODEL ** -0.5),
        "w_down": nrm(ks[16], (DEPTH, D_FF, D_MODEL), D_FF ** -0.5),
    }


def reference(x_prompt, x_sample, cache_kv_w128, cache_kv_w512, cache_kv_w2048, norm_mix, w_in,
              q_norm, k_norm, rel_bias, gmlp_v_norm, gmlp_w, gmlp_b, w_out, norm_ffn, w_up, w_down):
    caches = (cache_kv_w128, cache_kv_w512, cache_kv_w2048)
    yp, ys = x_prompt, x_sample
    new_p = [[] for _ in range(N_DIL)]
    new_s = [[] for _ in range(N_DIL)]
    gv = []
    for l in range(DEPTH):
        params = (norm_mix[l], w_in[l], q_norm[l], k_norm[l], gmlp_v_norm[l], gmlp_w[l], gmlp_b[l],
                  w_out[l], norm_ffn[l], w_up[l], w_down[l], rel_bias)
        yp, kv_p, _ = decoder_layer(yp, None, *params)
        ys, kv_s, g_s = decoder_layer(ys, (caches[0][l], caches[1][l], caches[2][l]), *params)
        for gi in range(N_DIL):
            new_p[gi].append(kv_p[gi])
            new_s[gi].append(kv_s[gi])
        gv.append(g_s)
    kv_w128_prompt = jnp.stack(new_p[0], axis=0)
    kv_w512_prompt = jnp.stack(new_p[1], axis=0)
    kv_w2048_prompt = jnp.stack(new_p[2], axis=0)
    kv_w128_sample = jnp.stack(new_s[0], axis=0)
    kv_w512_sample = jnp.stack(new_s[1], axis=0)
    kv_w2048_sample = jnp.stack(new_s[2], axis=0)
    gmlp_v_sample = jnp.stack(gv, axis=0)
    return (yp, ys, kv_w128_prompt, kv_w512_prompt, kv_w2048_prompt,
            kv_w128_sample, kv_w512_sample, kv_w2048_sample, gmlp_v_sample)
```

```python
import contextlib
import os
import numpy as np
import concourse.bass as bass
import concourse.mybir as mybir
from concourse.bass_utils import run_bass_kernel_spmd

F32 = mybir.dt.float32
BF16 = mybir.dt.bfloat16
AF = mybir.ActivationFunctionType
ALU = mybir.AluOpType
AX = mybir.AxisListType

D = 2048
NT = 1024
NH = 2048
TS = 4
NCOL = NT + TS
DIN = 5632
DFF = 8192
EPS = 1e-6
DILS = (1, 4, 16)
SCALE = 128 ** -0.5
SAME_ENG_SYNC = True

ENGS = ("pe", "act", "dve", "pool", "sp")


class Op:
    __slots__ = ("eng", "fn", "deps", "dma", "signal", "seq", "target")


class Prog:
    def __init__(self):
        self.ops = []
        self.lw = {}
        self.rd = {}
        self.dcnt = {}
        self.last = {}

    def add(self, eng, fn, r=(), w=(), dma=None, extra=()):
        idx = len(self.ops)
        deps = set(extra)
        pr = tuple(k for k in r if isinstance(k, str) and k[:2] == 'ps' and k[2:].isdigit())
        if pr:
            r = tuple(k for k in r if k not in pr)
            w = tuple(w) + pr
        for k in r:
            if k in self.lw:
                deps.add(self.lw[k])
        for k in w:
            if k in self.lw:
                deps.add(self.lw[k])
            deps.update(self.rd.get(k, ()))
        op = Op()
        op.eng, op.fn, op.deps, op.dma, op.signal, op.seq, op.target = eng, fn, deps, dma, False, 0, 0
        if dma is not None:
            c = self.dcnt.get(dma, 0) + 16
            self.dcnt[dma] = c
            op.target = c
        for k in r:
            self.rd.setdefault(k, []).append(idx)
        for k in w:
            self.lw[k] = idx
            self.rd[k] = []
        self.ops.append(op)
        if fn is not None:
            self.last[eng] = idx
        return idx

    def barrier(self):
        alld = set(self.last.values())
        for k, v in self.lw.items():
            alld.add(v)
        for e in ENGS:
            self.add(e, None, extra=tuple(alld))

    def emit(self, nc, es):
        limit = int(os.environ.get('MK_NOPS', '0'))
        if limit:
            self.ops = self.ops[:limit]
            for e in ENGS:
                op = Op()
                op.eng, op.fn, op.deps, op.dma, op.signal, op.seq, op.target = e, None, set(range(limit)), None, False, 0, 0
                self.ops.append(op)
        ops = self.ops
        if os.environ.get('MK_DUMP'):
            for i, op in enumerate(ops):
                print(i, op.eng, op.dma, sorted(op.deps)[-6:], getattr(op.fn, '__name__', None))
        for op in ops:
            op.deps = set(d for d in op.deps if ops[d].fn is not None)
            for d in op.deps:
                if ops[d].dma is None:
                    ops[d].signal = True
        cnt = {e: 0 for e in ENGS}
        for op in ops:
            if op.dma is None and op.signal:
                cnt[op.eng] += 1
                op.seq = cnt[op.eng]
        esem = {e: es.enter_context(nc.semaphore("sem_" + e)) for e in ENGS}
        dsem = {}
        for i, k in enumerate(self.dcnt):
            dsem[k] = es.enter_context(nc.semaphore("dsem%d" % i))
        block = es.enter_context(nc.Block())
        reg = {"pe": block.tensor, "act": block.scalar, "dve": block.vector,
               "pool": block.gpsimd, "sp": block.sync}
        for e in ENGS:
            mine = [op for op in ops if op.eng == e]

            def body(eng, e=e, mine=mine):
                waited = {}
                for op in mine:
                    need = {}
                    for d in op.deps:
                        p = ops[d]
                        if p.dma is not None:
                            s, v = dsem[p.dma], p.target
                        else:
                            if p.eng == e and (e == "pe" or not SAME_ENG_SYNC):
                                continue
                            s, v = esem[p.eng], p.seq
                        key = id(s)
                        if key not in need or need[key][1] < v:
                            need[key] = (s, v)
                    for key, (s, v) in need.items():
                        if waited.get(key, 0) < v:
                            eng.wait_ge(s, v)
                            waited[key] = v
                    if op.fn is not None:
                        ins = op.fn(eng)
                        if op.dma is not None:
                            ins.then_inc(dsem[op.dma], 16)
                        elif op.signal:
                            ins.then_inc(esem[e], 1)
            reg[e](body)


def bucket(dist):
    if dist < 16:
        return dist
    v = 16 + int(np.float32(np.log(np.float32(dist) / np.float32(16)) / np.float32(np.log(128.0)) * np.float32(16)))
    return min(v, 31)


def t5_bucket_np(dist):
    import math
    d = np.maximum(dist, 1).astype(np.float32)
    rnd = np.rint if os.environ.get('MK_BUCKET_RINT', '0') == '1' else np.trunc
    large = 16 + rnd(np.log(d / np.float32(16)) / np.float32(math.log(2048 / 16)) * np.float32(16)).astype(np.int32)
    large = np.minimum(large, 31)
    return np.where(dist < 16, dist, large)


def host_constants():
    oh = np.zeros((33, 3, 384), np.float32)
    for g, dil in enumerate(DILS):
        sub = np.arange(384) - 128
        ok = (sub >= 0) & (sub <= 128)
        b = t5_bucket_np(np.clip(sub, 0, 128).astype(np.int32) * dil)
        for u in range(384):
            if ok[u]:
                oh[b[u], g, u] = 1.0
            else:
                oh[32, g, u] = 1.0
    ident = np.eye(128, dtype=np.float32)
    jm = np.ascontiguousarray(ident[::-1])
    return oh.reshape(33, 1152), ident, jm


def build_program():
    nc = bass.Bass("TRN2", target_bir_lowering=False)

    def din(name, shape):
        return nc.dram_tensor(name, shape, F32, kind="ExternalInput").ap()

    def dout(name, shape):
        return nc.dram_tensor(name, shape, F32, kind="ExternalOutput").ap()

    xm = din("xm", [NT, D]); xh = din("xh", [NH, D]); xs = din("xs", [TS, D])
    kvalid = din("kvalid", [128, 21])
    ck = [din("ck0", [2, 128, 512]), din("ck1", [2, 512, 512]), din("ck2", [2, 2048, 512])]
    w_in = din("w_in", [D, DIN]); w_out = din("w_out", [1024, D])
    w_up = din("w_up", [D, DFF]); w_down = din("w_down", [DFF, D])
    norm_mix = din("norm_mix", [16, 128]); norm_ffn = din("norm_ffn", [16, 128])
    q_norm = din("q_norm", [1, 128]); k_norm = din("k_norm", [1, 128])
    rel_bias = din("rel_bias", [32, 12]); gvn = din("gmlp_v_norm", [1, 512])
    gw = din("gmlp_w", [4, 128, 128]); gb = din("gmlp_b", [4, 128])
    oh_d = din("oh", [33, 1152]); ident_d = din("ident", [128, 128]); jm_d = din("jm", [128, 128])

    y = dout("y", [NT, D]); ys = dout("ys", [TS, D])
    kvp = [dout("kvp0", [2, 128, 512]), dout("kvp1", [2, 512, 512]), dout("kvp2", [2, 1024, 512])]
    kvs = dout("kvs", [3, 2, TS, 512]); gvs = dout("gvs", [TS, 512])
    escr = dout("escr", [12, 384])
    DBG = os.environ.get('MK_DBG')
    if DBG:
        dbg_rx = dout("dbg_rx", [128, 16384])
        dbg_cat = nc.dram_tensor("dbg_cat", [128, 8 * NCOL], BF16, kind="ExternalOutput").ap()
        dbg_hT = nc.dram_tensor("dbg_hT", [128, 16 * NCOL], BF16, kind="ExternalOutput").ap()

    P = Prog()
    es = contextlib.ExitStack()
    STOP = int(os.environ.get('MK_STOP', '99'))
    SUB = int(os.environ.get('MK_SUB', '255'))

    def sb(name, shape, dt):
        return es.enter_context(nc.sbuf_tensor(name, shape, dt))

    hT = sb("hT", [128, 16, NCOL], BF16)
    ring = [sb("ring%d" % i, [128, 8192], BF16) for i in range(4)]
    RX = sb("RX", [128, 16384], F32)
    catT = sb("catT", [128, 8, NCOL], BF16)
    xt = sb("xt", [128, D], F32)
    xb = sb("xb", [128, D], BF16)
    hTh = sb("hTh", [128, 16, 128], BF16)
    kfK = sb("kfK", [128, 512], F32)
    vf = sb("vf", [128, 512], F32)
    PTb = [sb("PT%d" % i, [128, 2, 2, 128], BF16) for i in range(2)]
    identb = sb("identb", [128, 128], BF16)
    identf = sb("identf", [128, 128], F32)
    onesb = sb("onesb", [128, 128], BF16)
    gmix = sb("gmix", [128, 16], F32)
    gffn = sb("gffn", [128, 16], F32)
    bufA = sb("bufA", [128, 512], F32)
    bufB = sb("bufB", [128, 512], F32)
    WmT = sb("WmT", [128, 4, 128], BF16)
    kval = sb("kval", [128, 21], F32)
    st_ssq = sb("st_ssq", [128, 1], F32)
    st_rs = sb("st_rs", [128, 1], F32)
    st4 = [sb("st4_%d" % i, [128, 4], F32) for i in range(4)]
    accOs = sb("accOs", [128, 4, TS], F32)
    accDs = sb("accDs", [128, 4, TS], F32)
    uTs = sb("uTs", [128, 4, TS], F32)
    ps = [es.enter_context(nc.psum_tensor("ps%d" % i, [128, 512], F32)) for i in range(8)]

    accO = RX[:, 0:4096].rearrange("p (h t) -> p h t", h=4)
    accD = RX[:, 4096:8192].rearrange("p (h t) -> p h t", h=4)
    EB = RX[:, 8192:11264].rearrange("p (g a q) -> p g a q", g=12, a=2)
    EBc2 = RX[:, 11264:11776].rearrange("p (h q) -> p h q", h=4)
    EBp2 = RX[:, 11776:12288].rearrange("p (h q) -> p h q", h=4)
    uT = RX[:, 8192:12288].rearrange("p (h t) -> p h t", h=4)
    rest = RX[:, 12288:16384]
    KTr = [rest[:, i * 256:(i + 1) * 256].bitcast(BF16).rearrange("p (h k) -> p h k", h=4) for i in range(5)]
    Vr = [rest[:, 1280 + i * 256:1280 + (i + 1) * 256].bitcast(BF16) for i in range(5)]
    QTr = [rest[:, 2560 + i * 256:2560 + (i + 1) * 256].bitcast(BF16).rearrange("p (h k) -> p h k", h=4) for i in range(3)]
    Ebuf = rest[:, 3328:3840].rearrange("p (a h q) -> p a h q", a=2, h=2)
    kfQ = RX[:, 12288 + 3840 - 512:12288 + 3840]
    kfQ = sb("kfQ", [128, 512], F32)
    x1 = RX[:, :].rearrange("p (i d) -> p i d", i=8)
    hTf = hT[:].rearrange("p c n -> p (c n)").bitcast(F32)
    oh_sb = hTf[:, 0:1152]
    Eall = hTf[:, 1152:2304].rearrange("p (g u) -> p g u", g=3)
    jm_sb = hTf[:, 2304:2432]
    Hh = hTf[:, 2432:2432 + 3072].rearrange("p (g a q) -> p g a q", g=12, a=2)
    tab33 = hTf[:, 5504:5516]
    wtmp = hTf[:, 5632:5632 + 512].rearrange("p (g s) -> p g s", g=4)
    vtmp = hTf[:, 6144:6144 + 128]
    aT = [catT[:].rearrange("p c n -> p (c n)")[:, i * 4096:(i + 1) * 4096].rearrange("p (f t) -> p f t", f=4) for i in range(2)]
    aTs = sb("aTs", [128, 2, 4, TS], BF16)
    rsc = [xb[:, i * 1024:(i + 1) * 1024].bitcast(F32) for i in range(2)]

    psb = [p[:].bitcast(BF16) for p in ps]

    def dma(q, out, in_, r, w, key):
        return P.add(q, lambda e, out=out, in_=in_: e.dma_start(out=out, in_=in_), r=r, w=w, dma=key)

    def bcast_rows(dram_ap_row, nparts, n):
        return bass.AP(tensor=dram_ap_row.tensor, offset=dram_ap_row.offset, ap=[[0, nparts], [1, n]])

    def bc_free(ap2, n):
        a = ap2.ap
        return bass.AP(tensor=ap2.tensor, offset=ap2.offset, ap=[list(a[0]), list(a[1]), [0, n]])

    wblocks = []

    def wview_rows(w, col0, ncols):
        return w.rearrange("(c p) n -> p c n", p=128)[:, :, col0:col0 + ncols]

    for g in (2, 1, 0):
        wblocks.append(("K%d" % g, wview_rows(w_in, 1536 + g * 512, 512), (16, 512)))
        wblocks.append(("V%d" % g, wview_rows(w_in, 3072 + g * 512, 512), (16, 512)))
        wblocks.append(("Q%d" % g, wview_rows(w_in, g * 512, 512), (16, 512)))
    wblocks.append(("U", wview_rows(w_in, 4608, 512), (16, 512)))
    wblocks.append(("G", wview_rows(w_in, 5120, 512), (16, 512)))
    wblocks.append(("O0", wview_rows(w_out, 0, 1024), (8, 1024)))
    wblocks.append(("O1", wview_rows(w_out, 1024, 1024), (8, 1024)))
    for n in range(16):
        wblocks.append(("UP%d" % n, wview_rows(w_up, n * 512, 512), (16, 512)))
        wblocks.append(("DN%d" % n, w_down[n * 512:(n + 1) * 512, :].rearrange("(c p) n -> p c n", p=128), (4, 2048)))
    wslot = {}
    wnext = [0]

    def wload_next():
        i = wnext[0]
        if i >= len(wblocks):
            return
        name, src, (c, n) = wblocks[i]
        slot = i % 4
        dst = ring[slot][:].rearrange("p (c n) -> p c n", c=c)
        wslot[name] = (slot, dst)
        dma("pool", dst, src, r=(), w=(("ring", slot),), key=("ring", slot))
        wnext[0] += 1

    def W(name):
        return wslot[name]

    def phases():
        dma("sp", identf[:], ident_d[:, :], (), ("identf",), "c0")
        dma("sp", jm_sb, jm_d[:, :], (), ("jm",), "c1")
        dma("sp", oh_sb[0:33, :], oh_d[:, :], (), ("oh",), "c2")
        dma("sp", tab33[0:32, :], rel_bias[:, :], (), ("tab",), "c3")
        dma("sp", kval[:], kvalid[:, :], (), ("kval",), "c4")
        dma("sp", bufA[:], bass.AP(tensor=q_norm.tensor, offset=0, ap=[[0, 128], [0, 4], [1, 128]]), (), ("bufA",), "c5")
        dma("sp", bufB[:], bass.AP(tensor=k_norm.tensor, offset=0, ap=[[0, 128], [0, 4], [1, 128]]), (), ("bufB",), "c6")
        dma("sp", vtmp[0:16, :], norm_mix[:, :], (), ("vtmp",), "c7")
        for _ in range(4):
            wload_next()
        P.add("dve", lambda e: e.tensor_copy(out=identb[:], in_=identf[:]), r=("identf",), w=("identb",))
        P.add("dve", lambda e: e.memset(onesb[:], 1.0), w=("onesb",))
        P.add("dve", lambda e: e.memset(tab33[32:33, :], -30000.0), w=("tab32",))
        if SUB & 1:
            P.add("pe", lambda e: e.transpose(out=ps[7][:, 0:16], in_=vtmp[0:16, :], identity=identf[0:16, 0:16]),
                  r=("vtmp", "identf"), w=("ps7",))
            P.add("act", lambda e: e.copy(out=gmix[:], in_=ps[7][:, 0:16]), r=("ps7",), w=("gmix",))
            dma("sp", vtmp[0:16, :], norm_ffn[:, :], (), ("vtmp",), "c7")
            P.add("pe", lambda e: e.transpose(out=ps[7][:, 0:16], in_=vtmp[0:16, :], identity=identf[0:16, 0:16]),
                  r=("vtmp", "identf"), w=("ps7",))
            P.add("act", lambda e: e.copy(out=gffn[:], in_=ps[7][:, 0:16]), r=("ps7",), w=("gffn",))
        if SUB & 2:
            dma("sp", wtmp, gw.rearrange("g t s -> t g s"), (), ("wtmp",), "c8")
            for gg in range(4):
                P.add("pool", lambda e, gg=gg: e.affine_select(out=wtmp[:, gg, :], in_=wtmp[:, gg, :], pattern=[[-1, 128]],
                                                               compare_op=ALU.is_ge, fill=0.0, base=0, channel_multiplier=1),
                      r=("wtmp",), w=("wtmp",))
            for gg in range(4):
                P.add("pe", lambda e, gg=gg: e.transpose(out=ps[6][:, gg * 128:(gg + 1) * 128], in_=wtmp[:, gg, :], identity=identf[:]),
                      r=("wtmp", "identf"), w=("ps6",))
            P.add("act", lambda e: e.copy(out=WmT[:].rearrange("p g t -> p (g t)"), in_=ps[6][:, :]), r=("ps6",), w=("WmT",))
        if SUB & 4:
            for g in range(3):
                P.add("pe", lambda e, g=g: e.matmul(out=ps[g][0:12, 0:384], lhsT=tab33[0:33, 0:12], rhs=oh_sb[0:33, g * 384:(g + 1) * 384],
                                                    start=True, stop=True),
                      r=("tab", "tab32", "oh"), w=("ps%d" % g,))
                P.add("act", lambda e, g=g: e.activation(out=Eall[0:12, g, :], in_=ps[g][0:12, 0:384], func=AF.Exp),
                      r=("ps%d" % g,), w=("Eall%d" % g,))
                dma("sp", escr[g * 4:(g + 1) * 4, :], Eall[g * 4:(g + 1) * 4, g, :], ("Eall%d" % g,), ("escr%d" % g,), "e%d" % g)
            for gh in range(12):
                src = bass.AP(tensor=escr.tensor, offset=gh * 384 + 1, ap=[[1, 128], [128, 2], [1, 128]])
                dma("sp", Hh[:, gh, :, :], src, ("escr%d" % (gh // 4),), ("Hh%d" % gh,), "h%d" % gh)
            Hf = Hh.rearrange("p g a q -> p (g a q)")
            EBf = EB.rearrange("p g a q -> p (g a q)")
            for i in range(6):
                P.add("pe", lambda e, i=i: e.matmul(out=ps[i][:, :], lhsT=jm_sb, rhs=Hf[:, i * 512:(i + 1) * 512], start=True, stop=True),
                      r=("jm", "Hh%d" % (2 * i), "Hh%d" % (2 * i + 1)), w=("ps%d" % i,))
                P.add("dve" if i % 2 else "act",
                      (lambda e, i=i: e.tensor_copy(out=EBf[:, i * 512:(i + 1) * 512], in_=ps[i][:, :])) if i % 2 else
                      (lambda e, i=i: e.copy(out=EBf[:, i * 512:(i + 1) * 512], in_=ps[i][:, :])),
                      r=("ps%d" % i,), w=("EB",))
        if SUB & 8:
            P.add("dve", lambda e: e.tensor_copy(out=EBc2, in_=EB[:, 8:12, 0, :]), r=("EB",), w=("EBc2",))
            P.add("dve", lambda e: e.memset(EBc2[0:64, :, 64:128], 0.0), w=("EBc2",))
            P.add("dve", lambda e: e.tensor_copy(out=EBp2[:, :, 0:64], in_=EB[:, 8:12, 1, 0:64]), r=("EB",), w=("EBp2",))
            P.add("dve", lambda e: e.tensor_copy(out=EBp2[:, :, 64:128], in_=EB[:, 8:12, 1, 0:64]), r=("EB",), w=("EBp2",))
        P.barrier()
        if STOP <= 1:
            return

        def norm_tile(x_src, npart, gains, dst_fn, dst_keys, xkey_extra=(), load=True, src_sb=None, src_keys=()):
            if load:
                dma("sp", xt[0:npart, :], x_src, (), ("xt",), "xt")
                src = xt[0:npart, :]
                skeys = ("xt",)
            else:
                src = src_sb
                skeys = tuple(src_keys)
            P.add("act", lambda e: e.activation(out=xb[0:npart, :], in_=src, func=AF.Square, accum_out=st_ssq[0:npart, :]),
                  r=skeys, w=("xb", "st_ssq"))
            P.add("act", lambda e: e.activation(out=st_rs[0:npart, :], in_=st_ssq[0:npart, :], func=AF.Sqrt, scale=1.0 / D, bias=EPS),
                  r=("st_ssq",), w=("st_rs",))
            P.add("dve", lambda e: e.reciprocal(out=st_rs[0:npart, :], in_=st_rs[0:npart, :]), r=("st_rs",), w=("st_rs",))
            P.add("dve", lambda e: e.tensor_scalar(out=xb[0:npart, :], in0=src, scalar1=st_rs[0:npart, 0:1], scalar2=None, op0=ALU.mult),
                  r=skeys + ("st_rs",), w=("xb",))
            for half in range(2):
                def tp(e, half=half):
                    ins = None
                    for c in range(8):
                        cc = half * 8 + c
                        ins = e.transpose(out=psb[half][:, c * 128:c * 128 + npart], in_=xb[0:npart, cc * 128:(cc + 1) * 128],
                                          identity=identb[0:npart, 0:npart])
                    return ins
                P.add("pe", tp, r=("xb", "identb"), w=("ps%d" % half,))
                src_ps = psb[half][:, :].rearrange("p (c n) -> p c n", c=8)[:, :, 0:npart]
                P.add("dve", lambda e, half=half, src_ps=src_ps: e.tensor_tensor(
                    out=dst_fn(half * 8, half * 8 + 8), in0=src_ps, in1=bc_free(gains[:, half * 8:half * 8 + 8], npart), op=ALU.mult),
                    r=("ps%d" % half, "gmix", "gffn"), w=tuple(dst_keys))

        for i in range(8):
            norm_tile(xm[i * 128:(i + 1) * 128, :], 128, gmix, lambda c0, c1, i=i: hT[:, c0:c1, i * 128:(i + 1) * 128], (("hT", i),))
        norm_tile(xs[:, :], TS, gmix, lambda c0, c1: hT[:, c0:c1, NT:NT + TS], (("hT", 8),))

        if STOP <= 2:
            return

        def proj(lhs_fn, rkeys, wname, bank, npart):
            slot, wv = W(wname)

            def f(e):
                ins = None
                for c in range(16):
                    ins = e.matmul(out=ps[bank][0:npart, :], lhsT=lhs_fn(c), rhs=wv[:, c, :], start=(c == 0), stop=(c == 15))
                return ins
            P.add("pe", f, r=tuple(rkeys) + (("ring", slot),), w=("ps%d" % bank,))

        def qk_norm(bank, npart, gainbuf, gkey, kf, kfkey, ssq, rs, skey):
            P.add("act", lambda e: e.activation(out=vf[0:npart, :], in_=ps[bank][0:npart, :], func=AF.Square),
                  r=("ps%d" % bank,), w=("vf",))
            P.add("dve", lambda e: e.tensor_reduce(out=ssq[0:npart, :], in_=vf[0:npart, :].rearrange("p (h d) -> p h d", h=4),
                                                   axis=AX.X, op=ALU.add), r=("vf",), w=(skey,))
            P.add("act", lambda e: e.activation(out=rs[0:npart, :], in_=ssq[0:npart, :], func=AF.Sqrt, scale=1.0 / 128, bias=EPS),
                  r=(skey,), w=(skey + "r",))
            P.add("dve", lambda e: e.reciprocal(out=rs[0:npart, :], in_=rs[0:npart, :]), r=(skey + "r",), w=(skey + "r",))
            for h in range(4):
                P.add("dve", lambda e, h=h: e.scalar_tensor_tensor(
                    out=kf[0:npart, h * 128:(h + 1) * 128], in0=ps[bank][0:npart, h * 128:(h + 1) * 128],
                    scalar=rs[0:npart, h:h + 1], in1=gainbuf[0:npart, h * 128:(h + 1) * 128], op0=ALU.mult, op1=ALU.mult),
                    r=("ps%d" % bank, skey + "r", gkey), w=(kfkey,))

        def tr4(kf, kfkey, npart, dst, dkey):
            def f(e):
                ins = None
                for h in range(4):
                    ins = e.transpose(out=ps[5][:, h * 128:h * 128 + npart], in_=kf[0:npart, h * 128:(h + 1) * 128],
                                      identity=identf[0:npart, 0:npart])
                return ins
            P.add("pe", f, r=(kfkey, "identf"), w=("ps5",))
            P.add("act", lambda e: e.copy(out=dst, in_=ps[5][:, :].rearrange("p (h k) -> p h k", h=4)[:, :, 0:npart]),
                  r=("ps5",), w=(dkey,))

        first_group = [True]

        def run_group(g):
            dil = DILS[g]
            items = []
            if g == 0:
                items.append(dict(kind="H", rows=xh[1920:2048, :], vcol=0))
                for i in range(8):
                    items.append(dict(kind="M", lhs=(lambda c, i=i: hT[:, c, i * 128:(i + 1) * 128]), hkeys=(("hT", i),),
                                      pos=("c", i), out=(7 == i and [(0, 128, kvp[0][:, 0:128, :])] or [])))
            elif g == 1:
                for r in range(4):
                    items.append(dict(kind="H", rows=xh[1536 + r:2048:4, :], vcol=1 + r))
                    for s in range(2):
                        items.append(dict(kind="M", lhs=(lambda c, s=s, r=r: hT[:, c, s * 512 + r:s * 512 + 512:4]),
                                          hkeys=tuple(("hT", 4 * s + k) for k in range(4)), pos=("s4", s, r),
                                          out=(s == 1 and [(0, 128, kvp[1][:, r:512:4, :])] or [])))
            else:
                for T in range(8):
                    items.append(dict(kind="H", rows=xh[2 * T:2048:16, :], vcol=5 + 2 * T))
                    items.append(dict(kind="H", rows=xh[2 * T + 1:2048:16, :], vcol=5 + 2 * T + 1))
                    items.append(dict(kind="M", lhs=(lambda c: hTh[:, c, :]), gather=T,
                                      hkeys=("hTh",), pos=("s16", T),
                                      out=[(0, 64, kvp[2][:, 2 * T:1024:16, :]), (64, 128, kvp[2][:, 2 * T + 1:1024:16, :])]))
            n_items = len(items)
            for idx, it in enumerate(items):
                it["idx"] = idx
                it["kslot"] = idx % 5
            qcount = [0]
            for it in items:
                if it["kind"] == "M":
                    it["qslot"] = qcount[0] % 3
                    qcount[0] += 1
            for idx, it in enumerate(items):
                if it["kind"] != "M":
                    continue
                if g == 2:
                    it["prev"] = [(items[idx - 2], 0, 64), (items[idx - 1], 64, 128)]
                else:
                    it["prev"] = [(items[idx - 1], 0, 128)]

            Kn, Vn, Qn = "K%d" % g, "V%d" % g, "Q%d" % g

            def stageA1(it):
                if it["kind"] == "H":
                    norm_tile(it["rows"], 128, gmix, lambda c0, c1: hTh[:, c0:c1, :], ("hTh",))
                    lhs = lambda c: hTh[:, c, :]
                    hk = ("hTh",)
                else:
                    lhs, hk = it["lhs"], it["hkeys"]
                    if "gather" in it:
                        T = it["gather"]
                        srcv = hT[:, :, 0:NT].rearrange("p c (m r) -> p c r m", r=16)
                        for rr in range(2):
                            P.add("dve", lambda e, T=T, rr=rr, srcv=srcv: e.tensor_copy(out=hTh[:, :, rr * 64:(rr + 1) * 64],
                                                                                      in_=srcv[:, :, 2 * T + rr, :]),
                                  r=tuple(("hT", k) for k in range(8)), w=("hTh",))
                it["_lhs"], it["_hk"] = lhs, hk
                if it["kind"] == "M":
                    proj(lhs, hk, Qn, 4, 128)
                    qk_norm(4, 128, bufA, "bufA", kfQ, "kfQ", st4[0], st4[1], "sq")
                proj(lhs, hk, Kn, 2, 128)
                qk_norm(2, 128, bufB, "bufB", kfK, "kfK", st4[2], st4[3], "sk")
                for (p0, p1, dst) in it.get("out", []):
                    dma("sp", dst[0], kfK[p0:p1, :], ("kfK",), (("o", "K", g),), ("stK", g))

            def stageB(it):
                if it["kind"] == "M":
                    tr4(kfQ, "kfQ", 128, QTr[it["qslot"]], ("QT", it["qslot"]))
                tr4(kfK, "kfK", 128, KTr[it["kslot"]], ("KT", it["kslot"]))

            def stageA2(it):
                proj(it["_lhs"], it["_hk"], Vn, 3, 128)
                ks = it["kslot"]
                P.add("act", lambda e: e.copy(out=Vr[ks], in_=ps[3][:, :]), r=("ps3",), w=(("V", ks),))
                outs = it.get("out", [])
                if outs:
                    P.add("dve", lambda e: e.tensor_copy(out=vf[:], in_=ps[3][:, :]), r=("ps3",), w=("vf",))
                    for (p0, p1, dst) in outs:
                        dma("sp", dst[1], vf[p0:p1, :], ("vf",), (("o", "V", g),), ("stV", g))

            def stageC(it):
                qs, ks = it["qslot"], it["kslot"]
                for hp in range(2):
                    bank = 6 if hp == 0 else 0
                    stv = ps[bank][:, :].rearrange("p (a h q) -> p a h q", a=2, h=2)

                    def f(e, hp=hp, stv=stv):
                        ins = None
                        for hh in range(2):
                            h = hp * 2 + hh
                            for (pit, c0, c1) in it["prev"]:
                                ins = e.matmul(out=stv[:, 0, hh, c0:c1], lhsT=KTr[pit["kslot"]][:, h, :], rhs=QTr[qs][:, h, c0:c1],
                                               start=True, stop=True)
                            ins = e.matmul(out=stv[:, 1, hh, :], lhsT=KTr[ks][:, h, :], rhs=QTr[qs][:, h, :], start=True, stop=True)
                        return ins
                    rk = [("QT", qs), ("KT", ks)] + [("KT", pit["kslot"]) for (pit, _, _) in it["prev"]]
                    P.add("pe", f, r=tuple(rk), w=("ps%d" % bank,))
                    P.add("act", lambda e, bank=bank: e.activation(out=Ebuf.rearrange("p a h q -> p (a h q)"), in_=ps[bank][:, :],
                                                                   func=AF.Exp, scale=SCALE), r=("ps%d" % bank,), w=("Ebuf",))
                    pt = PTb[hp]
                    if g == 2:
                        ebc = EBc2[:, hp * 2:hp * 2 + 2, :]
                        ebp = EBp2[:, hp * 2:hp * 2 + 2, :]
                        ekeys = ("EBc2", "EBp2")
                    else:
                        ebc = EB[:, g * 4 + hp * 2:g * 4 + hp * 2 + 2, 0, :]
                        ebp = EB[:, g * 4 + hp * 2:g * 4 + hp * 2 + 2, 1, :]
                        ekeys = ("EB",)
                    P.add("dve", lambda e, pt=pt, ebc=ebc: e.tensor_tensor(out=pt[:, 1, :, :], in0=Ebuf[:, 1, :, :], in1=ebc, op=ALU.mult),
                          r=("Ebuf",) + ekeys, w=(("PT", hp),))
                    for (pit, c0, c1) in it["prev"]:
                        if pit["kind"] == "H":
                            vc = pit["vcol"]
                            P.add("dve", lambda e, pt=pt, ebp=ebp, c0=c0, c1=c1, vc=vc: e.scalar_tensor_tensor(
                                out=pt[:, 0, :, c0:c1], in0=Ebuf[:, 0, :, c0:c1], scalar=kval[:, vc:vc + 1], in1=ebp[:, :, c0:c1],
                                op0=ALU.mult, op1=ALU.mult), r=("Ebuf", "kval") + ekeys, w=(("PT", hp),))
                        else:
                            P.add("dve", lambda e, pt=pt, ebp=ebp, c0=c0, c1=c1: e.tensor_tensor(
                                out=pt[:, 0, :, c0:c1], in0=Ebuf[:, 0, :, c0:c1], in1=ebp[:, :, c0:c1], op=ALU.mult),
                                r=("Ebuf",) + ekeys, w=(("PT", hp),))

            def stageD(it):
                ks = it["kslot"]
                kind, *pp = it["pos"]
                for hp in range(2):
                    bank = 7 if hp == 0 else 1
                    od = ps[bank][:, :].rearrange("p (a h q) -> p a h q", a=2, h=2)
                    pt = PTb[hp]

                    def f(e, hp=hp, od=od, pt=pt):
                        ins = None
                        for hh in range(2):
                            h = hp * 2 + hh
                            ins = e.matmul(out=od[:, 0, hh, :], lhsT=Vr[ks][:, h * 128:(h + 1) * 128], rhs=pt[:, 1, hh, :],
                                           start=True, stop=False)
                            np_ = len(it["prev"])
                            for j, (pit, c0, c1) in enumerate(it["prev"]):
                                ins = e.matmul(out=od[:, 0, hh, c0:c1], lhsT=Vr[pit["kslot"]][:, h * 128:(h + 1) * 128],
                                               rhs=pt[:, 0, hh, c0:c1], start=False, stop=(j == np_ - 1))
                        ins = e.matmul(out=od[:, 1, :, :], lhsT=onesb[:], rhs=pt[:, 1, :, :], start=True, stop=False)
                        ins = e.matmul(out=od[:, 1, :, :], lhsT=onesb[:], rhs=pt[:, 0, :, :], start=False, stop=True)
                        return ins
                    rk = [("PT", hp), ("V", ks), "onesb"] + [("V", pit["kslot"]) for (pit, _, _) in it["prev"]]
                    P.add("pe", f, r=tuple(rk), w=("ps%d" % bank,))
                    for a, acc, akey in ((0, accO, "accO"), (1, accD, "accD")):
                        hs = slice(hp * 2, hp * 2 + 2)
                        if kind == "c":
                            i = pp[0]
                            dst = acc[:, hs, i * 128:(i + 1) * 128]
                            src = od[:, a, :, :]
                        elif kind == "s4":
                            s, r = pp
                            dst = acc[:, hs, s * 512 + r:s * 512 + 512:4]
                            src = od[:, a, :, :]
                        else:
                            T = pp[0]
                            dst = acc[:, hs, :].rearrange("p h (m r) -> p h r m", r=16)[:, :, 2 * T:2 * T + 2, :]
                            src = od[:, a, :, :].rearrange("p h (r m) -> p h r m", r=2)
                        if first_group[0]:
                            P.add("dve", lambda e, dst=dst, src=src: e.tensor_copy(out=dst, in_=src),
                                  r=("ps%d" % bank,), w=(akey,))
                        else:
                            P.add("dve", lambda e, dst=dst, src=src: e.tensor_tensor(out=dst, in0=dst, in1=src, op=ALU.add),
                                  r=("ps%d" % bank, akey), w=(akey,))

            for n in range(n_items + 2):
                if 0 <= n - 2 < n_items and items[n - 2]["kind"] == "M":
                    stageC(items[n - 2])
                if 0 <= n - 1 < n_items:
                    stageB(items[n - 1])
                if n < n_items:
                    stageA1(items[n])
                if n < n_items:
                    stageA2(items[n])
                if 0 <= n - 2 < n_items and items[n - 2]["kind"] == "M":
                    stageD(items[n - 2])
            sl = lambda c: hT[:, c, NT:NT + TS]
            proj(sl, (("hT", 8),), Qn, 4, TS)
            qk_norm(4, TS, bufA, "bufA", kfQ, "kfQ", st4[0], st4[1], "sq")
            proj(sl, (("hT", 8),), Kn, 2, TS)
            qk_norm(2, TS, bufB, "bufB", kfK, "kfK", st4[2], st4[3], "sk")
            dma("sp", kvs[g, 0], kfK[0:TS, :], ("kfK",), (("o", "Ks", g),), ("stKs", g))
            tr4(kfQ, "kfQ", TS, QTr[0][:, :, 0:TS], ("QT", 0))
            tr4(kfK, "kfK", TS, KTr[4][:, :, 0:TS], ("KT", 4))
            proj(sl, (("hT", 8),), Vn, 3, TS)
            P.add("act", lambda e: e.copy(out=Vr[4][0:TS, :], in_=ps[3][0:TS, :]), r=("ps3",), w=(("V", 4),))
            P.add("dve", lambda e: e.tensor_copy(out=vf[0:TS, :], in_=ps[3][0:TS, :]), r=("ps3",), w=("vf",))
            dma("sp", kvs[g, 1], vf[0:TS, :], ("vf",), (("o", "Vs", g),), ("stVs", g))
            ntile = 1 if g == 0 else 4
            for t in range(ntile):
                rows = slice(0, 128) if g == 0 else slice(t, dil * 128, dil)
                dma("sp", xt[:, t * 512:(t + 1) * 512], ck[g][0, rows, :], (), ("xt",), "xt")
                tr4(xt[:, t * 512:(t + 1) * 512], "xt", 128, KTr[t], ("KT", t))
                dma("sp", kfQ[:, :], ck[g][1, rows, :], (), ("kfQ",), "ckv")
                P.add("dve", lambda e, t=t: e.tensor_copy(out=Vr[t], in_=kfQ[:, :]), r=("kfQ",), w=(("V", t),))
            sc = ps[6]
            od = ps[7]

            def fsc(e):
                ins = None
                for h in range(4):
                    if g == 0:
                        ins = e.matmul(out=sc[:, h * 4:(h + 1) * 4], lhsT=KTr[0][:, h, :], rhs=QTr[0][:, h, 0:TS], start=True, stop=True)
                    else:
                        for t in range(4):
                            ins = e.matmul(out=sc[:, h * 4 + t:h * 4 + t + 1], lhsT=KTr[t][:, h, :], rhs=QTr[0][:, h, t:t + 1],
                                           start=True, stop=True)
                    ins = e.matmul(out=sc[0:TS, 16 + h * 4:16 + (h + 1) * 4], lhsT=KTr[4][:, h, 0:TS], rhs=QTr[0][:, h, 0:TS],
                                   start=True, stop=True)
                return ins
            P.add("pe", fsc, r=(("QT", 0), ("KT", 4)) + tuple(("KT", t) for t in range(ntile)), w=("ps6",))
            Es = kfK[:, 0:32]
            PTs = PTb[0][:].rearrange("p a h q -> p (a h q)")[:, 0:32]
            P.add("act", lambda e: e.activation(out=Es[:, 0:16], in_=sc[:, 0:16], func=AF.Exp, scale=SCALE), r=("ps6",), w=("kfK",))
            P.add("act", lambda e: e.activation(out=Es[0:TS, 16:32], in_=sc[0:TS, 16:32], func=AF.Exp, scale=SCALE), r=("ps6",), w=("kfK",))
            E3 = Es[:, 0:16].rearrange("p (h t) -> p h t", h=4)
            P3 = PTs[:, 0:16].rearrange("p (h t) -> p h t", h=4)
            En = Es[0:TS, 16:32].rearrange("p (h t) -> p h t", h=4)
            Pn = PTs[0:TS, 16:32].rearrange("p (h t) -> p h t", h=4)
            if g == 0:
                ebc_s = EB[:, 0:4, 1, 0:TS]
            else:
                ebc_s = bc_free(EB[:, g * 4:(g + 1) * 4, 1, 0], TS)
            P.add("dve", lambda e: e.tensor_tensor(out=P3, in0=E3, in1=ebc_s, op=ALU.mult), r=("kfK", "EB"), w=(("PT", 0),))
            ebn_s = EB[0:TS, g * 4:(g + 1) * 4, 0, 0:TS]
            if g == 0:
                P.add("dve", lambda e: e.tensor_tensor(out=Pn, in0=En, in1=ebn_s, op=ALU.mult), r=("kfK", "EB"), w=(("PT", 0),))
            else:
                P.add("dve", lambda e: e.tensor_tensor(out=En, in0=En, in1=ebn_s, op=ALU.mult), r=("kfK", "EB"), w=("kfK",))
                idb = bass.AP(tensor=identf[:].tensor, offset=identf[:].offset, ap=[list(identf[:].ap[0][:1]) + [TS], [0, 4], [1, TS]])
                P.add("dve", lambda e: e.tensor_tensor(out=Pn, in0=En, in1=idb, op=ALU.mult), r=("kfK", "identf"), w=(("PT", 0),))

            def fpv(e):
                ins = None
                for h in range(4):
                    hc = slice(h * 128, (h + 1) * 128)
                    ins = e.matmul(out=od[:, h * 4:(h + 1) * 4], lhsT=Vr[4][0:TS, hc], rhs=PTs[0:TS, 16 + h * 4:16 + (h + 1) * 4],
                                   start=True, stop=False)
                    if g == 0:
                        ins = e.matmul(out=od[:, h * 4:(h + 1) * 4], lhsT=Vr[0][:, hc], rhs=PTs[:, h * 4:(h + 1) * 4], start=False, stop=True)
                    else:
                        for t in range(4):
                            ins = e.matmul(out=od[:, h * 4 + t:h * 4 + t + 1], lhsT=Vr[t][:, hc], rhs=PTs[:, h * 4 + t:h * 4 + t + 1],
                                           start=False, stop=(t == 3))
                ins = e.matmul(out=od[:, 16:32], lhsT=onesb[0:TS, :], rhs=PTs[0:TS, 16:32], start=True, stop=False)
                ins = e.matmul(out=od[:, 16:32], lhsT=onesb[:, :], rhs=PTs[:, 0:16], start=False, stop=True)
                return ins
            P.add("pe", fpv, r=(("PT", 0), ("V", 4), "onesb") + tuple(("V", t) for t in range(ntile)), w=("ps7",))
            for a, acc, akey in ((0, accOs, "accOs"), (1, accDs, "accDs")):
                dst = acc[:].rearrange("p h t -> p (h t)")
                src = od[:, a * 16:(a + 1) * 16]
                if first_group[0]:
                    P.add("dve", lambda e, dst=dst, src=src: e.tensor_copy(out=dst, in_=src), r=("ps7",), w=(akey,))
                else:
                    P.add("dve", lambda e, dst=dst, src=src: e.tensor_tensor(out=dst, in0=dst, in1=src, op=ALU.add),
                          r=("ps7", akey), w=(akey,))
            first_group[0] = False
            return items

        group_items = {}
        for g in (2, 1, 0):
            if STOP <= 3 + (2 - g):
                return
            group_items[g] = run_group(g)
            for _ in range(3):
                wload_next()

        if STOP <= 6:
            return

        dma("sp", bufA[:], bcast_rows(gvn, 128, 512), (), ("bufA",), "c5")
        dma("sp", bufB[:], bass.AP(tensor=gb.tensor, offset=0, ap=[[0, 128], [1, 512]]), (), ("bufB",), "c6")
        slotU, wU = W("U")
        bk = [2, 3, 4]
        bi = 0
        for gg in range(4):
            for th in range(3):
                bank = bk[bi % 3]; bi += 1
                if th < 2:
                    c0, c1, n = th * 512, th * 512 + 512, 512
                    rk = tuple(("hT", 4 * th + k) for k in range(4))
                    dst = uT[:, gg, c0:c1]
                    dkey = "uT"
                else:
                    c0, c1, n = NT, NT + TS, TS
                    rk = (("hT", 8),)
                    dst = uTs[:, gg, :]
                    dkey = "uTs"

                def f(e, gg=gg, c0=c0, c1=c1, n=n, bank=bank):
                    ins = None
                    for c in range(16):
                        ins = e.matmul(out=ps[bank][:, 0:n], lhsT=wU[:, c, gg * 128:(gg + 1) * 128], rhs=hT[:, c, c0:c1],
                                       start=(c == 0), stop=(c == 15))
                    return ins
                P.add("pe", f, r=rk + (("ring", slotU),), w=("ps%d" % bank,))
                P.add("act", lambda e, dst=dst, n=n, bank=bank: e.activation(out=dst, in_=ps[bank][:, 0:n], func=AF.Gelu),
                      r=("ps%d" % bank,), w=(dkey, "EB", "EBc2", "EBp2") if th < 2 else (dkey,))
        gtile = Vr[0]
        for i in range(9):
            npart = 128 if i < 8 else TS
            cols = slice(i * 128, (i + 1) * 128) if i < 8 else slice(NT, NT + TS)
            proj(lambda c, cols=cols: hT[:, c, cols], (("hT", i),), "G", 2, npart)
            P.add("act", lambda e, npart=npart: e.activation(out=kfQ[0:npart, :], in_=ps[2][0:npart, :], func=AF.Gelu),
                  r=("ps2",), w=("kfQ",))
            P.add("act", lambda e, npart=npart: e.activation(out=vf[0:npart, :], in_=kfQ[0:npart, :], func=AF.Square,
                                                             accum_out=st_ssq[0:npart, :]), r=("kfQ",), w=("vf", "st_ssq"))
            P.add("act", lambda e, npart=npart: e.activation(out=st_rs[0:npart, :], in_=st_ssq[0:npart, :], func=AF.Sqrt,
                                                             scale=1.0 / 512, bias=EPS), r=("st_ssq",), w=("st_rs",))
            P.add("dve", lambda e, npart=npart: e.reciprocal(out=st_rs[0:npart, :], in_=st_rs[0:npart, :]), r=("st_rs",), w=("st_rs",))
            if i < 8:
                P.add("dve", lambda e: e.scalar_tensor_tensor(out=gtile, in0=kfQ[:, :], scalar=st_rs[:, 0:1], in1=bufA[:, :],
                                                              op0=ALU.mult, op1=ALU.mult), r=("kfQ", "st_rs", "bufA"), w=(("V", 0),))
            else:
                P.add("dve", lambda e: e.scalar_tensor_tensor(out=vf[0:TS, :], in0=kfQ[0:TS, :], scalar=st_rs[0:TS, 0:1], in1=bufA[0:TS, :],
                                                              op0=ALU.mult, op1=ALU.mult), r=("kfQ", "st_rs", "bufA"), w=("vf",))
                dma("sp", gvs[:, :], vf[0:TS, :], ("vf",), (("o", "G"),), "stG")
                P.add("dve", lambda e: e.tensor_copy(out=gtile[0:TS, :], in_=vf[0:TS, :]), r=("vf",), w=(("V", 0),))
            nq = npart

            def fm(e, npart=npart):
                ins = None
                for gg in range(4):
                    ins = e.matmul(out=ps[5][:, gg * 128:gg * 128 + npart], lhsT=gtile[0:npart, gg * 128:(gg + 1) * 128],
                                   rhs=WmT[0:npart, gg, 0:npart], start=True, stop=True)
                return ins
            P.add("pe", fm, r=(("V", 0), "WmT"), w=("ps5",))
            mixv = ps[5][:, :].rearrange("p (g t) -> p g t", g=4)[:, :, 0:npart]
            bbv = bufB[:, :].rearrange("p (g t) -> p g t", g=4)[:, :, 0:npart]
            tmpv = kfK[:, :].rearrange("p (g t) -> p g t", g=4)[:, :, 0:npart]
            P.add("dve", lambda e, mixv=mixv, bbv=bbv, tmpv=tmpv: e.tensor_tensor(out=tmpv, in0=mixv, in1=bbv, op=ALU.add),
                  r=("ps5", "bufB"), w=("kfK",))
            uv = uT[:, :, cols] if i < 8 else uTs[:, :, :]
            P.add("dve", lambda e, tmpv=tmpv, uv=uv, cols=cols: e.tensor_tensor(out=catT[:, 4:8, cols], in0=tmpv, in1=uv, op=ALU.mult),
                  r=("kfK", "uT", "uTs"), w=(("cat", i),))
        wload_next(); wload_next()

        if STOP <= 7:
            return

        for h in range(4):
            P.add("dve", lambda e, h=h: e.reciprocal(out=accD[:, h, :], in_=accD[:, h, :]), r=("accD",), w=("accD",))
            P.add("dve", lambda e, h=h: e.tensor_tensor(out=catT[:, h, 0:NT], in0=accO[:, h, :], in1=accD[:, h, :], op=ALU.mult),
                  r=("accO", "accD"), w=tuple(("cat", i) for i in range(8)))
        P.add("dve", lambda e: e.reciprocal(out=accDs[:], in_=accDs[:]), r=("accDs",), w=("accDs",))
        P.add("dve", lambda e: e.tensor_tensor(out=catT[:, 0:4, NT:NT + TS], in0=accOs[:], in1=accDs[:], op=ALU.mult),
              r=("accOs", "accDs"), w=(("cat", 8),))
        P.barrier()

        if STOP <= 8:
            return

        def outproj(i):
            npart = 128 if i < 8 else TS
            cols = slice(i * 128, (i + 1) * 128) if i < 8 else slice(NT, NT + TS)
            src = xm[i * 128:(i + 1) * 128, :] if i < 8 else xs[:, :]
            dma("sp", xt[0:npart, :], src, (), ("xt",), "xt")
            for cb in range(4):
                slot, wv = W("O%d" % (cb // 2))
                bank = 2 + cb

                def f(e, wv=wv, cb=cb, bank=bank):
                    ins = None
                    for c in range(8):
                        ins = e.matmul(out=ps[bank][0:npart, :], lhsT=catT[:, c, cols], rhs=wv[:, c, (cb % 2) * 512:(cb % 2) * 512 + 512],
                                       start=(c == 0), stop=(c == 7))
                    return ins
                P.add("pe", f, r=(("cat", i), ("ring", slot)), w=("ps%d" % bank,))
                dst = x1[:, i, cb * 512:(cb + 1) * 512] if i < 8 else xt[0:TS, cb * 512:(cb + 1) * 512]
                P.add("dve", lambda e, dst=dst, bank=bank, cb=cb: e.tensor_tensor(out=dst, in0=ps[bank][0:npart, :],
                                                                              in1=xt[0:npart, cb * 512:(cb + 1) * 512], op=ALU.add),
                      r=("ps%d" % bank, "xt"), w=(("x1", i),) if i < 8 else ("xt",))
            if i < 8:
                norm_tile(None, 128, gffn, lambda c0, c1: hT[:, c0:c1, cols], (("hT", i),), load=False,
                          src_sb=x1[:, i, :], src_keys=(("x1", i),))
            else:
                norm_tile(None, TS, gffn, lambda c0, c1: hT[:, c0:c1, cols], (("hT", 8),), load=False,
                          src_sb=xt[0:TS, :], src_keys=("xt",))

        for i in range(9):
            outproj(i)
        P.barrier()
        wload_next(); wload_next()

        if STOP <= 9:
            return

        upb = [0]
        dnb = [0]

        def up(n):
            slot, wv = W("UP%d" % n)
            for fc in range(4):
                for th in range(3):
                    bank = upb[0] % 4; upb[0] += 1
                    if th < 2:
                        c0, c1, nn = th * 512, th * 512 + 512, 512
                        rk = tuple(("hT", 4 * th + k) for k in range(4))
                        dst = aT[n % 2][:, fc, c0:c1]
                        dkey = ("aT", n % 2)
                    else:
                        c0, c1, nn = NT, NT + TS, TS
                        rk = (("hT", 8),)
                        dst = aTs[:, n % 2, fc, :]
                        dkey = ("aTs", n % 2)

                    def f(e, fc=fc, c0=c0, c1=c1, nn=nn, bank=bank):
                        ins = None
                        for c in range(16):
                            ins = e.matmul(out=ps[bank][:, 0:nn], lhsT=wv[:, c, fc * 128:(fc + 1) * 128], rhs=hT[:, c, c0:c1],
                                           start=(c == 0), stop=(c == 15))
                        return ins
                    P.add("pe", f, r=rk + (("ring", slot),), w=("ps%d" % bank,))
                    rs_ = rsc[bank % 2]
                    P.add("act", lambda e, nn=nn, bank=bank, rs_=rs_: e.activation(out=rs_[:, 0:nn], in_=ps[bank][:, 0:nn], func=AF.Relu),
                          r=("ps%d" % bank,), w=(("rsc", bank % 2),))
                    P.add("act", lambda e, nn=nn, rs_=rs_, dst=dst: e.activation(out=dst, in_=rs_[:, 0:nn], func=AF.Square),
                          r=(("rsc", bank % 2),), w=(dkey,))

        def down(n):
            slot, wv = W("DN%d" % n)
            for i in range(9):
                npart = 128 if i < 8 else TS
                for cb in range(4):
                    bank = 4 + dnb[0] % 4; dnb[0] += 1

                    def f(e, i=i, cb=cb, bank=bank, npart=npart):
                        ins = None
                        for fc in range(4):
                            lhsT = aT[n % 2][:, fc, i * 128:(i + 1) * 128] if i < 8 else aTs[:, n % 2, fc, :]
                            ins = e.matmul(out=ps[bank][0:npart, :], lhsT=lhsT, rhs=wv[:, fc, cb * 512:(cb + 1) * 512],
                                           start=(fc == 0), stop=(fc == 3))
                        return ins
                    rk = (("aT", n % 2), ("ring", slot)) if i < 8 else (("aTs", n % 2), ("ring", slot))
                    P.add("pe", f, r=rk, w=("ps%d" % bank,))
                    dst = x1[:, i, cb * 512:(cb + 1) * 512] if i < 8 else xt[0:TS, cb * 512:(cb + 1) * 512]
                    key = ("x1", i) if i < 8 else "xt"
                    P.add("dve", lambda e, dst=dst, bank=bank, npart=npart: e.tensor_tensor(out=dst, in0=ps[bank][0:npart, :], in1=dst, op=ALU.add),
                          r=("ps%d" % bank, key), w=(key,))

        for n in range(17):
            if n < 16:
                up(n)
                if n >= 1:
                    pass
            if n >= 1:
                down(n - 1)
                wload_next(); wload_next()
        for i in range(8):
            dma("sp", y[i * 128:(i + 1) * 128, :], x1[:, i, :], (("x1", i),), (("o", "y", i),), ("sty", i))
        dma("sp", ys[:, :], xt[0:TS, :], ("xt",), (("o", "ys"),), "stys")

    phases()
    if DBG:
        P.barrier()
        dma("sp", dbg_rx[:, :], RX[:, :], (), (("o", "dbgrx"),), "dbg0")
        dma("sp", dbg_cat[:, :], catT[:].rearrange("p c n -> p (c n)"), (), (("o", "dbgcat"),), "dbg1")
        dma("sp", dbg_hT[:, :], hT[:].rearrange("p c n -> p (c n)"), (), (("o", "dbghT"),), "dbg2")
    P.barrier()
    P.emit(nc, es)
    es.close()
    return nc


_CACHE = {}


def kernel(x_prompt, x_sample, cache_kv_w128, cache_kv_w512, cache_kv_w2048, norm_mix, w_in,
           q_norm, k_norm, rel_bias, gmlp_v_norm, gmlp_w, gmlp_b, w_out, norm_ffn, w_up, w_down):
    f = lambda a: np.ascontiguousarray(np.asarray(a, dtype=np.float32))
    x_prompt = f(x_prompt); x_sample = f(x_sample)
    caches = [f(cache_kv_w128), f(cache_kv_w512), f(cache_kv_w2048)]
    if "nc" not in _CACHE:
        _CACHE["nc"] = build_program()
    nc = _CACHE["nc"]
    oh, ident, jm = host_constants()
    shared = {
        "w_in": f(w_in)[0], "w_out": f(w_out)[0], "w_up": f(w_up)[0], "w_down": f(w_down)[0],
        "norm_mix": f(norm_mix).reshape(16, 128), "norm_ffn": f(norm_ffn).reshape(16, 128),
        "q_norm": f(q_norm).reshape(1, 128), "k_norm": f(k_norm).reshape(1, 128),
        "rel_bias": f(rel_bias), "gmlp_v_norm": f(gmlp_v_norm).reshape(1, 512),
        "gmlp_w": f(gmlp_w)[0], "gmlp_b": f(gmlp_b)[0], "oh": oh, "ident": ident, "jm": jm,
    }
    in_maps = []
    for c in range(8):
        b, j = c // 4, c % 4
        q0 = j * NT
        xh = np.zeros((NH, D), np.float32)
        valid = np.zeros((NH,), np.float32)
        lo = q0 - NH
        s = max(lo, 0)
        if q0 > 0:
            xh[s - lo:] = x_prompt[b, s:q0]
            valid[s - lo:] = 1.0
        kv = np.zeros((128, 21), np.float32)
        kv[:, 0] = valid[1920:2048]
        for r in range(4):
            kv[:, 1 + r] = valid[1536 + r:2048:4]
        for r in range(16):
            kv[:, 5 + r] = valid[r:2048:16]
        m = dict(shared)
        m["xm"] = np.ascontiguousarray(x_prompt[b, q0:q0 + NT])
        m["xh"] = xh
        m["xs"] = np.ascontiguousarray(x_sample[c])
        m["kvalid"] = kv
        for g in range(3):
            m["ck%d" % g] = np.ascontiguousarray(caches[g][0, c].reshape(2, -1, 512))
        in_maps.append(m)
    res = run_bass_kernel_spmd(nc, in_maps, core_ids=list(range(8)))
    R = res.results
    yp = np.zeros((2, 4096, D), np.float32)
    ysm = np.zeros((8, TS, D), np.float32)
    kvp_out = [np.zeros((1, 2, 2, w, 4, 128), np.float32) for w in (128, 512, 2048)]
    kvs_out = [np.zeros((1, 8, 2, TS, 4, 128), np.float32) for _ in range(3)]
    gv = np.zeros((1, 8, TS, 512), np.float32)
    for c in range(8):
        b, j = c // 4, c % 4
        yp[b, j * NT:(j + 1) * NT] = R[c]["y"]
        ysm[c] = R[c]["ys"]
        if j == 3:
            kvp_out[0][0, b] = R[c]["kvp0"].reshape(2, 128, 4, 128)
            kvp_out[1][0, b] = R[c]["kvp1"].reshape(2, 512, 4, 128)
        if j >= 2:
            kvp_out[2][0, b, :, (j - 2) * 1024:(j - 1) * 1024] = R[c]["kvp2"].reshape(2, 1024, 4, 128)
        for g in range(3):
            kvs_out[g][0, c] = R[c]["kvs"][g].reshape(2, TS, 4, 128)
        gv[0, c] = R[c]["gvs"]
    return (yp, ysm, kvp_out[0], kvp_out[1], kvp_out[2], kvs_out[0], kvs_out[1], kvs_out[2], gv)
```

```python
import contextlib
import os
import numpy as np
import concourse.bass as bass
import concourse.mybir as mybir
from concourse.bass_utils import run_bass_kernel_spmd

F32 = mybir.dt.float32
BF16 = mybir.dt.bfloat16
AF = mybir.ActivationFunctionType
ALU = mybir.AluOpType
AX = mybir.AxisListType

D = 2048
NT = 1024
NH = 2048
TS = 4
NCOL = NT + TS
DIN = 5632
DFF = 8192
EPS = 1e-6
DILS = (1, 4, 16)
SCALE = 128 ** -0.5
SAME_ENG_SYNC = os.environ.get('MK_SES', '1') == '1'

ENGS = ("pe", "act", "dve", "pool", "sp")


class Op:
    __slots__ = ("eng", "fn", "deps", "dma", "signal", "seq", "target")


class Prog:
    def __init__(self):
        self.ops = []
        self.lw = {}
        self.rd = {}
        self.dcnt = {}
        self.last = {}
        self.auto_r = ()

    def add(self, eng, fn, r=(), w=(), dma=None, extra=()):
        idx = len(self.ops)
        deps = set(extra)
        r = tuple(r) + tuple(self.auto_r)
        pr = tuple(k for k in r if isinstance(k, str) and k[:2] == 'ps' and k[2:].isdigit())
        if pr:
            r = tuple(k for k in r if k not in pr)
            w = tuple(w) + pr
        for k in r:
            if k in self.lw:
                deps.add(self.lw[k])
        for k in w:
            if k in self.lw:
                deps.add(self.lw[k])
            deps.update(self.rd.get(k, ()))
        op = Op()
        op.eng, op.fn, op.deps, op.dma, op.signal, op.seq, op.target = eng, fn, deps, dma, False, 0, 0
        if dma is not None:
            c = self.dcnt.get(dma, 0) + 16
            self.dcnt[dma] = c
            op.target = c
        for k in r:
            self.rd.setdefault(k, []).append(idx)
        for k in w:
            self.lw[k] = idx
            self.rd[k] = []
        self.ops.append(op)
        if fn is not None:
            self.last[eng] = idx
        return idx

    def barrier(self):
        alld = set(self.last.values())
        for k, v in self.lw.items():
            alld.add(v)
        for e in ENGS:
            self.add(e, None, extra=tuple(alld))

    def emit(self, nc, es):
        limit = int(os.environ.get('MK_NOPS', '0'))
        if limit:
            self.ops = self.ops[:limit]
            for e in ENGS:
                op = Op()
                op.eng, op.fn, op.deps, op.dma, op.signal, op.seq, op.target = e, None, set(range(limit)), None, False, 0, 0
                self.ops.append(op)
        ops = self.ops
        if os.environ.get('MK_DUMP'):
            for i, op in enumerate(ops):
                print(i, op.eng, op.dma, sorted(op.deps)[-6:], getattr(op.fn, '__name__', None))
        for op in ops:
            op.deps = set(d for d in op.deps if ops[d].fn is not None)
            for d in op.deps:
                if ops[d].dma is None:
                    ops[d].signal = True
        cnt = {e: 0 for e in ENGS}
        for op in ops:
            if op.dma is None and op.signal:
                cnt[op.eng] += 1
                op.seq = cnt[op.eng]
        esem = {e: es.enter_context(nc.semaphore("sem_" + e)) for e in ENGS}
        dsem = {}
        for i, k in enumerate(self.dcnt):
            dsem[k] = es.enter_context(nc.semaphore("dsem%d" % i))
        block = es.enter_context(nc.Block())
        reg = {"pe": block.tensor, "act": block.scalar, "dve": block.vector,
               "pool": block.gpsimd, "sp": block.sync}
        for e in ENGS:
            mine = [op for op in ops if op.eng == e]

            def body(eng, e=e, mine=mine):
                waited = {}
                for op in mine:
                    need = {}
                    for d in op.deps:
                        p = ops[d]
                        if p.dma is not None:
                            s, v = dsem[p.dma], p.target
                        else:
                            if p.eng == e and (e == "pe" or not SAME_ENG_SYNC):
                                continue
                            s, v = esem[p.eng], p.seq
                        key = id(s)
                        if key not in need or need[key][1] < v:
                            need[key] = (s, v)
                    for key, (s, v) in need.items():
                        if waited.get(key, 0) < v:
                            eng.wait_ge(s, v)
                            waited[key] = v
                    if op.fn is not None:
                        ins = op.fn(eng)
                        if op.dma is not None:
                            ins.then_inc(dsem[op.dma], 16)
                        elif op.signal:
                            ins.then_inc(esem[e], 1)
            reg[e](body)


def bucket(dist):
    if dist < 16:
        return dist
    v = 16 + int(np.float32(np.log(np.float32(dist) / np.float32(16)) / np.float32(np.log(128.0)) * np.float32(16)))
    return min(v, 31)


def t5_bucket_np(dist):
    import math
    d = np.maximum(dist, 1).astype(np.float32)
    rnd = np.rint if os.environ.get('MK_BUCKET_RINT', '0') == '1' else np.trunc
    large = 16 + rnd(np.log(d / np.float32(16)) / np.float32(math.log(2048 / 16)) * np.float32(16)).astype(np.int32)
    large = np.minimum(large, 31)
    return np.where(dist < 16, dist, large)


def host_constants():
    oh = np.zeros((33, 3, 384), np.float32)
    for g, dil in enumerate(DILS):
        sub = np.arange(384) - 128
        ok = (sub >= 0) & (sub <= 128)
        b = t5_bucket_np(np.clip(sub, 0, 128).astype(np.int32) * dil)
        for u in range(384):
            if ok[u]:
                oh[b[u], g, u] = 1.0
            else:
                oh[32, g, u] = 1.0
    ident = np.eye(128, dtype=np.float32)
    jm = np.ascontiguousarray(ident[::-1])
    return oh.reshape(33, 1152), ident, jm


def build_program():
    nc = bass.Bass("TRN2", target_bir_lowering=False)

    def din(name, shape):
        return nc.dram_tensor(name, shape, F32, kind="ExternalInput").ap()

    def dout(name, shape):
        return nc.dram_tensor(name, shape, F32, kind="ExternalOutput").ap()

    xm = din("xm", [NT, D]); xh = din("xh", [NH, D]); xs = din("xs", [TS, D])
    kvalid = din("kvalid", [128, 21])
    ck = [din("ck0", [2, 128, 512]), din("ck1", [2, 512, 512]), din("ck2", [2, 2048, 512])]
    w_in = din("w_in", [D, DIN]); w_out = din("w_out", [1024, D])
    w_up = din("w_up", [D, DFF]); w_down = din("w_down", [DFF, D])
    norm_mix = din("norm_mix", [16, 128]); norm_ffn = din("norm_ffn", [16, 128])
    q_norm = din("q_norm", [1, 128]); k_norm = din("k_norm", [1, 128])
    rel_bias = din("rel_bias", [32, 12]); gvn = din("gmlp_v_norm", [1, 512])
    gw = din("gmlp_w", [4, 128, 128]); gb = din("gmlp_b", [4, 128])
    oh_d = din("oh", [33, 1152]); ident_d = din("ident", [128, 128]); jm_d = din("jm", [128, 128])

    y = dout("y", [NT, D]); ys = dout("ys", [TS, D])
    kvp = [dout("kvp0", [2, 128, 512]), dout("kvp1", [2, 512, 512]), dout("kvp2", [2, 1024, 512])]
    kvs = dout("kvs", [3, 2, TS, 512]); gvs = dout("gvs", [TS, 512])
    escr = dout("escr", [12, 384])
    DBG = os.environ.get('MK_DBG')
    if DBG:
        dbg_rx = dout("dbg_rx", [128, 16384])
        dbg_cat = nc.dram_tensor("dbg_cat", [128, 8 * NCOL], BF16, kind="ExternalOutput").ap()
        dbg_hT = nc.dram_tensor("dbg_hT", [128, 16 * NCOL], BF16, kind="ExternalOutput").ap()

    P = Prog()
    es = contextlib.ExitStack()
    STOP = int(os.environ.get('MK_STOP', '99'))
    SUB = int(os.environ.get('MK_SUB', '255'))

    def sb(name, shape, dt):
        return es.enter_context(nc.sbuf_tensor(name, shape, dt))

    hT = sb("hT", [128, 16, NCOL], BF16)
    ring = [sb("ring%d" % i, [128, 8192], BF16) for i in range(4)]
    RX = sb("RX", [128, 16384], F32)
    catT = sb("catT", [128, 8, NCOL], BF16)
    xt = sb("xt", [128, D], F32)
    xb = sb("xb", [128, D], BF16)
    hTh = sb("hTh", [128, 16, 128], BF16)
    kfK = sb("kfK", [128, 512], F32)
    vf = sb("vf", [128, 512], F32)
    PTb = [sb("PT%d" % i, [128, 2, 2, 128], BF16) for i in range(2)]
    identb = sb("identb", [128, 128], BF16)
    identf = sb("identf", [128, 128], F32)
    onesb = sb("onesb", [128, 128], BF16)
    gmix = sb("gmix", [128, 16], F32)
    gffn = sb("gffn", [128, 16], F32)
    bufA = sb("bufA", [128, 512], F32)
    bufB = sb("bufB", [128, 512], F32)
    WmT = sb("WmT", [128, 4, 128], BF16)
    kval = sb("kval", [128, 21], F32)
    st_ssq = sb("st_ssq", [128, 1], F32)
    st_rs = sb("st_rs", [128, 1], F32)
    st4 = [sb("st4_%d" % i, [128, 4], F32) for i in range(4)]
    accOs = sb("accOs", [128, 4, TS], F32)
    accDs = sb("accDs", [128, 4, TS], F32)
    uTs = sb("uTs", [128, 4, TS], F32)
    ps = [es.enter_context(nc.psum_tensor("ps%d" % i, [128, 512], F32)) for i in range(8)]

    accO = RX[:, 0:4096].rearrange("p (h t) -> p h t", h=4)
    accD = RX[:, 4096:8192].rearrange("p (h t) -> p h t", h=4)
    EB = RX[:, 8192:11264].rearrange("p (g a q) -> p g a q", g=12, a=2)
    EBc2 = RX[:, 11264:11776].rearrange("p (h q) -> p h q", h=4)
    EBp2 = RX[:, 11776:12288].rearrange("p (h q) -> p h q", h=4)
    uT = RX[:, 8192:12288].rearrange("p (h t) -> p h t", h=4)
    rest = RX[:, 12288:16384]
    KTr = [rest[:, i * 256:(i + 1) * 256].bitcast(BF16).rearrange("p (h k) -> p h k", h=4) for i in range(5)]
    Vr = [rest[:, 1280 + i * 256:1280 + (i + 1) * 256].bitcast(BF16) for i in range(5)]
    QTr = [rest[:, 2560 + i * 256:2560 + (i + 1) * 256].bitcast(BF16).rearrange("p (h k) -> p h k", h=4) for i in range(3)]
    Ebuf = rest[:, 3328:3840].rearrange("p (a h q) -> p a h q", a=2, h=2)
    kfQ = RX[:, 12288 + 3840 - 512:12288 + 3840]
    kfQ = sb("kfQ", [128, 512], F32)
    x1 = RX[:, :].rearrange("p (i d) -> p i d", i=8)
    catb = catT[:].rearrange("p c n -> p (c n)")
    xt2 = catb[:, 0:4096].bitcast(F32)
    hTh2 = catb[:, 4096:6144].rearrange("p (c n) -> p c n", c=16)
    kfK2 = catb[:, 6144:7168].bitcast(F32)
    kfQ2 = catb[:, 7168:8192].bitcast(F32)
    XT = [(xt, "xt"), (xt2, "xt2")]
    XT2_OK = [True]
    HTH = [(hTh, "hTh"), (hTh2, "hTh2")]
    KFK = [(kfK, "kfK"), (kfK2, "kfK2")]
    KFQ = [(kfQ, "kfQ"), (kfQ2, "kfQ2")]
    hTf = RX
    oh_sb = hTf[:, 0:1152]
    Eall = hTf[:, 1152:2304].rearrange("p (g u) -> p g u", g=3)
    jm_sb = hTf[:, 2304:2432]
    Hh = hTf[:, 2432:2432 + 3072].rearrange("p (g a q) -> p g a q", g=12, a=2)
    tab33 = hTf[:, 5504:5516]
    wtmp = hTf[:, 5632:5632 + 512].rearrange("p (g s) -> p g s", g=4)
    vtmp = hTf[:, 6144:6144 + 128]
    aT = [catT[:].rearrange("p c n -> p (c n)")[:, i * 4096:(i + 1) * 4096].rearrange("p (f t) -> p f t", f=4) for i in range(2)]
    aTs = sb("aTs", [128, 2, 4, TS], BF16)
    rsc = [xb[:, i * 1024:(i + 1) * 1024].bitcast(F32) for i in range(2)]

    psb = [p[:].bitcast(BF16) for p in ps]

    def dma(q, out, in_, r, w, key):
        return P.add(q, lambda e, out=out, in_=in_: e.dma_start(out=out, in_=in_), r=r, w=w, dma=key)

    def bcast_rows(dram_ap_row, nparts, n):
        return bass.AP(tensor=dram_ap_row.tensor, offset=dram_ap_row.offset, ap=[[0, nparts], [1, n]])

    def bc_free(ap2, n):
        a = ap2.ap
        return bass.AP(tensor=ap2.tensor, offset=ap2.offset, ap=[list(a[0]), list(a[1]), [0, n]])

    wblocks = []

    def wview_rows(w, col0, ncols):
        return w.rearrange("(c p) n -> p c n", p=128)[:, :, col0:col0 + ncols]

    for g in (2, 1, 0):
        wblocks.append(("K%d" % g, wview_rows(w_in, 1536 + g * 512, 512), (16, 512)))
        wblocks.append(("V%d" % g, wview_rows(w_in, 3072 + g * 512, 512), (16, 512)))
        wblocks.append(("Q%d" % g, wview_rows(w_in, g * 512, 512), (16, 512)))
    wblocks.append(("U", wview_rows(w_in, 4608, 512), (16, 512)))
    wblocks.append(("G", wview_rows(w_in, 5120, 512), (16, 512)))
    wblocks.append(("O0", wview_rows(w_out, 0, 1024), (8, 1024)))
    wblocks.append(("O1", wview_rows(w_out, 1024, 1024), (8, 1024)))
    for n in range(16):
        wblocks.append(("UP%d" % n, wview_rows(w_up, n * 512, 512), (16, 512)))
        wblocks.append(("DN%d" % n, w_down[n * 512:(n + 1) * 512, :].rearrange("(c p) n -> p c n", p=128), (4, 2048)))
    wslot = {}
    wnext = [0]

    def wload_next():
        i = wnext[0]
        if i >= len(wblocks):
            return
        name, src, (c, n) = wblocks[i]
        slot = i % 4
        dst = ring[slot][:].rearrange("p (c n) -> p c n", c=c)
        wslot[name] = (slot, dst)
        dma("pool", dst, src, r=(), w=(("ring", slot),), key=("ring", slot))
        wnext[0] += 1

    def W(name):
        return wslot[name]

    def phases():
        P.auto_r = ("accO", "accD")
        dma("sp", identf[:], ident_d[:, :], (), ("identf",), "c0")
        dma("sp", jm_sb, jm_d[:, :], (), ("jm",), "c1")
        dma("sp", oh_sb[0:33, :], oh_d[:, :], (), ("oh",), "c2")
        dma("sp", tab33[0:32, :], rel_bias[:, :], (), ("tab",), "c3")
        dma("sp", kval[:], kvalid[:, :], (), ("kval",), "c4")
        dma("sp", bufA[:], bass.AP(tensor=q_norm.tensor, offset=0, ap=[[0, 128], [0, 4], [1, 128]]), (), ("bufA",), "c5")
        dma("sp", bufB[:], bass.AP(tensor=k_norm.tensor, offset=0, ap=[[0, 128], [0, 4], [1, 128]]), (), ("bufB",), "c6")
        dma("sp", vtmp[0:16, :], norm_mix[:, :], (), ("vtmp",), "c7")
        for _ in range(4):
            wload_next()
        P.add("dve", lambda e: e.tensor_copy(out=identb[:], in_=identf[:]), r=("identf",), w=("identb",))
        P.add("dve", lambda e: e.memset(onesb[:], 1.0), w=("onesb",))
        P.add("dve", lambda e: e.memset(tab33[32:33, :], -30000.0), w=("tab32",))
        if SUB & 1:
            P.add("pe", lambda e: e.transpose(out=ps[7][:, 0:16], in_=vtmp[0:16, :], identity=identf[0:16, 0:16]),
                  r=("vtmp", "identf"), w=("ps7",))
            P.add("act", lambda e: e.copy(out=gmix[:], in_=ps[7][:, 0:16]), r=("ps7",), w=("gmix",))
            dma("sp", vtmp[0:16, :], norm_ffn[:, :], (), ("vtmp",), "c7")
            P.add("pe", lambda e: e.transpose(out=ps[7][:, 0:16], in_=vtmp[0:16, :], identity=identf[0:16, 0:16]),
                  r=("vtmp", "identf"), w=("ps7",))
            P.add("act", lambda e: e.copy(out=gffn[:], in_=ps[7][:, 0:16]), r=("ps7",), w=("gffn",))
        if SUB & 2:
            dma("sp", wtmp, gw.rearrange("g t s -> t g s"), (), ("wtmp",), "c8")
            for gg in range(4):
                P.add("pool", lambda e, gg=gg: e.affine_select(out=wtmp[:, gg, :], in_=wtmp[:, gg, :], pattern=[[-1, 128]],
                                                               compare_op=ALU.is_ge, fill=0.0, base=0, channel_multiplier=1),
                      r=("wtmp",), w=("wtmp",))
            for gg in range(4):
                P.add("pe", lambda e, gg=gg: e.transpose(out=ps[6][:, gg * 128:(gg + 1) * 128], in_=wtmp[:, gg, :], identity=identf[:]),
                      r=("wtmp", "identf"), w=("ps6",))
            P.add("act", lambda e: e.copy(out=WmT[:].rearrange("p g t -> p (g t)"), in_=ps[6][:, :]), r=("ps6",), w=("WmT",))
        if SUB & 4:
            for g in range(3):
                P.add("pe", lambda e, g=g: e.matmul(out=ps[g][0:12, 0:384], lhsT=tab33[0:33, 0:12], rhs=oh_sb[0:33, g * 384:(g + 1) * 384],
                                                    start=True, stop=True),
                      r=("tab", "tab32", "oh"), w=("ps%d" % g,))
                P.add("act", lambda e, g=g: e.activation(out=Eall[0:12, g, :], in_=ps[g][0:12, 0:384], func=AF.Exp),
                      r=("ps%d" % g,), w=("Eall%d" % g,))
                dma("sp", escr[g * 4:(g + 1) * 4, :], Eall[g * 4:(g + 1) * 4, g, :], ("Eall%d" % g,), ("escr%d" % g,), "e%d" % g)
            for gh in range(12):
                src = bass.AP(tensor=escr.tensor, offset=gh * 384 + 1, ap=[[1, 128], [128, 2], [1, 128]])
                dma("sp", Hh[:, gh, :, :], src, ("escr%d" % (gh // 4),), ("Hh%d" % gh,), "h%d" % gh)
            Hf = Hh.rearrange("p g a q -> p (g a q)")
            EBf = EB.rearrange("p g a q -> p (g a q)")
            for i in range(6):
                P.add("pe", lambda e, i=i: e.matmul(out=ps[i][:, :], lhsT=jm_sb, rhs=Hf[:, i * 512:(i + 1) * 512], start=True, stop=True),
                      r=("jm", "Hh%d" % (2 * i), "Hh%d" % (2 * i + 1)), w=("ps%d" % i,))
                P.add("dve" if i % 2 else "act",
                      (lambda e, i=i: e.tensor_copy(out=EBf[:, i * 512:(i + 1) * 512], in_=ps[i][:, :])) if i % 2 else
                      (lambda e, i=i: e.copy(out=EBf[:, i * 512:(i + 1) * 512], in_=ps[i][:, :])),
                      r=("ps%d" % i,), w=("EB",))
        if SUB & 8:
            P.add("dve", lambda e: e.tensor_copy(out=EBc2, in_=EB[:, 8:12, 0, :]), r=("EB",), w=("EBc2",))
            P.add("dve", lambda e: e.memset(EBc2[0:64, :, 64:128], 0.0), w=("EBc2",))
            P.add("dve", lambda e: e.tensor_copy(out=EBp2[:, :, 0:64], in_=EB[:, 8:12, 1, 0:64]), r=("EB",), w=("EBp2",))
            P.add("dve", lambda e: e.tensor_copy(out=EBp2[:, :, 64:128], in_=EB[:, 8:12, 1, 0:64]), r=("EB",), w=("EBp2",))
        P.auto_r = ()
        if STOP <= 1:
            return

        xsel = [0]

        def norm_tile(x_src, npart, gains, dst_fn, dst_keys, xkey_extra=(), load=True, src_sb=None, src_keys=(), defer=False):
            if load:
                xbuf, xkey = XT[xsel[0] % 2] if XT2_OK[0] else XT[0]
                xsel[0] += 1
                dma("sp", xbuf[0:npart, :], x_src, (), (xkey,), xkey)
                src = xbuf[0:npart, :]
                skeys = (xkey,)
            else:
                src = src_sb
                skeys = tuple(src_keys)
            P.add("act", lambda e: e.activation(out=xb[0:npart, :], in_=src, func=AF.Square, accum_out=st_ssq[0:npart, :]),
                  r=skeys, w=("xb", "st_ssq"))
            P.add("act", lambda e: e.activation(out=st_rs[0:npart, :], in_=st_ssq[0:npart, :], func=AF.Sqrt, scale=1.0 / D, bias=EPS),
                  r=("st_ssq",), w=("st_rs",))
            P.add("dve", lambda e: e.reciprocal(out=st_rs[0:npart, :], in_=st_rs[0:npart, :]), r=("st_rs",), w=("st_rs",))
            P.add("dve", lambda e: e.tensor_scalar(out=xb[0:npart, :], in0=src, scalar1=st_rs[0:npart, 0:1], scalar2=None, op0=ALU.mult),
                  r=skeys + ("st_rs",), w=("xb",))
            def part2():
              for half in range(2):
                def tp(e, half=half):
                    ins = None
                    for c in range(8):
                        cc = half * 8 + c
                        ins = e.transpose(out=psb[half][:, c * 128:c * 128 + npart], in_=xb[0:npart, cc * 128:(cc + 1) * 128],
                                          identity=identb[0:npart, 0:npart])
                    return ins
                P.add("pe", tp, r=("xb", "identb"), w=("ps%d" % half,))
                src_ps = psb[half][:, :].rearrange("p (c n) -> p c n", c=8)[:, :, 0:npart]
                P.add("dve", lambda e, half=half, src_ps=src_ps: e.tensor_tensor(
                    out=dst_fn(half * 8, half * 8 + 8), in0=src_ps, in1=bc_free(gains[:, half * 8:half * 8 + 8], npart), op=ALU.mult),
                    r=("ps%d" % half, "gmix", "gffn"), w=tuple(dst_keys))
            if defer:
                return part2
            part2()

        for i in range(8):
            norm_tile(xm[i * 128:(i + 1) * 128, :], 128, gmix, lambda c0, c1, i=i: hT[:, c0:c1, i * 128:(i + 1) * 128], (("hT", i),))
        norm_tile(xs[:, :], TS, gmix, lambda c0, c1: hT[:, c0:c1, NT:NT + TS], (("hT", 8),))

        if STOP <= 2:
            return

        def proj(lhs_fn, rkeys, wname, bank, npart):
            slot, wv = W(wname)

            def f(e):
                ins = None
                for c in range(16):
                    ins = e.matmul(out=ps[bank][0:npart, :], lhsT=lhs_fn(c), rhs=wv[:, c, :], start=(c == 0), stop=(c == 15))
                return ins
            P.add("pe", f, r=tuple(rkeys) + (("ring", slot),), w=("ps%d" % bank,))

        def qk_norm(bank, npart, gainbuf, gkey, kf, kfkey, ssq, rs, skey):
            P.add("act", lambda e: e.activation(out=vf[0:npart, :], in_=ps[bank][0:npart, :], func=AF.Square),
                  r=("ps%d" % bank,), w=("vf",))
            P.add("dve", lambda e: e.tensor_reduce(out=ssq[0:npart, :], in_=vf[0:npart, :].rearrange("p (h d) -> p h d", h=4),
                                                   axis=AX.X, op=ALU.add), r=("vf",), w=(skey,))
            P.add("act", lambda e: e.activation(out=rs[0:npart, :], in_=ssq[0:npart, :], func=AF.Sqrt, scale=1.0 / 128, bias=EPS),
                  r=(skey,), w=(skey + "r",))
            P.add("dve", lambda e: e.reciprocal(out=rs[0:npart, :], in_=rs[0:npart, :]), r=(skey + "r",), w=(skey + "r",))
            for h in range(4):
                P.add("dve", lambda e, h=h: e.scalar_tensor_tensor(
                    out=kf[0:npart, h * 128:(h + 1) * 128], in0=ps[bank][0:npart, h * 128:(h + 1) * 128],
                    scalar=rs[0:npart, h:h + 1], in1=gainbuf[0:npart, h * 128:(h + 1) * 128], op0=ALU.mult, op1=ALU.mult),
                    r=("ps%d" % bank, skey + "r", gkey), w=(kfkey,))

        def tr4(kf, kfkey, npart, dst, dkey):
            def f(e):
                ins = None
                for h in range(4):
                    ins = e.transpose(out=ps[5][:, h * 128:h * 128 + npart], in_=kf[0:npart, h * 128:(h + 1) * 128],
                                      identity=identf[0:npart, 0:npart])
                return ins
            P.add("pe", f, r=(kfkey, "identf"), w=("ps5",))
            P.add("act", lambda e: e.copy(out=dst, in_=ps[5][:, :].rearrange("p (h k) -> p h k", h=4)[:, :, 0:npart]),
                  r=("ps5",), w=(dkey,))

        first_group = [True]

        def run_group(g):
            dil = DILS[g]
            items = []
            if g == 0:
                items.append(dict(kind="H", rows=xh[1920:2048, :], vcol=0))
                for i in range(8):
                    items.append(dict(kind="M", lhs=(lambda c, i=i: hT[:, c, i * 128:(i + 1) * 128]), hkeys=(("hT", i),),
                                      pos=("c", i), out=(7 == i and [(0, 128, kvp[0][:, 0:128, :])] or [])))
            elif g == 1:
                for r in range(4):
                    items.append(dict(kind="H", rows=xh[1536 + r:2048:4, :], vcol=1 + r))
                    for s in range(2):
                        items.append(dict(kind="M", lhs=(lambda c, s=s, r=r: hT[:, c, s * 512 + r:s * 512 + 512:4]),
                                          hkeys=tuple(("hT", 4 * s + k) for k in range(4)), pos=("s4", s, r),
                                          out=(s == 1 and [(0, 128, kvp[1][:, r:512:4, :])] or [])))
            else:
                for T in range(8):
                    items.append(dict(kind="H", rows=xh[2 * T:2048:16, :], vcol=5 + 2 * T))
                    items.append(dict(kind="H", rows=xh[2 * T + 1:2048:16, :], vcol=5 + 2 * T + 1))
                    items.append(dict(kind="M", lhs=(lambda c: hTh[:, c, :]), gather=T,
                                      hkeys=("hTh",), pos=("s16", T),
                                      out=[(0, 64, kvp[2][:, 2 * T:1024:16, :]), (64, 128, kvp[2][:, 2 * T + 1:1024:16, :])]))
            n_items = len(items)
            for idx, it in enumerate(items):
                it["idx"] = idx
                it["kslot"] = idx % 5
            qcount = [0]
            for it in items:
                if it["kind"] == "M":
                    it["qslot"] = qcount[0] % 3
                    it["qbuf"] = qcount[0] % 2
                    qcount[0] += 1
            for idx, it in enumerate(items):
                if it["kind"] != "M":
                    continue
                if g == 2:
                    it["prev"] = [(items[idx - 2], 0, 64), (items[idx - 1], 64, 128)]
                else:
                    it["prev"] = [(items[idx - 1], 0, 128)]

            Kn, Vn, Qn = "K%d" % g, "V%d" % g, "Q%d" % g

            hsel = [0]

            def stageN(it):
                if it["kind"] == "H" or "gather" in it:
                    hb, hkey = HTH[hsel[0] % 2]
                    hsel[0] += 1
                    it["_n2"] = None
                    if it["kind"] == "H":
                        it["_n2"] = norm_tile(it["rows"], 128, gmix, lambda c0, c1, hb=hb: hb[:, c0:c1, :], (hkey,), defer=True)
                    else:
                        T = it["gather"]
                        srcv = hT[:, :, 0:NT].rearrange("p c (m r) -> p c r m", r=16)
                        for rr in range(2):
                            P.add("dve", lambda e, T=T, rr=rr, srcv=srcv, hb=hb: e.tensor_copy(out=hb[:, :, rr * 64:(rr + 1) * 64],
                                                                                             in_=srcv[:, :, 2 * T + rr, :]),
                                  r=tuple(("hT", k) for k in range(8)), w=(hkey,))
                    it["_lhs"], it["_hk"] = (lambda c, hb=hb: hb[:, c, :]), (hkey,)
                else:
                    it["_lhs"], it["_hk"] = it["lhs"], it["hkeys"]

            def stageA1(it):
                lhs, hk = it["_lhs"], it["_hk"]
                kK, kKkey = KFK[it["idx"] % 2]
                it["_kfK"] = (kK, kKkey)
                if it["kind"] == "M":
                    kQ, kQkey = KFQ[it["qbuf"]]
                    it["_kfQ"] = (kQ, kQkey)
                    proj(lhs, hk, Qn, 4, 128)
                    qk_norm(4, 128, bufA, "bufA", kQ, kQkey, st4[0], st4[1], "sq")
                proj(lhs, hk, Kn, 2, 128)
                qk_norm(2, 128, bufB, "bufB", kK, kKkey, st4[2], st4[3], "sk")
                for (p0, p1, dst) in it.get("out", []):
                    dma("sp", dst[0], kK[p0:p1, :], (kKkey,), (("o", "K", g),), ("stK", g, it["idx"] % 2))

            def stageB(it):
                if it["kind"] == "M":
                    kQ, kQkey = it["_kfQ"]
                    tr4(kQ, kQkey, 128, QTr[it["qslot"]], ("QT", it["qslot"]))
                kK, kKkey = it["_kfK"]
                tr4(kK, kKkey, 128, KTr[it["kslot"]], ("KT", it["kslot"]))

            def stageA2(it):
                proj(it["_lhs"], it["_hk"], Vn, 3, 128)
                ks = it["kslot"]
                P.add("act", lambda e: e.copy(out=Vr[ks], in_=ps[3][:, :]), r=("ps3",), w=(("V", ks),))
                outs = it.get("out", [])
                if outs:
                    P.add("dve", lambda e: e.tensor_copy(out=vf[:], in_=ps[3][:, :]), r=("ps3",), w=("vf",))
                    for (p0, p1, dst) in outs:
                        dma("sp", dst[1], vf[p0:p1, :], ("vf",), (("o", "V", g),), ("stV", g))

            def stageC(it):
                qs, ks = it["qslot"], it["kslot"]
                for hp in range(2):
                    bank = 6 if hp == 0 else 0
                    stv = ps[bank][:, :].rearrange("p (a h q) -> p a h q", a=2, h=2)

                    def f(e, hp=hp, stv=stv):
                        ins = None
                        for hh in range(2):
                            h = hp * 2 + hh
                            for (pit, c0, c1) in it["prev"]:
                                ins = e.matmul(out=stv[:, 0, hh, c0:c1], lhsT=KTr[pit["kslot"]][:, h, :], rhs=QTr[qs][:, h, c0:c1],
                                               start=True, stop=True)
                            ins = e.matmul(out=stv[:, 1, hh, :], lhsT=KTr[ks][:, h, :], rhs=QTr[qs][:, h, :], start=True, stop=True)
                        return ins
                    rk = [("QT", qs), ("KT", ks)] + [("KT", pit["kslot"]) for (pit, _, _) in it["prev"]]
                    P.add("pe", f, r=tuple(rk), w=("ps%d" % bank,))
                    P.add("act", lambda e, bank=bank: e.activation(out=Ebuf.rearrange("p a h q -> p (a h q)"), in_=ps[bank][:, :],
                                                                   func=AF.Exp, scale=SCALE), r=("ps%d" % bank,), w=("Ebuf",))
                    pt = PTb[hp]
                    if g == 2:
                        ebc = EBc2[:, hp * 2:hp * 2 + 2, :]
                        ebp = EBp2[:, hp * 2:hp * 2 + 2, :]
                        ekeys = ("EBc2", "EBp2")
                    else:
                        ebc = EB[:, g * 4 + hp * 2:g * 4 + hp * 2 + 2, 0, :]
                        ebp = EB[:, g * 4 + hp * 2:g * 4 + hp * 2 + 2, 1, :]
                        ekeys = ("EB",)
                    P.add("dve", lambda e, pt=pt, ebc=ebc: e.tensor_tensor(out=pt[:, 1, :, :], in0=Ebuf[:, 1, :, :], in1=ebc, op=ALU.mult),
                          r=("Ebuf",) + ekeys, w=(("PT", hp),))
                    for (pit, c0, c1) in it["prev"]:
                        if pit["kind"] == "H":
                            vc = pit["vcol"]
                            P.add("dve", lambda e, pt=pt, ebp=ebp, c0=c0, c1=c1, vc=vc: e.scalar_tensor_tensor(
                                out=pt[:, 0, :, c0:c1], in0=Ebuf[:, 0, :, c0:c1], scalar=kval[:, vc:vc + 1], in1=ebp[:, :, c0:c1],
                                op0=ALU.mult, op1=ALU.mult), r=("Ebuf", "kval") + ekeys, w=(("PT", hp),))
                        else:
                            P.add("dve", lambda e, pt=pt, ebp=ebp, c0=c0, c1=c1: e.tensor_tensor(
                                out=pt[:, 0, :, c0:c1], in0=Ebuf[:, 0, :, c0:c1], in1=ebp[:, :, c0:c1], op=ALU.mult),
                                r=("Ebuf",) + ekeys, w=(("PT", hp),))

            def stageD(it):
                ks = it["kslot"]
                kind, *pp = it["pos"]
                for hp in range(2):
                    bank = 7 if hp == 0 else 1
                    od = ps[bank][:, :].rearrange("p (a h q) -> p a h q", a=2, h=2)
                    pt = PTb[hp]

                    def f(e, hp=hp, od=od, pt=pt):
                        ins = None
                        for hh in range(2):
                            h = hp * 2 + hh
                            ins = e.matmul(out=od[:, 0, hh, :], lhsT=Vr[ks][:, h * 128:(h + 1) * 128], rhs=pt[:, 1, hh, :],
                                           start=True, stop=False)
                            np_ = len(it["prev"])
                            for j, (pit, c0, c1) in enumerate(it["prev"]):
                                ins = e.matmul(out=od[:, 0, hh, c0:c1], lhsT=Vr[pit["kslot"]][:, h * 128:(h + 1) * 128],
                                               rhs=pt[:, 0, hh, c0:c1], start=False, stop=(j == np_ - 1))
                        ins = e.matmul(out=od[:, 1, :, :], lhsT=onesb[:], rhs=pt[:, 1, :, :], start=True, stop=False)
                        ins = e.matmul(out=od[:, 1, :, :], lhsT=onesb[:], rhs=pt[:, 0, :, :], start=False, stop=True)
                        return ins
                    rk = [("PT", hp), ("V", ks), "onesb"] + [("V", pit["kslot"]) for (pit, _, _) in it["prev"]]
                    P.add("pe", f, r=tuple(rk), w=("ps%d" % bank,))
                    for a, acc, akey in ((0, accO, "accO"), (1, accD, "accD")):
                        hs = slice(hp * 2, hp * 2 + 2)
                        if kind == "c":
                            i = pp[0]
                            dst = acc[:, hs, i * 128:(i + 1) * 128]
                            src = od[:, a, :, :]
                        elif kind == "s4":
                            s, r = pp
                            dst = acc[:, hs, s * 512 + r:s * 512 + 512:4]
                            src = od[:, a, :, :]
                        else:
                            T = pp[0]
                            dst = acc[:, hs, :].rearrange("p h (m r) -> p h r m", r=16)[:, :, 2 * T:2 * T + 2, :]
                            src = od[:, a, :, :].rearrange("p h (r m) -> p h r m", r=2)
                        if first_group[0]:
                            P.add("dve", lambda e, dst=dst, src=src: e.tensor_copy(out=dst, in_=src),
                                  r=("ps%d" % bank,), w=(akey,))
                        else:
                            P.add("dve", lambda e, dst=dst, src=src: e.tensor_tensor(out=dst, in0=dst, in1=src, op=ALU.add),
                                  r=("ps%d" % bank, akey), w=(akey,))

            def stageN2(it):
                f2 = it.get("_n2")
                if f2 is not None:
                    f2()
            stageN(items[0])
            stageN2(items[0])
            for n in range(n_items + 2):
                if 0 <= n - 2 < n_items and items[n - 2]["kind"] == "M":
                    stageC(items[n - 2])
                if n + 1 < n_items:
                    stageN(items[n + 1])
                if n < n_items:
                    stageA1(items[n])
                if 0 <= n - 1 < n_items:
                    stageB(items[n - 1])
                if n + 1 < n_items:
                    stageN2(items[n + 1])
                if n < n_items:
                    stageA2(items[n])
                if 0 <= n - 2 < n_items and items[n - 2]["kind"] == "M":
                    stageD(items[n - 2])
            sl = lambda c: hT[:, c, NT:NT + TS]
            proj(sl, (("hT", 8),), Qn, 4, TS)
            qk_norm(4, TS, bufA, "bufA", kfQ, "kfQ", st4[0], st4[1], "sq")
            proj(sl, (("hT", 8),), Kn, 2, TS)
            qk_norm(2, TS, bufB, "bufB", kfK, "kfK", st4[2], st4[3], "sk")
            dma("sp", kvs[g, 0], kfK[0:TS, :], ("kfK",), (("o", "Ks", g),), ("stKs", g))
            tr4(kfQ, "kfQ", TS, QTr[0][:, :, 0:TS], ("QT", 0))
            tr4(kfK, "kfK", TS, KTr[4][:, :, 0:TS], ("KT", 4))
            proj(sl, (("hT", 8),), Vn, 3, TS)
            P.add("act", lambda e: e.copy(out=Vr[4][0:TS, :], in_=ps[3][0:TS, :]), r=("ps3",), w=(("V", 4),))
            P.add("dve", lambda e: e.tensor_copy(out=vf[0:TS, :], in_=ps[3][0:TS, :]), r=("ps3",), w=("vf",))
            dma("sp", kvs[g, 1], vf[0:TS, :], ("vf",), (("o", "Vs", g),), ("stVs", g))
            ntile = 1 if g == 0 else 4
            for t in range(ntile):
                rows = slice(0, 128) if g == 0 else slice(t, dil * 128, dil)
                dma("sp", xt[:, t * 512:(t + 1) * 512], ck[g][0, rows, :], (), ("xt",), "xt")
                tr4(xt[:, t * 512:(t + 1) * 512], "xt", 128, KTr[t], ("KT", t))
                dma("sp", kfQ[:, :], ck[g][1, rows, :], (), ("kfQ",), "ckv")
                P.add("dve", lambda e, t=t: e.tensor_copy(out=Vr[t], in_=kfQ[:, :]), r=("kfQ",), w=(("V", t),))
            sc = ps[6]
            od = ps[7]

            def fsc(e):
                ins = None
                for h in range(4):
                    if g == 0:
                        ins = e.matmul(out=sc[:, h * 4:(h + 1) * 4], lhsT=KTr[0][:, h, :], rhs=QTr[0][:, h, 0:TS], start=True, stop=True)
                    else:
                        for t in range(4):
                            ins = e.matmul(out=sc[:, h * 4 + t:h * 4 + t + 1], lhsT=KTr[t][:, h, :], rhs=QTr[0][:, h, t:t + 1],
                                           start=True, stop=True)
                    ins = e.matmul(out=sc[0:TS, 16 + h * 4:16 + (h + 1) * 4], lhsT=KTr[4][:, h, 0:TS], rhs=QTr[0][:, h, 0:TS],
                                   start=True, stop=True)
                return ins
            P.add("pe", fsc, r=(("QT", 0), ("KT", 4)) + tuple(("KT", t) for t in range(ntile)), w=("ps6",))
            Es = kfK[:, 0:32]
            PTs = PTb[0][:].rearrange("p a h q -> p (a h q)")[:, 0:32]
            P.add("act", lambda e: e.activation(out=Es[:, 0:16], in_=sc[:, 0:16], func=AF.Exp, scale=SCALE), r=("ps6",), w=("kfK",))
            P.add("act", lambda e: e.activation(out=Es[0:TS, 16:32], in_=sc[0:TS, 16:32], func=AF.Exp, scale=SCALE), r=("ps6",), w=("kfK",))
            E3 = Es[:, 0:16].rearrange("p (h t) -> p h t", h=4)
            P3 = PTs[:, 0:16].rearrange("p (h t) -> p h t", h=4)
            En = Es[0:TS, 16:32].rearrange("p (h t) -> p h t", h=4)
            Pn = PTs[0:TS, 16:32].rearrange("p (h t) -> p h t", h=4)
            if g == 0:
                ebc_s = EB[:, 0:4, 1, 0:TS]
            else:
                ebc_s = bc_free(EB[:, g * 4:(g + 1) * 4, 1, 0], TS)
            P.add("dve", lambda e: e.tensor_tensor(out=P3, in0=E3, in1=ebc_s, op=ALU.mult), r=("kfK", "EB"), w=(("PT", 0),))
            ebn_s = EB[0:TS, g * 4:(g + 1) * 4, 0, 0:TS]
            if g == 0:
                P.add("dve", lambda e: e.tensor_tensor(out=Pn, in0=En, in1=ebn_s, op=ALU.mult), r=("kfK", "EB"), w=(("PT", 0),))
            else:
                P.add("dve", lambda e: e.tensor_tensor(out=En, in0=En, in1=ebn_s, op=ALU.mult), r=("kfK", "EB"), w=("kfK",))
                idb = bass.AP(tensor=identf[:].tensor, offset=identf[:].offset, ap=[list(identf[:].ap[0][:1]) + [TS], [0, 4], [1, TS]])
                P.add("dve", lambda e: e.tensor_tensor(out=Pn, in0=En, in1=idb, op=ALU.mult), r=("kfK", "identf"), w=(("PT", 0),))

            def fpv(e):
                ins = None
                for h in range(4):
                    hc = slice(h * 128, (h + 1) * 128)
                    ins = e.matmul(out=od[:, h * 4:(h + 1) * 4], lhsT=Vr[4][0:TS, hc], rhs=PTs[0:TS, 16 + h * 4:16 + (h + 1) * 4],
                                   start=True, stop=False)
                    if g == 0:
                        ins = e.matmul(out=od[:, h * 4:(h + 1) * 4], lhsT=Vr[0][:, hc], rhs=PTs[:, h * 4:(h + 1) * 4], start=False, stop=True)
                    else:
                        for t in range(4):
                            ins = e.matmul(out=od[:, h * 4 + t:h * 4 + t + 1], lhsT=Vr[t][:, hc], rhs=PTs[:, h * 4 + t:h * 4 + t + 1],
                                           start=False, stop=(t == 3))
                ins = e.matmul(out=od[:, 16:32], lhsT=onesb[0:TS, :], rhs=PTs[0:TS, 16:32], start=True, stop=False)
                ins = e.matmul(out=od[:, 16:32], lhsT=onesb[:, :], rhs=PTs[:, 0:16], start=False, stop=True)
                return ins
            P.add("pe", fpv, r=(("PT", 0), ("V", 4), "onesb") + tuple(("V", t) for t in range(ntile)), w=("ps7",))
            for a, acc, akey in ((0, accOs, "accOs"), (1, accDs, "accDs")):
                dst = acc[:].rearrange("p h t -> p (h t)")
                src = od[:, a * 16:(a + 1) * 16]
                if first_group[0]:
                    P.add("dve", lambda e, dst=dst, src=src: e.tensor_copy(out=dst, in_=src), r=("ps7",), w=(akey,))
                else:
                    P.add("dve", lambda e, dst=dst, src=src: e.tensor_tensor(out=dst, in0=dst, in1=src, op=ALU.add),
                          r=("ps7", akey), w=(akey,))
            first_group[0] = False
            return items

        group_items = {}
        for g in (2, 1, 0):
            if STOP <= 3 + (2 - g):
                return
            group_items[g] = run_group(g)
            for _ in range(3):
                wload_next()

        if STOP <= 6:
            return

        XT2_OK[0] = False
        P.barrier()
        dma("sp", bufA[:], bcast_rows(gvn, 128, 512), (), ("bufA",), "c5")
        dma("sp", bufB[:], bass.AP(tensor=gb.tensor, offset=0, ap=[[0, 128], [1, 512]]), (), ("bufB",), "c6")
        slotU, wU = W("U")
        bk = [2, 3, 4]
        bi = 0
        for gg in range(4):
            for th in range(3):
                bank = bk[bi % 3]; bi += 1
                if th < 2:
                    c0, c1, n = th * 512, th * 512 + 512, 512
                    rk = tuple(("hT", 4 * th + k) for k in range(4))
                    dst = uT[:, gg, c0:c1]
                    dkey = "uT"
                else:
                    c0, c1, n = NT, NT + TS, TS
                    rk = (("hT", 8),)
                    dst = uTs[:, gg, :]
                    dkey = "uTs"

                def f(e, gg=gg, c0=c0, c1=c1, n=n, bank=bank):
                    ins = None
                    for c in range(16):
                        ins = e.matmul(out=ps[bank][:, 0:n], lhsT=wU[:, c, gg * 128:(gg + 1) * 128], rhs=hT[:, c, c0:c1],
                                       start=(c == 0), stop=(c == 15))
                    return ins
                P.add("pe", f, r=rk + (("ring", slotU),), w=("ps%d" % bank,))
                P.add("act", lambda e, dst=dst, n=n, bank=bank: e.activation(out=dst, in_=ps[bank][:, 0:n], func=AF.Gelu),
                      r=("ps%d" % bank,), w=(dkey, "EB", "EBc2", "EBp2") if th < 2 else (dkey,))
        gtile = Vr[0]
        for i in range(9):
            npart = 128 if i < 8 else TS
            cols = slice(i * 128, (i + 1) * 128) if i < 8 else slice(NT, NT + TS)
            proj(lambda c, cols=cols: hT[:, c, cols], (("hT", i),), "G", 2, npart)
            P.add("act", lambda e, npart=npart: e.activation(out=kfQ[0:npart, :], in_=ps[2][0:npart, :], func=AF.Gelu),
                  r=("ps2",), w=("kfQ",))
            P.add("act", lambda e, npart=npart: e.activation(out=vf[0:npart, :], in_=kfQ[0:npart, :], func=AF.Square,
                                                             accum_out=st_ssq[0:npart, :]), r=("kfQ",), w=("vf", "st_ssq"))
            P.add("act", lambda e, npart=npart: e.activation(out=st_rs[0:npart, :], in_=st_ssq[0:npart, :], func=AF.Sqrt,
                                                             scale=1.0 / 512, bias=EPS), r=("st_ssq",), w=("st_rs",))
            P.add("dve", lambda e, npart=npart: e.reciprocal(out=st_rs[0:npart, :], in_=st_rs[0:npart, :]), r=("st_rs",), w=("st_rs",))
            if i < 8:
                P.add("dve", lambda e: e.scalar_tensor_tensor(out=gtile, in0=kfQ[:, :], scalar=st_rs[:, 0:1], in1=bufA[:, :],
                                                              op0=ALU.mult, op1=ALU.mult), r=("kfQ", "st_rs", "bufA"), w=(("V", 0),))
            else:
                P.add("dve", lambda e: e.scalar_tensor_tensor(out=vf[0:TS, :], in0=kfQ[0:TS, :], scalar=st_rs[0:TS, 0:1], in1=bufA[0:TS, :],
                                                              op0=ALU.mult, op1=ALU.mult), r=("kfQ", "st_rs", "bufA"), w=("vf",))
                dma("sp", gvs[:, :], vf[0:TS, :], ("vf",), (("o", "G"),), "stG")
                P.add("dve", lambda e: e.tensor_copy(out=gtile[0:TS, :], in_=vf[0:TS, :]), r=("vf",), w=(("V", 0),))
            nq = npart

            def fm(e, npart=npart):
                ins = None
                for gg in range(4):
                    ins = e.matmul(out=ps[5][:, gg * 128:gg * 128 + npart], lhsT=gtile[0:npart, gg * 128:(gg + 1) * 128],
                                   rhs=WmT[0:npart, gg, 0:npart], start=True, stop=True)
                return ins
            P.add("pe", fm, r=(("V", 0), "WmT"), w=("ps5",))
            mixv = ps[5][:, :].rearrange("p (g t) -> p g t", g=4)[:, :, 0:npart]
            bbv = bufB[:, :].rearrange("p (g t) -> p g t", g=4)[:, :, 0:npart]
            tmpv = kfK[:, :].rearrange("p (g t) -> p g t", g=4)[:, :, 0:npart]
            P.add("dve", lambda e, mixv=mixv, bbv=bbv, tmpv=tmpv: e.tensor_tensor(out=tmpv, in0=mixv, in1=bbv, op=ALU.add),
                  r=("ps5", "bufB"), w=("kfK",))
            uv = uT[:, :, cols] if i < 8 else uTs[:, :, :]
            P.add("dve", lambda e, tmpv=tmpv, uv=uv, cols=cols: e.tensor_tensor(out=catT[:, 4:8, cols], in0=tmpv, in1=uv, op=ALU.mult),
                  r=("kfK", "uT", "uTs"), w=(("cat", i),))
        wload_next(); wload_next()

        if STOP <= 7:
            return

        for h in range(4):
            P.add("dve", lambda e, h=h: e.reciprocal(out=accD[:, h, :], in_=accD[:, h, :]), r=("accD",), w=("accD",))
            P.add("dve", lambda e, h=h: e.tensor_tensor(out=catT[:, h, 0:NT], in0=accO[:, h, :], in1=accD[:, h, :], op=ALU.mult),
                  r=("accO", "accD"), w=tuple(("cat", i) for i in range(8)))
        P.add("dve", lambda e: e.reciprocal(out=accDs[:], in_=accDs[:]), r=("accDs",), w=("accDs",))
        P.add("dve", lambda e: e.tensor_tensor(out=catT[:, 0:4, NT:NT + TS], in0=accOs[:], in1=accDs[:], op=ALU.mult),
              r=("accOs", "accDs"), w=(("cat", 8),))
        P.barrier()

        if STOP <= 8:
            return

        for i in range(8):
            dma("sp", x1[:, i, :], xm[i * 128:(i + 1) * 128, :], (), (("x1", i),), ("x1l", i))
        dma("sp", xt[0:TS, :], xs[:, :], (), ("xt",), "xt")

        def outproj(i):
            npart = 128 if i < 8 else TS
            cols = slice(i * 128, (i + 1) * 128) if i < 8 else slice(NT, NT + TS)
            for cb in range(4):
                slot, wv = W("O%d" % (cb // 2))
                bank = 2 + cb

                def f(e, wv=wv, cb=cb, bank=bank):
                    ins = None
                    for c in range(8):
                        ins = e.matmul(out=ps[bank][0:npart, :], lhsT=catT[:, c, cols], rhs=wv[:, c, (cb % 2) * 512:(cb % 2) * 512 + 512],
                                       start=(c == 0), stop=(c == 7))
                    return ins
                P.add("pe", f, r=(("cat", i), ("ring", slot)), w=("ps%d" % bank,))
                dst = x1[:, i, cb * 512:(cb + 1) * 512] if i < 8 else xt[0:TS, cb * 512:(cb + 1) * 512]
                key = ("x1", i) if i < 8 else "xt"
                P.add("dve", lambda e, dst=dst, bank=bank: e.tensor_tensor(out=dst, in0=ps[bank][0:npart, :], in1=dst, op=ALU.add),
                      r=("ps%d" % bank, key), w=(key,))

        def norm2(i):
            cols = slice(i * 128, (i + 1) * 128) if i < 8 else slice(NT, NT + TS)
            if i < 8:
                norm_tile(None, 128, gffn, lambda c0, c1: hT[:, c0:c1, cols], (("hT", i),), load=False,
                          src_sb=x1[:, i, :], src_keys=(("x1", i),))
            else:
                norm_tile(None, TS, gffn, lambda c0, c1: hT[:, c0:c1, cols], (("hT", 8),), load=False,
                          src_sb=xt[0:TS, :], src_keys=("xt",))

        outproj(0)
        for i in range(1, 9):
            outproj(i)
            norm2(i - 1)
        norm2(8)
        P.barrier()
        wload_next(); wload_next()

        if STOP <= 9:
            return

        upb = [0]
        dnb = [0]

        def up(n):
            slot, wv = W("UP%d" % n)
            for fc in range(4):
                for th in range(3):
                    bank = upb[0] % 4; upb[0] += 1
                    if th < 2:
                        c0, c1, nn = th * 512, th * 512 + 512, 512
                        rk = tuple(("hT", 4 * th + k) for k in range(4))
                        dst = aT[n % 2][:, fc, c0:c1]
                        dkey = ("aT", n % 2)
                    else:
                        c0, c1, nn = NT, NT + TS, TS
                        rk = (("hT", 8),)
                        dst = aTs[:, n % 2, fc, :]
                        dkey = ("aTs", n % 2)

                    def f(e, fc=fc, c0=c0, c1=c1, nn=nn, bank=bank):
                        ins = None
                        for c in range(16):
                            ins = e.matmul(out=ps[bank][:, 0:nn], lhsT=wv[:, c, fc * 128:(fc + 1) * 128], rhs=hT[:, c, c0:c1],
                                           start=(c == 0), stop=(c == 15))
                        return ins
                    P.add("pe", f, r=rk + (("ring", slot),), w=("ps%d" % bank,))
                    rs_ = rsc[bank % 2]
                    P.add("act", lambda e, nn=nn, bank=bank, rs_=rs_: e.activation(out=rs_[:, 0:nn], in_=ps[bank][:, 0:nn], func=AF.Relu),
                          r=("ps%d" % bank,), w=(("rsc", bank % 2),))
                    P.add("act", lambda e, nn=nn, rs_=rs_, dst=dst: e.activation(out=dst, in_=rs_[:, 0:nn], func=AF.Square),
                          r=(("rsc", bank % 2),), w=(dkey,))

        def down(n):
            slot, wv = W("DN%d" % n)
            for i in range(9):
                npart = 128 if i < 8 else TS
                for cb in range(4):
                    bank = 4 + dnb[0] % 4; dnb[0] += 1

                    def f(e, i=i, cb=cb, bank=bank, npart=npart):
                        ins = None
                        for fc in range(4):
                            lhsT = aT[n % 2][:, fc, i * 128:(i + 1) * 128] if i < 8 else aTs[:, n % 2, fc, :]
                            ins = e.matmul(out=ps[bank][0:npart, :], lhsT=lhsT, rhs=wv[:, fc, cb * 512:(cb + 1) * 512],
                                           start=(fc == 0), stop=(fc == 3))
                        return ins
                    rk = (("aT", n % 2), ("ring", slot)) if i < 8 else (("aTs", n % 2), ("ring", slot))
                    P.add("pe", f, r=rk, w=("ps%d" % bank,))
                    dst = x1[:, i, cb * 512:(cb + 1) * 512] if i < 8 else xt[0:TS, cb * 512:(cb + 1) * 512]
                    key = ("x1", i) if i < 8 else "xt"
                    P.add("dve", lambda e, dst=dst, bank=bank, npart=npart: e.tensor_tensor(out=dst, in0=ps[bank][0:npart, :], in1=dst, op=ALU.add),
                          r=("ps%d" % bank, key), w=(key,))

        for n in range(17):
            if n < 16:
                up(n)
                if n >= 1:
                    pass
            if n >= 1:
                down(n - 1)
                wload_next(); wload_next()
        for i in range(8):
            dma("sp", y[i * 128:(i + 1) * 128, :], x1[:, i, :], (("x1", i),), (("o", "y", i),), ("sty", i))
        dma("sp", ys[:, :], xt[0:TS, :], ("xt",), (("o", "ys"),), "stys")

    phases()
    if DBG:
        P.barrier()
        dma("sp", dbg_rx[:, :], RX[:, :], (), (("o", "dbgrx"),), "dbg0")
        dma("sp", dbg_cat[:, :], catT[:].rearrange("p c n -> p (c n)"), (), (("o", "dbgcat"),), "dbg1")
        dma("sp", dbg_hT[:, :], hT[:].rearrange("p c n -> p (c n)"), (), (("o", "dbghT"),), "dbg2")
    P.barrier()
    P.emit(nc, es)
    es.close()
    return nc


_CACHE = {}


def kernel(x_prompt, x_sample, cache_kv_w128, cache_kv_w512, cache_kv_w2048, norm_mix, w_in,
           q_norm, k_norm, rel_bias, gmlp_v_norm, gmlp_w, gmlp_b, w_out, norm_ffn, w_up, w_down):
    f = lambda a: np.ascontiguousarray(np.asarray(a, dtype=np.float32))
    x_prompt = f(x_prompt); x_sample = f(x_sample)
    caches = [f(cache_kv_w128), f(cache_kv_w512), f(cache_kv_w2048)]
    if "nc" not in _CACHE:
        _CACHE["nc"] = build_program()
    nc = _CACHE["nc"]
    oh, ident, jm = host_constants()
    shared = {
        "w_in": f(w_in)[0], "w_out": f(w_out)[0], "w_up": f(w_up)[0], "w_down": f(w_down)[0],
        "norm_mix": f(norm_mix).reshape(16, 128), "norm_ffn": f(norm_ffn).reshape(16, 128),
        "q_norm": f(q_norm).reshape(1, 128), "k_norm": f(k_norm).reshape(1, 128),
        "rel_bias": f(rel_bias), "gmlp_v_norm": f(gmlp_v_norm).reshape(1, 512),
        "gmlp_w": f(gmlp_w)[0], "gmlp_b": f(gmlp_b)[0], "oh": oh, "ident": ident, "jm": jm,
    }
    in_maps = []
    for c in range(8):
        b, j = c // 4, c % 4
        q0 = j * NT
        xh = np.zeros((NH, D), np.float32)
        valid = np.zeros((NH,), np.float32)
        lo = q0 - NH
        s = max(lo, 0)
        if q0 > 0:
            xh[s - lo:] = x_prompt[b, s:q0]
            valid[s - lo:] = 1.0
        kv = np.zeros((128, 21), np.float32)
        kv[:, 0] = valid[1920:2048]
        for r in range(4):
            kv[:, 1 + r] = valid[1536 + r:2048:4]
        for r in range(16):
            kv[:, 5 + r] = valid[r:2048:16]
        m = dict(shared)
        m["xm"] = np.ascontiguousarray(x_prompt[b, q0:q0 + NT])
        m["xh"] = xh
        m["xs"] = np.ascontiguousarray(x_sample[c])
        m["kvalid"] = kv
        for g in range(3):
            m["ck%d" % g] = np.ascontiguousarray(caches[g][0, c].reshape(2, -1, 512))
        in_maps.append(m)
    res = run_bass_kernel_spmd(nc, in_maps, core_ids=list(range(8)))
    R = res.results
    yp = np.zeros((2, 4096, D), np.float32)
    ysm = np.zeros((8, TS, D), np.float32)
    kvp_out = [np.zeros((1, 2, 2, w, 4, 128), np.float32) for w in (128, 512, 2048)]
    kvs_out = [np.zeros((1, 8, 2, TS, 4, 128), np.float32) for _ in range(3)]
    gv = np.zeros((1, 8, TS, 512), np.float32)
    for c in range(8):
        b, j = c // 4, c % 4
        yp[b, j * NT:(j + 1) * NT] = R[c]["y"]
        ysm[c] = R[c]["ys"]
        if j == 3:
            kvp_out[0][0, b] = R[c]["kvp0"].reshape(2, 128, 4, 128)
            kvp_out[1][0, b] = R[c]["kvp1"].reshape(2, 512, 4, 128)
        if j >= 2:
            kvp_out[2][0, b, :, (j - 2) * 1024:(j - 1) * 1024] = R[c]["kvp2"].reshape(2, 1024, 4, 128)
        for g in range(3):
            kvs_out[g][0, c] = R[c]["kvs"][g].reshape(2, TS, 4, 128)
        gv[0, c] = R[c]["gvs"]
    return (yp, ysm, kvp_out[0], kvp_out[1], kvp_out[2], kvs_out[0], kvs_out[1], kvs_out[2], gv)
```

```python
import contextlib
import os
import numpy as np
import concourse.bass as bass
import concourse.mybir as mybir
from concourse.bass_utils import run_bass_kernel_spmd

F32 = mybir.dt.float32
BF16 = mybir.dt.bfloat16
AF = mybir.ActivationFunctionType
ALU = mybir.AluOpType
AX = mybir.AxisListType

D = 2048
NT = 1024
NH = 2048
TS = 4
NCOL = NT + TS
DIN = 5632
DFF = 8192
EPS = 1e-6
DILS = (1, 4, 16)
SCALE = 128 ** -0.5
SAME_ENG_SYNC = os.environ.get('MK_SES', '1') == '1'

ENGS = ("pe", "act", "dve", "pool", "sp")


class Op:
    __slots__ = ("eng", "fn", "deps", "dma", "signal", "seq", "target")


class Prog:
    def __init__(self):
        self.ops = []
        self.lw = {}
        self.rd = {}
        self.dcnt = {}
        self.last = {}
        self.auto_r = ()

    def add(self, eng, fn, r=(), w=(), dma=None, extra=()):
        idx = len(self.ops)
        deps = set(extra)
        r = tuple(r) + tuple(self.auto_r)
        pr = tuple(k for k in r if isinstance(k, str) and k[:2] == 'ps' and k[2:].isdigit())
        if pr:
            r = tuple(k for k in r if k not in pr)
            w = tuple(w) + pr
        for k in r:
            if k in self.lw:
                deps.add(self.lw[k])
        for k in w:
            if k in self.lw:
                deps.add(self.lw[k])
            deps.update(self.rd.get(k, ()))
        op = Op()
        op.eng, op.fn, op.deps, op.dma, op.signal, op.seq, op.target = eng, fn, deps, dma, False, 0, 0
        if dma is not None:
            c = self.dcnt.get(dma, 0) + 16
            self.dcnt[dma] = c
            op.target = c
        for k in r:
            self.rd.setdefault(k, []).append(idx)
        for k in w:
            self.lw[k] = idx
            self.rd[k] = []
        self.ops.append(op)
        if fn is not None:
            self.last[eng] = idx
        return idx

    def barrier(self):
        alld = set(self.last.values())
        for k, v in self.lw.items():
            alld.add(v)
        for e in ENGS:
            self.add(e, None, extra=tuple(alld))

    def emit(self, nc, es):
        limit = int(os.environ.get('MK_NOPS', '0'))
        if limit:
            self.ops = self.ops[:limit]
            for e in ENGS:
                op = Op()
                op.eng, op.fn, op.deps, op.dma, op.signal, op.seq, op.target = e, None, set(range(limit)), None, False, 0, 0
                self.ops.append(op)
        ops = self.ops
        if os.environ.get('MK_DUMP'):
            for i, op in enumerate(ops):
                print(i, op.eng, op.dma, sorted(op.deps)[-6:], getattr(op.fn, '__name__', None))
        for op in ops:
            op.deps = set(d for d in op.deps if ops[d].fn is not None)
            for d in op.deps:
                if ops[d].dma is None:
                    ops[d].signal = True
        cnt = {e: 0 for e in ENGS}
        for op in ops:
            if op.dma is None and op.signal:
                cnt[op.eng] += 1
                op.seq = cnt[op.eng]
        esem = {e: es.enter_context(nc.semaphore("sem_" + e)) for e in ENGS}
        dsem = {}
        for i, k in enumerate(self.dcnt):
            dsem[k] = es.enter_context(nc.semaphore("dsem%d" % i))
        block = es.enter_context(nc.Block())
        reg = {"pe": block.tensor, "act": block.scalar, "dve": block.vector,
               "pool": block.gpsimd, "sp": block.sync}
        for e in ENGS:
            mine = [op for op in ops if op.eng == e]

            def body(eng, e=e, mine=mine):
                waited = {}
                for op in mine:
                    need = {}
                    for d in op.deps:
                        p = ops[d]
                        if p.dma is not None:
                            s, v = dsem[p.dma], p.target
                        else:
                            if p.eng == e and (e == "pe" or not SAME_ENG_SYNC):
                                continue
                            s, v = esem[p.eng], p.seq
                        key = id(s)
                        if key not in need or need[key][1] < v:
                            need[key] = (s, v)
                    for key, (s, v) in need.items():
                        if waited.get(key, 0) < v:
                            eng.wait_ge(s, v)
                            waited[key] = v
                    if op.fn is not None:
                        ins = op.fn(eng)
                        if op.dma is not None:
                            ins.then_inc(dsem[op.dma], 16)
                        elif op.signal:
                            ins.then_inc(esem[e], 1)
            reg[e](body)


def bucket(dist):
    if dist < 16:
        return dist
    v = 16 + int(np.float32(np.log(np.float32(dist) / np.float32(16)) / np.float32(np.log(128.0)) * np.float32(16)))
    return min(v, 31)


def t5_bucket_np(dist):
    import math
    d = np.maximum(dist, 1).astype(np.float32)
    rnd = np.rint if os.environ.get('MK_BUCKET_RINT', '0') == '1' else np.trunc
    large = 16 + rnd(np.log(d / np.float32(16)) / np.float32(math.log(2048 / 16)) * np.float32(16)).astype(np.int32)
    large = np.minimum(large, 31)
    return np.where(dist < 16, dist, large)


def host_constants():
    oh = np.zeros((33, 3, 384), np.float32)
    for g, dil in enumerate(DILS):
        sub = np.arange(384) - 128
        ok = (sub >= 0) & (sub <= 128)
        b = t5_bucket_np(np.clip(sub, 0, 128).astype(np.int32) * dil)
        for u in range(384):
            if ok[u]:
                oh[b[u], g, u] = 1.0
            else:
                oh[32, g, u] = 1.0
    ident = np.eye(128, dtype=np.float32)
    jm = np.ascontiguousarray(ident[::-1])
    return oh.reshape(33, 1152), ident, jm


def build_program():
    nc = bass.Bass("TRN2", target_bir_lowering=False)

    def din(name, shape):
        return nc.dram_tensor(name, shape, F32, kind="ExternalInput").ap()

    def dout(name, shape):
        return nc.dram_tensor(name, shape, F32, kind="ExternalOutput").ap()

    xm = din("xm", [NT, D]); xh = din("xh", [NH, D]); xs = din("xs", [TS, D])
    kvalid = din("kvalid", [128, 21])
    ck = [din("ck0", [2, 128, 512]), din("ck1", [2, 512, 512]), din("ck2", [2, 2048, 512])]
    w_in = din("w_in", [D, DIN]); w_out = din("w_out", [1024, D])
    w_up = din("w_up", [D, DFF]); w_down = din("w_down", [DFF, D])
    norm_mix = din("norm_mix", [16, 128]); norm_ffn = din("norm_ffn", [16, 128])
    q_norm = din("q_norm", [1, 128]); k_norm = din("k_norm", [1, 128])
    rel_bias = din("rel_bias", [32, 12]); gvn = din("gmlp_v_norm", [1, 512])
    gw = din("gmlp_w", [4, 128, 128]); gb = din("gmlp_b", [4, 128])
    oh_d = din("oh", [33, 1152]); ident_d = din("ident", [128, 128]); jm_d = din("jm", [128, 128])

    y = dout("y", [NT, D]); ys = dout("ys", [TS, D])
    kvp = [dout("kvp0", [2, 128, 512]), dout("kvp1", [2, 512, 512]), dout("kvp2", [2, 1024, 512])]
    kvs = dout("kvs", [3, 2, TS, 512]); gvs = dout("gvs", [TS, 512])
    escr = dout("escr", [12, 384])
    DBG = os.environ.get('MK_DBG')
    if DBG:
        dbg_rx = dout("dbg_rx", [128, 16384])
        dbg_cat = nc.dram_tensor("dbg_cat", [128, 8 * NCOL], BF16, kind="ExternalOutput").ap()
        dbg_hT = nc.dram_tensor("dbg_hT", [128, 16 * NCOL], BF16, kind="ExternalOutput").ap()

    P = Prog()
    es = contextlib.ExitStack()
    STOP = int(os.environ.get('MK_STOP', '99'))
    SUB = int(os.environ.get('MK_SUB', '255'))

    def sb(name, shape, dt):
        return es.enter_context(nc.sbuf_tensor(name, shape, dt))

    hT = sb("hT", [128, 16, NCOL], BF16)
    ring = [sb("ring%d" % i, [128, 8192], BF16) for i in range(4)]
    RX = sb("RX", [128, 16384], F32)
    catT = sb("catT", [128, 8, NCOL], BF16)
    xt = sb("xt", [128, D], F32)
    xb = sb("xb", [128, D], BF16)
    hTh = sb("hTh", [128, 16, 128], BF16)
    kfK = sb("kfK", [128, 512], F32)
    vf = sb("vf", [128, 512], F32)
    PTb = [sb("PT%d" % i, [128, 2, 2, 128], BF16) for i in range(2)]
    identb = sb("identb", [128, 128], BF16)
    identf = sb("identf", [128, 128], F32)
    onesb = sb("onesb", [128, 128], BF16)
    gmix = sb("gmix", [128, 16], F32)
    gffn = sb("gffn", [128, 16], F32)
    bufA = sb("bufA", [128, 512], F32)
    bufB = sb("bufB", [128, 512], F32)
    WmT = sb("WmT", [128, 4, 128], BF16)
    kval = sb("kval", [128, 21], F32)
    st_ssq = sb("st_ssq", [128, 1], F32)
    st_rs = sb("st_rs", [128, 1], F32)
    st4 = [sb("st4_%d" % i, [128, 4], F32) for i in range(4)]
    accOs = sb("accOs", [128, 4, TS], F32)
    accDs = sb("accDs", [128, 4, TS], F32)
    uTs = sb("uTs", [128, 4, TS], F32)
    ps = [es.enter_context(nc.psum_tensor("ps%d" % i, [128, 512], F32)) for i in range(8)]

    accO = RX[:, 0:4096].rearrange("p (h t) -> p h t", h=4)
    accD = RX[:, 4096:8192].rearrange("p (h t) -> p h t", h=4)
    EB = RX[:, 8192:11264].rearrange("p (g a q) -> p g a q", g=12, a=2)
    EBc2 = RX[:, 11264:11776].rearrange("p (h q) -> p h q", h=4)
    EBp2 = RX[:, 11776:12288].rearrange("p (h q) -> p h q", h=4)
    uT = RX[:, 8192:12288].rearrange("p (h t) -> p h t", h=4)
    rest = RX[:, 12288:16384]
    KTr = [rest[:, i * 256:(i + 1) * 256].bitcast(BF16).rearrange("p (h k) -> p h k", h=4) for i in range(5)]
    Vr = [rest[:, 1280 + i * 256:1280 + (i + 1) * 256].bitcast(BF16) for i in range(5)]
    QTr = [rest[:, 2560 + i * 256:2560 + (i + 1) * 256].bitcast(BF16).rearrange("p (h k) -> p h k", h=4) for i in range(3)]
    Ebuf = rest[:, 3328:3840].rearrange("p (a h q) -> p a h q", a=2, h=2)
    kfQ = RX[:, 12288 + 3840 - 512:12288 + 3840]
    kfQ = sb("kfQ", [128, 512], F32)
    x1 = RX[:, :].rearrange("p (i d) -> p i d", i=8)
    catb = catT[:].rearrange("p c n -> p (c n)")
    xt2 = catb[:, 0:4096].bitcast(F32)
    hTh2 = catb[:, 4096:6144].rearrange("p (c n) -> p c n", c=16)
    kfK2 = catb[:, 6144:7168].bitcast(F32)
    kfQ2 = catb[:, 7168:8192].bitcast(F32)
    XT = [(xt, "xt"), (xt2, "xt2")]
    XT2_OK = [True]
    HTH = [(hTh, "hTh"), (hTh2, "hTh2")]
    KFK = [(kfK, "kfK"), (kfK2, "kfK2")]
    KFQ = [(kfQ, "kfQ"), (kfQ2, "kfQ2")]
    hTf = RX
    oh_sb = hTf[:, 0:1152]
    Eall = hTf[:, 1152:2304].rearrange("p (g u) -> p g u", g=3)
    jm_sb = hTf[:, 2304:2432]
    Hh = hTf[:, 2432:2432 + 3072].rearrange("p (g a q) -> p g a q", g=12, a=2)
    tab33 = hTf[:, 5504:5516]
    wtmp = hTf[:, 5632:5632 + 512].rearrange("p (g s) -> p g s", g=4)
    vtmp = hTf[:, 6144:6144 + 128]
    aT = [catT[:].rearrange("p c n -> p (c n)")[:, i * 4096:(i + 1) * 4096].rearrange("p (f t) -> p f t", f=4) for i in range(2)]
    aTs = sb("aTs", [128, 2, 4, TS], BF16)
    rsc = [xb[:, i * 1024:(i + 1) * 1024].bitcast(F32) for i in range(2)]

    psb = [p[:].bitcast(BF16) for p in ps]

    def dma(q, out, in_, r, w, key):
        return P.add(q, lambda e, out=out, in_=in_: e.dma_start(out=out, in_=in_), r=r, w=w, dma=key)

    def bcast_rows(dram_ap_row, nparts, n):
        return bass.AP(tensor=dram_ap_row.tensor, offset=dram_ap_row.offset, ap=[[0, nparts], [1, n]])

    def bc_free(ap2, n):
        a = ap2.ap
        return bass.AP(tensor=ap2.tensor, offset=ap2.offset, ap=[list(a[0]), list(a[1]), [0, n]])

    wblocks = []

    def wview_rows(w, col0, ncols):
        return w.rearrange("(c p) n -> p c n", p=128)[:, :, col0:col0 + ncols]

    for g in (2, 1, 0):
        wblocks.append(("K%d" % g, wview_rows(w_in, 1536 + g * 512, 512), (16, 512)))
        wblocks.append(("V%d" % g, wview_rows(w_in, 3072 + g * 512, 512), (16, 512)))
        wblocks.append(("Q%d" % g, wview_rows(w_in, g * 512, 512), (16, 512)))
    wblocks.append(("U", wview_rows(w_in, 4608, 512), (16, 512)))
    wblocks.append(("G", wview_rows(w_in, 5120, 512), (16, 512)))
    wblocks.append(("O0", wview_rows(w_out, 0, 1024), (8, 1024)))
    wblocks.append(("O1", wview_rows(w_out, 1024, 1024), (8, 1024)))
    for n in range(16):
        wblocks.append(("UP%d" % n, wview_rows(w_up, n * 512, 512), (16, 512)))
        wblocks.append(("DN%d" % n, w_down[n * 512:(n + 1) * 512, :].rearrange("(c p) n -> p c n", p=128), (4, 2048)))
    wslot = {}
    wnext = [0]

    def wload_next():
        i = wnext[0]
        if i >= len(wblocks):
            return
        name, src, (c, n) = wblocks[i]
        slot = i % 4
        dst = ring[slot][:].rearrange("p (c n) -> p c n", c=c)
        wslot[name] = (slot, dst)
        dma("pool", dst, src, r=(), w=(("ring", slot),), key=("ring", slot))
        wnext[0] += 1

    def W(name):
        return wslot[name]

    def phases():
        P.auto_r = ("accO", "accD")
        dma("sp", identf[:], ident_d[:, :], (), ("identf",), "c0")
        dma("sp", jm_sb, jm_d[:, :], (), ("jm",), "c1")
        dma("sp", oh_sb[0:33, :], oh_d[:, :], (), ("oh",), "c2")
        dma("sp", tab33[0:32, :], rel_bias[:, :], (), ("tab",), "c3")
        dma("sp", kval[:], kvalid[:, :], (), ("kval",), "c4")
        dma("sp", bufA[:], bass.AP(tensor=q_norm.tensor, offset=0, ap=[[0, 128], [0, 4], [1, 128]]), (), ("bufA",), "c5")
        dma("sp", bufB[:], bass.AP(tensor=k_norm.tensor, offset=0, ap=[[0, 128], [0, 4], [1, 128]]), (), ("bufB",), "c6")
        dma("sp", vtmp[0:16, :], norm_mix[:, :], (), ("vtmp",), "c7")
        for _ in range(4):
            wload_next()
        P.add("dve", lambda e: e.tensor_copy(out=identb[:], in_=identf[:]), r=("identf",), w=("identb",))
        P.add("dve", lambda e: e.memset(onesb[:], 1.0), w=("onesb",))
        P.add("dve", lambda e: e.memset(tab33[32:33, :], -30000.0), w=("tab32",))
        if SUB & 1:
            P.add("pe", lambda e: e.transpose(out=ps[7][:, 0:16], in_=vtmp[0:16, :], identity=identf[0:16, 0:16]),
                  r=("vtmp", "identf"), w=("ps7",))
            P.add("act", lambda e: e.copy(out=gmix[:], in_=ps[7][:, 0:16]), r=("ps7",), w=("gmix",))
            dma("sp", vtmp[0:16, :], norm_ffn[:, :], (), ("vtmp",), "c7")
            P.add("pe", lambda e: e.transpose(out=ps[7][:, 0:16], in_=vtmp[0:16, :], identity=identf[0:16, 0:16]),
                  r=("vtmp", "identf"), w=("ps7",))
            P.add("act", lambda e: e.copy(out=gffn[:], in_=ps[7][:, 0:16]), r=("ps7",), w=("gffn",))
        if SUB & 2:
            dma("sp", wtmp, gw.rearrange("g t s -> t g s"), (), ("wtmp",), "c8")
            for gg in range(4):
                P.add("pool", lambda e, gg=gg: e.affine_select(out=wtmp[:, gg, :], in_=wtmp[:, gg, :], pattern=[[-1, 128]],
                                                               compare_op=ALU.is_ge, fill=0.0, base=0, channel_multiplier=1),
                      r=("wtmp",), w=("wtmp",))
            for gg in range(4):
                P.add("pe", lambda e, gg=gg: e.transpose(out=ps[6][:, gg * 128:(gg + 1) * 128], in_=wtmp[:, gg, :], identity=identf[:]),
                      r=("wtmp", "identf"), w=("ps6",))
            P.add("act", lambda e: e.copy(out=WmT[:].rearrange("p g t -> p (g t)"), in_=ps[6][:, :]), r=("ps6",), w=("WmT",))
        if SUB & 4:
            for g in range(3):
                P.add("pe", lambda e, g=g: e.matmul(out=ps[g][0:12, 0:384], lhsT=tab33[0:33, 0:12], rhs=oh_sb[0:33, g * 384:(g + 1) * 384],
                                                    start=True, stop=True),
                      r=("tab", "tab32", "oh"), w=("ps%d" % g,))
                P.add("act", lambda e, g=g: e.activation(out=Eall[0:12, g, :], in_=ps[g][0:12, 0:384], func=AF.Exp),
                      r=("ps%d" % g,), w=("Eall%d" % g,))
                dma("sp", escr[g * 4:(g + 1) * 4, :], Eall[g * 4:(g + 1) * 4, g, :], ("Eall%d" % g,), ("escr%d" % g,), "e%d" % g)
            for gh in range(12):
                src = bass.AP(tensor=escr.tensor, offset=gh * 384 + 1, ap=[[1, 128], [128, 2], [1, 128]])
                dma("sp", Hh[:, gh, :, :], src, ("escr%d" % (gh // 4),), ("Hh%d" % gh,), "h%d" % gh)
            Hf = Hh.rearrange("p g a q -> p (g a q)")
            EBf = EB.rearrange("p g a q -> p (g a q)")
            for i in range(6):
                P.add("pe", lambda e, i=i: e.matmul(out=ps[i][:, :], lhsT=jm_sb, rhs=Hf[:, i * 512:(i + 1) * 512], start=True, stop=True),
                      r=("jm", "Hh%d" % (2 * i), "Hh%d" % (2 * i + 1)), w=("ps%d" % i,))
                P.add("dve" if i % 2 else "act",
                      (lambda e, i=i: e.tensor_copy(out=EBf[:, i * 512:(i + 1) * 512], in_=ps[i][:, :])) if i % 2 else
                      (lambda e, i=i: e.copy(out=EBf[:, i * 512:(i + 1) * 512], in_=ps[i][:, :])),
                      r=("ps%d" % i,), w=("EB",))
        if SUB & 8:
            P.add("dve", lambda e: e.tensor_copy(out=EBc2, in_=EB[:, 8:12, 0, :]), r=("EB",), w=("EBc2",))
            P.add("dve", lambda e: e.memset(EBc2[0:64, :, 64:128], 0.0), w=("EBc2",))
            P.add("dve", lambda e: e.tensor_copy(out=EBp2[:, :, 0:64], in_=EB[:, 8:12, 1, 0:64]), r=("EB",), w=("EBp2",))
            P.add("dve", lambda e: e.tensor_copy(out=EBp2[:, :, 64:128], in_=EB[:, 8:12, 1, 0:64]), r=("EB",), w=("EBp2",))
        P.auto_r = ()
        if STOP <= 1:
            return

        xsel = [0]

        def norm_parts(x_src, npart, gains, dst_fn, dst_keys, load=True, src_sb=None, src_keys=()):
            if load:
                xbuf, xkey = XT[xsel[0] % 2] if XT2_OK[0] else XT[0]
                xsel[0] += 1
                src = xbuf[0:npart, :]
                skeys = (xkey,)
            else:
                src = src_sb
                skeys = tuple(src_keys)

            def n0():
                if load:
                    dma("sp", src, x_src, (), skeys, skeys[0])

            def n1():
                P.add("act", lambda e: e.activation(out=xb[0:npart, :], in_=src, func=AF.Square, accum_out=st_ssq[0:npart, :]),
                      r=skeys, w=("xb", "st_ssq"))
                P.add("act", lambda e: e.activation(out=st_rs[0:npart, :], in_=st_ssq[0:npart, :], func=AF.Sqrt, scale=1.0 / D, bias=EPS),
                      r=("st_ssq",), w=("st_rs",))
                P.add("dve", lambda e: e.reciprocal(out=st_rs[0:npart, :], in_=st_rs[0:npart, :]), r=("st_rs",), w=("st_rs",))
                P.add("dve", lambda e: e.tensor_scalar(out=xb[0:npart, :], in0=src, scalar1=st_rs[0:npart, 0:1], scalar2=None, op0=ALU.mult),
                      r=skeys + ("st_rs",), w=("xb",))

            def n2():
                for half in range(2):
                    def tp(e, half=half):
                        ins = None
                        for c in range(8):
                            cc = half * 8 + c
                            ins = e.transpose(out=psb[half][:, c * 128:c * 128 + npart], in_=xb[0:npart, cc * 128:(cc + 1) * 128],
                                              identity=identb[0:npart, 0:npart])
                        return ins
                    P.add("pe", tp, r=("xb", "identb"), w=("ps%d" % half,))
                    src_ps = psb[half][:, :].rearrange("p (c n) -> p c n", c=8)[:, :, 0:npart]
                    P.add("dve", lambda e, half=half, src_ps=src_ps: e.tensor_tensor(
                        out=dst_fn(half * 8, half * 8 + 8), in0=src_ps, in1=bc_free(gains[:, half * 8:half * 8 + 8], npart), op=ALU.mult),
                        r=("ps%d" % half, "gmix", "gffn"), w=tuple(dst_keys))
            return n0, n1, n2

        def norm_tile(x_src, npart, gains, dst_fn, dst_keys, load=True, src_sb=None, src_keys=()):
            n0, n1, n2 = norm_parts(x_src, npart, gains, dst_fn, dst_keys, load=load, src_sb=src_sb, src_keys=src_keys)
            n0(); n1(); n2()

        p1 = [norm_parts(xm[i * 128:(i + 1) * 128, :], 128, gmix, (lambda c0, c1, i=i: hT[:, c0:c1, i * 128:(i + 1) * 128]), (("hT", i),))
              for i in range(8)]
        p1[0][0](); p1[1][0]()
        for i in range(8):
            p1[i][1]()
            p1[i][2]()
            if i + 2 < 8:
                p1[i + 2][0]()
        norm_tile(xs[:, :], TS, gmix, lambda c0, c1: hT[:, c0:c1, NT:NT + TS], (("hT", 8),))

        if STOP <= 2:
            return

        def proj(lhs_fn, rkeys, wname, bank, npart):
            slot, wv = W(wname)

            def f(e):
                ins = None
                for c in range(16):
                    ins = e.matmul(out=ps[bank][0:npart, :], lhsT=lhs_fn(c), rhs=wv[:, c, :], start=(c == 0), stop=(c == 15))
                return ins
            P.add("pe", f, r=tuple(rkeys) + (("ring", slot),), w=("ps%d" % bank,))

        def qk_norm(bank, npart, gainbuf, gkey, kf, kfkey, ssq, rs, skey):
            P.add("act", lambda e: e.activation(out=vf[0:npart, :], in_=ps[bank][0:npart, :], func=AF.Square),
                  r=("ps%d" % bank,), w=("vf",))
            P.add("dve", lambda e: e.tensor_reduce(out=ssq[0:npart, :], in_=vf[0:npart, :].rearrange("p (h d) -> p h d", h=4),
                                                   axis=AX.X, op=ALU.add), r=("vf",), w=(skey,))
            P.add("act", lambda e: e.activation(out=rs[0:npart, :], in_=ssq[0:npart, :], func=AF.Sqrt, scale=1.0 / 128, bias=EPS),
                  r=(skey,), w=(skey + "r",))
            P.add("dve", lambda e: e.reciprocal(out=rs[0:npart, :], in_=rs[0:npart, :]), r=(skey + "r",), w=(skey + "r",))
            for h in range(4):
                P.add("dve", lambda e, h=h: e.scalar_tensor_tensor(
                    out=kf[0:npart, h * 128:(h + 1) * 128], in0=ps[bank][0:npart, h * 128:(h + 1) * 128],
                    scalar=rs[0:npart, h:h + 1], in1=gainbuf[0:npart, h * 128:(h + 1) * 128], op0=ALU.mult, op1=ALU.mult),
                    r=("ps%d" % bank, skey + "r", gkey), w=(kfkey,))

        def tr4(kf, kfkey, npart, dst, dkey):
            def f(e):
                ins = None
                for h in range(4):
                    ins = e.transpose(out=ps[5][:, h * 128:h * 128 + npart], in_=kf[0:npart, h * 128:(h + 1) * 128],
                                      identity=identf[0:npart, 0:npart])
                return ins
            P.add("pe", f, r=(kfkey, "identf"), w=("ps5",))
            P.add("act", lambda e: e.copy(out=dst, in_=ps[5][:, :].rearrange("p (h k) -> p h k", h=4)[:, :, 0:npart]),
                  r=("ps5",), w=(dkey,))

        first_group = [True]

        def run_group(g):
            dil = DILS[g]
            items = []
            if g == 0:
                items.append(dict(kind="H", rows=xh[1920:2048, :], vcol=0))
                for i in range(8):
                    items.append(dict(kind="M", lhs=(lambda c, i=i: hT[:, c, i * 128:(i + 1) * 128]), hkeys=(("hT", i),),
                                      pos=("c", i), out=(7 == i and [(0, 128, kvp[0][:, 0:128, :])] or [])))
            elif g == 1:
                for r in range(4):
                    items.append(dict(kind="H", rows=xh[1536 + r:2048:4, :], vcol=1 + r))
                    for s in range(2):
                        items.append(dict(kind="M", lhs=(lambda c, s=s, r=r: hT[:, c, s * 512 + r:s * 512 + 512:4]),
                                          hkeys=tuple(("hT", 4 * s + k) for k in range(4)), pos=("s4", s, r),
                                          out=(s == 1 and [(0, 128, kvp[1][:, r:512:4, :])] or [])))
            else:
                for T in range(8):
                    items.append(dict(kind="H", rows=xh[2 * T:2048:16, :], vcol=5 + 2 * T))
                    items.append(dict(kind="H", rows=xh[2 * T + 1:2048:16, :], vcol=5 + 2 * T + 1))
                    items.append(dict(kind="M", lhs=(lambda c: hTh[:, c, :]), gather=T,
                                      hkeys=("hTh",), pos=("s16", T),
                                      out=[(0, 64, kvp[2][:, 2 * T:1024:16, :]), (64, 128, kvp[2][:, 2 * T + 1:1024:16, :])]))
            n_items = len(items)
            for idx, it in enumerate(items):
                it["idx"] = idx
                it["kslot"] = idx % 5
            qcount = [0]
            for it in items:
                if it["kind"] == "M":
                    it["qslot"] = qcount[0] % 3
                    it["qbuf"] = qcount[0] % 2
                    qcount[0] += 1
            for idx, it in enumerate(items):
                if it["kind"] != "M":
                    continue
                if g == 2:
                    it["prev"] = [(items[idx - 2], 0, 64), (items[idx - 1], 64, 128)]
                else:
                    it["prev"] = [(items[idx - 1], 0, 128)]

            Kn, Vn, Qn = "K%d" % g, "V%d" % g, "Q%d" % g

            hsel = [0]

            def stageN(it):
                if it["kind"] == "H" or "gather" in it:
                    hb, hkey = HTH[hsel[0] % 2]
                    hsel[0] += 1
                    if it["kind"] == "H":
                        it["_n"] = norm_parts(it["rows"], 128, gmix, (lambda c0, c1, hb=hb: hb[:, c0:c1, :]), (hkey,))
                    else:
                        T = it["gather"]
                        srcv = hT[:, :, 0:NT].rearrange("p c (m r) -> p c r m", r=16)

                        def gat(T=T, srcv=srcv, hb=hb, hkey=hkey):
                            for rr in range(2):
                                P.add("dve", lambda e, rr=rr: e.tensor_copy(out=hb[:, :, rr * 64:(rr + 1) * 64], in_=srcv[:, :, 2 * T + rr, :]),
                                      r=tuple(("hT", k) for k in range(8)), w=(hkey,))
                        it["_n"] = (None, None, gat)
                    it["_lhs"], it["_hk"] = (lambda c, hb=hb: hb[:, c, :]), (hkey,)
                else:
                    it["_lhs"], it["_hk"] = it["lhs"], it["hkeys"]

            def stageA1(it):
                lhs, hk = it["_lhs"], it["_hk"]
                kK, kKkey = KFK[it["idx"] % 2]
                it["_kfK"] = (kK, kKkey)
                if it["kind"] == "M":
                    kQ, kQkey = KFQ[it["qbuf"]]
                    it["_kfQ"] = (kQ, kQkey)
                    proj(lhs, hk, Qn, 4, 128)
                    qk_norm(4, 128, bufA, "bufA", kQ, kQkey, st4[0], st4[1], "sq")
                proj(lhs, hk, Kn, 2, 128)
                qk_norm(2, 128, bufB, "bufB", kK, kKkey, st4[2], st4[3], "sk")
                for (p0, p1, dst) in it.get("out", []):
                    dma("sp", dst[0], kK[p0:p1, :], (kKkey,), (("o", "K", g),), ("stK", g, it["idx"] % 2))

            def stageB(it):
                if it["kind"] == "M":
                    kQ, kQkey = it["_kfQ"]
                    tr4(kQ, kQkey, 128, QTr[it["qslot"]], ("QT", it["qslot"]))
                kK, kKkey = it["_kfK"]
                tr4(kK, kKkey, 128, KTr[it["kslot"]], ("KT", it["kslot"]))

            def stageA2(it):
                proj(it["_lhs"], it["_hk"], Vn, 3, 128)
                ks = it["kslot"]
                P.add("act", lambda e: e.copy(out=Vr[ks], in_=ps[3][:, :]), r=("ps3",), w=(("V", ks),))
                outs = it.get("out", [])
                if outs:
                    P.add("dve", lambda e: e.tensor_copy(out=vf[:], in_=ps[3][:, :]), r=("ps3",), w=("vf",))
                    for (p0, p1, dst) in outs:
                        dma("sp", dst[1], vf[p0:p1, :], ("vf",), (("o", "V", g),), ("stV", g))

            def stageC(it):
                qs, ks = it["qslot"], it["kslot"]
                for hp in range(2):
                    bank = 6 if hp == 0 else 0
                    stv = ps[bank][:, :].rearrange("p (a h q) -> p a h q", a=2, h=2)

                    def f(e, hp=hp, stv=stv):
                        ins = None
                        for hh in range(2):
                            h = hp * 2 + hh
                            for (pit, c0, c1) in it["prev"]:
                                ins = e.matmul(out=stv[:, 0, hh, c0:c1], lhsT=KTr[pit["kslot"]][:, h, :], rhs=QTr[qs][:, h, c0:c1],
                                               start=True, stop=True)
                            ins = e.matmul(out=stv[:, 1, hh, :], lhsT=KTr[ks][:, h, :], rhs=QTr[qs][:, h, :], start=True, stop=True)
                        return ins
                    rk = [("QT", qs), ("KT", ks)] + [("KT", pit["kslot"]) for (pit, _, _) in it["prev"]]
                    P.add("pe", f, r=tuple(rk), w=("ps%d" % bank,))
                    P.add("act", lambda e, bank=bank: e.activation(out=Ebuf.rearrange("p a h q -> p (a h q)"), in_=ps[bank][:, :],
                                                                   func=AF.Exp, scale=SCALE), r=("ps%d" % bank,), w=("Ebuf",))
                    pt = PTb[hp]
                    if g == 2:
                        ebc = EBc2[:, hp * 2:hp * 2 + 2, :]
                        ebp = EBp2[:, hp * 2:hp * 2 + 2, :]
                        ekeys = ("EBc2", "EBp2")
                    else:
                        ebc = EB[:, g * 4 + hp * 2:g * 4 + hp * 2 + 2, 0, :]
                        ebp = EB[:, g * 4 + hp * 2:g * 4 + hp * 2 + 2, 1, :]
                        ekeys = ("EB",)
                    P.add("dve", lambda e, pt=pt, ebc=ebc: e.tensor_tensor(out=pt[:, 1, :, :], in0=Ebuf[:, 1, :, :], in1=ebc, op=ALU.mult),
                          r=("Ebuf",) + ekeys, w=(("PT", hp),))
                    for (pit, c0, c1) in it["prev"]:
                        if pit["kind"] == "H":
                            vc = pit["vcol"]
                            P.add("dve", lambda e, pt=pt, ebp=ebp, c0=c0, c1=c1, vc=vc: e.scalar_tensor_tensor(
                                out=pt[:, 0, :, c0:c1], in0=Ebuf[:, 0, :, c0:c1], scalar=kval[:, vc:vc + 1], in1=ebp[:, :, c0:c1],
                                op0=ALU.mult, op1=ALU.mult), r=("Ebuf", "kval") + ekeys, w=(("PT", hp),))
                        else:
                            P.add("dve", lambda e, pt=pt, ebp=ebp, c0=c0, c1=c1: e.tensor_tensor(
                                out=pt[:, 0, :, c0:c1], in0=Ebuf[:, 0, :, c0:c1], in1=ebp[:, :, c0:c1], op=ALU.mult),
                                r=("Ebuf",) + ekeys, w=(("PT", hp),))

            def stageD(it):
                ks = it["kslot"]
                kind, *pp = it["pos"]
                for hp in range(2):
                    bank = 7 if hp == 0 else 1
                    od = ps[bank][:, :].rearrange("p (a h q) -> p a h q", a=2, h=2)
                    pt = PTb[hp]

                    def f(e, hp=hp, od=od, pt=pt):
                        ins = None
                        for hh in range(2):
                            h = hp * 2 + hh
                            ins = e.matmul(out=od[:, 0, hh, :], lhsT=Vr[ks][:, h * 128:(h + 1) * 128], rhs=pt[:, 1, hh, :],
                                           start=True, stop=False)
                            np_ = len(it["prev"])
                            for j, (pit, c0, c1) in enumerate(it["prev"]):
                                ins = e.matmul(out=od[:, 0, hh, c0:c1], lhsT=Vr[pit["kslot"]][:, h * 128:(h + 1) * 128],
                                               rhs=pt[:, 0, hh, c0:c1], start=False, stop=(j == np_ - 1))
                        ins = e.matmul(out=od[:, 1, :, :], lhsT=onesb[:], rhs=pt[:, 1, :, :], start=True, stop=False)
                        ins = e.matmul(out=od[:, 1, :, :], lhsT=onesb[:], rhs=pt[:, 0, :, :], start=False, stop=True)
                        return ins
                    rk = [("PT", hp), ("V", ks), "onesb"] + [("V", pit["kslot"]) for (pit, _, _) in it["prev"]]
                    P.add("pe", f, r=tuple(rk), w=("ps%d" % bank,))
                    for a, acc, akey in ((0, accO, "accO"), (1, accD, "accD")):
                        hs = slice(hp * 2, hp * 2 + 2)
                        if kind == "c":
                            i = pp[0]
                            dst = acc[:, hs, i * 128:(i + 1) * 128]
                            src = od[:, a, :, :]
                        elif kind == "s4":
                            s, r = pp
                            dst = acc[:, hs, s * 512 + r:s * 512 + 512:4]
                            src = od[:, a, :, :]
                        else:
                            T = pp[0]
                            dst = acc[:, hs, :].rearrange("p h (m r) -> p h r m", r=16)[:, :, 2 * T:2 * T + 2, :]
                            src = od[:, a, :, :].rearrange("p h (r m) -> p h r m", r=2)
                        if first_group[0]:
                            P.add("act", lambda e, dst=dst, src=src: e.copy(out=dst, in_=src),
                                  r=("ps%d" % bank,), w=(akey,))
                        else:
                            P.add("dve", lambda e, dst=dst, src=src: e.tensor_tensor(out=dst, in0=dst, in1=src, op=ALU.add),
                                  r=("ps%d" % bank, akey), w=(akey,))

            def npart_(k, j):
                if 0 <= k < n_items:
                    fn = items[k].get("_n", (None, None, None))[j]
                    if fn is not None:
                        fn()
            for it in items:
                stageN(it)
            npart_(0, 0); npart_(1, 0); npart_(0, 1); npart_(0, 2); npart_(1, 1)
            for n in range(n_items + 2):
                npart_(n + 2, 0)
                if 0 <= n - 2 < n_items and items[n - 2]["kind"] == "M":
                    stageC(items[n - 2])
                npart_(n + 1, 2)
                npart_(n + 2, 1)
                if n < n_items:
                    stageA1(items[n])
                if 0 <= n - 1 < n_items:
                    stageB(items[n - 1])
                if n < n_items:
                    stageA2(items[n])
                if 0 <= n - 2 < n_items and items[n - 2]["kind"] == "M":
                    stageD(items[n - 2])
            sl = lambda c: hT[:, c, NT:NT + TS]
            proj(sl, (("hT", 8),), Qn, 4, TS)
            qk_norm(4, TS, bufA, "bufA", kfQ, "kfQ", st4[0], st4[1], "sq")
            proj(sl, (("hT", 8),), Kn, 2, TS)
            qk_norm(2, TS, bufB, "bufB", kfK, "kfK", st4[2], st4[3], "sk")
            dma("sp", kvs[g, 0], kfK[0:TS, :], ("kfK",), (("o", "Ks", g),), ("stKs", g))
            tr4(kfQ, "kfQ", TS, QTr[0][:, :, 0:TS], ("QT", 0))
            tr4(kfK, "kfK", TS, KTr[4][:, :, 0:TS], ("KT", 4))
            proj(sl, (("hT", 8),), Vn, 3, TS)
            P.add("act", lambda e: e.copy(out=Vr[4][0:TS, :], in_=ps[3][0:TS, :]), r=("ps3",), w=(("V", 4),))
            P.add("dve", lambda e: e.tensor_copy(out=vf[0:TS, :], in_=ps[3][0:TS, :]), r=("ps3",), w=("vf",))
            dma("sp", kvs[g, 1], vf[0:TS, :], ("vf",), (("o", "Vs", g),), ("stVs", g))
            ntile = 1 if g == 0 else 4
            for t in range(ntile):
                rows = slice(0, 128) if g == 0 else slice(t, dil * 128, dil)
                dma("sp", xt[:, t * 512:(t + 1) * 512], ck[g][0, rows, :], (), ("xt",), "xt")
                tr4(xt[:, t * 512:(t + 1) * 512], "xt", 128, KTr[t], ("KT", t))
                dma("sp", kfQ[:, :], ck[g][1, rows, :], (), ("kfQ",), "ckv")
                P.add("dve", lambda e, t=t: e.tensor_copy(out=Vr[t], in_=kfQ[:, :]), r=("kfQ",), w=(("V", t),))
            sc = ps[6]
            od = ps[7]

            def fsc(e):
                ins = None
                for h in range(4):
                    if g == 0:
                        ins = e.matmul(out=sc[:, h * 4:(h + 1) * 4], lhsT=KTr[0][:, h, :], rhs=QTr[0][:, h, 0:TS], start=True, stop=True)
                    else:
                        for t in range(4):
                            ins = e.matmul(out=sc[:, h * 4 + t:h * 4 + t + 1], lhsT=KTr[t][:, h, :], rhs=QTr[0][:, h, t:t + 1],
                                           start=True, stop=True)
                    ins = e.matmul(out=sc[0:TS, 16 + h * 4:16 + (h + 1) * 4], lhsT=KTr[4][:, h, 0:TS], rhs=QTr[0][:, h, 0:TS],
                                   start=True, stop=True)
                return ins
            P.add("pe", fsc, r=(("QT", 0), ("KT", 4)) + tuple(("KT", t) for t in range(ntile)), w=("ps6",))
            Es = kfK[:, 0:32]
            PTs = PTb[0][:].rearrange("p a h q -> p (a h q)")[:, 0:32]
            P.add("act", lambda e: e.activation(out=Es[:, 0:16], in_=sc[:, 0:16], func=AF.Exp, scale=SCALE), r=("ps6",), w=("kfK",))
            P.add("act", lambda e: e.activation(out=Es[0:TS, 16:32], in_=sc[0:TS, 16:32], func=AF.Exp, scale=SCALE), r=("ps6",), w=("kfK",))
            E3 = Es[:, 0:16].rearrange("p (h t) -> p h t", h=4)
            P3 = PTs[:, 0:16].rearrange("p (h t) -> p h t", h=4)
            En = Es[0:TS, 16:32].rearrange("p (h t) -> p h t", h=4)
            Pn = PTs[0:TS, 16:32].rearrange("p (h t) -> p h t", h=4)
            if g == 0:
                ebc_s = EB[:, 0:4, 1, 0:TS]
            else:
                ebc_s = bc_free(EB[:, g * 4:(g + 1) * 4, 1, 0], TS)
            P.add("dve", lambda e: e.tensor_tensor(out=P3, in0=E3, in1=ebc_s, op=ALU.mult), r=("kfK", "EB"), w=(("PT", 0),))
            ebn_s = EB[0:TS, g * 4:(g + 1) * 4, 0, 0:TS]
            if g == 0:
                P.add("dve", lambda e: e.tensor_tensor(out=Pn, in0=En, in1=ebn_s, op=ALU.mult), r=("kfK", "EB"), w=(("PT", 0),))
            else:
                P.add("dve", lambda e: e.tensor_tensor(out=En, in0=En, in1=ebn_s, op=ALU.mult), r=("kfK", "EB"), w=("kfK",))
                idb = bass.AP(tensor=identf[:].tensor, offset=identf[:].offset, ap=[list(identf[:].ap[0][:1]) + [TS], [0, 4], [1, TS]])
                P.add("dve", lambda e: e.tensor_tensor(out=Pn, in0=En, in1=idb, op=ALU.mult), r=("kfK", "identf"), w=(("PT", 0),))

            def fpv(e):
                ins = None
                for h in range(4):
                    hc = slice(h * 128, (h + 1) * 128)
                    ins = e.matmul(out=od[:, h * 4:(h + 1) * 4], lhsT=Vr[4][0:TS, hc], rhs=PTs[0:TS, 16 + h * 4:16 + (h + 1) * 4],
                                   start=True, stop=False)
                    if g == 0:
                        ins = e.matmul(out=od[:, h * 4:(h + 1) * 4], lhsT=Vr[0][:, hc], rhs=PTs[:, h * 4:(h + 1) * 4], start=False, stop=True)
                    else:
                        for t in range(4):
                            ins = e.matmul(out=od[:, h * 4 + t:h * 4 + t + 1], lhsT=Vr[t][:, hc], rhs=PTs[:, h * 4 + t:h * 4 + t + 1],
                                           start=False, stop=(t == 3))
                ins = e.matmul(out=od[:, 16:32], lhsT=onesb[0:TS, :], rhs=PTs[0:TS, 16:32], start=True, stop=False)
                ins = e.matmul(out=od[:, 16:32], lhsT=onesb[:, :], rhs=PTs[:, 0:16], start=False, stop=True)
                return ins
            P.add("pe", fpv, r=(("PT", 0), ("V", 4), "onesb") + tuple(("V", t) for t in range(ntile)), w=("ps7",))
            for a, acc, akey in ((0, accOs, "accOs"), (1, accDs, "accDs")):
                dst = acc[:].rearrange("p h t -> p (h t)")
                src = od[:, a * 16:(a + 1) * 16]
                if first_group[0]:
                    P.add("dve", lambda e, dst=dst, src=src: e.tensor_copy(out=dst, in_=src), r=("ps7",), w=(akey,))
                else:
                    P.add("dve", lambda e, dst=dst, src=src: e.tensor_tensor(out=dst, in0=dst, in1=src, op=ALU.add),
                          r=("ps7", akey), w=(akey,))
            first_group[0] = False
            return items

        group_items = {}
        for g in (2, 1, 0):
            if STOP <= 3 + (2 - g):
                return
            group_items[g] = run_group(g)
            for _ in range(3):
                wload_next()

        if STOP <= 6:
            return

        XT2_OK[0] = False
        P.barrier()
        dma("sp", bufA[:], bcast_rows(gvn, 128, 512), (), ("bufA",), "c5")
        dma("sp", bufB[:], bass.AP(tensor=gb.tensor, offset=0, ap=[[0, 128], [1, 512]]), (), ("bufB",), "c6")
        slotU, wU = W("U")
        bk = [2, 3, 4]
        bi = 0
        for gg in range(4):
            for th in range(3):
                bank = bk[bi % 3]; bi += 1
                if th < 2:
                    c0, c1, n = th * 512, th * 512 + 512, 512
                    rk = tuple(("hT", 4 * th + k) for k in range(4))
                    dst = uT[:, gg, c0:c1]
                    dkey = "uT"
                else:
                    c0, c1, n = NT, NT + TS, TS
                    rk = (("hT", 8),)
                    dst = uTs[:, gg, :]
                    dkey = "uTs"

                def f(e, gg=gg, c0=c0, c1=c1, n=n, bank=bank):
                    ins = None
                    for c in range(16):
                        ins = e.matmul(out=ps[bank][:, 0:n], lhsT=wU[:, c, gg * 128:(gg + 1) * 128], rhs=hT[:, c, c0:c1],
                                       start=(c == 0), stop=(c == 15))
                    return ins
                P.add("pe", f, r=rk + (("ring", slotU),), w=("ps%d" % bank,))
                P.add("act", lambda e, dst=dst, n=n, bank=bank: e.activation(out=dst, in_=ps[bank][:, 0:n], func=AF.Gelu),
                      r=("ps%d" % bank,), w=(dkey, "EB", "EBc2", "EBp2") if th < 2 else (dkey,))
        gtile = Vr[0]
        for i in range(9):
            npart = 128 if i < 8 else TS
            cols = slice(i * 128, (i + 1) * 128) if i < 8 else slice(NT, NT + TS)
            proj(lambda c, cols=cols: hT[:, c, cols], (("hT", i),), "G", 2, npart)
            P.add("act", lambda e, npart=npart: e.activation(out=kfQ[0:npart, :], in_=ps[2][0:npart, :], func=AF.Gelu),
                  r=("ps2",), w=("kfQ",))
            P.add("act", lambda e, npart=npart: e.activation(out=vf[0:npart, :], in_=kfQ[0:npart, :], func=AF.Square,
                                                             accum_out=st_ssq[0:npart, :]), r=("kfQ",), w=("vf", "st_ssq"))
            P.add("act", lambda e, npart=npart: e.activation(out=st_rs[0:npart, :], in_=st_ssq[0:npart, :], func=AF.Sqrt,
                                                             scale=1.0 / 512, bias=EPS), r=("st_ssq",), w=("st_rs",))
            P.add("dve", lambda e, npart=npart: e.reciprocal(out=st_rs[0:npart, :], in_=st_rs[0:npart, :]), r=("st_rs",), w=("st_rs",))
            if i < 8:
                P.add("dve", lambda e: e.scalar_tensor_tensor(out=gtile, in0=kfQ[:, :], scalar=st_rs[:, 0:1], in1=bufA[:, :],
                                                              op0=ALU.mult, op1=ALU.mult), r=("kfQ", "st_rs", "bufA"), w=(("V", 0),))
            else:
                P.add("dve", lambda e: e.scalar_tensor_tensor(out=vf[0:TS, :], in0=kfQ[0:TS, :], scalar=st_rs[0:TS, 0:1], in1=bufA[0:TS, :],
                                                              op0=ALU.mult, op1=ALU.mult), r=("kfQ", "st_rs", "bufA"), w=("vf",))
                dma("sp", gvs[:, :], vf[0:TS, :], ("vf",), (("o", "G"),), "stG")
                P.add("dve", lambda e: e.tensor_copy(out=gtile[0:TS, :], in_=vf[0:TS, :]), r=("vf",), w=(("V", 0),))
            nq = npart

            def fm(e, npart=npart):
                ins = None
                for gg in range(4):
                    ins = e.matmul(out=ps[5][:, gg * 128:gg * 128 + npart], lhsT=gtile[0:npart, gg * 128:(gg + 1) * 128],
                                   rhs=WmT[0:npart, gg, 0:npart], start=True, stop=True)
                return ins
            P.add("pe", fm, r=(("V", 0), "WmT"), w=("ps5",))
            mixv = ps[5][:, :].rearrange("p (g t) -> p g t", g=4)[:, :, 0:npart]
            bbv = bufB[:, :].rearrange("p (g t) -> p g t", g=4)[:, :, 0:npart]
            tmpv = kfK[:, :].rearrange("p (g t) -> p g t", g=4)[:, :, 0:npart]
            P.add("dve", lambda e, mixv=mixv, bbv=bbv, tmpv=tmpv: e.tensor_tensor(out=tmpv, in0=mixv, in1=bbv, op=ALU.add),
                  r=("ps5", "bufB"), w=("kfK",))
            uv = uT[:, :, cols] if i < 8 else uTs[:, :, :]
            P.add("dve", lambda e, tmpv=tmpv, uv=uv, cols=cols: e.tensor_tensor(out=catT[:, 4:8, cols], in0=tmpv, in1=uv, op=ALU.mult),
                  r=("kfK", "uT", "uTs"), w=(("cat", i),))
        wload_next(); wload_next()

        if STOP <= 7:
            return

        for h in range(4):
            P.add("dve", lambda e, h=h: e.reciprocal(out=accD[:, h, :], in_=accD[:, h, :]), r=("accD",), w=("accD",))
            P.add("dve", lambda e, h=h: e.tensor_tensor(out=catT[:, h, 0:NT], in0=accO[:, h, :], in1=accD[:, h, :], op=ALU.mult),
                  r=("accO", "accD"), w=tuple(("cat", i) for i in range(8)))
        P.add("dve", lambda e: e.reciprocal(out=accDs[:], in_=accDs[:]), r=("accDs",), w=("accDs",))
        P.add("dve", lambda e: e.tensor_tensor(out=catT[:, 0:4, NT:NT + TS], in0=accOs[:], in1=accDs[:], op=ALU.mult),
              r=("accOs", "accDs"), w=(("cat", 8),))
        P.barrier()

        if STOP <= 8:
            return

        for i in range(8):
            dma("sp", x1[:, i, :], xm[i * 128:(i + 1) * 128, :], (), (("x1", i),), ("x1l", i))
        dma("sp", xt[0:TS, :], xs[:, :], (), ("xt",), "xt")

        def outproj(i):
            npart = 128 if i < 8 else TS
            cols = slice(i * 128, (i + 1) * 128) if i < 8 else slice(NT, NT + TS)
            for cb in range(4):
                slot, wv = W("O%d" % (cb // 2))
                bank = 2 + cb

                def f(e, wv=wv, cb=cb, bank=bank):
                    ins = None
                    for c in range(8):
                        ins = e.matmul(out=ps[bank][0:npart, :], lhsT=catT[:, c, cols], rhs=wv[:, c, (cb % 2) * 512:(cb % 2) * 512 + 512],
                                       start=(c == 0), stop=(c == 7))
                    return ins
                P.add("pe", f, r=(("cat", i), ("ring", slot)), w=("ps%d" % bank,))
                dst = x1[:, i, cb * 512:(cb + 1) * 512] if i < 8 else xt[0:TS, cb * 512:(cb + 1) * 512]
                key = ("x1", i) if i < 8 else "xt"
                P.add("dve", lambda e, dst=dst, bank=bank: e.tensor_tensor(out=dst, in0=ps[bank][0:npart, :], in1=dst, op=ALU.add),
                      r=("ps%d" % bank, key), w=(key,))

        def norm2(i):
            cols = slice(i * 128, (i + 1) * 128) if i < 8 else slice(NT, NT + TS)
            if i < 8:
                norm_tile(None, 128, gffn, lambda c0, c1: hT[:, c0:c1, cols], (("hT", i),), load=False,
                          src_sb=x1[:, i, :], src_keys=(("x1", i),))
            else:
                norm_tile(None, TS, gffn, lambda c0, c1: hT[:, c0:c1, cols], (("hT", 8),), load=False,
                          src_sb=xt[0:TS, :], src_keys=("xt",))

        outproj(0)
        for i in range(1, 9):
            outproj(i)
            norm2(i - 1)
        norm2(8)
        P.barrier()
        wload_next(); wload_next()

        if STOP <= 9:
            return

        upb = [0]
        dnb = [0]

        def up(n):
            slot, wv = W("UP%d" % n)
            for fc in range(4):
                for th in range(3):
                    bank = upb[0] % 4; upb[0] += 1
                    if th < 2:
                        c0, c1, nn = th * 512, th * 512 + 512, 512
                        rk = tuple(("hT", 4 * th + k) for k in range(4))
                        dst = aT[n % 2][:, fc, c0:c1]
                        dkey = ("aT", n % 2)
                    else:
                        c0, c1, nn = NT, NT + TS, TS
                        rk = (("hT", 8),)
                        dst = aTs[:, n % 2, fc, :]
                        dkey = ("aTs", n % 2)

                    def f(e, fc=fc, c0=c0, c1=c1, nn=nn, bank=bank):
                        ins = None
                        for c in range(16):
                            ins = e.matmul(out=ps[bank][:, 0:nn], lhsT=wv[:, c, fc * 128:(fc + 1) * 128], rhs=hT[:, c, c0:c1],
                                           start=(c == 0), stop=(c == 15))
                        return ins
                    P.add("pe", f, r=rk + (("ring", slot),), w=("ps%d" % bank,))
                    rs_ = rsc[bank % 2]
                    P.add("act", lambda e, nn=nn, bank=bank, rs_=rs_: e.activation(out=rs_[:, 0:nn], in_=ps[bank][:, 0:nn], func=AF.Relu),
                          r=("ps%d" % bank,), w=(("rsc", bank % 2),))
                    P.add("act", lambda e, nn=nn, rs_=rs_, dst=dst: e.activation(out=dst, in_=rs_[:, 0:nn], func=AF.Square),
                          r=(("rsc", bank % 2),), w=(dkey,))

        def down(n):
            slot, wv = W("DN%d" % n)
            for i in range(9):
                npart = 128 if i < 8 else TS
                for cb in range(4):
                    bank = 4 + dnb[0] % 4; dnb[0] += 1

                    def f(e, i=i, cb=cb, bank=bank, npart=npart):
                        ins = None
                        for fc in range(4):
                            lhsT = aT[n % 2][:, fc, i * 128:(i + 1) * 128] if i < 8 else aTs[:, n % 2, fc, :]
                            ins = e.matmul(out=ps[bank][0:npart, :], lhsT=lhsT, rhs=wv[:, fc, cb * 512:(cb + 1) * 512],
                                           start=(fc == 0), stop=(fc == 3))
                        return ins
                    rk = (("aT", n % 2), ("ring", slot)) if i < 8 else (("aTs", n % 2), ("ring", slot))
                    P.add("pe", f, r=rk, w=("ps%d" % bank,))
                    dst = x1[:, i, cb * 512:(cb + 1) * 512] if i < 8 else xt[0:TS, cb * 512:(cb + 1) * 512]
                    key = ("x1", i) if i < 8 else "xt"
                    P.add("dve", lambda e, dst=dst, bank=bank, npart=npart: e.tensor_tensor(out=dst, in0=ps[bank][0:npart, :], in1=dst, op=ALU.add),
                          r=("ps%d" % bank, key), w=(key,))

        for n in range(17):
            if n < 16:
                up(n)
                if n >= 1:
                    pass
            if n >= 1:
                down(n - 1)
                wload_next(); wload_next()
        for i in range(8):
            dma("sp", y[i * 128:(i + 1) * 128, :], x1[:, i, :], (("x1", i),), (("o", "y", i),), ("sty", i))
        dma("sp", ys[:, :], xt[0:TS, :], ("xt",), (("o", "ys"),), "stys")

    phases()
    if DBG:
        P.barrier()
        dma("sp", dbg_rx[:, :], RX[:, :], (), (("o", "dbgrx"),), "dbg0")
        dma("sp", dbg_cat[:, :], catT[:].rearrange("p c n -> p (c n)"), (), (("o", "dbgcat"),), "dbg1")
        dma("sp", dbg_hT[:, :], hT[:].rearrange("p c n -> p (c n)"), (), (("o", "dbghT"),), "dbg2")
    P.barrier()
    P.emit(nc, es)
    es.close()
    return nc


_CACHE = {}


def kernel(x_prompt, x_sample, cache_kv_w128, cache_kv_w512, cache_kv_w2048, norm_mix, w_in,
           q_norm, k_norm, rel_bias, gmlp_v_norm, gmlp_w, gmlp_b, w_out, norm_ffn, w_up, w_down):
    f = lambda a: np.ascontiguousarray(np.asarray(a, dtype=np.float32))
    x_prompt = f(x_prompt); x_sample = f(x_sample)
    caches = [f(cache_kv_w128), f(cache_kv_w512), f(cache_kv_w2048)]
    if "nc" not in _CACHE:
        _CACHE["nc"] = build_program()
    nc = _CACHE["nc"]
    oh, ident, jm = host_constants()
    shared = {
        "w_in": f(w_in)[0], "w_out": f(w_out)[0], "w_up": f(w_up)[0], "w_down": f(w_down)[0],
        "norm_mix": f(norm_mix).reshape(16, 128), "norm_ffn": f(norm_ffn).reshape(16, 128),
        "q_norm": f(q_norm).reshape(1, 128), "k_norm": f(k_norm).reshape(1, 128),
        "rel_bias": f(rel_bias), "gmlp_v_norm": f(gmlp_v_norm).reshape(1, 512),
        "gmlp_w": f(gmlp_w)[0], "gmlp_b": f(gmlp_b)[0], "oh": oh, "ident": ident, "jm": jm,
    }
    in_maps = []
    for c in range(8):
        b, j = c // 4, c % 4
        q0 = j * NT
        xh = np.zeros((NH, D), np.float32)
        valid = np.zeros((NH,), np.float32)
        lo = q0 - NH
        s = max(lo, 0)
        if q0 > 0:
            xh[s - lo:] = x_prompt[b, s:q0]
            valid[s - lo:] = 1.0
        kv = np.zeros((128, 21), np.float32)
        kv[:, 0] = valid[1920:2048]
        for r in range(4):
            kv[:, 1 + r] = valid[1536 + r:2048:4]
        for r in range(16):
            kv[:, 5 + r] = valid[r:2048:16]
        m = dict(shared)
        m["xm"] = np.ascontiguousarray(x_prompt[b, q0:q0 + NT])
        m["xh"] = xh
        m["xs"] = np.ascontiguousarray(x_sample[c])
        m["kvalid"] = kv
        for g in range(3):
            m["ck%d" % g] = np.ascontiguousarray(caches[g][0, c].reshape(2, -1, 512))
        in_maps.append(m)
    res = run_bass_kernel_spmd(nc, in_maps, core_ids=list(range(8)))
    R = res.results
    yp = np.zeros((2, 4096, D), np.float32)
    ysm = np.zeros((8, TS, D), np.float32)
    kvp_out = [np.zeros((1, 2, 2, w, 4, 128), np.float32) for w in (128, 512, 2048)]
    kvs_out = [np.zeros((1, 8, 2, TS, 4, 128), np.float32) for _ in range(3)]
    gv = np.zeros((1, 8, TS, 512), np.float32)
    for c in range(8):
        b, j = c // 4, c % 4
        yp[b, j * NT:(j + 1) * NT] = R[c]["y"]
        ysm[c] = R[c]["ys"]
        if j == 3:
            kvp_out[0][0, b] = R[c]["kvp0"].reshape(2, 128, 4, 128)
            kvp_out[1][0, b] = R[c]["kvp1"].reshape(2, 512, 4, 128)
        if j >= 2:
            kvp_out[2][0, b, :, (j - 2) * 1024:(j - 1) * 1024] = R[c]["kvp2"].reshape(2, 1024, 4, 128)
        for g in range(3):
            kvs_out[g][0, c] = R[c]["kvs"][g].reshape(2, TS, 4, 128)
        gv[0, c] = R[c]["gvs"]
    return (yp, ysm, kvp_out[0], kvp_out[1], kvp_out[2], kvs_out[0], kvs_out[1], kvs_out[2], gv)
```

```python
import contextlib
import os
import numpy as np
import concourse.bass as bass
import concourse.mybir as mybir
from concourse.bass_utils import run_bass_kernel_spmd

F32 = mybir.dt.float32
BF16 = mybir.dt.bfloat16
AF = mybir.ActivationFunctionType
ALU = mybir.AluOpType
AX = mybir.AxisListType

D = 2048
NT = 1024
NH = 2048
TS = 4
NCOL = NT + TS
DIN = 5632
DFF = 8192
EPS = 1e-6
DILS = (1, 4, 16)
SCALE = 128 ** -0.5
SAME_ENG_SYNC = os.environ.get('MK_SES', '1') == '1'

ENGS = ("pe", "act", "dve", "pool", "sp")


class Op:
    __slots__ = ("eng", "fn", "deps", "dma", "signal", "seq", "target")


class Prog:
    def __init__(self):
        self.ops = []
        self.lw = {}
        self.rd = {}
        self.dcnt = {}
        self.last = {}
        self.auto_r = ()

    def add(self, eng, fn, r=(), w=(), dma=None, extra=()):
        idx = len(self.ops)
        deps = set(extra)
        r = tuple(r) + tuple(self.auto_r)
        pr = tuple(k for k in r if isinstance(k, str) and k[:2] == 'ps' and k[2:].isdigit())
        if pr:
            r = tuple(k for k in r if k not in pr)
            w = tuple(w) + pr
        for k in r:
            if k in self.lw:
                deps.add(self.lw[k])
        for k in w:
            if k in self.lw:
                deps.add(self.lw[k])
            deps.update(self.rd.get(k, ()))
        op = Op()
        op.eng, op.fn, op.deps, op.dma, op.signal, op.seq, op.target = eng, fn, deps, dma, False, 0, 0
        if dma is not None:
            c = self.dcnt.get(dma, 0) + 16
            self.dcnt[dma] = c
            op.target = c
        for k in r:
            self.rd.setdefault(k, []).append(idx)
        for k in w:
            self.lw[k] = idx
            self.rd[k] = []
        self.ops.append(op)
        if fn is not None:
            self.last[eng] = idx
        return idx

    def barrier(self):
        alld = set(self.last.values())
        for k, v in self.lw.items():
            alld.add(v)
        for e in ENGS:
            self.add(e, None, extra=tuple(alld))

    def emit(self, nc, es):
        limit = int(os.environ.get('MK_NOPS', '0'))
        if limit:
            self.ops = self.ops[:limit]
            for e in ENGS:
                op = Op()
                op.eng, op.fn, op.deps, op.dma, op.signal, op.seq, op.target = e, None, set(range(limit)), None, False, 0, 0
                self.ops.append(op)
        ops = self.ops
        if os.environ.get('MK_DUMP'):
            for i, op in enumerate(ops):
                print(i, op.eng, op.dma, sorted(op.deps)[-6:], getattr(op.fn, '__name__', None))
        for op in ops:
            op.deps = set(d for d in op.deps if ops[d].fn is not None)
            for d in op.deps:
                if ops[d].dma is None:
                    ops[d].signal = True
        cnt = {e: 0 for e in ENGS}
        for op in ops:
            if op.dma is None and op.signal:
                cnt[op.eng] += 1
                op.seq = cnt[op.eng]
        esem = {e: es.enter_context(nc.semaphore("sem_" + e)) for e in ENGS}
        dsem = {}
        for i, k in enumerate(self.dcnt):
            dsem[k] = es.enter_context(nc.semaphore("dsem%d" % i))
        block = es.enter_context(nc.Block())
        reg = {"pe": block.tensor, "act": block.scalar, "dve": block.vector,
               "pool": block.gpsimd, "sp": block.sync}
        for e in ENGS:
            mine = [op for op in ops if op.eng == e]

            def body(eng, e=e, mine=mine):
                waited = {}
                for op in mine:
                    need = {}
                    for d in op.deps:
                        p = ops[d]
                        if p.dma is not None:
                            s, v = dsem[p.dma], p.target
                        else:
                            if p.eng == e and (e == "pe" or not SAME_ENG_SYNC):
                                continue
                            s, v = esem[p.eng], p.seq
                        key = id(s)
                        if key not in need or need[key][1] < v:
                            need[key] = (s, v)
                    for key, (s, v) in need.items():
                        if waited.get(key, 0) < v:
                            eng.wait_ge(s, v)
                            waited[key] = v
                    if op.fn is not None:
                        ins = op.fn(eng)
                        if op.dma is not None:
                            ins.then_inc(dsem[op.dma], 16)
                        elif op.signal:
                            ins.then_inc(esem[e], 1)
            reg[e](body)


def bucket(dist):
    if dist < 16:
        return dist
    v = 16 + int(np.float32(np.log(np.float32(dist) / np.float32(16)) / np.float32(np.log(128.0)) * np.float32(16)))
    return min(v, 31)


def t5_bucket_np(dist):
    import math
    d = np.maximum(dist, 1).astype(np.float32)
    rnd = np.rint if os.environ.get('MK_BUCKET_RINT', '0') == '1' else np.trunc
    large = 16 + rnd(np.log(d / np.float32(16)) / np.float32(math.log(2048 / 16)) * np.float32(16)).astype(np.int32)
    large = np.minimum(large, 31)
    return np.where(dist < 16, dist, large)


def host_constants():
    oh = np.zeros((33, 3, 384), np.float32)
    for g, dil in enumerate(DILS):
        sub = np.arange(384) - 128
        ok = (sub >= 0) & (sub <= 128)
        b = t5_bucket_np(np.clip(sub, 0, 128).astype(np.int32) * dil)
        for u in range(384):
            if ok[u]:
                oh[b[u], g, u] = 1.0
            else:
                oh[32, g, u] = 1.0
    ident = np.eye(128, dtype=np.float32)
    jm = np.ascontiguousarray(ident[::-1])
    return oh.reshape(33, 1152), ident, jm


def build_program():
    nc = bass.Bass("TRN2", target_bir_lowering=False)

    def din(name, shape):
        return nc.dram_tensor(name, shape, F32, kind="ExternalInput").ap()

    def dout(name, shape):
        return nc.dram_tensor(name, shape, F32, kind="ExternalOutput").ap()

    xm = din("xm", [NT, D]); xh = din("xh", [NH, D]); xs = din("xs", [TS, D])
    kvalid = din("kvalid", [128, 21])
    ck = [din("ck0", [2, 128, 512]), din("ck1", [2, 512, 512]), din("ck2", [2, 2048, 512])]
    w_in = din("w_in", [D, DIN]); w_out = din("w_out", [1024, D])
    w_up = din("w_up", [D, DFF]); w_down = din("w_down", [DFF, D])
    norm_mix = din("norm_mix", [16, 128]); norm_ffn = din("norm_ffn", [16, 128])
    q_norm = din("q_norm", [1, 128]); k_norm = din("k_norm", [1, 128])
    rel_bias = din("rel_bias", [32, 12]); gvn = din("gmlp_v_norm", [1, 512])
    gw = din("gmlp_w", [4, 128, 128]); gb = din("gmlp_b", [4, 128])
    oh_d = din("oh", [33, 1152]); ident_d = din("ident", [128, 128]); jm_d = din("jm", [128, 128])

    y = dout("y", [NT, D]); ys = dout("ys", [TS, D])
    kvp = [dout("kvp0", [2, 128, 512]), dout("kvp1", [2, 512, 512]), dout("kvp2", [2, 1024, 512])]
    kvs = dout("kvs", [3, 2, TS, 512]); gvs = dout("gvs", [TS, 512])
    escr = dout("escr", [12, 384])
    DBG = os.environ.get('MK_DBG')
    if DBG:
        dbg_rx = dout("dbg_rx", [128, 16384])
        dbg_cat = nc.dram_tensor("dbg_cat", [128, 8 * NCOL], BF16, kind="ExternalOutput").ap()
        dbg_hT = nc.dram_tensor("dbg_hT", [128, 16 * NCOL], BF16, kind="ExternalOutput").ap()

    P = Prog()
    es = contextlib.ExitStack()
    STOP = int(os.environ.get('MK_STOP', '99'))
    SUB = int(os.environ.get('MK_SUB', '255'))

    def sb(name, shape, dt):
        return es.enter_context(nc.sbuf_tensor(name, shape, dt))

    hT = sb("hT", [128, 16, NCOL], BF16)
    ring = [sb("ring%d" % i, [128, 8192], BF16) for i in range(4)]
    RX = sb("RX", [128, 16384], F32)
    catT = sb("catT", [128, 8, NCOL], BF16)
    xt = sb("xt", [128, D], F32)
    xb = sb("xb", [128, D], BF16)
    hTh = sb("hTh", [128, 16, 128], BF16)
    kfK = sb("kfK", [128, 512], F32)
    vf = sb("vf", [128, 512], F32)
    PTb = [sb("PT%d" % i, [128, 2, 2, 128], BF16) for i in range(2)]
    identb = sb("identb", [128, 128], BF16)
    identf = sb("identf", [128, 128], F32)
    onesb = sb("onesb", [128, 128], BF16)
    gmix = sb("gmix", [128, 16], F32)
    gffn = sb("gffn", [128, 16], F32)
    bufA = sb("bufA", [128, 512], F32)
    bufB = sb("bufB", [128, 512], F32)
    WmT = sb("WmT", [128, 4, 128], BF16)
    kval = sb("kval", [128, 21], F32)
    st_ssq = sb("st_ssq", [128, 1], F32)
    st_rs = sb("st_rs", [128, 1], F32)
    st4 = [sb("st4_%d" % i, [128, 4], F32) for i in range(4)]
    accOs = sb("accOs", [128, 4, TS], F32)
    accDs = sb("accDs", [128, 4, TS], F32)
    uTs = sb("uTs", [128, 4, TS], F32)
    ps = [es.enter_context(nc.psum_tensor("ps%d" % i, [128, 512], F32)) for i in range(8)]

    accO = RX[:, 0:4096].rearrange("p (h t) -> p h t", h=4)
    accD = RX[:, 4096:8192].rearrange("p (h t) -> p h t", h=4)
    EB = RX[:, 8192:11264].rearrange("p (g a q) -> p g a q", g=12, a=2)
    EBc2 = RX[:, 11264:11776].rearrange("p (h q) -> p h q", h=4)
    EBp2 = RX[:, 11776:12288].rearrange("p (h q) -> p h q", h=4)
    uT = RX[:, 8192:12288].rearrange("p (h t) -> p h t", h=4)
    rest = RX[:, 12288:16384]
    KTr = [rest[:, i * 256:(i + 1) * 256].bitcast(BF16).rearrange("p (h k) -> p h k", h=4) for i in range(5)]
    Vr = [rest[:, 1280 + i * 256:1280 + (i + 1) * 256].bitcast(BF16) for i in range(5)]
    QTr = [rest[:, 2560 + i * 256:2560 + (i + 1) * 256].bitcast(BF16).rearrange("p (h k) -> p h k", h=4) for i in range(3)]
    Ebuf = rest[:, 3328:3840].rearrange("p (a h q) -> p a h q", a=2, h=2)
    kfQ = RX[:, 12288 + 3840 - 512:12288 + 3840]
    kfQ = sb("kfQ", [128, 512], F32)
    x1 = RX[:, :].rearrange("p (i d) -> p i d", i=8)
    catb = catT[:].rearrange("p c n -> p (c n)")
    xt2 = catb[:, 0:4096].bitcast(F32)
    hTh2 = catb[:, 4096:6144].rearrange("p (c n) -> p c n", c=16)
    kfK2 = catb[:, 6144:7168].bitcast(F32)
    kfQ2 = catb[:, 7168:8192].bitcast(F32)
    XT = [(xt, "xt"), (xt2, "xt2")]
    XT2_OK = [True]
    HTH = [(hTh, "hTh"), (hTh2, "hTh2")]
    KFK = [(kfK, "kfK"), (kfK2, "kfK2")]
    KFQ = [(kfQ, "kfQ"), (kfQ2, "kfQ2")]
    hTf = RX
    oh_sb = hTf[:, 0:1152]
    Eall = hTf[:, 1152:2304].rearrange("p (g u) -> p g u", g=3)
    jm_sb = hTf[:, 2304:2432]
    Hh = hTf[:, 2432:2432 + 3072].rearrange("p (g a q) -> p g a q", g=12, a=2)
    tab33 = hTf[:, 5504:5516]
    wtmp = hTf[:, 5632:5632 + 512].rearrange("p (g s) -> p g s", g=4)
    vtmp = hTf[:, 6144:6144 + 128]
    aT = [catT[:].rearrange("p c n -> p (c n)")[:, i * 4096:(i + 1) * 4096].rearrange("p (f t) -> p f t", f=4) for i in range(2)]
    aTs = sb("aTs", [128, 2, 4, TS], BF16)
    rsc = [xb[:, i * 1024:(i + 1) * 1024].bitcast(F32) for i in range(2)]

    psb = [p[:].bitcast(BF16) for p in ps]

    def dma(q, out, in_, r, w, key):
        return P.add(q, lambda e, out=out, in_=in_: e.dma_start(out=out, in_=in_), r=r, w=w, dma=key)

    def bcast_rows(dram_ap_row, nparts, n):
        return bass.AP(tensor=dram_ap_row.tensor, offset=dram_ap_row.offset, ap=[[0, nparts], [1, n]])

    def bc_free(ap2, n):
        a = ap2.ap
        return bass.AP(tensor=ap2.tensor, offset=ap2.offset, ap=[list(a[0]), list(a[1]), [0, n]])

    wblocks = []

    def wview_rows(w, col0, ncols):
        return w.rearrange("(c p) n -> p c n", p=128)[:, :, col0:col0 + ncols]

    for g in (2, 1, 0):
        wblocks.append(("K%d" % g, wview_rows(w_in, 1536 + g * 512, 512), (16, 512)))
        wblocks.append(("V%d" % g, wview_rows(w_in, 3072 + g * 512, 512), (16, 512)))
        wblocks.append(("Q%d" % g, wview_rows(w_in, g * 512, 512), (16, 512)))
    wblocks.append(("U", wview_rows(w_in, 4608, 512), (16, 512)))
    wblocks.append(("G", wview_rows(w_in, 5120, 512), (16, 512)))
    wblocks.append(("O0", wview_rows(w_out, 0, 1024), (8, 1024)))
    wblocks.append(("O1", wview_rows(w_out, 1024, 1024), (8, 1024)))
    for n in range(16):
        wblocks.append(("UP%d" % n, wview_rows(w_up, n * 512, 512), (16, 512)))
        wblocks.append(("DN%d" % n, w_down[n * 512:(n + 1) * 512, :].rearrange("(c p) n -> p c n", p=128), (4, 2048)))
    wslot = {}
    wnext = [0]

    def wload_next():
        i = wnext[0]
        if i >= len(wblocks):
            return
        name, src, (c, n) = wblocks[i]
        slot = i % 4
        dst = ring[slot][:].rearrange("p (c n) -> p c n", c=c)
        wslot[name] = (slot, dst)
        dma("pool", dst, src, r=(), w=(("ring", slot),), key=("ring", slot))
        wnext[0] += 1

    def W(name):
        return wslot[name]

    def phases():
        P.auto_r = ("accO", "accD")
        dma("sp", identf[:], ident_d[:, :], (), ("identf",), "c0")
        dma("sp", jm_sb, jm_d[:, :], (), ("jm",), "c1")
        dma("sp", oh_sb[0:33, :], oh_d[:, :], (), ("oh",), "c2")
        dma("sp", tab33[0:32, :], rel_bias[:, :], (), ("tab",), "c3")
        dma("sp", kval[:], kvalid[:, :], (), ("kval",), "c4")
        dma("sp", bufA[:], bass.AP(tensor=q_norm.tensor, offset=0, ap=[[0, 128], [0, 4], [1, 128]]), (), ("bufA",), "c5")
        dma("sp", bufB[:], bass.AP(tensor=k_norm.tensor, offset=0, ap=[[0, 128], [0, 4], [1, 128]]), (), ("bufB",), "c6")
        dma("sp", vtmp[0:16, :], norm_mix[:, :], (), ("vtmp",), "c7")
        for _ in range(4):
            wload_next()
        P.add("dve", lambda e: e.tensor_copy(out=identb[:], in_=identf[:]), r=("identf",), w=("identb",))
        P.add("dve", lambda e: e.memset(onesb[:], 1.0), w=("onesb",))
        P.add("dve", lambda e: e.memset(tab33[32:33, :], -30000.0), w=("tab32",))
        if SUB & 1:
            P.add("pe", lambda e: e.transpose(out=ps[7][:, 0:16], in_=vtmp[0:16, :], identity=identf[0:16, 0:16]),
                  r=("vtmp", "identf"), w=("ps7",))
            P.add("act", lambda e: e.copy(out=gmix[:], in_=ps[7][:, 0:16]), r=("ps7",), w=("gmix",))
            dma("sp", vtmp[0:16, :], norm_ffn[:, :], (), ("vtmp",), "c7")
            P.add("pe", lambda e: e.transpose(out=ps[7][:, 0:16], in_=vtmp[0:16, :], identity=identf[0:16, 0:16]),
                  r=("vtmp", "identf"), w=("ps7",))
            P.add("act", lambda e: e.copy(out=gffn[:], in_=ps[7][:, 0:16]), r=("ps7",), w=("gffn",))
        if SUB & 2:
            dma("sp", wtmp, gw.rearrange("g t s -> t g s"), (), ("wtmp",), "c8")
            for gg in range(4):
                P.add("pool", lambda e, gg=gg: e.affine_select(out=wtmp[:, gg, :], in_=wtmp[:, gg, :], pattern=[[-1, 128]],
                                                               compare_op=ALU.is_ge, fill=0.0, base=0, channel_multiplier=1),
                      r=("wtmp",), w=("wtmp",))
            for gg in range(4):
                P.add("pe", lambda e, gg=gg: e.transpose(out=ps[6][:, gg * 128:(gg + 1) * 128], in_=wtmp[:, gg, :], identity=identf[:]),
                      r=("wtmp", "identf"), w=("ps6",))
            P.add("act", lambda e: e.copy(out=WmT[:].rearrange("p g t -> p (g t)"), in_=ps[6][:, :]), r=("ps6",), w=("WmT",))
        if SUB & 4:
            for g in range(3):
                P.add("pe", lambda e, g=g: e.matmul(out=ps[g][0:12, 0:384], lhsT=tab33[0:33, 0:12], rhs=oh_sb[0:33, g * 384:(g + 1) * 384],
                                                    start=True, stop=True),
                      r=("tab", "tab32", "oh"), w=("ps%d" % g,))
                P.add("act", lambda e, g=g: e.activation(out=Eall[0:12, g, :], in_=ps[g][0:12, 0:384], func=AF.Exp),
                      r=("ps%d" % g,), w=("Eall%d" % g,))
                dma("sp", escr[g * 4:(g + 1) * 4, :], Eall[g * 4:(g + 1) * 4, g, :], ("Eall%d" % g,), ("escr%d" % g,), "e%d" % g)
            for gh in range(12):
                src = bass.AP(tensor=escr.tensor, offset=gh * 384 + 1, ap=[[1, 128], [128, 2], [1, 128]])
                dma("sp", Hh[:, gh, :, :], src, ("escr%d" % (gh // 4),), ("Hh%d" % gh,), "h%d" % gh)
            Hf = Hh.rearrange("p g a q -> p (g a q)")
            EBf = EB.rearrange("p g a q -> p (g a q)")
            for i in range(6):
                P.add("pe", lambda e, i=i: e.matmul(out=ps[i][:, :], lhsT=jm_sb, rhs=Hf[:, i * 512:(i + 1) * 512], start=True, stop=True),
                      r=("jm", "Hh%d" % (2 * i), "Hh%d" % (2 * i + 1)), w=("ps%d" % i,))
                P.add("dve" if i % 2 else "act",
                      (lambda e, i=i: e.tensor_copy(out=EBf[:, i * 512:(i + 1) * 512], in_=ps[i][:, :])) if i % 2 else
                      (lambda e, i=i: e.copy(out=EBf[:, i * 512:(i + 1) * 512], in_=ps[i][:, :])),
                      r=("ps%d" % i,), w=("EB",))
        if SUB & 8:
            P.add("dve", lambda e: e.tensor_copy(out=EBc2, in_=EB[:, 8:12, 0, :]), r=("EB",), w=("EBc2",))
            P.add("dve", lambda e: e.memset(EBc2[0:64, :, 64:128], 0.0), w=("EBc2",))
            P.add("dve", lambda e: e.tensor_copy(out=EBp2[:, :, 0:64], in_=EB[:, 8:12, 1, 0:64]), r=("EB",), w=("EBp2",))
            P.add("dve", lambda e: e.tensor_copy(out=EBp2[:, :, 64:128], in_=EB[:, 8:12, 1, 0:64]), r=("EB",), w=("EBp2",))
        P.auto_r = ()
        if STOP <= 1:
            return

        xsel = [0]

        def norm_parts(x_src, npart, gains, dst_fn, dst_keys, load=True, src_sb=None, src_keys=()):
            if load:
                xbuf, xkey = XT[xsel[0] % 2] if XT2_OK[0] else XT[0]
                xsel[0] += 1
                src = xbuf[0:npart, :]
                skeys = (xkey,)
            else:
                src = src_sb
                skeys = tuple(src_keys)

            def n0():
                if load:
                    dma("sp", src, x_src, (), skeys, skeys[0])

            def n1():
                P.add("act", lambda e: e.activation(out=xb[0:npart, :], in_=src, func=AF.Square, accum_out=st_ssq[0:npart, :]),
                      r=skeys, w=("xb", "st_ssq"))
                P.add("act", lambda e: e.activation(out=st_rs[0:npart, :], in_=st_ssq[0:npart, :], func=AF.Sqrt, scale=1.0 / D, bias=EPS),
                      r=("st_ssq",), w=("st_rs",))
                P.add("dve", lambda e: e.reciprocal(out=st_rs[0:npart, :], in_=st_rs[0:npart, :]), r=("st_rs",), w=("st_rs",))
                P.add("dve", lambda e: e.tensor_scalar(out=xb[0:npart, :], in0=src, scalar1=st_rs[0:npart, 0:1], scalar2=None, op0=ALU.mult),
                      r=skeys + ("st_rs",), w=("xb",))

            def n2():
                for half in range(2):
                    def tp(e, half=half):
                        ins = None
                        for c in range(8):
                            cc = half * 8 + c
                            ins = e.transpose(out=psb[half][:, c * 128:c * 128 + npart], in_=xb[0:npart, cc * 128:(cc + 1) * 128],
                                              identity=identb[0:npart, 0:npart])
                        return ins
                    P.add("pe", tp, r=("xb", "identb"), w=("ps%d" % half,))
                    src_ps = psb[half][:, :].rearrange("p (c n) -> p c n", c=8)[:, :, 0:npart]
                    P.add("dve", lambda e, half=half, src_ps=src_ps: e.tensor_tensor(
                        out=dst_fn(half * 8, half * 8 + 8), in0=src_ps, in1=bc_free(gains[:, half * 8:half * 8 + 8], npart), op=ALU.mult),
                        r=("ps%d" % half, "gmix", "gffn"), w=tuple(dst_keys))
            return n0, n1, n2

        def norm_tile(x_src, npart, gains, dst_fn, dst_keys, load=True, src_sb=None, src_keys=()):
            n0, n1, n2 = norm_parts(x_src, npart, gains, dst_fn, dst_keys, load=load, src_sb=src_sb, src_keys=src_keys)
            n0(); n1(); n2()

        p1 = [norm_parts(xm[i * 128:(i + 1) * 128, :], 128, gmix, (lambda c0, c1, i=i: hT[:, c0:c1, i * 128:(i + 1) * 128]), (("hT", i),))
              for i in range(8)]
        p1[0][0](); p1[1][0]()
        for i in range(8):
            p1[i][1]()
            p1[i][2]()
            if i + 2 < 8:
                p1[i + 2][0]()
        norm_tile(xs[:, :], TS, gmix, lambda c0, c1: hT[:, c0:c1, NT:NT + TS], (("hT", 8),))

        if STOP <= 2:
            return

        def proj(lhs_fn, rkeys, wname, bank, npart):
            slot, wv = W(wname)

            def f(e):
                ins = None
                for c in range(16):
                    ins = e.matmul(out=ps[bank][0:npart, :], lhsT=lhs_fn(c), rhs=wv[:, c, :], start=(c == 0), stop=(c == 15))
                return ins
            P.add("pe", f, r=tuple(rkeys) + (("ring", slot),), w=("ps%d" % bank,))

        def qk_norm(bank, npart, gainbuf, gkey, kf, kfkey, ssq, rs, skey):
            P.add("act", lambda e: e.activation(out=vf[0:npart, :], in_=ps[bank][0:npart, :], func=AF.Square),
                  r=("ps%d" % bank,), w=("vf",))
            P.add("dve", lambda e: e.tensor_reduce(out=ssq[0:npart, :], in_=vf[0:npart, :].rearrange("p (h d) -> p h d", h=4),
                                                   axis=AX.X, op=ALU.add), r=("vf",), w=(skey,))
            P.add("act", lambda e: e.activation(out=rs[0:npart, :], in_=ssq[0:npart, :], func=AF.Sqrt, scale=1.0 / 128, bias=EPS),
                  r=(skey,), w=(skey + "r",))
            P.add("dve", lambda e: e.reciprocal(out=rs[0:npart, :], in_=rs[0:npart, :]), r=(skey + "r",), w=(skey + "r",))
            for h in range(4):
                P.add("dve", lambda e, h=h: e.scalar_tensor_tensor(
                    out=kf[0:npart, h * 128:(h + 1) * 128], in0=ps[bank][0:npart, h * 128:(h + 1) * 128],
                    scalar=rs[0:npart, h:h + 1], in1=gainbuf[0:npart, h * 128:(h + 1) * 128], op0=ALU.mult, op1=ALU.mult),
                    r=("ps%d" % bank, skey + "r", gkey), w=(kfkey,))

        def tr4(kf, kfkey, npart, dst, dkey):
            def f(e):
                ins = None
                for h in range(4):
                    ins = e.transpose(out=ps[5][:, h * 128:h * 128 + npart], in_=kf[0:npart, h * 128:(h + 1) * 128],
                                      identity=identf[0:npart, 0:npart])
                return ins
            P.add("pe", f, r=(kfkey, "identf"), w=("ps5",))
            P.add("act", lambda e: e.copy(out=dst, in_=ps[5][:, :].rearrange("p (h k) -> p h k", h=4)[:, :, 0:npart]),
                  r=("ps5",), w=(dkey,))

        first_group = [True]

        def run_group(g):
            dil = DILS[g]
            items = []
            if g == 0:
                items.append(dict(kind="H", rows=xh[1920:2048, :], vcol=0))
                for i in range(8):
                    items.append(dict(kind="M", lhs=(lambda c, i=i: hT[:, c, i * 128:(i + 1) * 128]), hkeys=(("hT", i),),
                                      pos=("c", i), out=(7 == i and [(0, 128, kvp[0][:, 0:128, :])] or [])))
            elif g == 1:
                for r in range(4):
                    items.append(dict(kind="H", rows=xh[1536 + r:2048:4, :], vcol=1 + r))
                    for s in range(2):
                        items.append(dict(kind="M", lhs=(lambda c, s=s, r=r: hT[:, c, s * 512 + r:s * 512 + 512:4]),
                                          hkeys=tuple(("hT", 4 * s + k) for k in range(4)), pos=("s4", s, r),
                                          out=(s == 1 and [(0, 128, kvp[1][:, r:512:4, :])] or [])))
            else:
                for T in range(8):
                    items.append(dict(kind="H", rows=xh[2 * T:2048:16, :], vcol=5 + 2 * T))
                    items.append(dict(kind="H", rows=xh[2 * T + 1:2048:16, :], vcol=5 + 2 * T + 1))
                    items.append(dict(kind="M", lhs=(lambda c: hTh[:, c, :]), gather=T,
                                      hkeys=("hTh",), pos=("s16", T),
                                      out=[(0, 64, kvp[2][:, 2 * T:1024:16, :]), (64, 128, kvp[2][:, 2 * T + 1:1024:16, :])]))
            n_items = len(items)
            for idx, it in enumerate(items):
                it["idx"] = idx
                it["kslot"] = idx % 5
            qcount = [0]
            for it in items:
                if it["kind"] == "M":
                    it["qslot"] = qcount[0] % 3
                    it["qbuf"] = qcount[0] % 2
                    qcount[0] += 1
            for idx, it in enumerate(items):
                if it["kind"] != "M":
                    continue
                if g == 2:
                    it["prev"] = [(items[idx - 2], 0, 64), (items[idx - 1], 64, 128)]
                else:
                    it["prev"] = [(items[idx - 1], 0, 128)]

            Kn, Vn, Qn = "K%d" % g, "V%d" % g, "Q%d" % g

            hsel = [0]

            def stageN(it):
                if it["kind"] == "H" or "gather" in it:
                    hb, hkey = HTH[hsel[0] % 2]
                    hsel[0] += 1
                    if it["kind"] == "H":
                        it["_n"] = norm_parts(it["rows"], 128, gmix, (lambda c0, c1, hb=hb: hb[:, c0:c1, :]), (hkey,))
                    else:
                        T = it["gather"]
                        srcv = hT[:, :, 0:NT].rearrange("p c (m r) -> p c r m", r=16)

                        def gat(T=T, srcv=srcv, hb=hb, hkey=hkey):
                            for rr in range(2):
                                P.add("dve", lambda e, rr=rr: e.tensor_copy(out=hb[:, :, rr * 64:(rr + 1) * 64], in_=srcv[:, :, 2 * T + rr, :]),
                                      r=tuple(("hT", k) for k in range(8)), w=(hkey,))
                        it["_n"] = (None, None, gat)
                    it["_lhs"], it["_hk"] = (lambda c, hb=hb: hb[:, c, :]), (hkey,)
                else:
                    it["_lhs"], it["_hk"] = it["lhs"], it["hkeys"]

            def stageA1(it):
                lhs, hk = it["_lhs"], it["_hk"]
                kK, kKkey = KFK[it["idx"] % 2]
                it["_kfK"] = (kK, kKkey)
                if it["kind"] == "M":
                    kQ, kQkey = KFQ[it["qbuf"]]
                    it["_kfQ"] = (kQ, kQkey)
                    proj(lhs, hk, Qn, 4, 128)
                    qk_norm(4, 128, bufA, "bufA", kQ, kQkey, st4[0], st4[1], "sq")
                proj(lhs, hk, Kn, 2, 128)
                qk_norm(2, 128, bufB, "bufB", kK, kKkey, st4[2], st4[3], "sk")
                for (p0, p1, dst) in it.get("out", []):
                    dma("sp", dst[0], kK[p0:p1, :], (kKkey,), (("o", "K", g),), ("stK", g, it["idx"] % 2))

            def stageB(it):
                if it["kind"] == "M":
                    kQ, kQkey = it["_kfQ"]
                    tr4(kQ, kQkey, 128, QTr[it["qslot"]], ("QT", it["qslot"]))
                kK, kKkey = it["_kfK"]
                tr4(kK, kKkey, 128, KTr[it["kslot"]], ("KT", it["kslot"]))

            def stageA2(it):
                proj(it["_lhs"], it["_hk"], Vn, 3, 128)
                ks = it["kslot"]
                P.add("act", lambda e: e.copy(out=Vr[ks], in_=ps[3][:, :]), r=("ps3",), w=(("V", ks),))
                outs = it.get("out", [])
                if outs:
                    P.add("dve", lambda e: e.tensor_copy(out=vf[:], in_=ps[3][:, :]), r=("ps3",), w=("vf",))
                    for (p0, p1, dst) in outs:
                        dma("sp", dst[1], vf[p0:p1, :], ("vf",), (("o", "V", g),), ("stV", g))

            def stageC(it):
                qs, ks = it["qslot"], it["kslot"]
                for hp in range(2):
                    bank = 6 if hp == 0 else 0
                    stv = ps[bank][:, :].rearrange("p (a h q) -> p a h q", a=2, h=2)

                    def f(e, hp=hp, stv=stv):
                        ins = None
                        for hh in range(2):
                            h = hp * 2 + hh
                            for (pit, c0, c1) in it["prev"]:
                                ins = e.matmul(out=stv[:, 0, hh, c0:c1], lhsT=KTr[pit["kslot"]][:, h, :], rhs=QTr[qs][:, h, c0:c1],
                                               start=True, stop=True)
                            ins = e.matmul(out=stv[:, 1, hh, :], lhsT=KTr[ks][:, h, :], rhs=QTr[qs][:, h, :], start=True, stop=True)
                        return ins
                    rk = [("QT", qs), ("KT", ks)] + [("KT", pit["kslot"]) for (pit, _, _) in it["prev"]]
                    P.add("pe", f, r=tuple(rk), w=("ps%d" % bank,))
                    P.add("act", lambda e, bank=bank: e.activation(out=Ebuf.rearrange("p a h q -> p (a h q)"), in_=ps[bank][:, :],
                                                                   func=AF.Exp, scale=SCALE), r=("ps%d" % bank,), w=("Ebuf",))
                    pt = PTb[hp]
                    if g == 2:
                        ebc = EBc2[:, hp * 2:hp * 2 + 2, :]
                        ebp = EBp2[:, hp * 2:hp * 2 + 2, :]
                        ekeys = ("EBc2", "EBp2")
                    else:
                        ebc = EB[:, g * 4 + hp * 2:g * 4 + hp * 2 + 2, 0, :]
                        ebp = EB[:, g * 4 + hp * 2:g * 4 + hp * 2 + 2, 1, :]
                        ekeys = ("EB",)
                    P.add("dve", lambda e, pt=pt, ebc=ebc: e.tensor_tensor(out=pt[:, 1, :, :], in0=Ebuf[:, 1, :, :], in1=ebc, op=ALU.mult),
                          r=("Ebuf",) + ekeys, w=(("PT", hp),))
                    for (pit, c0, c1) in it["prev"]:
                        if pit["kind"] == "H":
                            vc = pit["vcol"]
                            P.add("dve", lambda e, pt=pt, ebp=ebp, c0=c0, c1=c1, vc=vc: e.scalar_tensor_tensor(
                                out=pt[:, 0, :, c0:c1], in0=Ebuf[:, 0, :, c0:c1], scalar=kval[:, vc:vc + 1], in1=ebp[:, :, c0:c1],
                                op0=ALU.mult, op1=ALU.mult), r=("Ebuf", "kval") + ekeys, w=(("PT", hp),))
                        else:
                            P.add("dve", lambda e, pt=pt, ebp=ebp, c0=c0, c1=c1: e.tensor_tensor(
                                out=pt[:, 0, :, c0:c1], in0=Ebuf[:, 0, :, c0:c1], in1=ebp[:, :, c0:c1], op=ALU.mult),
                                r=("Ebuf",) + ekeys, w=(("PT", hp),))

            def stageD(it):
                ks = it["kslot"]
                kind, *pp = it["pos"]
                for hp in range(2):
                    bank = 7 if hp == 0 else 1
                    od = ps[bank][:, :].rearrange("p (a h q) -> p a h q", a=2, h=2)
                    pt = PTb[hp]

                    def f(e, hp=hp, od=od, pt=pt):
                        ins = None
                        for hh in range(2):
                            h = hp * 2 + hh
                            ins = e.matmul(out=od[:, 0, hh, :], lhsT=Vr[ks][:, h * 128:(h + 1) * 128], rhs=pt[:, 1, hh, :],
                                           start=True, stop=False)
                            np_ = len(it["prev"])
                            for j, (pit, c0, c1) in enumerate(it["prev"]):
                                ins = e.matmul(out=od[:, 0, hh, c0:c1], lhsT=Vr[pit["kslot"]][:, h * 128:(h + 1) * 128],
                                               rhs=pt[:, 0, hh, c0:c1], start=False, stop=(j == np_ - 1))
                        ins = e.matmul(out=od[:, 1, :, :], lhsT=onesb[:], rhs=pt[:, 1, :, :], start=True, stop=False)
                        ins = e.matmul(out=od[:, 1, :, :], lhsT=onesb[:], rhs=pt[:, 0, :, :], start=False, stop=True)
                        return ins
                    rk = [("PT", hp), ("V", ks), "onesb"] + [("V", pit["kslot"]) for (pit, _, _) in it["prev"]]
                    P.add("pe", f, r=tuple(rk), w=("ps%d" % bank,))
                    for a, acc, akey in ((0, accO, "accO"), (1, accD, "accD")):
                        hs = slice(hp * 2, hp * 2 + 2)
                        if kind == "c":
                            i = pp[0]
                            dst = acc[:, hs, i * 128:(i + 1) * 128]
                            src = od[:, a, :, :]
                        elif kind == "s4":
                            s, r = pp
                            dst = acc[:, hs, s * 512 + r:s * 512 + 512:4]
                            src = od[:, a, :, :]
                        else:
                            T = pp[0]
                            dst = acc[:, hs, :].rearrange("p h (m r) -> p h r m", r=16)[:, :, 2 * T:2 * T + 2, :]
                            src = od[:, a, :, :].rearrange("p h (r m) -> p h r m", r=2)
                        if first_group[0]:
                            P.add("act", lambda e, dst=dst, src=src: e.copy(out=dst, in_=src),
                                  r=("ps%d" % bank,), w=(akey,))
                        else:
                            P.add("dve", lambda e, dst=dst, src=src: e.tensor_tensor(out=dst, in0=dst, in1=src, op=ALU.add),
                                  r=("ps%d" % bank, akey), w=(akey,))

            def npart_(k, j):
                if 0 <= k < n_items:
                    fn = items[k].get("_n", (None, None, None))[j]
                    if fn is not None:
                        fn()
            for it in items:
                stageN(it)
            npart_(0, 0); npart_(1, 0); npart_(0, 1); npart_(0, 2); npart_(1, 1)
            for n in range(n_items + 2):
                npart_(n + 2, 0)
                if 0 <= n - 2 < n_items and items[n - 2]["kind"] == "M":
                    stageC(items[n - 2])
                npart_(n + 1, 2)
                npart_(n + 2, 1)
                if n < n_items:
                    stageA1(items[n])
                if 0 <= n - 1 < n_items:
                    stageB(items[n - 1])
                if n < n_items:
                    stageA2(items[n])
                if 0 <= n - 2 < n_items and items[n - 2]["kind"] == "M":
                    stageD(items[n - 2])
            sl = lambda c: hT[:, c, NT:NT + TS]
            proj(sl, (("hT", 8),), Qn, 4, TS)
            qk_norm(4, TS, bufA, "bufA", kfQ, "kfQ", st4[0], st4[1], "sq")
            proj(sl, (("hT", 8),), Kn, 2, TS)
            qk_norm(2, TS, bufB, "bufB", kfK, "kfK", st4[2], st4[3], "sk")
            dma("sp", kvs[g, 0], kfK[0:TS, :], ("kfK",), (("o", "Ks", g),), ("stKs", g))
            tr4(kfQ, "kfQ", TS, QTr[0][:, :, 0:TS], ("QT", 0))
            tr4(kfK, "kfK", TS, KTr[4][:, :, 0:TS], ("KT", 4))
            proj(sl, (("hT", 8),), Vn, 3, TS)
            P.add("act", lambda e: e.copy(out=Vr[4][0:TS, :], in_=ps[3][0:TS, :]), r=("ps3",), w=(("V", 4),))
            P.add("dve", lambda e: e.tensor_copy(out=vf[0:TS, :], in_=ps[3][0:TS, :]), r=("ps3",), w=("vf",))
            dma("sp", kvs[g, 1], vf[0:TS, :], ("vf",), (("o", "Vs", g),), ("stVs", g))
            ntile = 1 if g == 0 else 4
            for t in range(ntile):
                rows = slice(0, 128) if g == 0 else slice(t, dil * 128, dil)
                dma("sp", xt[:, t * 512:(t + 1) * 512], ck[g][0, rows, :], (), ("xt",), "xt")
                tr4(xt[:, t * 512:(t + 1) * 512], "xt", 128, KTr[t], ("KT", t))
                dma("sp", kfQ[:, :], ck[g][1, rows, :], (), ("kfQ",), "ckv")
                P.add("dve", lambda e, t=t: e.tensor_copy(out=Vr[t], in_=kfQ[:, :]), r=("kfQ",), w=(("V", t),))
            sc = ps[6]
            od = ps[7]

            def fsc(e):
                ins = None
                for h in range(4):
                    if g == 0:
                        ins = e.matmul(out=sc[:, h * 4:(h + 1) * 4], lhsT=KTr[0][:, h, :], rhs=QTr[0][:, h, 0:TS], start=True, stop=True)
                    else:
                        for t in range(4):
                            ins = e.matmul(out=sc[:, h * 4 + t:h * 4 + t + 1], lhsT=KTr[t][:, h, :], rhs=QTr[0][:, h, t:t + 1],
                                           start=True, stop=True)
                    ins = e.matmul(out=sc[0:TS, 16 + h * 4:16 + (h + 1) * 4], lhsT=KTr[4][:, h, 0:TS], rhs=QTr[0][:, h, 0:TS],
                                   start=True, stop=True)
                return ins
            P.add("pe", fsc, r=(("QT", 0), ("KT", 4)) + tuple(("KT", t) for t in range(ntile)), w=("ps6",))
            Es = kfK[:, 0:32]
            PTs = PTb[0][:].rearrange("p a h q -> p (a h q)")[:, 0:32]
            P.add("act", lambda e: e.activation(out=Es[:, 0:16], in_=sc[:, 0:16], func=AF.Exp, scale=SCALE), r=("ps6",), w=("kfK",))
            P.add("act", lambda e: e.activation(out=Es[0:TS, 16:32], in_=sc[0:TS, 16:32], func=AF.Exp, scale=SCALE), r=("ps6",), w=("kfK",))
            E3 = Es[:, 0:16].rearrange("p (h t) -> p h t", h=4)
            P3 = PTs[:, 0:16].rearrange("p (h t) -> p h t", h=4)
            En = Es[0:TS, 16:32].rearrange("p (h t) -> p h t", h=4)
            Pn = PTs[0:TS, 16:32].rearrange("p (h t) -> p h t", h=4)
            if g == 0:
                ebc_s = EB[:, 0:4, 1, 0:TS]
            else:
                ebc_s = bc_free(EB[:, g * 4:(g + 1) * 4, 1, 0], TS)
            P.add("dve", lambda e: e.tensor_tensor(out=P3, in0=E3, in1=ebc_s, op=ALU.mult), r=("kfK", "EB"), w=(("PT", 0),))
            ebn_s = EB[0:TS, g * 4:(g + 1) * 4, 0, 0:TS]
            if g == 0:
                P.add("dve", lambda e: e.tensor_tensor(out=Pn, in0=En, in1=ebn_s, op=ALU.mult), r=("kfK", "EB"), w=(("PT", 0),))
            else:
                P.add("dve", lambda e: e.tensor_tensor(out=En, in0=En, in1=ebn_s, op=ALU.mult), r=("kfK", "EB"), w=("kfK",))
                idb = bass.AP(tensor=identf[:].tensor, offset=identf[:].offset, ap=[list(identf[:].ap[0][:1]) + [TS], [0, 4], [1, TS]])
                P.add("dve", lambda e: e.tensor_tensor(out=Pn, in0=En, in1=idb, op=ALU.mult), r=("kfK", "identf"), w=(("PT", 0),))

            def fpv(e):
                ins = None
                for h in range(4):
                    hc = slice(h * 128, (h + 1) * 128)
                    ins = e.matmul(out=od[:, h * 4:(h + 1) * 4], lhsT=Vr[4][0:TS, hc], rhs=PTs[0:TS, 16 + h * 4:16 + (h + 1) * 4],
                                   start=True, stop=False)
                    if g == 0:
                        ins = e.matmul(out=od[:, h * 4:(h + 1) * 4], lhsT=Vr[0][:, hc], rhs=PTs[:, h * 4:(h + 1) * 4], start=False, stop=True)
                    else:
                        for t in range(4):
                            ins = e.matmul(out=od[:, h * 4 + t:h * 4 + t + 1], lhsT=Vr[t][:, hc], rhs=PTs[:, h * 4 + t:h * 4 + t + 1],
                                           start=False, stop=(t == 3))
                ins = e.matmul(out=od[:, 16:32], lhsT=onesb[0:TS, :], rhs=PTs[0:TS, 16:32], start=True, stop=False)
                ins = e.matmul(out=od[:, 16:32], lhsT=onesb[:, :], rhs=PTs[:, 0:16], start=False, stop=True)
                return ins
            P.add("pe", fpv, r=(("PT", 0), ("V", 4), "onesb") + tuple(("V", t) for t in range(ntile)), w=("ps7",))
            for a, acc, akey in ((0, accOs, "accOs"), (1, accDs, "accDs")):
                dst = acc[:].rearrange("p h t -> p (h t)")
                src = od[:, a * 16:(a + 1) * 16]
                if first_group[0]:
                    P.add("dve", lambda e, dst=dst, src=src: e.tensor_copy(out=dst, in_=src), r=("ps7",), w=(akey,))
                else:
                    P.add("dve", lambda e, dst=dst, src=src: e.tensor_tensor(out=dst, in0=dst, in1=src, op=ALU.add),
                          r=("ps7", akey), w=(akey,))
            first_group[0] = False
            return items

        group_items = {}
        for g in (2, 1, 0):
            if STOP <= 3 + (2 - g):
                return
            group_items[g] = run_group(g)
            for _ in range(3):
                wload_next()

        if STOP <= 6:
            return

        XT2_OK[0] = False
        P.barrier()
        for h in range(4):
            P.add("dve", lambda e, h=h: e.reciprocal(out=accD[:, h, :], in_=accD[:, h, :]), r=("accD",), w=("accD",))
            P.add("dve", lambda e, h=h: e.tensor_tensor(out=catT[:, h, 0:NT], in0=accO[:, h, :], in1=accD[:, h, :], op=ALU.mult),
                  r=("accO", "accD"), w=tuple(("cat", i) for i in range(8)))
        P.add("dve", lambda e: e.reciprocal(out=accDs[:], in_=accDs[:]), r=("accDs",), w=("accDs",))
        P.add("dve", lambda e: e.tensor_tensor(out=catT[:, 0:4, NT:NT + TS], in0=accOs[:], in1=accDs[:], op=ALU.mult),
              r=("accOs", "accDs"), w=(("cat", 8),))
        dma("sp", bufA[:], bcast_rows(gvn, 128, 512), (), ("bufA",), "c5")
        dma("sp", bufB[:], bass.AP(tensor=gb.tensor, offset=0, ap=[[0, 128], [1, 512]]), (), ("bufB",), "c6")
        slotU, wU = W("U")
        bk = [2, 3, 4]
        bi = 0
        for gg in range(4):
            for th in range(3):
                bank = bk[bi % 3]; bi += 1
                if th < 2:
                    c0, c1, n = th * 512, th * 512 + 512, 512
                    rk = tuple(("hT", 4 * th + k) for k in range(4))
                    dst = uT[:, gg, c0:c1]
                    dkey = "uT"
                else:
                    c0, c1, n = NT, NT + TS, TS
                    rk = (("hT", 8),)
                    dst = uTs[:, gg, :]
                    dkey = "uTs"

                def f(e, gg=gg, c0=c0, c1=c1, n=n, bank=bank):
                    ins = None
                    for c in range(16):
                        ins = e.matmul(out=ps[bank][:, 0:n], lhsT=wU[:, c, gg * 128:(gg + 1) * 128], rhs=hT[:, c, c0:c1],
                                       start=(c == 0), stop=(c == 15))
                    return ins
                P.add("pe", f, r=rk + (("ring", slotU),), w=("ps%d" % bank,))
                P.add("act", lambda e, dst=dst, n=n, bank=bank: e.activation(out=dst, in_=ps[bank][:, 0:n], func=AF.Gelu),
                      r=("ps%d" % bank,), w=(dkey, "EB", "EBc2", "EBp2") if th < 2 else (dkey,))
        gtile = Vr[0]
        for i in range(9):
            npart = 128 if i < 8 else TS
            cols = slice(i * 128, (i + 1) * 128) if i < 8 else slice(NT, NT + TS)
            proj(lambda c, cols=cols: hT[:, c, cols], (("hT", i),), "G", 2, npart)
            P.add("act", lambda e, npart=npart: e.activation(out=kfQ[0:npart, :], in_=ps[2][0:npart, :], func=AF.Gelu),
                  r=("ps2",), w=("kfQ",))
            P.add("act", lambda e, npart=npart: e.activation(out=vf[0:npart, :], in_=kfQ[0:npart, :], func=AF.Square,
                                                             accum_out=st_ssq[0:npart, :]), r=("kfQ",), w=("vf", "st_ssq"))
            P.add("act", lambda e, npart=npart: e.activation(out=st_rs[0:npart, :], in_=st_ssq[0:npart, :], func=AF.Sqrt,
                                                             scale=1.0 / 512, bias=EPS), r=("st_ssq",), w=("st_rs",))
            P.add("dve", lambda e, npart=npart: e.reciprocal(out=st_rs[0:npart, :], in_=st_rs[0:npart, :]), r=("st_rs",), w=("st_rs",))
            if i < 8:
                P.add("dve", lambda e: e.scalar_tensor_tensor(out=gtile, in0=kfQ[:, :], scalar=st_rs[:, 0:1], in1=bufA[:, :],
                                                              op0=ALU.mult, op1=ALU.mult), r=("kfQ", "st_rs", "bufA"), w=(("V", 0),))
            else:
                P.add("dve", lambda e: e.scalar_tensor_tensor(out=vf[0:TS, :], in0=kfQ[0:TS, :], scalar=st_rs[0:TS, 0:1], in1=bufA[0:TS, :],
                                                              op0=ALU.mult, op1=ALU.mult), r=("kfQ", "st_rs", "bufA"), w=("vf",))
                dma("sp", gvs[:, :], vf[0:TS, :], ("vf",), (("o", "G"),), "stG")
                P.add("dve", lambda e: e.tensor_copy(out=gtile[0:TS, :], in_=vf[0:TS, :]), r=("vf",), w=(("V", 0),))
            nq = npart

            def fm(e, npart=npart):
                ins = None
                for gg in range(4):
                    ins = e.matmul(out=ps[5][:, gg * 128:gg * 128 + npart], lhsT=gtile[0:npart, gg * 128:(gg + 1) * 128],
                                   rhs=WmT[0:npart, gg, 0:npart], start=True, stop=True)
                return ins
            P.add("pe", fm, r=(("V", 0), "WmT"), w=("ps5",))
            mixv = ps[5][:, :].rearrange("p (g t) -> p g t", g=4)[:, :, 0:npart]
            bbv = bufB[:, :].rearrange("p (g t) -> p g t", g=4)[:, :, 0:npart]
            tmpv = kfK[:, :].rearrange("p (g t) -> p g t", g=4)[:, :, 0:npart]
            P.add("dve", lambda e, mixv=mixv, bbv=bbv, tmpv=tmpv: e.tensor_tensor(out=tmpv, in0=mixv, in1=bbv, op=ALU.add),
                  r=("ps5", "bufB"), w=("kfK",))
            uv = uT[:, :, cols] if i < 8 else uTs[:, :, :]
            P.add("dve", lambda e, tmpv=tmpv, uv=uv, cols=cols: e.tensor_tensor(out=catT[:, 4:8, cols], in0=tmpv, in1=uv, op=ALU.mult),
                  r=("kfK", "uT", "uTs"), w=(("cat", i),))
        wload_next(); wload_next()

        if STOP <= 7:
            return

        P.barrier()

        if STOP <= 8:
            return

        for i in range(8):
            dma("sp", x1[:, i, :], xm[i * 128:(i + 1) * 128, :], (), (("x1", i),), ("x1l", i))
        dma("sp", xt[0:TS, :], xs[:, :], (), ("xt",), "xt")

        def outproj(i):
            npart = 128 if i < 8 else TS
            cols = slice(i * 128, (i + 1) * 128) if i < 8 else slice(NT, NT + TS)
            for cb in range(4):
                slot, wv = W("O%d" % (cb // 2))
                bank = 2 + cb

                def f(e, wv=wv, cb=cb, bank=bank):
                    ins = None
                    for c in range(8):
                        ins = e.matmul(out=ps[bank][0:npart, :], lhsT=catT[:, c, cols], rhs=wv[:, c, (cb % 2) * 512:(cb % 2) * 512 + 512],
                                       start=(c == 0), stop=(c == 7))
                    return ins
                P.add("pe", f, r=(("cat", i), ("ring", slot)), w=("ps%d" % bank,))
                dst = x1[:, i, cb * 512:(cb + 1) * 512] if i < 8 else xt[0:TS, cb * 512:(cb + 1) * 512]
                key = ("x1", i) if i < 8 else "xt"
                P.add("dve", lambda e, dst=dst, bank=bank: e.tensor_tensor(out=dst, in0=ps[bank][0:npart, :], in1=dst, op=ALU.add),
                      r=("ps%d" % bank, key), w=(key,))

        def norm2(i):
            cols = slice(i * 128, (i + 1) * 128) if i < 8 else slice(NT, NT + TS)
            if i < 8:
                norm_tile(None, 128, gffn, lambda c0, c1: hT[:, c0:c1, cols], (("hT", i),), load=False,
                          src_sb=x1[:, i, :], src_keys=(("x1", i),))
            else:
                norm_tile(None, TS, gffn, lambda c0, c1: hT[:, c0:c1, cols], (("hT", 8),), load=False,
                          src_sb=xt[0:TS, :], src_keys=("xt",))

        outproj(0)
        for i in range(1, 9):
            outproj(i)
            norm2(i - 1)
        norm2(8)
        P.barrier()
        wload_next(); wload_next()

        if STOP <= 9:
            return

        upb = [0]
        dnb = [0]

        def up(n):
            slot, wv = W("UP%d" % n)
            for fc in range(4):
                for th in range(3):
                    bank = upb[0] % 4; upb[0] += 1
                    if th < 2:
                        c0, c1, nn = th * 512, th * 512 + 512, 512
                        rk = tuple(("hT", 4 * th + k) for k in range(4))
                        dst = aT[n % 2][:, fc, c0:c1]
                        dkey = ("aT", n % 2)
                    else:
                        c0, c1, nn = NT, NT + TS, TS
                        rk = (("hT", 8),)
                        dst = aTs[:, n % 2, fc, :]
                        dkey = ("aTs", n % 2)

                    def f(e, fc=fc, c0=c0, c1=c1, nn=nn, bank=bank):
                        ins = None
                        for c in range(16):
                            ins = e.matmul(out=ps[bank][:, 0:nn], lhsT=wv[:, c, fc * 128:(fc + 1) * 128], rhs=hT[:, c, c0:c1],
                                           start=(c == 0), stop=(c == 15))
                        return ins
                    P.add("pe", f, r=rk + (("ring", slot),), w=("ps%d" % bank,))
                    rs_ = rsc[bank % 2]
                    P.add("act", lambda e, nn=nn, bank=bank, rs_=rs_: e.activation(out=rs_[:, 0:nn], in_=ps[bank][:, 0:nn], func=AF.Relu),
                          r=("ps%d" % bank,), w=(("rsc", bank % 2),))
                    P.add("act", lambda e, nn=nn, rs_=rs_, dst=dst: e.activation(out=dst, in_=rs_[:, 0:nn], func=AF.Square),
                          r=(("rsc", bank % 2),), w=(dkey,))

        def down(n):
            slot, wv = W("DN%d" % n)
            for i in range(9):
                npart = 128 if i < 8 else TS
                for cb in range(4):
                    bank = 4 + dnb[0] % 4; dnb[0] += 1

                    def f(e, i=i, cb=cb, bank=bank, npart=npart):
                        ins = None
                        for fc in range(4):
                            lhsT = aT[n % 2][:, fc, i * 128:(i + 1) * 128] if i < 8 else aTs[:, n % 2, fc, :]
                            ins = e.matmul(out=ps[bank][0:npart, :], lhsT=lhsT, rhs=wv[:, fc, cb * 512:(cb + 1) * 512],
                                           start=(fc == 0), stop=(fc == 3))
                        return ins
                    rk = (("aT", n % 2), ("ring", slot)) if i < 8 else (("aTs", n % 2), ("ring", slot))
                    P.add("pe", f, r=rk, w=("ps%d" % bank,))
                    dst = x1[:, i, cb * 512:(cb + 1) * 512] if i < 8 else xt[0:TS, cb * 512:(cb + 1) * 512]
                    key = ("x1", i) if i < 8 else "xt"
                    P.add("dve", lambda e, dst=dst, bank=bank, npart=npart: e.tensor_tensor(out=dst, in0=ps[bank][0:npart, :], in1=dst, op=ALU.add),
                          r=("ps%d" % bank, key), w=(key,))

        for n in range(17):
            if n < 16:
                up(n)
                if n >= 1:
                    pass
            if n >= 1:
                down(n - 1)
                wload_next(); wload_next()
        for i in range(8):
            dma("sp", y[i * 128:(i + 1) * 128, :], x1[:, i, :], (("x1", i),), (("o", "y", i),), ("sty", i))
        dma("sp", ys[:, :], xt[0:TS, :], ("xt",), (("o", "ys"),), "stys")

    phases()
    if DBG:
        P.barrier()
        dma("sp", dbg_rx[:, :], RX[:, :], (), (("o", "dbgrx"),), "dbg0")
        dma("sp", dbg_cat[:, :], catT[:].rearrange("p c n -> p (c n)"), (), (("o", "dbgcat"),), "dbg1")
        dma("sp", dbg_hT[:, :], hT[:].rearrange("p c n -> p (c n)"), (), (("o", "dbghT"),), "dbg2")
    P.barrier()
    P.emit(nc, es)
    es.close()
    return nc


_CACHE = {}


def kernel(x_prompt, x_sample, cache_kv_w128, cache_kv_w512, cache_kv_w2048, norm_mix, w_in,
           q_norm, k_norm, rel_bias, gmlp_v_norm, gmlp_w, gmlp_b, w_out, norm_ffn, w_up, w_down):
    f = lambda a: np.ascontiguousarray(np.asarray(a, dtype=np.float32))
    x_prompt = f(x_prompt); x_sample = f(x_sample)
    caches = [f(cache_kv_w128), f(cache_kv_w512), f(cache_kv_w2048)]
    if "nc" not in _CACHE:
        _CACHE["nc"] = build_program()
    nc = _CACHE["nc"]
    oh, ident, jm = host_constants()
    shared = {
        "w_in": f(w_in)[0], "w_out": f(w_out)[0], "w_up": f(w_up)[0], "w_down": f(w_down)[0],
        "norm_mix": f(norm_mix).reshape(16, 128), "norm_ffn": f(norm_ffn).reshape(16, 128),
        "q_norm": f(q_norm).reshape(1, 128), "k_norm": f(k_norm).reshape(1, 128),
        "rel_bias": f(rel_bias), "gmlp_v_norm": f(gmlp_v_norm).reshape(1, 512),
        "gmlp_w": f(gmlp_w)[0], "gmlp_b": f(gmlp_b)[0], "oh": oh, "ident": ident, "jm": jm,
    }
    in_maps = []
    for c in range(8):
        b, j = c // 4, c % 4
        q0 = j * NT
        xh = np.zeros((NH, D), np.float32)
        valid = np.zeros((NH,), np.float32)
        lo = q0 - NH
        s = max(lo, 0)
        if q0 > 0:
            xh[s - lo:] = x_prompt[b, s:q0]
            valid[s - lo:] = 1.0
        kv = np.zeros((128, 21), np.float32)
        kv[:, 0] = valid[1920:2048]
        for r in range(4):
            kv[:, 1 + r] = valid[1536 + r:2048:4]
        for r in range(16):
            kv[:, 5 + r] = valid[r:2048:16]
        m = dict(shared)
        m["xm"] = np.ascontiguousarray(x_prompt[b, q0:q0 + NT])
        m["xh"] = xh
        m["xs"] = np.ascontiguousarray(x_sample[c])
        m["kvalid"] = kv
        for g in range(3):
            m["ck%d" % g] = np.ascontiguousarray(caches[g][0, c].reshape(2, -1, 512))
        in_maps.append(m)
    res = run_bass_kernel_spmd(nc, in_maps, core_ids=list(range(8)))
    R = res.results
    yp = np.zeros((2, 4096, D), np.float32)
    ysm = np.zeros((8, TS, D), np.float32)
    kvp_out = [np.zeros((1, 2, 2, w, 4, 128), np.float32) for w in (128, 512, 2048)]
    kvs_out = [np.zeros((1, 8, 2, TS, 4, 128), np.float32) for _ in range(3)]
    gv = np.zeros((1, 8, TS, 512), np.float32)
    for c in range(8):
        b, j = c // 4, c % 4
        yp[b, j * NT:(j + 1) * NT] = R[c]["y"]
        ysm[c] = R[c]["ys"]
        if j == 3:
            kvp_out[0][0, b] = R[c]["kvp0"].reshape(2, 128, 4, 128)
            kvp_out[1][0, b] = R[c]["kvp1"].reshape(2, 512, 4, 128)
        if j >= 2:
            kvp_out[2][0, b, :, (j - 2) * 1024:(j - 1) * 1024] = R[c]["kvp2"].reshape(2, 1024, 4, 128)
        for g in range(3):
            kvs_out[g][0, c] = R[c]["kvs"][g].reshape(2, TS, 4, 128)
        gv[0, c] = R[c]["gvs"]
    return (yp, ysm, kvp_out[0], kvp_out[1], kvp_out[2], kvs_out[0], kvs_out[1], kvs_out[2], gv)
```

```python
import contextlib
import os
import numpy as np
import concourse.bass as bass
import concourse.mybir as mybir
from concourse.bass_utils import run_bass_kernel_spmd

F32 = mybir.dt.float32
BF16 = mybir.dt.bfloat16
AF = mybir.ActivationFunctionType
ALU = mybir.AluOpType
AX = mybir.AxisListType

D = 2048
NT = 1024
NH = 2048
TS = 4
NCOL = NT + TS
DIN = 5632
DFF = 8192
EPS = 1e-6
DILS = (1, 4, 16)
SCALE = 128 ** -0.5
SAME_ENG_SYNC = os.environ.get('MK_SES', '1') == '1'

ENGS = ("pe", "act", "dve", "pool", "sp")


class Op:
    __slots__ = ("eng", "fn", "deps", "dma", "signal", "seq", "target")


class Prog:
    def __init__(self):
        self.ops = []
        self.lw = {}
        self.rd = {}
        self.dcnt = {}
        self.last = {}
        self.auto_r = ()

    def add(self, eng, fn, r=(), w=(), dma=None, extra=()):
        idx = len(self.ops)
        deps = set(extra)
        r = tuple(r) + tuple(self.auto_r)
        pr = tuple(k for k in r if isinstance(k, str) and k[:2] == 'ps' and k[2:].isdigit())
        if pr:
            r = tuple(k for k in r if k not in pr)
            w = tuple(w) + pr
        for k in r:
            if k in self.lw:
                deps.add(self.lw[k])
        for k in w:
            if k in self.lw:
                deps.add(self.lw[k])
            deps.update(self.rd.get(k, ()))
        op = Op()
        op.eng, op.fn, op.deps, op.dma, op.signal, op.seq, op.target = eng, fn, deps, dma, False, 0, 0
        if dma is not None:
            c = self.dcnt.get(dma, 0) + 16
            self.dcnt[dma] = c
            op.target = c
        for k in r:
            self.rd.setdefault(k, []).append(idx)
        for k in w:
            self.lw[k] = idx
            self.rd[k] = []
        self.ops.append(op)
        if fn is not None:
            self.last[eng] = idx
        return idx

    def barrier(self):
        alld = set(self.last.values())
        for k, v in self.lw.items():
            alld.add(v)
        for e in ENGS:
            self.add(e, None, extra=tuple(alld))

    def emit(self, nc, es):
        limit = int(os.environ.get('MK_NOPS', '0'))
        if limit:
            self.ops = self.ops[:limit]
            for e in ENGS:
                op = Op()
                op.eng, op.fn, op.deps, op.dma, op.signal, op.seq, op.target = e, None, set(range(limit)), None, False, 0, 0
                self.ops.append(op)
        ops = self.ops
        if os.environ.get('MK_DUMP'):
            for i, op in enumerate(ops):
                print(i, op.eng, op.dma, sorted(op.deps)[-6:], getattr(op.fn, '__name__', None))
        for op in ops:
            op.deps = set(d for d in op.deps if ops[d].fn is not None)
            for d in op.deps:
                if ops[d].dma is None:
                    ops[d].signal = True
        cnt = {e: 0 for e in ENGS}
        for op in ops:
            if op.dma is None and op.signal:
                cnt[op.eng] += 1
                op.seq = cnt[op.eng]
        esem = {e: es.enter_context(nc.semaphore("sem_" + e)) for e in ENGS}
        dsem = {}
        for i, k in enumerate(self.dcnt):
            dsem[k] = es.enter_context(nc.semaphore("dsem%d" % i))
        block = es.enter_context(nc.Block())
        reg = {"pe": block.tensor, "act": block.scalar, "dve": block.vector,
               "pool": block.gpsimd, "sp": block.sync}
        for e in ENGS:
            mine = [op for op in ops if op.eng == e]

            def body(eng, e=e, mine=mine):
                waited = {}
                for op in mine:
                    need = {}
                    for d in op.deps:
                        p = ops[d]
                        if p.dma is not None:
                            s, v = dsem[p.dma], p.target
                        else:
                            if p.eng == e and (e == "pe" or not SAME_ENG_SYNC):
                                continue
                            s, v = esem[p.eng], p.seq
                        key = id(s)
                        if key not in need or need[key][1] < v:
                            need[key] = (s, v)
                    for key, (s, v) in need.items():
                        if waited.get(key, 0) < v:
                            eng.wait_ge(s, v)
                            waited[key] = v
                    if op.fn is not None:
                        ins = op.fn(eng)
                        if op.dma is not None:
                            ins.then_inc(dsem[op.dma], 16)
                        elif op.signal:
                            ins.then_inc(esem[e], 1)
            reg[e](body)


def bucket(dist):
    if dist < 16:
        return dist
    v = 16 + int(np.float32(np.log(np.float32(dist) / np.float32(16)) / np.float32(np.log(128.0)) * np.float32(16)))
    return min(v, 31)


def t5_bucket_np(dist):
    import math
    d = np.maximum(dist, 1).astype(np.float32)
    rnd = np.rint if os.environ.get('MK_BUCKET_RINT', '0') == '1' else np.trunc
    large = 16 + rnd(np.log(d / np.float32(16)) / np.float32(math.log(2048 / 16)) * np.float32(16)).astype(np.int32)
    large = np.minimum(large, 31)
    return np.where(dist < 16, dist, large)


def host_constants():
    oh = np.zeros((33, 3, 384), np.float32)
    for g, dil in enumerate(DILS):
        sub = np.arange(384) - 128
        ok = (sub >= 0) & (sub <= 128)
        b = t5_bucket_np(np.clip(sub, 0, 128).astype(np.int32) * dil)
        for u in range(384):
            if ok[u]:
                oh[b[u], g, u] = 1.0
            else:
                oh[32, g, u] = 1.0
    ident = np.eye(128, dtype=np.float32)
    jm = np.ascontiguousarray(ident[::-1])
    return oh.reshape(33, 1152), ident, jm


def build_program():
    nc = bass.Bass("TRN2", target_bir_lowering=False)

    def din(name, shape):
        return nc.dram_tensor(name, shape, F32, kind="ExternalInput").ap()

    def dout(name, shape):
        return nc.dram_tensor(name, shape, F32, kind="ExternalOutput").ap()

    xm = din("xm", [NT, D]); xh = din("xh", [NH, D]); xs = din("xs", [TS, D])
    kvalid = din("kvalid", [128, 21])
    ck = [din("ck0", [2, 128, 512]), din("ck1", [2, 512, 512]), din("ck2", [2, 2048, 512])]
    w_in = din("w_in", [D, DIN]); w_out = din("w_out", [1024, D])
    w_up = din("w_up", [D, DFF]); w_down = din("w_down", [DFF, D])
    norm_mix = din("norm_mix", [16, 128]); norm_ffn = din("norm_ffn", [16, 128])
    q_norm = din("q_norm", [1, 128]); k_norm = din("k_norm", [1, 128])
    rel_bias = din("rel_bias", [32, 12]); gvn = din("gmlp_v_norm", [1, 512])
    gw = din("gmlp_w", [4, 128, 128]); gb = din("gmlp_b", [4, 128])
    oh_d = din("oh", [33, 1152]); ident_d = din("ident", [128, 128]); jm_d = din("jm", [128, 128])

    y = dout("y", [NT, D]); ys = dout("ys", [TS, D])
    kvp = [dout("kvp0", [2, 128, 512]), dout("kvp1", [2, 512, 512]), dout("kvp2", [2, 1024, 512])]
    kvs = dout("kvs", [3, 2, TS, 512]); gvs = dout("gvs", [TS, 512])
    escr = dout("escr", [12, 384])
    DBG = os.environ.get('MK_DBG')
    if DBG:
        dbg_rx = dout("dbg_rx", [128, 16384])
        dbg_cat = nc.dram_tensor("dbg_cat", [128, 8 * NCOL], BF16, kind="ExternalOutput").ap()
        dbg_hT = nc.dram_tensor("dbg_hT", [128, 16 * NCOL], BF16, kind="ExternalOutput").ap()

    P = Prog()
    es = contextlib.ExitStack()
    STOP = int(os.environ.get('MK_STOP', '99'))
    SUB = int(os.environ.get('MK_SUB', '255'))

    def sb(name, shape, dt):
        return es.enter_context(nc.sbuf_tensor(name, shape, dt))

    hT = sb("hT", [128, 16, NCOL], BF16)
    ring = [sb("ring%d" % i, [128, 8192], BF16) for i in range(4)]
    RX = sb("RX", [128, 16384], F32)
    catT = sb("catT", [128, 8, NCOL], BF16)
    xt = sb("xt", [128, D], F32)
    xb = sb("xb", [128, D], BF16)
    hTh = sb("hTh", [128, 16, 128], BF16)
    kfK = sb("kfK", [128, 512], F32)
    vf = sb("vf", [128, 512], F32)
    PTb = [sb("PT%d" % i, [128, 2, 2, 128], BF16) for i in range(2)]
    identb = sb("identb", [128, 128], BF16)
    identf = sb("identf", [128, 128], F32)
    onesb = sb("onesb", [128, 128], BF16)
    gmix = sb("gmix", [128, 16], F32)
    gffn = sb("gffn", [128, 16], F32)
    bufA = sb("bufA", [128, 512], F32)
    bufB = sb("bufB", [128, 512], F32)
    WmT = sb("WmT", [128, 4, 128], BF16)
    kval = sb("kval", [128, 21], F32)
    st_ssq = sb("st_ssq", [128, 1], F32)
    junk = sb("junk", [128, 2], BF16)
    st_rs = sb("st_rs", [128, 1], F32)
    st4 = [sb("st4_%d" % i, [128, 4], F32) for i in range(4)]
    accOs = sb("accOs", [128, 4, TS], F32)
    accDs = sb("accDs", [128, 4, TS], F32)
    uTs = sb("uTs", [128, 4, TS], F32)
    ps = [es.enter_context(nc.psum_tensor("ps%d" % i, [128, 512], F32)) for i in range(8)]

    accO = RX[:, 0:4096].rearrange("p (h t) -> p h t", h=4)
    accD = RX[:, 4096:8192].rearrange("p (h t) -> p h t", h=4)
    EB = RX[:, 8192:11264].rearrange("p (g a q) -> p g a q", g=12, a=2)
    EBc2 = RX[:, 11264:11776].rearrange("p (h q) -> p h q", h=4)
    EBp2 = RX[:, 11776:12288].rearrange("p (h q) -> p h q", h=4)
    uT = RX[:, 8192:12288].rearrange("p (h t) -> p h t", h=4)
    rest = RX[:, 12288:16384]
    KTr = [rest[:, i * 256:(i + 1) * 256].bitcast(BF16).rearrange("p (h k) -> p h k", h=4) for i in range(5)]
    Vr = [rest[:, 1280 + i * 256:1280 + (i + 1) * 256].bitcast(BF16) for i in range(5)]
    QTr = [rest[:, 2560 + i * 256:2560 + (i + 1) * 256].bitcast(BF16).rearrange("p (h k) -> p h k", h=4) for i in range(3)]
    Ebuf = rest[:, 3328:3840].rearrange("p (a h q) -> p a h q", a=2, h=2)
    kfQ = RX[:, 12288 + 3840 - 512:12288 + 3840]
    kfQ = sb("kfQ", [128, 512], F32)
    x1 = RX[:, :].rearrange("p (i d) -> p i d", i=8)
    catb = catT[:].rearrange("p c n -> p (c n)")
    xt2 = catb[:, 0:4096].bitcast(F32)
    hTh2 = catb[:, 4096:6144].rearrange("p (c n) -> p c n", c=16)
    kfK2 = catb[:, 6144:7168].bitcast(F32)
    kfQ2 = catb[:, 7168:8192].bitcast(F32)
    XT = [(xt, "xt"), (xt2, "xt2")]
    XT2_OK = [True]
    HTH = [(hTh, "hTh"), (hTh2, "hTh2")]
    KFK = [(kfK, "kfK"), (kfK2, "kfK2")]
    KFQ = [(kfQ, "kfQ"), (kfQ2, "kfQ2")]
    hTf = RX
    oh_sb = hTf[:, 0:1152]
    Eall = hTf[:, 1152:2304].rearrange("p (g u) -> p g u", g=3)
    jm_sb = hTf[:, 2304:2432]
    Hh = hTf[:, 2432:2432 + 3072].rearrange("p (g a q) -> p g a q", g=12, a=2)
    tab33 = hTf[:, 5504:5516]
    wtmp = hTf[:, 5632:5632 + 512].rearrange("p (g s) -> p g s", g=4)
    vtmp = hTf[:, 6144:6144 + 128]
    aT = [catT[:].rearrange("p c n -> p (c n)")[:, i * 4096:(i + 1) * 4096].rearrange("p (f t) -> p f t", f=4) for i in range(2)]
    aTs = sb("aTs", [128, 2, 4, TS], BF16)
    rsc = [xb[:, i * 1024:(i + 1) * 1024].bitcast(F32) for i in range(2)]

    psb = [p[:].bitcast(BF16) for p in ps]

    def dma(q, out, in_, r, w, key):
        return P.add(q, lambda e, out=out, in_=in_: e.dma_start(out=out, in_=in_), r=r, w=w, dma=key)

    def bcast_rows(dram_ap_row, nparts, n):
        return bass.AP(tensor=dram_ap_row.tensor, offset=dram_ap_row.offset, ap=[[0, nparts], [1, n]])

    def bc_free(ap2, n):
        a = ap2.ap
        return bass.AP(tensor=ap2.tensor, offset=ap2.offset, ap=[list(a[0]), list(a[1]), [0, n]])

    wblocks = []

    def wview_rows(w, col0, ncols):
        return w.rearrange("(c p) n -> p c n", p=128)[:, :, col0:col0 + ncols]

    for g in (2, 1, 0):
        wblocks.append(("K%d" % g, wview_rows(w_in, 1536 + g * 512, 512), (16, 512)))
        wblocks.append(("V%d" % g, wview_rows(w_in, 3072 + g * 512, 512), (16, 512)))
        wblocks.append(("Q%d" % g, wview_rows(w_in, g * 512, 512), (16, 512)))
    wblocks.append(("U", wview_rows(w_in, 4608, 512), (16, 512)))
    wblocks.append(("G", wview_rows(w_in, 5120, 512), (16, 512)))
    wblocks.append(("O0", wview_rows(w_out, 0, 1024), (8, 1024)))
    wblocks.append(("O1", wview_rows(w_out, 1024, 1024), (8, 1024)))
    for n in range(16):
        wblocks.append(("UP%d" % n, wview_rows(w_up, n * 512, 512), (16, 512)))
        wblocks.append(("DN%d" % n, w_down[n * 512:(n + 1) * 512, :].rearrange("(c p) n -> p c n", p=128), (4, 2048)))
    wslot = {}
    wnext = [0]

    def wload_next():
        i = wnext[0]
        if i >= len(wblocks):
            return
        name, src, (c, n) = wblocks[i]
        slot = i % 4
        dst = ring[slot][:].rearrange("p (c n) -> p c n", c=c)
        wslot[name] = (slot, dst)
        dma("pool", dst, src, r=(), w=(("ring", slot),), key=("ring", slot))
        wnext[0] += 1

    def W(name):
        return wslot[name]

    def phases():
        P.auto_r = ("accO", "accD")
        dma("sp", identf[:], ident_d[:, :], (), ("identf",), "c0")
        dma("sp", jm_sb, jm_d[:, :], (), ("jm",), "c1")
        dma("sp", oh_sb[0:33, :], oh_d[:, :], (), ("oh",), "c2")
        dma("sp", tab33[0:32, :], rel_bias[:, :], (), ("tab",), "c3")
        dma("sp", kval[:], kvalid[:, :], (), ("kval",), "c4")
        dma("sp", bufA[:], bass.AP(tensor=q_norm.tensor, offset=0, ap=[[0, 128], [0, 4], [1, 128]]), (), ("bufA",), "c5")
        dma("sp", bufB[:], bass.AP(tensor=k_norm.tensor, offset=0, ap=[[0, 128], [0, 4], [1, 128]]), (), ("bufB",), "c6")
        dma("sp", vtmp[0:16, :], norm_mix[:, :], (), ("vtmp",), "c7")
        for _ in range(4):
            wload_next()
        P.add("dve", lambda e: e.tensor_copy(out=identb[:], in_=identf[:]), r=("identf",), w=("identb",))
        P.add("dve", lambda e: e.memset(onesb[:], 1.0), w=("onesb",))
        P.add("dve", lambda e: e.memset(tab33[32:33, :], -30000.0), w=("tab32",))
        if SUB & 1:
            P.add("pe", lambda e: e.transpose(out=ps[7][:, 0:16], in_=vtmp[0:16, :], identity=identf[0:16, 0:16]),
                  r=("vtmp", "identf"), w=("ps7",))
            P.add("act", lambda e: e.copy(out=gmix[:], in_=ps[7][:, 0:16]), r=("ps7",), w=("gmix",))
            dma("sp", vtmp[0:16, :], norm_ffn[:, :], (), ("vtmp",), "c7")
            P.add("pe", lambda e: e.transpose(out=ps[7][:, 0:16], in_=vtmp[0:16, :], identity=identf[0:16, 0:16]),
                  r=("vtmp", "identf"), w=("ps7",))
            P.add("act", lambda e: e.copy(out=gffn[:], in_=ps[7][:, 0:16]), r=("ps7",), w=("gffn",))
        if SUB & 2:
            dma("sp", wtmp, gw.rearrange("g t s -> t g s"), (), ("wtmp",), "c8")
            for gg in range(4):
                P.add("pool", lambda e, gg=gg: e.affine_select(out=wtmp[:, gg, :], in_=wtmp[:, gg, :], pattern=[[-1, 128]],
                                                               compare_op=ALU.is_ge, fill=0.0, base=0, channel_multiplier=1),
                      r=("wtmp",), w=("wtmp",))
            for gg in range(4):
                P.add("pe", lambda e, gg=gg: e.transpose(out=ps[6][:, gg * 128:(gg + 1) * 128], in_=wtmp[:, gg, :], identity=identf[:]),
                      r=("wtmp", "identf"), w=("ps6",))
            P.add("act", lambda e: e.copy(out=WmT[:].rearrange("p g t -> p (g t)"), in_=ps[6][:, :]), r=("ps6",), w=("WmT",))
        if SUB & 4:
            for g in range(3):
                P.add("pe", lambda e, g=g: e.matmul(out=ps[g][0:12, 0:384], lhsT=tab33[0:33, 0:12], rhs=oh_sb[0:33, g * 384:(g + 1) * 384],
                                                    start=True, stop=True),
                      r=("tab", "tab32", "oh"), w=("ps%d" % g,))
                P.add("act", lambda e, g=g: e.activation(out=Eall[0:12, g, :], in_=ps[g][0:12, 0:384], func=AF.Exp),
                      r=("ps%d" % g,), w=("Eall%d" % g,))
                dma("sp", escr[g * 4:(g + 1) * 4, :], Eall[g * 4:(g + 1) * 4, g, :], ("Eall%d" % g,), ("escr%d" % g,), "e%d" % g)
            for gh in range(12):
                src = bass.AP(tensor=escr.tensor, offset=gh * 384 + 1, ap=[[1, 128], [128, 2], [1, 128]])
                dma("sp", Hh[:, gh, :, :], src, ("escr%d" % (gh // 4),), ("Hh%d" % gh,), "h%d" % gh)
            Hf = Hh.rearrange("p g a q -> p (g a q)")
            EBf = EB.rearrange("p g a q -> p (g a q)")
            for i in range(6):
                P.add("pe", lambda e, i=i: e.matmul(out=ps[i][:, :], lhsT=jm_sb, rhs=Hf[:, i * 512:(i + 1) * 512], start=True, stop=True),
                      r=("jm", "Hh%d" % (2 * i), "Hh%d" % (2 * i + 1)), w=("ps%d" % i,))
                P.add("dve" if i % 2 else "act",
                      (lambda e, i=i: e.tensor_copy(out=EBf[:, i * 512:(i + 1) * 512], in_=ps[i][:, :])) if i % 2 else
                      (lambda e, i=i: e.copy(out=EBf[:, i * 512:(i + 1) * 512], in_=ps[i][:, :])),
                      r=("ps%d" % i,), w=("EB",))
        if SUB & 8:
            P.add("dve", lambda e: e.tensor_copy(out=EBc2, in_=EB[:, 8:12, 0, :]), r=("EB",), w=("EBc2",))
            P.add("dve", lambda e: e.memset(EBc2[0:64, :, 64:128], 0.0), w=("EBc2",))
            P.add("dve", lambda e: e.tensor_copy(out=EBp2[:, :, 0:64], in_=EB[:, 8:12, 1, 0:64]), r=("EB",), w=("EBp2",))
            P.add("dve", lambda e: e.tensor_copy(out=EBp2[:, :, 64:128], in_=EB[:, 8:12, 1, 0:64]), r=("EB",), w=("EBp2",))
        P.auto_r = ()
        if STOP <= 1:
            return

        xsel = [0]

        def norm_parts(x_src, npart, gains, dst_fn, dst_keys, load=True, src_sb=None, src_keys=()):
            if load:
                xbuf, xkey = XT[xsel[0] % 2] if XT2_OK[0] else XT[0]
                xsel[0] += 1
                src = xbuf[0:npart, :]
                skeys = (xkey,)
            else:
                src = src_sb
                skeys = tuple(src_keys)

            def n0():
                if load:
                    dma("sp", src, x_src, (), skeys, skeys[0])

            def n1():
                junk_ap = bass.AP(tensor=junk[:].tensor, offset=junk[:].offset, ap=[[junk[:].ap[0][0], npart], [0, D]])
                P.add("act", lambda e: e.activation(out=junk_ap, in_=src, func=AF.Square, accum_out=st_ssq[0:npart, :]),
                      r=skeys, w=("st_ssq",))
                P.add("act", lambda e: e.activation(out=st_rs[0:npart, :], in_=st_ssq[0:npart, :], func=AF.Sqrt, scale=1.0 / D, bias=EPS),
                      r=("st_ssq",), w=("st_rs",))
                P.add("dve", lambda e: e.reciprocal(out=st_rs[0:npart, :], in_=st_rs[0:npart, :]), r=("st_rs",), w=("st_rs",))
                P.add("dve", lambda e: e.tensor_scalar(out=xb[0:npart, :], in0=src, scalar1=st_rs[0:npart, 0:1], scalar2=None, op0=ALU.mult),
                      r=skeys + ("st_rs",), w=("xb",))

            def n2():
                for half in range(2):
                    def tp(e, half=half):
                        ins = None
                        for c in range(8):
                            cc = half * 8 + c
                            ins = e.transpose(out=psb[half][:, c * 128:c * 128 + npart], in_=xb[0:npart, cc * 128:(cc + 1) * 128],
                                              identity=identb[0:npart, 0:npart])
                        return ins
                    P.add("pe", tp, r=("xb", "identb"), w=("ps%d" % half,))
                    src_ps = psb[half][:, :].rearrange("p (c n) -> p c n", c=8)[:, :, 0:npart]
                    P.add("dve", lambda e, half=half, src_ps=src_ps: e.tensor_tensor(
                        out=dst_fn(half * 8, half * 8 + 8), in0=src_ps, in1=bc_free(gains[:, half * 8:half * 8 + 8], npart), op=ALU.mult),
                        r=("ps%d" % half, "gmix", "gffn"), w=tuple(dst_keys))
            return n0, n1, n2

        def norm_tile(x_src, npart, gains, dst_fn, dst_keys, load=True, src_sb=None, src_keys=()):
            n0, n1, n2 = norm_parts(x_src, npart, gains, dst_fn, dst_keys, load=load, src_sb=src_sb, src_keys=src_keys)
            n0(); n1(); n2()

        p1 = [norm_parts(xm[i * 128:(i + 1) * 128, :], 128, gmix, (lambda c0, c1, i=i: hT[:, c0:c1, i * 128:(i + 1) * 128]), (("hT", i),))
              for i in range(8)]
        p1[0][0](); p1[1][0]()
        for i in range(8):
            p1[i][1]()
            p1[i][2]()
            if i + 2 < 8:
                p1[i + 2][0]()
        norm_tile(xs[:, :], TS, gmix, lambda c0, c1: hT[:, c0:c1, NT:NT + TS], (("hT", 8),))

        if STOP <= 2:
            return

        def proj(lhs_fn, rkeys, wname, bank, npart):
            slot, wv = W(wname)

            def f(e):
                ins = None
                for c in range(16):
                    ins = e.matmul(out=ps[bank][0:npart, :], lhsT=lhs_fn(c), rhs=wv[:, c, :], start=(c == 0), stop=(c == 15))
                return ins
            P.add("pe", f, r=tuple(rkeys) + (("ring", slot),), w=("ps%d" % bank,))

        def qk_norm(bank, npart, gainbuf, gkey, kf, kfkey, ssq, rs, skey):
            jq = bass.AP(tensor=junk[:].tensor, offset=junk[:].offset, ap=[[junk[:].ap[0][0], npart], [0, 128]])

            def fsq(e):
                ins = None
                for h in range(4):
                    ins = e.activation(out=jq, in_=ps[bank][0:npart, h * 128:(h + 1) * 128], func=AF.Square,
                                       accum_out=ssq[0:npart, h:h + 1])
                return ins
            P.add("act", fsq, r=("ps%d" % bank,), w=(skey,))
            P.add("act", lambda e: e.activation(out=rs[0:npart, :], in_=ssq[0:npart, :], func=AF.Sqrt, scale=1.0 / 128, bias=EPS),
                  r=(skey,), w=(skey + "r",))
            P.add("dve", lambda e: e.reciprocal(out=rs[0:npart, :], in_=rs[0:npart, :]), r=(skey + "r",), w=(skey + "r",))
            for h in range(4):
                P.add("dve", lambda e, h=h: e.scalar_tensor_tensor(
                    out=kf[0:npart, h * 128:(h + 1) * 128], in0=ps[bank][0:npart, h * 128:(h + 1) * 128],
                    scalar=rs[0:npart, h:h + 1], in1=gainbuf[0:npart, h * 128:(h + 1) * 128], op0=ALU.mult, op1=ALU.mult),
                    r=("ps%d" % bank, skey + "r", gkey), w=(kfkey,))

        def tr4(kf, kfkey, npart, dst, dkey):
            def f(e):
                ins = None
                for h in range(4):
                    ins = e.transpose(out=ps[5][:, h * 128:h * 128 + npart], in_=kf[0:npart, h * 128:(h + 1) * 128],
                                      identity=identf[0:npart, 0:npart])
                return ins
            P.add("pe", f, r=(kfkey, "identf"), w=("ps5",))
            P.add("act", lambda e: e.copy(out=dst, in_=ps[5][:, :].rearrange("p (h k) -> p h k", h=4)[:, :, 0:npart]),
                  r=("ps5",), w=(dkey,))

        first_group = [True]

        def run_group(g):
            dil = DILS[g]
            items = []
            if g == 0:
                items.append(dict(kind="H", rows=xh[1920:2048, :], vcol=0))
                for i in range(8):
                    items.append(dict(kind="M", lhs=(lambda c, i=i: hT[:, c, i * 128:(i + 1) * 128]), hkeys=(("hT", i),),
                                      pos=("c", i), out=(7 == i and [(0, 128, kvp[0][:, 0:128, :])] or [])))
            elif g == 1:
                for r in range(4):
                    items.append(dict(kind="H", rows=xh[1536 + r:2048:4, :], vcol=1 + r))
                    for s in range(2):
                        items.append(dict(kind="M", lhs=(lambda c, s=s, r=r: hT[:, c, s * 512 + r:s * 512 + 512:4]),
                                          hkeys=tuple(("hT", 4 * s + k) for k in range(4)), pos=("s4", s, r),
                                          out=(s == 1 and [(0, 128, kvp[1][:, r:512:4, :])] or [])))
            else:
                for T in range(8):
                    items.append(dict(kind="H", rows=xh[2 * T:2048:16, :], vcol=5 + 2 * T))
                    items.append(dict(kind="H", rows=xh[2 * T + 1:2048:16, :], vcol=5 + 2 * T + 1))
                    items.append(dict(kind="M", lhs=(lambda c: hTh[:, c, :]), gather=T,
                                      hkeys=("hTh",), pos=("s16", T),
                                      out=[(0, 64, kvp[2][:, 2 * T:1024:16, :]), (64, 128, kvp[2][:, 2 * T + 1:1024:16, :])]))
            n_items = len(items)
            for idx, it in enumerate(items):
                it["idx"] = idx
                it["kslot"] = idx % 5
            qcount = [0]
            for it in items:
                if it["kind"] == "M":
                    it["qslot"] = qcount[0] % 3
                    it["qbuf"] = qcount[0] % 2
                    qcount[0] += 1
            for idx, it in enumerate(items):
                if it["kind"] != "M":
                    continue
                if g == 2:
                    it["prev"] = [(items[idx - 2], 0, 64), (items[idx - 1], 64, 128)]
                else:
                    it["prev"] = [(items[idx - 1], 0, 128)]

            Kn, Vn, Qn = "K%d" % g, "V%d" % g, "Q%d" % g

            hsel = [0]

            def stageN(it):
                if it["kind"] == "H" or "gather" in it:
                    hb, hkey = HTH[hsel[0] % 2]
                    hsel[0] += 1
                    if it["kind"] == "H":
                        it["_n"] = norm_parts(it["rows"], 128, gmix, (lambda c0, c1, hb=hb: hb[:, c0:c1, :]), (hkey,))
                    else:
                        T = it["gather"]
                        srcv = hT[:, :, 0:NT].rearrange("p c (m r) -> p c r m", r=16)

                        def gat(T=T, srcv=srcv, hb=hb, hkey=hkey):
                            for rr in range(2):
                                P.add("dve", lambda e, rr=rr: e.tensor_copy(out=hb[:, :, rr * 64:(rr + 1) * 64], in_=srcv[:, :, 2 * T + rr, :]),
                                      r=tuple(("hT", k) for k in range(8)), w=(hkey,))
                        it["_n"] = (None, None, gat)
                    it["_lhs"], it["_hk"] = (lambda c, hb=hb: hb[:, c, :]), (hkey,)
                else:
                    it["_lhs"], it["_hk"] = it["lhs"], it["hkeys"]

            def stageA1(it):
                lhs, hk = it["_lhs"], it["_hk"]
                kK, kKkey = KFK[it["idx"] % 2]
                it["_kfK"] = (kK, kKkey)
                if it["kind"] == "M":
                    kQ, kQkey = KFQ[it["qbuf"]]
                    it["_kfQ"] = (kQ, kQkey)
                    proj(lhs, hk, Qn, 4, 128)
                    qk_norm(4, 128, bufA, "bufA", kQ, kQkey, st4[0], st4[1], "sq")
                proj(lhs, hk, Kn, 2, 128)
                qk_norm(2, 128, bufB, "bufB", kK, kKkey, st4[2], st4[3], "sk")
                for (p0, p1, dst) in it.get("out", []):
                    dma("sp", dst[0], kK[p0:p1, :], (kKkey,), (("o", "K", g),), ("stK", g, it["idx"] % 2))

            def stageB(it):
                if it["kind"] == "M":
                    kQ, kQkey = it["_kfQ"]
                    tr4(kQ, kQkey, 128, QTr[it["qslot"]], ("QT", it["qslot"]))
                kK, kKkey = it["_kfK"]
                tr4(kK, kKkey, 128, KTr[it["kslot"]], ("KT", it["kslot"]))

            def stageA2(it):
                proj(it["_lhs"], it["_hk"], Vn, 3, 128)
                ks = it["kslot"]
                P.add("act", lambda e: e.copy(out=Vr[ks], in_=ps[3][:, :]), r=("ps3",), w=(("V", ks),))
                outs = it.get("out", [])
                if outs:
                    P.add("dve", lambda e: e.tensor_copy(out=vf[:], in_=ps[3][:, :]), r=("ps3",), w=("vf",))
                    for (p0, p1, dst) in outs:
                        dma("sp", dst[1], vf[p0:p1, :], ("vf",), (("o", "V", g),), ("stV", g))

            def stageC(it):
                qs, ks = it["qslot"], it["kslot"]
                for hp in range(2):
                    bank = 6 if hp == 0 else 0
                    stv = ps[bank][:, :].rearrange("p (a h q) -> p a h q", a=2, h=2)

                    def f(e, hp=hp, stv=stv):
                        ins = None
                        for hh in range(2):
                            h = hp * 2 + hh
                            for (pit, c0, c1) in it["prev"]:
                                ins = e.matmul(out=stv[:, 0, hh, c0:c1], lhsT=KTr[pit["kslot"]][:, h, :], rhs=QTr[qs][:, h, c0:c1],
                                               start=True, stop=True)
                            ins = e.matmul(out=stv[:, 1, hh, :], lhsT=KTr[ks][:, h, :], rhs=QTr[qs][:, h, :], start=True, stop=True)
                        return ins
                    rk = [("QT", qs), ("KT", ks)] + [("KT", pit["kslot"]) for (pit, _, _) in it["prev"]]
                    P.add("pe", f, r=tuple(rk), w=("ps%d" % bank,))
                    P.add("act", lambda e, bank=bank: e.activation(out=Ebuf.rearrange("p a h q -> p (a h q)"), in_=ps[bank][:, :],
                                                                   func=AF.Exp, scale=SCALE), r=("ps%d" % bank,), w=("Ebuf",))
                    pt = PTb[hp]
                    if g == 2:
                        ebc = EBc2[:, hp * 2:hp * 2 + 2, :]
                        ebp = EBp2[:, hp * 2:hp * 2 + 2, :]
                        ekeys = ("EBc2", "EBp2")
                    else:
                        ebc = EB[:, g * 4 + hp * 2:g * 4 + hp * 2 + 2, 0, :]
                        ebp = EB[:, g * 4 + hp * 2:g * 4 + hp * 2 + 2, 1, :]
                        ekeys = ("EB",)
                    P.add("dve", lambda e, pt=pt, ebc=ebc: e.tensor_tensor(out=pt[:, 1, :, :], in0=Ebuf[:, 1, :, :], in1=ebc, op=ALU.mult),
                          r=("Ebuf",) + ekeys, w=(("PT", hp),))
                    for (pit, c0, c1) in it["prev"]:
                        if pit["kind"] == "H":
                            vc = pit["vcol"]
                            P.add("dve", lambda e, pt=pt, ebp=ebp, c0=c0, c1=c1, vc=vc: e.scalar_tensor_tensor(
                                out=pt[:, 0, :, c0:c1], in0=Ebuf[:, 0, :, c0:c1], scalar=kval[:, vc:vc + 1], in1=ebp[:, :, c0:c1],
                                op0=ALU.mult, op1=ALU.mult), r=("Ebuf", "kval") + ekeys, w=(("PT", hp),))
                        else:
                            P.add("dve", lambda e, pt=pt, ebp=ebp, c0=c0, c1=c1: e.tensor_tensor(
                                out=pt[:, 0, :, c0:c1], in0=Ebuf[:, 0, :, c0:c1], in1=ebp[:, :, c0:c1], op=ALU.mult),
                                r=("Ebuf",) + ekeys, w=(("PT", hp),))

            def stageD(it):
                ks = it["kslot"]
                kind, *pp = it["pos"]
                for hp in range(2):
                    bank = 7 if hp == 0 else 1
                    od = ps[bank][:, :].rearrange("p (a h q) -> p a h q", a=2, h=2)
                    pt = PTb[hp]

                    def f(e, hp=hp, od=od, pt=pt):
                        ins = None
                        for hh in range(2):
                            h = hp * 2 + hh
                            ins = e.matmul(out=od[:, 0, hh, :], lhsT=Vr[ks][:, h * 128:(h + 1) * 128], rhs=pt[:, 1, hh, :],
                                           start=True, stop=False)
                            np_ = len(it["prev"])
                            for j, (pit, c0, c1) in enumerate(it["prev"]):
                                ins = e.matmul(out=od[:, 0, hh, c0:c1], lhsT=Vr[pit["kslot"]][:, h * 128:(h + 1) * 128],
                                               rhs=pt[:, 0, hh, c0:c1], start=False, stop=(j == np_ - 1))
                        ins = e.matmul(out=od[:, 1, :, :], lhsT=onesb[:], rhs=pt[:, 1, :, :], start=True, stop=False)
                        ins = e.matmul(out=od[:, 1, :, :], lhsT=onesb[:], rhs=pt[:, 0, :, :], start=False, stop=True)
                        return ins
                    rk = [("PT", hp), ("V", ks), "onesb"] + [("V", pit["kslot"]) for (pit, _, _) in it["prev"]]
                    P.add("pe", f, r=tuple(rk), w=("ps%d" % bank,))
                    for a, acc, akey in ((0, accO, "accO"), (1, accD, "accD")):
                        hs = slice(hp * 2, hp * 2 + 2)
                        if kind == "c":
                            i = pp[0]
                            dst = acc[:, hs, i * 128:(i + 1) * 128]
                            src = od[:, a, :, :]
                        elif kind == "s4":
                            s, r = pp
                            dst = acc[:, hs, s * 512 + r:s * 512 + 512:4]
                            src = od[:, a, :, :]
                        else:
                            T = pp[0]
                            dst = acc[:, hs, :].rearrange("p h (m r) -> p h r m", r=16)[:, :, 2 * T:2 * T + 2, :]
                            src = od[:, a, :, :].rearrange("p h (r m) -> p h r m", r=2)
                        if first_group[0]:
                            P.add("act", lambda e, dst=dst, src=src: e.copy(out=dst, in_=src),
                                  r=("ps%d" % bank,), w=(akey,))
                        else:
                            P.add("dve", lambda e, dst=dst, src=src: e.tensor_tensor(out=dst, in0=dst, in1=src, op=ALU.add),
                                  r=("ps%d" % bank, akey), w=(akey,))

            def npart_(k, j):
                if 0 <= k < n_items:
                    fn = items[k].get("_n", (None, None, None))[j]
                    if fn is not None:
                        fn()
            for it in items:
                stageN(it)
            npart_(0, 0); npart_(1, 0); npart_(0, 1); npart_(0, 2); npart_(1, 1)
            for n in range(n_items + 2):
                npart_(n + 2, 0)
                if 0 <= n - 2 < n_items and items[n - 2]["kind"] == "M":
                    stageC(items[n - 2])
                npart_(n + 1, 2)
                npart_(n + 2, 1)
                if n < n_items:
                    stageA1(items[n])
                if 0 <= n - 1 < n_items:
                    stageB(items[n - 1])
                if n < n_items:
                    stageA2(items[n])
                if 0 <= n - 2 < n_items and items[n - 2]["kind"] == "M":
                    stageD(items[n - 2])
            sl = lambda c: hT[:, c, NT:NT + TS]
            proj(sl, (("hT", 8),), Qn, 4, TS)
            qk_norm(4, TS, bufA, "bufA", kfQ, "kfQ", st4[0], st4[1], "sq")
            proj(sl, (("hT", 8),), Kn, 2, TS)
            qk_norm(2, TS, bufB, "bufB", kfK, "kfK", st4[2], st4[3], "sk")
            dma("sp", kvs[g, 0], kfK[0:TS, :], ("kfK",), (("o", "Ks", g),), ("stKs", g))
            tr4(kfQ, "kfQ", TS, QTr[0][:, :, 0:TS], ("QT", 0))
            tr4(kfK, "kfK", TS, KTr[4][:, :, 0:TS], ("KT", 4))
            proj(sl, (("hT", 8),), Vn, 3, TS)
            P.add("act", lambda e: e.copy(out=Vr[4][0:TS, :], in_=ps[3][0:TS, :]), r=("ps3",), w=(("V", 4),))
            P.add("dve", lambda e: e.tensor_copy(out=vf[0:TS, :], in_=ps[3][0:TS, :]), r=("ps3",), w=("vf",))
            dma("sp", kvs[g, 1], vf[0:TS, :], ("vf",), (("o", "Vs", g),), ("stVs", g))
            ntile = 1 if g == 0 else 4
            for t in range(ntile):
                rows = slice(0, 128) if g == 0 else slice(t, dil * 128, dil)
                dma("sp", xt[:, t * 512:(t + 1) * 512], ck[g][0, rows, :], (), ("xt",), "xt")
                tr4(xt[:, t * 512:(t + 1) * 512], "xt", 128, KTr[t], ("KT", t))
                dma("sp", kfQ[:, :], ck[g][1, rows, :], (), ("kfQ",), "ckv")
                P.add("dve", lambda e, t=t: e.tensor_copy(out=Vr[t], in_=kfQ[:, :]), r=("kfQ",), w=(("V", t),))
            sc = ps[6]
            od = ps[7]

            def fsc(e):
                ins = None
                for h in range(4):
                    if g == 0:
                        ins = e.matmul(out=sc[:, h * 4:(h + 1) * 4], lhsT=KTr[0][:, h, :], rhs=QTr[0][:, h, 0:TS], start=True, stop=True)
                    else:
                        for t in range(4):
                            ins = e.matmul(out=sc[:, h * 4 + t:h * 4 + t + 1], lhsT=KTr[t][:, h, :], rhs=QTr[0][:, h, t:t + 1],
                                           start=True, stop=True)
                    ins = e.matmul(out=sc[0:TS, 16 + h * 4:16 + (h + 1) * 4], lhsT=KTr[4][:, h, 0:TS], rhs=QTr[0][:, h, 0:TS],
                                   start=True, stop=True)
                return ins
            P.add("pe", fsc, r=(("QT", 0), ("KT", 4)) + tuple(("KT", t) for t in range(ntile)), w=("ps6",))
            Es = kfK[:, 0:32]
            PTs = PTb[0][:].rearrange("p a h q -> p (a h q)")[:, 0:32]
            P.add("act", lambda e: e.activation(out=Es[:, 0:16], in_=sc[:, 0:16], func=AF.Exp, scale=SCALE), r=("ps6",), w=("kfK",))
            P.add("act", lambda e: e.activation(out=Es[0:TS, 16:32], in_=sc[0:TS, 16:32], func=AF.Exp, scale=SCALE), r=("ps6",), w=("kfK",))
            E3 = Es[:, 0:16].rearrange("p (h t) -> p h t", h=4)
            P3 = PTs[:, 0:16].rearrange("p (h t) -> p h t", h=4)
            En = Es[0:TS, 16:32].rearrange("p (h t) -> p h t", h=4)
            Pn = PTs[0:TS, 16:32].rearrange("p (h t) -> p h t", h=4)
            if g == 0:
                ebc_s = EB[:, 0:4, 1, 0:TS]
            else:
                ebc_s = bc_free(EB[:, g * 4:(g + 1) * 4, 1, 0], TS)
            P.add("dve", lambda e: e.tensor_tensor(out=P3, in0=E3, in1=ebc_s, op=ALU.mult), r=("kfK", "EB"), w=(("PT", 0),))
            ebn_s = EB[0:TS, g * 4:(g + 1) * 4, 0, 0:TS]
            if g == 0:
                P.add("dve", lambda e: e.tensor_tensor(out=Pn, in0=En, in1=ebn_s, op=ALU.mult), r=("kfK", "EB"), w=(("PT", 0),))
            else:
                P.add("dve", lambda e: e.tensor_tensor(out=En, in0=En, in1=ebn_s, op=ALU.mult), r=("kfK", "EB"), w=("kfK",))
                idb = bass.AP(tensor=identf[:].tensor, offset=identf[:].offset, ap=[list(identf[:].ap[0][:1]) + [TS], [0, 4], [1, TS]])
                P.add("dve", lambda e: e.tensor_tensor(out=Pn, in0=En, in1=idb, op=ALU.mult), r=("kfK", "identf"), w=(("PT", 0),))

            def fpv(e):
                ins = None
                for h in range(4):
                    hc = slice(h * 128, (h + 1) * 128)
                    ins = e.matmul(out=od[:, h * 4:(h + 1) * 4], lhsT=Vr[4][0:TS, hc], rhs=PTs[0:TS, 16 + h * 4:16 + (h + 1) * 4],
                                   start=True, stop=False)
                    if g == 0:
                        ins = e.matmul(out=od[:, h * 4:(h + 1) * 4], lhsT=Vr[0][:, hc], rhs=PTs[:, h * 4:(h + 1) * 4], start=False, stop=True)
                    else:
                        for t in range(4):
                            ins = e.matmul(out=od[:, h * 4 + t:h * 4 + t + 1], lhsT=Vr[t][:, hc], rhs=PTs[:, h * 4 + t:h * 4 + t + 1],
                                           start=False, stop=(t == 3))
                ins = e.matmul(out=od[:, 16:32], lhsT=onesb[0:TS, :], rhs=PTs[0:TS, 16:32], start=True, stop=False)
                ins = e.matmul(out=od[:, 16:32], lhsT=onesb[:, :], rhs=PTs[:, 0:16], start=False, stop=True)
                return ins
            P.add("pe", fpv, r=(("PT", 0), ("V", 4), "onesb") + tuple(("V", t) for t in range(ntile)), w=("ps7",))
            for a, acc, akey in ((0, accOs, "accOs"), (1, accDs, "accDs")):
                dst = acc[:].rearrange("p h t -> p (h t)")
                src = od[:, a * 16:(a + 1) * 16]
                if first_group[0]:
                    P.add("dve", lambda e, dst=dst, src=src: e.tensor_copy(out=dst, in_=src), r=("ps7",), w=(akey,))
                else:
                    P.add("dve", lambda e, dst=dst, src=src: e.tensor_tensor(out=dst, in0=dst, in1=src, op=ALU.add),
                          r=("ps7", akey), w=(akey,))
            first_group[0] = False
            return items

        group_items = {}
        for g in (2, 1, 0):
            if STOP <= 3 + (2 - g):
                return
            group_items[g] = run_group(g)
            for _ in range(3):
                wload_next()

        if STOP <= 6:
            return

        XT2_OK[0] = False
        P.barrier()
        for h in range(4):
            P.add("dve", lambda e, h=h: e.reciprocal(out=accD[:, h, :], in_=accD[:, h, :]), r=("accD",), w=("accD",))
            P.add("dve", lambda e, h=h: e.tensor_tensor(out=catT[:, h, 0:NT], in0=accO[:, h, :], in1=accD[:, h, :], op=ALU.mult),
                  r=("accO", "accD"), w=tuple(("cat", i) for i in range(8)))
        P.add("dve", lambda e: e.reciprocal(out=accDs[:], in_=accDs[:]), r=("accDs",), w=("accDs",))
        P.add("dve", lambda e: e.tensor_tensor(out=catT[:, 0:4, NT:NT + TS], in0=accOs[:], in1=accDs[:], op=ALU.mult),
              r=("accOs", "accDs"), w=(("cat", 8),))
        dma("sp", bufA[:], bcast_rows(gvn, 128, 512), (), ("bufA",), "c5")
        dma("sp", bufB[:], bass.AP(tensor=gb.tensor, offset=0, ap=[[0, 128], [1, 512]]), (), ("bufB",), "c6")
        slotU, wU = W("U")
        bk = [2, 3, 4]
        bi = 0
        for gg in range(4):
            for th in range(3):
                bank = bk[bi % 3]; bi += 1
                if th < 2:
                    c0, c1, n = th * 512, th * 512 + 512, 512
                    rk = tuple(("hT", 4 * th + k) for k in range(4))
                    dst = uT[:, gg, c0:c1]
                    dkey = "uT"
                else:
                    c0, c1, n = NT, NT + TS, TS
                    rk = (("hT", 8),)
                    dst = uTs[:, gg, :]
                    dkey = "uTs"

                def f(e, gg=gg, c0=c0, c1=c1, n=n, bank=bank):
                    ins = None
                    for c in range(16):
                        ins = e.matmul(out=ps[bank][:, 0:n], lhsT=wU[:, c, gg * 128:(gg + 1) * 128], rhs=hT[:, c, c0:c1],
                                       start=(c == 0), stop=(c == 15))
                    return ins
                P.add("pe", f, r=rk + (("ring", slotU),), w=("ps%d" % bank,))
                P.add("act", lambda e, dst=dst, n=n, bank=bank: e.activation(out=dst, in_=ps[bank][:, 0:n], func=AF.Gelu),
                      r=("ps%d" % bank,), w=(dkey, "EB", "EBc2", "EBp2") if th < 2 else (dkey,))
        gtile = Vr[0]
        for i in range(9):
            npart = 128 if i < 8 else TS
            cols = slice(i * 128, (i + 1) * 128) if i < 8 else slice(NT, NT + TS)
            proj(lambda c, cols=cols: hT[:, c, cols], (("hT", i),), "G", 2, npart)
            P.add("act", lambda e, npart=npart: e.activation(out=kfQ[0:npart, :], in_=ps[2][0:npart, :], func=AF.Gelu),
                  r=("ps2",), w=("kfQ",))
            jk = bass.AP(tensor=junk[:].tensor, offset=junk[:].offset, ap=[[junk[:].ap[0][0], npart], [0, 512]])
            P.add("act", lambda e, npart=npart, jk=jk: e.activation(out=jk, in_=kfQ[0:npart, :], func=AF.Square,
                                                                    accum_out=st_ssq[0:npart, :]), r=("kfQ",), w=("st_ssq",))
            P.add("act", lambda e, npart=npart: e.activation(out=st_rs[0:npart, :], in_=st_ssq[0:npart, :], func=AF.Sqrt,
                                                             scale=1.0 / 512, bias=EPS), r=("st_ssq",), w=("st_rs",))
            P.add("dve", lambda e, npart=npart: e.reciprocal(out=st_rs[0:npart, :], in_=st_rs[0:npart, :]), r=("st_rs",), w=("st_rs",))
            if i < 8:
                P.add("dve", lambda e: e.scalar_tensor_tensor(out=gtile, in0=kfQ[:, :], scalar=st_rs[:, 0:1], in1=bufA[:, :],
                                                              op0=ALU.mult, op1=ALU.mult), r=("kfQ", "st_rs", "bufA"), w=(("V", 0),))
            else:
                P.add("dve", lambda e: e.scalar_tensor_tensor(out=vf[0:TS, :], in0=kfQ[0:TS, :], scalar=st_rs[0:TS, 0:1], in1=bufA[0:TS, :],
                                                              op0=ALU.mult, op1=ALU.mult), r=("kfQ", "st_rs", "bufA"), w=("vf",))
                dma("sp", gvs[:, :], vf[0:TS, :], ("vf",), (("o", "G"),), "stG")
                P.add("dve", lambda e: e.tensor_copy(out=gtile[0:TS, :], in_=vf[0:TS, :]), r=("vf",), w=(("V", 0),))
            nq = npart

            def fm(e, npart=npart):
                ins = None
                for gg in range(4):
                    ins = e.matmul(out=ps[5][:, gg * 128:gg * 128 + npart], lhsT=gtile[0:npart, gg * 128:(gg + 1) * 128],
                                   rhs=WmT[0:npart, gg, 0:npart], start=True, stop=True)
                return ins
            P.add("pe", fm, r=(("V", 0), "WmT"), w=("ps5",))
            mixv = ps[5][:, :].rearrange("p (g t) -> p g t", g=4)[:, :, 0:npart]
            bbv = bufB[:, :].rearrange("p (g t) -> p g t", g=4)[:, :, 0:npart]
            tmpv = kfK[:, :].rearrange("p (g t) -> p g t", g=4)[:, :, 0:npart]
            P.add("dve", lambda e, mixv=mixv, bbv=bbv, tmpv=tmpv: e.tensor_tensor(out=tmpv, in0=mixv, in1=bbv, op=ALU.add),
                  r=("ps5", "bufB"), w=("kfK",))
            uv = uT[:, :, cols] if i < 8 else uTs[:, :, :]
            P.add("dve", lambda e, tmpv=tmpv, uv=uv, cols=cols: e.tensor_tensor(out=catT[:, 4:8, cols], in0=tmpv, in1=uv, op=ALU.mult),
                  r=("kfK", "uT", "uTs"), w=(("cat", i),))
        wload_next(); wload_next()

        if STOP <= 7:
            return

        P.barrier()

        if STOP <= 8:
            return

        for i in range(8):
            dma("sp", x1[:, i, :], xm[i * 128:(i + 1) * 128, :], (), (("x1", i),), ("x1l", i))
        dma("sp", xt[0:TS, :], xs[:, :], (), ("xt",), "xt")

        def outproj(i):
            npart = 128 if i < 8 else TS
            cols = slice(i * 128, (i + 1) * 128) if i < 8 else slice(NT, NT + TS)
            for cb in range(4):
                slot, wv = W("O%d" % (cb // 2))
                bank = 2 + cb

                def f(e, wv=wv, cb=cb, bank=bank):
                    ins = None
                    for c in range(8):
                        ins = e.matmul(out=ps[bank][0:npart, :], lhsT=catT[:, c, cols], rhs=wv[:, c, (cb % 2) * 512:(cb % 2) * 512 + 512],
                                       start=(c == 0), stop=(c == 7))
                    return ins
                P.add("pe", f, r=(("cat", i), ("ring", slot)), w=("ps%d" % bank,))
                dst = x1[:, i, cb * 512:(cb + 1) * 512] if i < 8 else xt[0:TS, cb * 512:(cb + 1) * 512]
                key = ("x1", i) if i < 8 else "xt"
                P.add("dve", lambda e, dst=dst, bank=bank: e.tensor_tensor(out=dst, in0=ps[bank][0:npart, :], in1=dst, op=ALU.add),
                      r=("ps%d" % bank, key), w=(key,))

        def norm2(i):
            cols = slice(i * 128, (i + 1) * 128) if i < 8 else slice(NT, NT + TS)
            if i < 8:
                norm_tile(None, 128, gffn, lambda c0, c1: hT[:, c0:c1, cols], (("hT", i),), load=False,
                          src_sb=x1[:, i, :], src_keys=(("x1", i),))
            else:
                norm_tile(None, TS, gffn, lambda c0, c1: hT[:, c0:c1, cols], (("hT", 8),), load=False,
                          src_sb=xt[0:TS, :], src_keys=("xt",))

        outproj(0)
        for i in range(1, 9):
            outproj(i)
            norm2(i - 1)
        norm2(8)
        P.barrier()
        wload_next(); wload_next()

        if STOP <= 9:
            return

        upb = [0]
        dnb = [0]

        def up(n):
            slot, wv = W("UP%d" % n)
            for fc in range(4):
                for th in range(3):
                    bank = upb[0] % 4; upb[0] += 1
                    if th < 2:
                        c0, c1, nn = th * 512, th * 512 + 512, 512
                        rk = tuple(("hT", 4 * th + k) for k in range(4))
                        dst = aT[n % 2][:, fc, c0:c1]
                        dkey = ("aT", n % 2)
                    else:
                        c0, c1, nn = NT, NT + TS, TS
                        rk = (("hT", 8),)
                        dst = aTs[:, n % 2, fc, :]
                        dkey = ("aTs", n % 2)

                    def f(e, fc=fc, c0=c0, c1=c1, nn=nn, bank=bank):
                        ins = None
                        for c in range(16):
                            ins = e.matmul(out=ps[bank][:, 0:nn], lhsT=wv[:, c, fc * 128:(fc + 1) * 128], rhs=hT[:, c, c0:c1],
                                           start=(c == 0), stop=(c == 15))
                        return ins
                    P.add("pe", f, r=rk + (("ring", slot),), w=("ps%d" % bank,))
                    rs_ = rsc[bank % 2]
                    P.add("act", lambda e, nn=nn, bank=bank, rs_=rs_: e.activation(out=rs_[:, 0:nn], in_=ps[bank][:, 0:nn], func=AF.Relu),
                          r=("ps%d" % bank,), w=(("rsc", bank % 2),))
                    P.add("act", lambda e, nn=nn, rs_=rs_, dst=dst: e.activation(out=dst, in_=rs_[:, 0:nn], func=AF.Square),
                          r=(("rsc", bank % 2),), w=(dkey,))

        def down(n):
            slot, wv = W("DN%d" % n)
            for i in range(9):
                npart = 128 if i < 8 else TS
                for cb in range(4):
                    bank = 4 + dnb[0] % 4; dnb[0] += 1

                    def f(e, i=i, cb=cb, bank=bank, npart=npart):
                        ins = None
                        for fc in range(4):
                            lhsT = aT[n % 2][:, fc, i * 128:(i + 1) * 128] if i < 8 else aTs[:, n % 2, fc, :]
                            ins = e.matmul(out=ps[bank][0:npart, :], lhsT=lhsT, rhs=wv[:, fc, cb * 512:(cb + 1) * 512],
                                           start=(fc == 0), stop=(fc == 3))
                        return ins
                    rk = (("aT", n % 2), ("ring", slot)) if i < 8 else (("aTs", n % 2), ("ring", slot))
                    P.add("pe", f, r=rk, w=("ps%d" % bank,))
                    dst = x1[:, i, cb * 512:(cb + 1) * 512] if i < 8 else xt[0:TS, cb * 512:(cb + 1) * 512]
                    key = ("x1", i) if i < 8 else "xt"
                    P.add("dve", lambda e, dst=dst, bank=bank, npart=npart: e.tensor_tensor(out=dst, in0=ps[bank][0:npart, :], in1=dst, op=ALU.add),
                          r=("ps%d" % bank, key), w=(key,))

        for n in range(17):
            if n < 16:
                up(n)
                if n >= 1:
                    pass
            if n >= 1:
                down(n - 1)
                wload_next(); wload_next()
        for i in range(8):
            dma("sp", y[i * 128:(i + 1) * 128, :], x1[:, i, :], (("x1", i),), (("o", "y", i),), ("sty", i))
        dma("sp", ys[:, :], xt[0:TS, :], ("xt",), (("o", "ys"),), "stys")

    phases()
    if DBG:
        P.barrier()
        dma("sp", dbg_rx[:, :], RX[:, :], (), (("o", "dbgrx"),), "dbg0")
        dma("sp", dbg_cat[:, :], catT[:].rearrange("p c n -> p (c n)"), (), (("o", "dbgcat"),), "dbg1")
        dma("sp", dbg_hT[:, :], hT[:].rearrange("p c n -> p (c n)"), (), (("o", "dbghT"),), "dbg2")
    P.barrier()
    P.emit(nc, es)
    es.close()
    return nc


_CACHE = {}


def kernel(x_prompt, x_sample, cache_kv_w128, cache_kv_w512, cache_kv_w2048, norm_mix, w_in,
           q_norm, k_norm, rel_bias, gmlp_v_norm, gmlp_w, gmlp_b, w_out, norm_ffn, w_up, w_down):
    f = lambda a: np.ascontiguousarray(np.asarray(a, dtype=np.float32))
    x_prompt = f(x_prompt); x_sample = f(x_sample)
    caches = [f(cache_kv_w128), f(cache_kv_w512), f(cache_kv_w2048)]
    if "nc" not in _CACHE:
        _CACHE["nc"] = build_program()
    nc = _CACHE["nc"]
    oh, ident, jm = host_constants()
    shared = {
        "w_in": f(w_in)[0], "w_out": f(w_out)[0], "w_up": f(w_up)[0], "w_down": f(w_down)[0],
        "norm_mix": f(norm_mix).reshape(16, 128), "norm_ffn": f(norm_ffn).reshape(16, 128),
        "q_norm": f(q_norm).reshape(1, 128), "k_norm": f(k_norm).reshape(1, 128),
        "rel_bias": f(rel_bias), "gmlp_v_norm": f(gmlp_v_norm).reshape(1, 512),
        "gmlp_w": f(gmlp_w)[0], "gmlp_b": f(gmlp_b)[0], "oh": oh, "ident": ident, "jm": jm,
    }
    in_maps = []
    for c in range(8):
        b, j = c // 4, c % 4
        q0 = j * NT
        xh = np.zeros((NH, D), np.float32)
        valid = np.zeros((NH,), np.float32)
        lo = q0 - NH
        s = max(lo, 0)
        if q0 > 0:
            xh[s - lo:] = x_prompt[b, s:q0]
            valid[s - lo:] = 1.0
        kv = np.zeros((128, 21), np.float32)
        kv[:, 0] = valid[1920:2048]
        for r in range(4):
            kv[:, 1 + r] = valid[1536 + r:2048:4]
        for r in range(16):
            kv[:, 5 + r] = valid[r:2048:16]
        m = dict(shared)
        m["xm"] = np.ascontiguousarray(x_prompt[b, q0:q0 + NT])
        m["xh"] = xh
        m["xs"] = np.ascontiguousarray(x_sample[c])
        m["kvalid"] = kv
        for g in range(3):
            m["ck%d" % g] = np.ascontiguousarray(caches[g][0, c].reshape(2, -1, 512))
        in_maps.append(m)
    res = run_bass_kernel_spmd(nc, in_maps, core_ids=list(range(8)))
    R = res.results
    yp = np.zeros((2, 4096, D), np.float32)
    ysm = np.zeros((8, TS, D), np.float32)
    kvp_out = [np.zeros((1, 2, 2, w, 4, 128), np.float32) for w in (128, 512, 2048)]
    kvs_out = [np.zeros((1, 8, 2, TS, 4, 128), np.float32) for _ in range(3)]
    gv = np.zeros((1, 8, TS, 512), np.float32)
    for c in range(8):
        b, j = c // 4, c % 4
        yp[b, j * NT:(j + 1) * NT] = R[c]["y"]
        ysm[c] = R[c]["ys"]
        if j == 3:
            kvp_out[0][0, b] = R[c]["kvp0"].reshape(2, 128, 4, 128)
            kvp_out[1][0, b] = R[c]["kvp1"].reshape(2, 512, 4, 128)
        if j >= 2:
            kvp_out[2][0, b, :, (j - 2) * 1024:(j - 1) * 1024] = R[c]["kvp2"].reshape(2, 1024, 4, 128)
        for g in range(3):
            kvs_out[g][0, c] = R[c]["kvs"][g].reshape(2, TS, 4, 128)
        gv[0, c] = R[c]["gvs"]
    return (yp, ysm, kvp_out[0], kvp_out[1], kvp_out[2], kvs_out[0], kvs_out[1], kvs_out[2], gv)
```

```python
import contextlib
import os
import numpy as np
import concourse.bass as bass
import concourse.mybir as mybir
from concourse.bass_utils import run_bass_kernel_spmd

F32 = mybir.dt.float32
BF16 = mybir.dt.bfloat16
AF = mybir.ActivationFunctionType
ALU = mybir.AluOpType
AX = mybir.AxisListType

D = 2048
NT = 1024
NH = 2048
TS = 4
NCOL = NT + TS
DIN = 5632
DFF = 8192
EPS = 1e-6
DILS = (1, 4, 16)
SCALE = 128 ** -0.5
SAME_ENG_SYNC = os.environ.get('MK_SES', '1') == '1'

ENGS = ("pe", "act", "dve", "pool", "sp")


class Op:
    __slots__ = ("eng", "fn", "deps", "dma", "signal", "seq", "target")


class Prog:
    def __init__(self):
        self.ops = []
        self.lw = {}
        self.rd = {}
        self.dcnt = {}
        self.last = {}
        self.auto_r = ()

    def add(self, eng, fn, r=(), w=(), dma=None, extra=()):
        idx = len(self.ops)
        deps = set(extra)
        r = tuple(r) + tuple(self.auto_r)
        pr = tuple(k for k in r if isinstance(k, str) and k[:2] == 'ps' and k[2:].isdigit())
        if pr:
            r = tuple(k for k in r if k not in pr)
            w = tuple(w) + pr
        for k in r:
            if k in self.lw:
                deps.add(self.lw[k])
        for k in w:
            if k in self.lw:
                deps.add(self.lw[k])
            deps.update(self.rd.get(k, ()))
        op = Op()
        op.eng, op.fn, op.deps, op.dma, op.signal, op.seq, op.target = eng, fn, deps, dma, False, 0, 0
        if dma is not None:
            c = self.dcnt.get(dma, 0) + 16
            self.dcnt[dma] = c
            op.target = c
        for k in r:
            self.rd.setdefault(k, []).append(idx)
        for k in w:
            self.lw[k] = idx
            self.rd[k] = []
        self.ops.append(op)
        if fn is not None:
            self.last[eng] = idx
        return idx

    def barrier(self):
        alld = set(self.last.values())
        for k, v in self.lw.items():
            alld.add(v)
        for e in ENGS:
            self.add(e, None, extra=tuple(alld))

    def emit(self, nc, es):
        limit = int(os.environ.get('MK_NOPS', '0'))
        if limit:
            self.ops = self.ops[:limit]
            for e in ENGS:
                op = Op()
                op.eng, op.fn, op.deps, op.dma, op.signal, op.seq, op.target = e, None, set(range(limit)), None, False, 0, 0
                self.ops.append(op)
        ops = self.ops
        if os.environ.get('MK_DUMP'):
            for i, op in enumerate(ops):
                print(i, op.eng, op.dma, sorted(op.deps)[-6:], getattr(op.fn, '__name__', None))
        for op in ops:
            op.deps = set(d for d in op.deps if ops[d].fn is not None)
            for d in op.deps:
                if ops[d].dma is None:
                    ops[d].signal = True
        cnt = {e: 0 for e in ENGS}
        for op in ops:
            if op.dma is None and op.signal:
                cnt[op.eng] += 1
                op.seq = cnt[op.eng]
        esem = {e: es.enter_context(nc.semaphore("sem_" + e)) for e in ENGS}
        dsem = {}
        for i, k in enumerate(self.dcnt):
            dsem[k] = es.enter_context(nc.semaphore("dsem%d" % i))
        block = es.enter_context(nc.Block())
        reg = {"pe": block.tensor, "act": block.scalar, "dve": block.vector,
               "pool": block.gpsimd, "sp": block.sync}
        for e in ENGS:
            mine = [op for op in ops if op.eng == e]

            def body(eng, e=e, mine=mine):
                waited = {}
                for op in mine:
                    need = {}
                    for d in op.deps:
                        p = ops[d]
                        if p.dma is not None:
                            s, v = dsem[p.dma], p.target
                        else:
                            if p.eng == e and (e == "pe" or not SAME_ENG_SYNC):
                                continue
                            s, v = esem[p.eng], p.seq
                        key = id(s)
                        if key not in need or need[key][1] < v:
                            need[key] = (s, v)
                    for key, (s, v) in need.items():
                        if waited.get(key, 0) < v:
                            eng.wait_ge(s, v)
                            waited[key] = v
                    if op.fn is not None:
                        ins = op.fn(eng)
                        if op.dma is not None:
                            ins.then_inc(dsem[op.dma], 16)
                        elif op.signal:
                            ins.then_inc(esem[e], 1)
            reg[e](body)


def bucket(dist):
    if dist < 16:
        return dist
    v = 16 + int(np.float32(np.log(np.float32(dist) / np.float32(16)) / np.float32(np.log(128.0)) * np.float32(16)))
    return min(v, 31)


def t5_bucket_np(dist):
    import math
    d = np.maximum(dist, 1).astype(np.float32)
    rnd = np.rint if os.environ.get('MK_BUCKET_RINT', '0') == '1' else np.trunc
    large = 16 + rnd(np.log(d / np.float32(16)) / np.float32(math.log(2048 / 16)) * np.float32(16)).astype(np.int32)
    large = np.minimum(large, 31)
    return np.where(dist < 16, dist, large)


def host_constants():
    oh = np.zeros((33, 3, 384), np.float32)
    for g, dil in enumerate(DILS):
        sub = np.arange(384) - 128
        ok = (sub >= 0) & (sub <= 128)
        b = t5_bucket_np(np.clip(sub, 0, 128).astype(np.int32) * dil)
        for u in range(384):
            if ok[u]:
                oh[b[u], g, u] = 1.0
            else:
                oh[32, g, u] = 1.0
    ident = np.eye(128, dtype=np.float32)
    jm = np.ascontiguousarray(ident[::-1])
    return oh.reshape(33, 1152), ident, jm


def build_program():
    nc = bass.Bass("TRN2", target_bir_lowering=False)

    def din(name, shape):
        return nc.dram_tensor(name, shape, F32, kind="ExternalInput").ap()

    def dout(name, shape):
        return nc.dram_tensor(name, shape, F32, kind="ExternalOutput").ap()

    xm = din("xm", [NT, D]); xh = din("xh", [NH, D]); xs = din("xs", [TS, D])
    kvalid = din("kvalid", [128, 21])
    ck = [din("ck0", [2, 128, 512]), din("ck1", [2, 512, 512]), din("ck2", [2, 2048, 512])]
    w_in = din("w_in", [D, DIN]); w_out = din("w_out", [1024, D])
    w_up = din("w_up", [D, DFF]); w_down = din("w_down", [DFF, D])
    norm_mix = din("norm_mix", [16, 128]); norm_ffn = din("norm_ffn", [16, 128])
    q_norm = din("q_norm", [1, 128]); k_norm = din("k_norm", [1, 128])
    rel_bias = din("rel_bias", [32, 12]); gvn = din("gmlp_v_norm", [1, 512])
    gw = din("gmlp_w", [4, 128, 128]); gb = din("gmlp_b", [4, 128])
    oh_d = din("oh", [33, 1152]); ident_d = din("ident", [128, 128]); jm_d = din("jm", [128, 128])

    y = dout("y", [NT, D]); ys = dout("ys", [TS, D])
    kvp = [dout("kvp0", [2, 128, 512]), dout("kvp1", [2, 512, 512]), dout("kvp2", [2, 1024, 512])]
    kvs = dout("kvs", [3, 2, TS, 512]); gvs = dout("gvs", [TS, 512])
    escr = dout("escr", [12, 384])
    DBG = os.environ.get('MK_DBG')
    if DBG:
        dbg_rx = dout("dbg_rx", [128, 16384])
        dbg_cat = nc.dram_tensor("dbg_cat", [128, 8 * NCOL], BF16, kind="ExternalOutput").ap()
        dbg_hT = nc.dram_tensor("dbg_hT", [128, 16 * NCOL], BF16, kind="ExternalOutput").ap()

    P = Prog()
    es = contextlib.ExitStack()
    STOP = int(os.environ.get('MK_STOP', '99'))
    SUB = int(os.environ.get('MK_SUB', '255'))

    def sb(name, shape, dt):
        return es.enter_context(nc.sbuf_tensor(name, shape, dt))

    hT = sb("hT", [128, 16, NCOL], BF16)
    ring = [sb("ring%d" % i, [128, 8192], BF16) for i in range(4)]
    RX = sb("RX", [128, 16384], F32)
    catT = sb("catT", [128, 8, NCOL], BF16)
    xt = sb("xt", [128, D], F32)
    xb = sb("xb", [128, D], BF16)
    hTh = sb("hTh", [128, 16, 128], BF16)
    kfK = sb("kfK", [128, 512], F32)
    vf = sb("vf", [128, 512], F32)
    PTb = [sb("PT%d" % i, [128, 2, 2, 128], BF16) for i in range(2)]
    identb = sb("identb", [128, 128], BF16)
    identf = sb("identf", [128, 128], F32)
    onesb = sb("onesb", [128, 128], BF16)
    gmix = sb("gmix", [128, 16], F32)
    gffn = sb("gffn", [128, 16], F32)
    bufA = sb("bufA", [128, 512], F32)
    bufB = sb("bufB", [128, 512], F32)
    WmT = sb("WmT", [128, 4, 128], BF16)
    kval = sb("kval", [128, 21], F32)
    st_ssq = sb("st_ssq", [128, 1], F32)
    junk = sb("junk", [128, 2], BF16)
    st_rs = sb("st_rs", [128, 1], F32)
    st4 = [sb("st4_%d" % i, [128, 4], F32) for i in range(4)]
    accOs = sb("accOs", [128, 4, TS], F32)
    accDs = sb("accDs", [128, 4, TS], F32)
    uTs = sb("uTs", [128, 4, TS], F32)
    ps = [es.enter_context(nc.psum_tensor("ps%d" % i, [128, 512], F32)) for i in range(8)]

    accO = RX[:, 0:4096].rearrange("p (h t) -> p h t", h=4)
    accD = RX[:, 4096:8192].rearrange("p (h t) -> p h t", h=4)
    EB = RX[:, 8192:11264].rearrange("p (g a q) -> p g a q", g=12, a=2)
    EBc2 = RX[:, 11264:11776].rearrange("p (h q) -> p h q", h=4)
    EBp2 = RX[:, 11776:12288].rearrange("p (h q) -> p h q", h=4)
    uT = RX[:, 8192:12288].rearrange("p (h t) -> p h t", h=4)
    rest = RX[:, 12288:16384]
    KTr = [rest[:, i * 256:(i + 1) * 256].bitcast(BF16).rearrange("p (h k) -> p h k", h=4) for i in range(5)]
    Vr = [rest[:, 1280 + i * 256:1280 + (i + 1) * 256].bitcast(BF16) for i in range(5)]
    QTr = [rest[:, 2560 + i * 256:2560 + (i + 1) * 256].bitcast(BF16).rearrange("p (h k) -> p h k", h=4) for i in range(3)]
    Ebuf = rest[:, 3328:3840].rearrange("p (a h q) -> p a h q", a=2, h=2)
    kfQ = RX[:, 12288 + 3840 - 512:12288 + 3840]
    kfQ = sb("kfQ", [128, 512], F32)
    x1 = RX[:, :].rearrange("p (i d) -> p i d", i=8)
    catb = catT[:].rearrange("p c n -> p (c n)")
    xt2 = catb[:, 0:4096].bitcast(F32)
    hTh2 = catb[:, 4096:6144].rearrange("p (c n) -> p c n", c=16)
    kfK2 = catb[:, 6144:7168].bitcast(F32)
    kfQ2 = catb[:, 7168:8192].bitcast(F32)
    XT = [(xt, "xt"), (xt2, "xt2")]
    XT2_OK = [True]
    HTH = [(hTh, "hTh"), (hTh2, "hTh2")]
    KFK = [(kfK, "kfK"), (kfK2, "kfK2")]
    KFQ = [(kfQ, "kfQ"), (kfQ2, "kfQ2")]
    hTf = RX
    oh_sb = hTf[:, 0:1152]
    Eall = hTf[:, 1152:2304].rearrange("p (g u) -> p g u", g=3)
    jm_sb = hTf[:, 2304:2432]
    Hh = hTf[:, 2432:2432 + 3072].rearrange("p (g a q) -> p g a q", g=12, a=2)
    tab33 = hTf[:, 5504:5516]
    wtmp = hTf[:, 5632:5632 + 512].rearrange("p (g s) -> p g s", g=4)
    vtmp = hTf[:, 6144:6144 + 128]
    aT = [catT[:].rearrange("p c n -> p (c n)")[:, i * 4096:(i + 1) * 4096].rearrange("p (f t) -> p f t", f=4) for i in range(2)]
    aTs = sb("aTs", [128, 2, 4, TS], BF16)
    rsc = [xb[:, i * 1024:(i + 1) * 1024].bitcast(F32) for i in range(2)]

    psb = [p[:].bitcast(BF16) for p in ps]

    def dma(q, out, in_, r, w, key):
        return P.add(q, lambda e, out=out, in_=in_: e.dma_start(out=out, in_=in_), r=r, w=w, dma=key)

    def bcast_rows(dram_ap_row, nparts, n):
        return bass.AP(tensor=dram_ap_row.tensor, offset=dram_ap_row.offset, ap=[[0, nparts], [1, n]])

    def bc_free(ap2, n):
        a = ap2.ap
        return bass.AP(tensor=ap2.tensor, offset=ap2.offset, ap=[list(a[0]), list(a[1]), [0, n]])

    wblocks = []

    def wview_rows(w, col0, ncols):
        return w.rearrange("(c p) n -> p c n", p=128)[:, :, col0:col0 + ncols]

    for g in (2, 1, 0):
        wblocks.append(("K%d" % g, wview_rows(w_in, 1536 + g * 512, 512), (16, 512)))
        wblocks.append(("V%d" % g, wview_rows(w_in, 3072 + g * 512, 512), (16, 512)))
        wblocks.append(("Q%d" % g, wview_rows(w_in, g * 512, 512), (16, 512)))
    wblocks.append(("U", wview_rows(w_in, 4608, 512), (16, 512)))
    wblocks.append(("G", wview_rows(w_in, 5120, 512), (16, 512)))
    wblocks.append(("O0", wview_rows(w_out, 0, 1024), (8, 1024)))
    wblocks.append(("O1", wview_rows(w_out, 1024, 1024), (8, 1024)))
    for n in range(16):
        wblocks.append(("UP%d" % n, wview_rows(w_up, n * 512, 512), (16, 512)))
        wblocks.append(("DN%d" % n, w_down[n * 512:(n + 1) * 512, :].rearrange("(c p) n -> p c n", p=128), (4, 2048)))
    wslot = {}
    wnext = [0]

    def wload_next():
        i = wnext[0]
        if i >= len(wblocks):
            return
        name, src, (c, n) = wblocks[i]
        slot = i % 4
        dst = ring[slot][:].rearrange("p (c n) -> p c n", c=c)
        wslot[name] = (slot, dst)
        dma("pool", dst, src, r=(), w=(("ring", slot),), key=("ring", slot))
        wnext[0] += 1

    def W(name):
        return wslot[name]

    def phases():
        P.auto_r = ("accO", "accD")
        dma("sp", identf[:], ident_d[:, :], (), ("identf",), "c0")
        dma("sp", jm_sb, jm_d[:, :], (), ("jm",), "c1")
        dma("sp", oh_sb[0:33, :], oh_d[:, :], (), ("oh",), "c2")
        dma("sp", tab33[0:32, :], rel_bias[:, :], (), ("tab",), "c3")
        dma("sp", kval[:], kvalid[:, :], (), ("kval",), "c4")
        dma("sp", bufA[:], bass.AP(tensor=q_norm.tensor, offset=0, ap=[[0, 128], [0, 4], [1, 128]]), (), ("bufA",), "c5")
        dma("sp", bufB[:], bass.AP(tensor=k_norm.tensor, offset=0, ap=[[0, 128], [0, 4], [1, 128]]), (), ("bufB",), "c6")
        dma("sp", vtmp[0:16, :], norm_mix[:, :], (), ("vtmp",), "c7")
        for _ in range(4):
            wload_next()
        P.add("dve", lambda e: e.tensor_copy(out=identb[:], in_=identf[:]), r=("identf",), w=("identb",))
        P.add("dve", lambda e: e.memset(onesb[:], 1.0), w=("onesb",))
        P.add("dve", lambda e: e.memset(tab33[32:33, :], -30000.0), w=("tab32",))
        if SUB & 1:
            P.add("pe", lambda e: e.transpose(out=ps[7][:, 0:16], in_=vtmp[0:16, :], identity=identf[0:16, 0:16]),
                  r=("vtmp", "identf"), w=("ps7",))
            P.add("act", lambda e: e.copy(out=gmix[:], in_=ps[7][:, 0:16]), r=("ps7",), w=("gmix",))
            dma("sp", vtmp[0:16, :], norm_ffn[:, :], (), ("vtmp",), "c7")
            P.add("pe", lambda e: e.transpose(out=ps[7][:, 0:16], in_=vtmp[0:16, :], identity=identf[0:16, 0:16]),
                  r=("vtmp", "identf"), w=("ps7",))
            P.add("act", lambda e: e.copy(out=gffn[:], in_=ps[7][:, 0:16]), r=("ps7",), w=("gffn",))
        if SUB & 2:
            dma("sp", wtmp, gw.rearrange("g t s -> t g s"), (), ("wtmp",), "c8")
            for gg in range(4):
                P.add("pool", lambda e, gg=gg: e.affine_select(out=wtmp[:, gg, :], in_=wtmp[:, gg, :], pattern=[[-1, 128]],
                                                               compare_op=ALU.is_ge, fill=0.0, base=0, channel_multiplier=1),
                      r=("wtmp",), w=("wtmp",))
            for gg in range(4):
                P.add("pe", lambda e, gg=gg: e.transpose(out=ps[6][:, gg * 128:(gg + 1) * 128], in_=wtmp[:, gg, :], identity=identf[:]),
                      r=("wtmp", "identf"), w=("ps6",))
            P.add("act", lambda e: e.copy(out=WmT[:].rearrange("p g t -> p (g t)"), in_=ps[6][:, :]), r=("ps6",), w=("WmT",))
        if SUB & 4:
            for g in range(3):
                P.add("pe", lambda e, g=g: e.matmul(out=ps[g][0:12, 0:384], lhsT=tab33[0:33, 0:12], rhs=oh_sb[0:33, g * 384:(g + 1) * 384],
                                                    start=True, stop=True),
                      r=("tab", "tab32", "oh"), w=("ps%d" % g,))
                P.add("act", lambda e, g=g: e.activation(out=Eall[0:12, g, :], in_=ps[g][0:12, 0:384], func=AF.Exp),
                      r=("ps%d" % g,), w=("Eall%d" % g,))
                dma("sp", escr[g * 4:(g + 1) * 4, :], Eall[g * 4:(g + 1) * 4, g, :], ("Eall%d" % g,), ("escr%d" % g,), "e%d" % g)
            for gh in range(12):
                src = bass.AP(tensor=escr.tensor, offset=gh * 384 + 1, ap=[[1, 128], [128, 2], [1, 128]])
                dma("sp", Hh[:, gh, :, :], src, ("escr%d" % (gh // 4),), ("Hh%d" % gh,), "h%d" % gh)
            Hf = Hh.rearrange("p g a q -> p (g a q)")
            EBf = EB.rearrange("p g a q -> p (g a q)")
            for i in range(6):
                P.add("pe", lambda e, i=i: e.matmul(out=ps[i][:, :], lhsT=jm_sb, rhs=Hf[:, i * 512:(i + 1) * 512], start=True, stop=True),
                      r=("jm", "Hh%d" % (2 * i), "Hh%d" % (2 * i + 1)), w=("ps%d" % i,))
                P.add("dve" if i % 2 else "act",
                      (lambda e, i=i: e.tensor_copy(out=EBf[:, i * 512:(i + 1) * 512], in_=ps[i][:, :])) if i % 2 else
                      (lambda e, i=i: e.copy(out=EBf[:, i * 512:(i + 1) * 512], in_=ps[i][:, :])),
                      r=("ps%d" % i,), w=("EB",))
        if SUB & 8:
            P.add("dve", lambda e: e.tensor_copy(out=EBc2, in_=EB[:, 8:12, 0, :]), r=("EB",), w=("EBc2",))
            P.add("dve", lambda e: e.memset(EBc2[0:64, :, 64:128], 0.0), w=("EBc2",))
            P.add("dve", lambda e: e.tensor_copy(out=EBp2[:, :, 0:64], in_=EB[:, 8:12, 1, 0:64]), r=("EB",), w=("EBp2",))
            P.add("dve", lambda e: e.tensor_copy(out=EBp2[:, :, 64:128], in_=EB[:, 8:12, 1, 0:64]), r=("EB",), w=("EBp2",))
        P.auto_r = ()
        if STOP <= 1:
            return

        xsel = [0]

        def norm_parts(x_src, npart, gains, dst_fn, dst_keys, load=True, src_sb=None, src_keys=()):
            if load:
                xbuf, xkey = XT[xsel[0] % 2] if XT2_OK[0] else XT[0]
                xsel[0] += 1
                src = xbuf[0:npart, :]
                skeys = (xkey,)
            else:
                src = src_sb
                skeys = tuple(src_keys)

            def n0():
                if load:
                    dma("sp", src, x_src, (), skeys, skeys[0])

            def n1():
                junk_ap = bass.AP(tensor=junk[:].tensor, offset=junk[:].offset, ap=[[junk[:].ap[0][0], npart], [0, D]])
                P.add("act", lambda e: e.activation(out=junk_ap, in_=src, func=AF.Square, accum_out=st_ssq[0:npart, :]),
                      r=skeys, w=("st_ssq",))
                P.add("act", lambda e: e.activation(out=st_rs[0:npart, :], in_=st_ssq[0:npart, :], func=AF.Sqrt, scale=1.0 / D, bias=EPS),
                      r=("st_ssq",), w=("st_rs",))
                P.add("dve", lambda e: e.reciprocal(out=st_rs[0:npart, :], in_=st_rs[0:npart, :]), r=("st_rs",), w=("st_rs",))
                P.add("dve", lambda e: e.tensor_scalar(out=xb[0:npart, :], in0=src, scalar1=st_rs[0:npart, 0:1], scalar2=None, op0=ALU.mult),
                      r=skeys + ("st_rs",), w=("xb",))

            def n2():
                for half in range(2):
                    def tp(e, half=half):
                        ins = None
                        for c in range(8):
                            cc = half * 8 + c
                            ins = e.transpose(out=psb[half][:, c * 128:c * 128 + npart], in_=xb[0:npart, cc * 128:(cc + 1) * 128],
                                              identity=identb[0:npart, 0:npart])
                        return ins
                    P.add("pe", tp, r=("xb", "identb"), w=("ps%d" % half,))
                    src_ps = psb[half][:, :].rearrange("p (c n) -> p c n", c=8)[:, :, 0:npart]
                    P.add("dve", lambda e, half=half, src_ps=src_ps: e.tensor_tensor(
                        out=dst_fn(half * 8, half * 8 + 8), in0=src_ps, in1=bc_free(gains[:, half * 8:half * 8 + 8], npart), op=ALU.mult),
                        r=("ps%d" % half, "gmix", "gffn"), w=tuple(dst_keys))
            return n0, n1, n2

        def norm_tile(x_src, npart, gains, dst_fn, dst_keys, load=True, src_sb=None, src_keys=()):
            n0, n1, n2 = norm_parts(x_src, npart, gains, dst_fn, dst_keys, load=load, src_sb=src_sb, src_keys=src_keys)
            n0(); n1(); n2()

        p1 = [norm_parts(xm[i * 128:(i + 1) * 128, :], 128, gmix, (lambda c0, c1, i=i: hT[:, c0:c1, i * 128:(i + 1) * 128]), (("hT", i),))
              for i in range(8)]
        p1[0][0](); p1[1][0]()
        for i in range(8):
            p1[i][1]()
            p1[i][2]()
            if i + 2 < 8:
                p1[i + 2][0]()
        norm_tile(xs[:, :], TS, gmix, lambda c0, c1: hT[:, c0:c1, NT:NT + TS], (("hT", 8),))

        if STOP <= 2:
            return

        def proj(lhs_fn, rkeys, wname, bank, npart):
            slot, wv = W(wname)

            def f(e):
                ins = None
                for c in range(16):
                    ins = e.matmul(out=ps[bank][0:npart, :], lhsT=lhs_fn(c), rhs=wv[:, c, :], start=(c == 0), stop=(c == 15))
                return ins
            P.add("pe", f, r=tuple(rkeys) + (("ring", slot),), w=("ps%d" % bank,))

        def qk_norm(bank, npart, gainbuf, gkey, kf, kfkey, ssq, rs, skey):
            jq = bass.AP(tensor=junk[:].tensor, offset=junk[:].offset, ap=[[junk[:].ap[0][0], npart], [0, 128]])

            def fsq(e):
                ins = None
                for h in range(4):
                    ins = e.activation(out=jq, in_=ps[bank][0:npart, h * 128:(h + 1) * 128], func=AF.Square,
                                       accum_out=ssq[0:npart, h:h + 1])
                return ins
            P.add("act", fsq, r=("ps%d" % bank,), w=(skey,))
            P.add("act", lambda e: e.activation(out=rs[0:npart, :], in_=ssq[0:npart, :], func=AF.Sqrt, scale=1.0 / 128, bias=EPS),
                  r=(skey,), w=(skey + "r",))
            P.add("dve", lambda e: e.reciprocal(out=rs[0:npart, :], in_=rs[0:npart, :]), r=(skey + "r",), w=(skey + "r",))
            for h in range(4):
                P.add("dve", lambda e, h=h: e.scalar_tensor_tensor(
                    out=kf[0:npart, h * 128:(h + 1) * 128], in0=ps[bank][0:npart, h * 128:(h + 1) * 128],
                    scalar=rs[0:npart, h:h + 1], in1=gainbuf[0:npart, h * 128:(h + 1) * 128], op0=ALU.mult, op1=ALU.mult),
                    r=("ps%d" % bank, skey + "r", gkey), w=(kfkey,))

        def tr4(kf, kfkey, npart, dst, dkey):
            def f(e):
                ins = None
                for h in range(4):
                    ins = e.transpose(out=ps[5][:, h * 128:h * 128 + npart], in_=kf[0:npart, h * 128:(h + 1) * 128],
                                      identity=identf[0:npart, 0:npart])
                return ins
            P.add("pe", f, r=(kfkey, "identf"), w=("ps5",))
            P.add("act", lambda e: e.copy(out=dst, in_=ps[5][:, :].rearrange("p (h k) -> p h k", h=4)[:, :, 0:npart]),
                  r=("ps5",), w=(dkey,))

        first_group = [True]

        def run_group(g):
            dil = DILS[g]
            items = []
            if g == 0:
                items.append(dict(kind="H", rows=xh[1920:2048, :], vcol=0))
                for i in range(8):
                    items.append(dict(kind="M", lhs=(lambda c, i=i: hT[:, c, i * 128:(i + 1) * 128]), hkeys=(("hT", i),),
                                      pos=("c", i), out=(7 == i and [(0, 128, kvp[0][:, 0:128, :])] or [])))
            elif g == 1:
                for r in range(4):
                    items.append(dict(kind="H", rows=xh[1536 + r:2048:4, :], vcol=1 + r))
                    for s in range(2):
                        items.append(dict(kind="M", lhs=(lambda c, s=s, r=r: hT[:, c, s * 512 + r:s * 512 + 512:4]),
                                          hkeys=tuple(("hT", 4 * s + k) for k in range(4)), pos=("s4", s, r),
                                          out=(s == 1 and [(0, 128, kvp[1][:, r:512:4, :])] or [])))
            else:
                for T in range(8):
                    items.append(dict(kind="H", rows=xh[2 * T:2048:16, :], vcol=5 + 2 * T))
                    items.append(dict(kind="H", rows=xh[2 * T + 1:2048:16, :], vcol=5 + 2 * T + 1))
                    items.append(dict(kind="M", lhs=(lambda c: hTh[:, c, :]), gather=T,
                                      hkeys=("hTh",), pos=("s16", T),
                                      out=[(0, 64, kvp[2][:, 2 * T:1024:16, :]), (64, 128, kvp[2][:, 2 * T + 1:1024:16, :])]))
            n_items = len(items)
            for idx, it in enumerate(items):
                it["idx"] = idx
                it["kslot"] = idx % 5
            qcount = [0]
            for it in items:
                if it["kind"] == "M":
                    it["qslot"] = qcount[0] % 3
                    it["qbuf"] = qcount[0] % 2
                    qcount[0] += 1
            for idx, it in enumerate(items):
                if it["kind"] != "M":
                    continue
                if g == 2:
                    it["prev"] = [(items[idx - 2], 0, 64), (items[idx - 1], 64, 128)]
                else:
                    it["prev"] = [(items[idx - 1], 0, 128)]

            Kn, Vn, Qn = "K%d" % g, "V%d" % g, "Q%d" % g

            hsel = [0]

            def stageN(it):
                if it["kind"] == "H" or "gather" in it:
                    hb, hkey = HTH[hsel[0] % 2]
                    hsel[0] += 1
                    if it["kind"] == "H":
                        it["_n"] = norm_parts(it["rows"], 128, gmix, (lambda c0, c1, hb=hb: hb[:, c0:c1, :]), (hkey,))
                    else:
                        T = it["gather"]
                        srcv = hT[:, :, 0:NT].rearrange("p c (m r) -> p c r m", r=16)

                        def gat(T=T, srcv=srcv, hb=hb, hkey=hkey):
                            for rr in range(2):
                                P.add("dve", lambda e, rr=rr: e.tensor_copy(out=hb[:, :, rr * 64:(rr + 1) * 64], in_=srcv[:, :, 2 * T + rr, :]),
                                      r=tuple(("hT", k) for k in range(8)), w=(hkey,))
                        it["_n"] = (None, None, gat)
                    it["_lhs"], it["_hk"] = (lambda c, hb=hb: hb[:, c, :]), (hkey,)
                else:
                    it["_lhs"], it["_hk"] = it["lhs"], it["hkeys"]

            def stageA1(it):
                lhs, hk = it["_lhs"], it["_hk"]
                kK, kKkey = KFK[it["idx"] % 2]
                it["_kfK"] = (kK, kKkey)
                if it["kind"] == "M":
                    kQ, kQkey = KFQ[it["qbuf"]]
                    it["_kfQ"] = (kQ, kQkey)
                    proj(lhs, hk, Qn, 4, 128)
                    qk_norm(4, 128, bufA, "bufA", kQ, kQkey, st4[0], st4[1], "sq")
                proj(lhs, hk, Kn, 2, 128)
                qk_norm(2, 128, bufB, "bufB", kK, kKkey, st4[2], st4[3], "sk")
                for (p0, p1, dst) in it.get("out", []):
                    dma("sp", dst[0], kK[p0:p1, :], (kKkey,), (("o", "K", g),), ("stK", g, it["idx"] % 2))

            def stageB(it):
                if it["kind"] == "M":
                    kQ, kQkey = it["_kfQ"]
                    tr4(kQ, kQkey, 128, QTr[it["qslot"]], ("QT", it["qslot"]))
                kK, kKkey = it["_kfK"]
                tr4(kK, kKkey, 128, KTr[it["kslot"]], ("KT", it["kslot"]))

            def stageA2(it):
                proj(it["_lhs"], it["_hk"], Vn, 3, 128)
                ks = it["kslot"]
                P.add("act", lambda e: e.copy(out=Vr[ks], in_=ps[3][:, :]), r=("ps3",), w=(("V", ks),))
                outs = it.get("out", [])
                if outs:
                    P.add("dve", lambda e: e.tensor_copy(out=vf[:], in_=ps[3][:, :]), r=("ps3",), w=("vf",))
                    for (p0, p1, dst) in outs:
                        dma("sp", dst[1], vf[p0:p1, :], ("vf",), (("o", "V", g),), ("stV", g))

            def stageC(it):
                qs, ks = it["qslot"], it["kslot"]
                for hp in range(2):
                    bank = 6 if hp == 0 else 0
                    stv = ps[bank][:, :].rearrange("p (a h q) -> p a h q", a=2, h=2)

                    def f(e, hp=hp, stv=stv):
                        ins = None
                        for hh in range(2):
                            h = hp * 2 + hh
                            for (pit, c0, c1) in it["prev"]:
                                ins = e.matmul(out=stv[:, 0, hh, c0:c1], lhsT=KTr[pit["kslot"]][:, h, :], rhs=QTr[qs][:, h, c0:c1],
                                               start=True, stop=True)
                            ins = e.matmul(out=stv[:, 1, hh, :], lhsT=KTr[ks][:, h, :], rhs=QTr[qs][:, h, :], start=True, stop=True)
                        return ins
                    rk = [("QT", qs), ("KT", ks)] + [("KT", pit["kslot"]) for (pit, _, _) in it["prev"]]
                    P.add("pe", f, r=tuple(rk), w=("ps%d" % bank,))
                    P.add("act", lambda e, bank=bank: e.activation(out=Ebuf.rearrange("p a h q -> p (a h q)"), in_=ps[bank][:, :],
                                                                   func=AF.Exp, scale=SCALE), r=("ps%d" % bank,), w=("Ebuf",))
                    pt = PTb[hp]
                    if g == 2:
                        ebc = EBc2[:, hp * 2:hp * 2 + 2, :]
                        ebp = EBp2[:, hp * 2:hp * 2 + 2, :]
                        ekeys = ("EBc2", "EBp2")
                    else:
                        ebc = EB[:, g * 4 + hp * 2:g * 4 + hp * 2 + 2, 0, :]
                        ebp = EB[:, g * 4 + hp * 2:g * 4 + hp * 2 + 2, 1, :]
                        ekeys = ("EB",)
                    P.add("dve", lambda e, pt=pt, ebc=ebc: e.tensor_tensor(out=pt[:, 1, :, :], in0=Ebuf[:, 1, :, :], in1=ebc, op=ALU.mult),
                          r=("Ebuf",) + ekeys, w=(("PT", hp),))
                    for (pit, c0, c1) in it["prev"]:
                        if pit["kind"] == "H":
                            vc = pit["vcol"]
                            P.add("dve", lambda e, pt=pt, ebp=ebp, c0=c0, c1=c1, vc=vc: e.scalar_tensor_tensor(
                                out=pt[:, 0, :, c0:c1], in0=Ebuf[:, 0, :, c0:c1], scalar=kval[:, vc:vc + 1], in1=ebp[:, :, c0:c1],
                                op0=ALU.mult, op1=ALU.mult), r=("Ebuf", "kval") + ekeys, w=(("PT", hp),))
                        else:
                            P.add("dve", lambda e, pt=pt, ebp=ebp, c0=c0, c1=c1: e.tensor_tensor(
                                out=pt[:, 0, :, c0:c1], in0=Ebuf[:, 0, :, c0:c1], in1=ebp[:, :, c0:c1], op=ALU.mult),
                                r=("Ebuf",) + ekeys, w=(("PT", hp),))

            def stageD(it):
                ks = it["kslot"]
                kind, *pp = it["pos"]
                for hp in range(2):
                    bank = 7 if hp == 0 else 1
                    od = ps[bank][:, :].rearrange("p (a h q) -> p a h q", a=2, h=2)
                    pt = PTb[hp]

                    def f(e, hp=hp, od=od, pt=pt):
                        ins = None
                        for hh in range(2):
                            h = hp * 2 + hh
                            ins = e.matmul(out=od[:, 0, hh, :], lhsT=Vr[ks][:, h * 128:(h + 1) * 128], rhs=pt[:, 1, hh, :],
                                           start=True, stop=False)
                            np_ = len(it["prev"])
                            for j, (pit, c0, c1) in enumerate(it["prev"]):
                                ins = e.matmul(out=od[:, 0, hh, c0:c1], lhsT=Vr[pit["kslot"]][:, h * 128:(h + 1) * 128],
                                               rhs=pt[:, 0, hh, c0:c1], start=False, stop=(j == np_ - 1))
                        ins = e.matmul(out=od[:, 1, :, :], lhsT=onesb[:], rhs=pt[:, 1, :, :], start=True, stop=False)
                        ins = e.matmul(out=od[:, 1, :, :], lhsT=onesb[:], rhs=pt[:, 0, :, :], start=False, stop=True)
                        return ins
                    rk = [("PT", hp), ("V", ks), "onesb"] + [("V", pit["kslot"]) for (pit, _, _) in it["prev"]]
                    P.add("pe", f, r=tuple(rk), w=("ps%d" % bank,))
                    for a, acc, akey in ((0, accO, "accO"), (1, accD, "accD")):
                        hs = slice(hp * 2, hp * 2 + 2)
                        if kind == "c":
                            i = pp[0]
                            dst = acc[:, hs, i * 128:(i + 1) * 128]
                            src = od[:, a, :, :]
                        elif kind == "s4":
                            s, r = pp
                            dst = acc[:, hs, s * 512 + r:s * 512 + 512:4]
                            src = od[:, a, :, :]
                        else:
                            T = pp[0]
                            dst = acc[:, hs, :].rearrange("p h (m r) -> p h r m", r=16)[:, :, 2 * T:2 * T + 2, :]
                            src = od[:, a, :, :].rearrange("p h (r m) -> p h r m", r=2)
                        if first_group[0]:
                            P.add("act", lambda e, dst=dst, src=src: e.copy(out=dst, in_=src),
                                  r=("ps%d" % bank,), w=(akey,))
                        else:
                            P.add("dve", lambda e, dst=dst, src=src: e.tensor_tensor(out=dst, in0=dst, in1=src, op=ALU.add),
                                  r=("ps%d" % bank, akey), w=(akey,))

            def npart_(k, j):
                if 0 <= k < n_items:
                    fn = items[k].get("_n", (None, None, None))[j]
                    if fn is not None:
                        fn()
            samp = {}

            def sample_proj():
                sl = lambda c: hT[:, c, NT:NT + TS]
                last_q = [it for it in items if it["kind"] == "M"][-1]["qbuf"]
                kQs, kQskey = KFQ[(last_q + 1) % 2]
                kKs, kKskey = KFK[n_items % 2]
                samp["kQ"], samp["kK"] = (kQs, kQskey), (kKs, kKskey)
                proj(sl, (("hT", 8),), Qn, 4, TS)
                qk_norm(4, TS, bufA, "bufA", kQs, kQskey, st4[0], st4[1], "sq")
                proj(sl, (("hT", 8),), Kn, 2, TS)
                qk_norm(2, TS, bufB, "bufB", kKs, kKskey, st4[2], st4[3], "sk")
                dma("sp", kvs[g, 0], kKs[0:TS, :], (kKskey,), (("o", "Ks", g),), ("stKs", g))
                proj(sl, (("hT", 8),), Vn, 3, TS)
                P.add("dve", lambda e: e.tensor_copy(out=vf[0:TS, :], in_=ps[3][0:TS, :]), r=("ps3",), w=("vf",))
                dma("sp", kvs[g, 1], vf[0:TS, :], ("vf",), (("o", "Vs", g),), ("stVs", g))
                for _ in range(3):
                    wload_next()

            for it in items:
                stageN(it)
            npart_(0, 0); npart_(1, 0); npart_(0, 1); npart_(0, 2); npart_(1, 1)
            for n in range(n_items + 2):
                if n == n_items:
                    sample_proj()
                npart_(n + 2, 0)
                if 0 <= n - 2 < n_items and items[n - 2]["kind"] == "M":
                    stageC(items[n - 2])
                npart_(n + 1, 2)
                npart_(n + 2, 1)
                if n < n_items:
                    stageA1(items[n])
                if 0 <= n - 1 < n_items:
                    stageB(items[n - 1])
                if n < n_items:
                    stageA2(items[n])
                if 0 <= n - 2 < n_items and items[n - 2]["kind"] == "M":
                    stageD(items[n - 2])
            kQs, kQskey = samp["kQ"]
            kKs, kKskey = samp["kK"]
            tr4(kQs, kQskey, TS, QTr[0][:, :, 0:TS], ("QT", 0))
            tr4(kKs, kKskey, TS, KTr[4][:, :, 0:TS], ("KT", 4))
            P.add("act", lambda e: e.copy(out=Vr[4][0:TS, :], in_=vf[0:TS, :]), r=("vf",), w=(("V", 4),))
            ntile = 1 if g == 0 else 4
            for t in range(ntile):
                rows = slice(0, 128) if g == 0 else slice(t, dil * 128, dil)
                dma("sp", xt[:, t * 512:(t + 1) * 512], ck[g][0, rows, :], (), ("xt",), "xt")
                tr4(xt[:, t * 512:(t + 1) * 512], "xt", 128, KTr[t], ("KT", t))
                dma("sp", kfQ[:, :], ck[g][1, rows, :], (), ("kfQ",), "ckv")
                P.add("dve", lambda e, t=t: e.tensor_copy(out=Vr[t], in_=kfQ[:, :]), r=("kfQ",), w=(("V", t),))
            sc = ps[6]
            od = ps[7]

            def fsc(e):
                ins = None
                for h in range(4):
                    if g == 0:
                        ins = e.matmul(out=sc[:, h * 4:(h + 1) * 4], lhsT=KTr[0][:, h, :], rhs=QTr[0][:, h, 0:TS], start=True, stop=True)
                    else:
                        for t in range(4):
                            ins = e.matmul(out=sc[:, h * 4 + t:h * 4 + t + 1], lhsT=KTr[t][:, h, :], rhs=QTr[0][:, h, t:t + 1],
                                           start=True, stop=True)
                    ins = e.matmul(out=sc[0:TS, 16 + h * 4:16 + (h + 1) * 4], lhsT=KTr[4][:, h, 0:TS], rhs=QTr[0][:, h, 0:TS],
                                   start=True, stop=True)
                return ins
            P.add("pe", fsc, r=(("QT", 0), ("KT", 4)) + tuple(("KT", t) for t in range(ntile)), w=("ps6",))
            Es = kfK[:, 0:32]
            PTs = PTb[0][:].rearrange("p a h q -> p (a h q)")[:, 0:32]
            P.add("act", lambda e: e.activation(out=Es[:, 0:16], in_=sc[:, 0:16], func=AF.Exp, scale=SCALE), r=("ps6",), w=("kfK",))
            P.add("act", lambda e: e.activation(out=Es[0:TS, 16:32], in_=sc[0:TS, 16:32], func=AF.Exp, scale=SCALE), r=("ps6",), w=("kfK",))
            E3 = Es[:, 0:16].rearrange("p (h t) -> p h t", h=4)
            P3 = PTs[:, 0:16].rearrange("p (h t) -> p h t", h=4)
            En = Es[0:TS, 16:32].rearrange("p (h t) -> p h t", h=4)
            Pn = PTs[0:TS, 16:32].rearrange("p (h t) -> p h t", h=4)
            if g == 0:
                ebc_s = EB[:, 0:4, 1, 0:TS]
            else:
                ebc_s = bc_free(EB[:, g * 4:(g + 1) * 4, 1, 0], TS)
            P.add("dve", lambda e: e.tensor_tensor(out=P3, in0=E3, in1=ebc_s, op=ALU.mult), r=("kfK", "EB"), w=(("PT", 0),))
            ebn_s = EB[0:TS, g * 4:(g + 1) * 4, 0, 0:TS]
            if g == 0:
                P.add("dve", lambda e: e.tensor_tensor(out=Pn, in0=En, in1=ebn_s, op=ALU.mult), r=("kfK", "EB"), w=(("PT", 0),))
            else:
                P.add("dve", lambda e: e.tensor_tensor(out=En, in0=En, in1=ebn_s, op=ALU.mult), r=("kfK", "EB"), w=("kfK",))
                idb = bass.AP(tensor=identf[:].tensor, offset=identf[:].offset, ap=[list(identf[:].ap[0][:1]) + [TS], [0, 4], [1, TS]])
                P.add("dve", lambda e: e.tensor_tensor(out=Pn, in0=En, in1=idb, op=ALU.mult), r=("kfK", "identf"), w=(("PT", 0),))

            def fpv(e):
                ins = None
                for h in range(4):
                    hc = slice(h * 128, (h + 1) * 128)
                    ins = e.matmul(out=od[:, h * 4:(h + 1) * 4], lhsT=Vr[4][0:TS, hc], rhs=PTs[0:TS, 16 + h * 4:16 + (h + 1) * 4],
                                   start=True, stop=False)
                    if g == 0:
                        ins = e.matmul(out=od[:, h * 4:(h + 1) * 4], lhsT=Vr[0][:, hc], rhs=PTs[:, h * 4:(h + 1) * 4], start=False, stop=True)
                    else:
                        for t in range(4):
                            ins = e.matmul(out=od[:, h * 4 + t:h * 4 + t + 1], lhsT=Vr[t][:, hc], rhs=PTs[:, h * 4 + t:h * 4 + t + 1],
                                           start=False, stop=(t == 3))
                ins = e.matmul(out=od[:, 16:32], lhsT=onesb[0:TS, :], rhs=PTs[0:TS, 16:32], start=True, stop=False)
                ins = e.matmul(out=od[:, 16:32], lhsT=onesb[:, :], rhs=PTs[:, 0:16], start=False, stop=True)
                return ins
            P.add("pe", fpv, r=(("PT", 0), ("V", 4), "onesb") + tuple(("V", t) for t in range(ntile)), w=("ps7",))
            for a, acc, akey in ((0, accOs, "accOs"), (1, accDs, "accDs")):
                dst = acc[:].rearrange("p h t -> p (h t)")
                src = od[:, a * 16:(a + 1) * 16]
                if first_group[0]:
                    P.add("dve", lambda e, dst=dst, src=src: e.tensor_copy(out=dst, in_=src), r=("ps7",), w=(akey,))
                else:
                    P.add("dve", lambda e, dst=dst, src=src: e.tensor_tensor(out=dst, in0=dst, in1=src, op=ALU.add),
                          r=("ps7", akey), w=(akey,))
            first_group[0] = False
            return items

        group_items = {}
        for g in (2, 1, 0):
            if STOP <= 3 + (2 - g):
                return
            group_items[g] = run_group(g)

        if STOP <= 6:
            return

        XT2_OK[0] = False
        P.barrier()
        for h in range(4):
            P.add("dve", lambda e, h=h: e.reciprocal(out=accD[:, h, :], in_=accD[:, h, :]), r=("accD",), w=("accD",))
            P.add("dve", lambda e, h=h: e.tensor_tensor(out=catT[:, h, 0:NT], in0=accO[:, h, :], in1=accD[:, h, :], op=ALU.mult),
                  r=("accO", "accD"), w=tuple(("cat", i) for i in range(8)))
        P.add("dve", lambda e: e.reciprocal(out=accDs[:], in_=accDs[:]), r=("accDs",), w=("accDs",))
        P.add("dve", lambda e: e.tensor_tensor(out=catT[:, 0:4, NT:NT + TS], in0=accOs[:], in1=accDs[:], op=ALU.mult),
              r=("accOs", "accDs"), w=(("cat", 8),))
        dma("sp", bufA[:], bcast_rows(gvn, 128, 512), (), ("bufA",), "c5")
        dma("sp", bufB[:], bass.AP(tensor=gb.tensor, offset=0, ap=[[0, 128], [1, 512]]), (), ("bufB",), "c6")
        slotU, wU = W("U")
        bk = [2, 3, 4]
        bi = 0
        for gg in range(4):
            for th in range(3):
                bank = bk[bi % 3]; bi += 1
                if th < 2:
                    c0, c1, n = th * 512, th * 512 + 512, 512
                    rk = tuple(("hT", 4 * th + k) for k in range(4))
                    dst = uT[:, gg, c0:c1]
                    dkey = "uT"
                else:
                    c0, c1, n = NT, NT + TS, TS
                    rk = (("hT", 8),)
                    dst = uTs[:, gg, :]
                    dkey = "uTs"

                def f(e, gg=gg, c0=c0, c1=c1, n=n, bank=bank):
                    ins = None
                    for c in range(16):
                        ins = e.matmul(out=ps[bank][:, 0:n], lhsT=wU[:, c, gg * 128:(gg + 1) * 128], rhs=hT[:, c, c0:c1],
                                       start=(c == 0), stop=(c == 15))
                    return ins
                P.add("pe", f, r=rk + (("ring", slotU),), w=("ps%d" % bank,))
                P.add("act", lambda e, dst=dst, n=n, bank=bank: e.activation(out=dst, in_=ps[bank][:, 0:n], func=AF.Gelu),
                      r=("ps%d" % bank,), w=(dkey, "EB", "EBc2", "EBp2") if th < 2 else (dkey,))
        gtile = Vr[0]
        for i in range(9):
            npart = 128 if i < 8 else TS
            cols = slice(i * 128, (i + 1) * 128) if i < 8 else slice(NT, NT + TS)
            proj(lambda c, cols=cols: hT[:, c, cols], (("hT", i),), "G", 2, npart)
            P.add("act", lambda e, npart=npart: e.activation(out=kfQ[0:npart, :], in_=ps[2][0:npart, :], func=AF.Gelu),
                  r=("ps2",), w=("kfQ",))
            jk = bass.AP(tensor=junk[:].tensor, offset=junk[:].offset, ap=[[junk[:].ap[0][0], npart], [0, 512]])
            P.add("act", lambda e, npart=npart, jk=jk: e.activation(out=jk, in_=kfQ[0:npart, :], func=AF.Square,
                                                                    accum_out=st_ssq[0:npart, :]), r=("kfQ",), w=("st_ssq",))
            P.add("act", lambda e, npart=npart: e.activation(out=st_rs[0:npart, :], in_=st_ssq[0:npart, :], func=AF.Sqrt,
                                                             scale=1.0 / 512, bias=EPS), r=("st_ssq",), w=("st_rs",))
            P.add("dve", lambda e, npart=npart: e.reciprocal(out=st_rs[0:npart, :], in_=st_rs[0:npart, :]), r=("st_rs",), w=("st_rs",))
            if i < 8:
                P.add("dve", lambda e: e.scalar_tensor_tensor(out=gtile, in0=kfQ[:, :], scalar=st_rs[:, 0:1], in1=bufA[:, :],
                                                              op0=ALU.mult, op1=ALU.mult), r=("kfQ", "st_rs", "bufA"), w=(("V", 0),))
            else:
                P.add("dve", lambda e: e.scalar_tensor_tensor(out=vf[0:TS, :], in0=kfQ[0:TS, :], scalar=st_rs[0:TS, 0:1], in1=bufA[0:TS, :],
                                                              op0=ALU.mult, op1=ALU.mult), r=("kfQ", "st_rs", "bufA"), w=("vf",))
                dma("sp", gvs[:, :], vf[0:TS, :], ("vf",), (("o", "G"),), "stG")
                P.add("dve", lambda e: e.tensor_copy(out=gtile[0:TS, :], in_=vf[0:TS, :]), r=("vf",), w=(("V", 0),))
            nq = npart

            def fm(e, npart=npart):
                ins = None
                for gg in range(4):
                    ins = e.matmul(out=ps[5][:, gg * 128:gg * 128 + npart], lhsT=gtile[0:npart, gg * 128:(gg + 1) * 128],
                                   rhs=WmT[0:npart, gg, 0:npart], start=True, stop=True)
                return ins
            P.add("pe", fm, r=(("V", 0), "WmT"), w=("ps5",))
            mixv = ps[5][:, :].rearrange("p (g t) -> p g t", g=4)[:, :, 0:npart]
            bbv = bufB[:, :].rearrange("p (g t) -> p g t", g=4)[:, :, 0:npart]
            tmpv = kfK[:, :].rearrange("p (g t) -> p g t", g=4)[:, :, 0:npart]
            P.add("dve", lambda e, mixv=mixv, bbv=bbv, tmpv=tmpv: e.tensor_tensor(out=tmpv, in0=mixv, in1=bbv, op=ALU.add),
                  r=("ps5", "bufB"), w=("kfK",))
            uv = uT[:, :, cols] if i < 8 else uTs[:, :, :]
            P.add("dve", lambda e, tmpv=tmpv, uv=uv, cols=cols: e.tensor_tensor(out=catT[:, 4:8, cols], in0=tmpv, in1=uv, op=ALU.mult),
                  r=("kfK", "uT", "uTs"), w=(("cat", i),))
        wload_next(); wload_next()

        if STOP <= 7:
            return

        P.barrier()

        if STOP <= 8:
            return

        for i in range(8):
            dma("sp", x1[:, i, :], xm[i * 128:(i + 1) * 128, :], (), (("x1", i),), ("x1l", i))
        dma("sp", xt[0:TS, :], xs[:, :], (), ("xt",), "xt")

        def outproj(i):
            npart = 128 if i < 8 else TS
            cols = slice(i * 128, (i + 1) * 128) if i < 8 else slice(NT, NT + TS)
            for cb in range(4):
                slot, wv = W("O%d" % (cb // 2))
                bank = 2 + cb

                def f(e, wv=wv, cb=cb, bank=bank):
                    ins = None
                    for c in range(8):
                        ins = e.matmul(out=ps[bank][0:npart, :], lhsT=catT[:, c, cols], rhs=wv[:, c, (cb % 2) * 512:(cb % 2) * 512 + 512],
                                       start=(c == 0), stop=(c == 7))
                    return ins
                P.add("pe", f, r=(("cat", i), ("ring", slot)), w=("ps%d" % bank,))
                dst = x1[:, i, cb * 512:(cb + 1) * 512] if i < 8 else xt[0:TS, cb * 512:(cb + 1) * 512]
                key = ("x1", i) if i < 8 else "xt"
                P.add("dve", lambda e, dst=dst, bank=bank: e.tensor_tensor(out=dst, in0=ps[bank][0:npart, :], in1=dst, op=ALU.add),
                      r=("ps%d" % bank, key), w=(key,))

        def norm2(i):
            cols = slice(i * 128, (i + 1) * 128) if i < 8 else slice(NT, NT + TS)
            if i < 8:
                norm_tile(None, 128, gffn, lambda c0, c1: hT[:, c0:c1, cols], (("hT", i),), load=False,
                          src_sb=x1[:, i, :], src_keys=(("x1", i),))
            else:
                norm_tile(None, TS, gffn, lambda c0, c1: hT[:, c0:c1, cols], (("hT", 8),), load=False,
                          src_sb=xt[0:TS, :], src_keys=("xt",))

        outproj(0)
        for i in range(1, 9):
            outproj(i)
            norm2(i - 1)
        norm2(8)
        P.barrier()
        wload_next(); wload_next()

        if STOP <= 9:
            return

        upb = [0]
        dnb = [0]

        def up(n):
            slot, wv = W("UP%d" % n)
            for fc in range(4):
                for th in range(3):
                    bank = upb[0] % 4; upb[0] += 1
                    if th < 2:
                        c0, c1, nn = th * 512, th * 512 + 512, 512
                        rk = tuple(("hT", 4 * th + k) for k in range(4))
                        dst = aT[n % 2][:, fc, c0:c1]
                        dkey = ("aT", n % 2)
                    else:
                        c0, c1, nn = NT, NT + TS, TS
                        rk = (("hT", 8),)
                        dst = aTs[:, n % 2, fc, :]
                        dkey = ("aTs", n % 2)

                    def f(e, fc=fc, c0=c0, c1=c1, nn=nn, bank=bank):
                        ins = None
                        for c in range(16):
                            ins = e.matmul(out=ps[bank][:, 0:nn], lhsT=wv[:, c, fc * 128:(fc + 1) * 128], rhs=hT[:, c, c0:c1],
                                           start=(c == 0), stop=(c == 15))
                        return ins
                    P.add("pe", f, r=rk + (("ring", slot),), w=("ps%d" % bank,))
                    rs_ = rsc[bank % 2]
                    P.add("act", lambda e, nn=nn, bank=bank, rs_=rs_: e.activation(out=rs_[:, 0:nn], in_=ps[bank][:, 0:nn], func=AF.Relu),
                          r=("ps%d" % bank,), w=(("rsc", bank % 2),))
                    P.add("act", lambda e, nn=nn, rs_=rs_, dst=dst: e.activation(out=dst, in_=rs_[:, 0:nn], func=AF.Square),
                          r=(("rsc", bank % 2),), w=(dkey,))

        def down(n):
            slot, wv = W("DN%d" % n)
            for i in range(9):
                npart = 128 if i < 8 else TS
                for cb in range(4):
                    bank = 4 + dnb[0] % 4; dnb[0] += 1

                    def f(e, i=i, cb=cb, bank=bank, npart=npart):
                        ins = None
                        for fc in range(4):
                            lhsT = aT[n % 2][:, fc, i * 128:(i + 1) * 128] if i < 8 else aTs[:, n % 2, fc, :]
                            ins = e.matmul(out=ps[bank][0:npart, :], lhsT=lhsT, rhs=wv[:, fc, cb * 512:(cb + 1) * 512],
                                           start=(fc == 0), stop=(fc == 3))
                        return ins
                    rk = (("aT", n % 2), ("ring", slot)) if i < 8 else (("aTs", n % 2), ("ring", slot))
                    P.add("pe", f, r=rk, w=("ps%d" % bank,))
                    dst = x1[:, i, cb * 512:(cb + 1) * 512] if i < 8 else xt[0:TS, cb * 512:(cb + 1) * 512]
                    key = ("x1", i) if i < 8 else "xt"
                    P.add("dve", lambda e, dst=dst, bank=bank, npart=npart: e.tensor_tensor(out=dst, in0=ps[bank][0:npart, :], in1=dst, op=ALU.add),
                          r=("ps%d" % bank, key), w=(key,))

        for n in range(17):
            if n < 16:
                up(n)
                if n >= 1:
                    pass
            if n >= 1:
                down(n - 1)
                wload_next(); wload_next()
        for i in range(8):
            dma("sp", y[i * 128:(i + 1) * 128, :], x1[:, i, :], (("x1", i),), (("o", "y", i),), ("sty", i))
        dma("sp", ys[:, :], xt[0:TS, :], ("xt",), (("o", "ys"),), "stys")

    phases()
    if DBG:
        P.barrier()
        dma("sp", dbg_rx[:, :], RX[:, :], (), (("o", "dbgrx"),), "dbg0")
        dma("sp", dbg_cat[:, :], catT[:].rearrange("p c n -> p (c n)"), (), (("o", "dbgcat"),), "dbg1")
        dma("sp", dbg_hT[:, :], hT[:].rearrange("p c n -> p (c n)"), (), (("o", "dbghT"),), "dbg2")
    P.barrier()
    P.emit(nc, es)
    es.close()
    return nc


_CACHE = {}


def kernel(x_prompt, x_sample, cache_kv_w128, cache_kv_w512, cache_kv_w2048, norm_mix, w_in,
           q_norm, k_norm, rel_bias, gmlp_v_norm, gmlp_w, gmlp_b, w_out, norm_ffn, w_up, w_down):
    f = lambda a: np.ascontiguousarray(np.asarray(a, dtype=np.float32))
    x_prompt = f(x_prompt); x_sample = f(x_sample)
    caches = [f(cache_kv_w128), f(cache_kv_w512), f(cache_kv_w2048)]
    if "nc" not in _CACHE:
        _CACHE["nc"] = build_program()
    nc = _CACHE["nc"]
    oh, ident, jm = host_constants()
    shared = {
        "w_in": f(w_in)[0], "w_out": f(w_out)[0], "w_up": f(w_up)[0], "w_down": f(w_down)[0],
        "norm_mix": f(norm_mix).reshape(16, 128), "norm_ffn": f(norm_ffn).reshape(16, 128),
        "q_norm": f(q_norm).reshape(1, 128), "k_norm": f(k_norm).reshape(1, 128),
        "rel_bias": f(rel_bias), "gmlp_v_norm": f(gmlp_v_norm).reshape(1, 512),
        "gmlp_w": f(gmlp_w)[0], "gmlp_b": f(gmlp_b)[0], "oh": oh, "ident": ident, "jm": jm,
    }
    in_maps = []
    for c in range(8):
        b, j = c // 4, c % 4
        q0 = j * NT
        xh = np.zeros((NH, D), np.float32)
        valid = np.zeros((NH,), np.float32)
        lo = q0 - NH
        s = max(lo, 0)
        if q0 > 0:
            xh[s - lo:] = x_prompt[b, s:q0]
            valid[s - lo:] = 1.0
        kv = np.zeros((128, 21), np.float32)
        kv[:, 0] = valid[1920:2048]
        for r in range(4):
            kv[:, 1 + r] = valid[1536 + r:2048:4]
        for r in range(16):
            kv[:, 5 + r] = valid[r:2048:16]
        m = dict(shared)
        m["xm"] = np.ascontiguousarray(x_prompt[b, q0:q0 + NT])
        m["xh"] = xh
        m["xs"] = np.ascontiguousarray(x_sample[c])
        m["kvalid"] = kv
        for g in range(3):
            m["ck%d" % g] = np.ascontiguousarray(caches[g][0, c].reshape(2, -1, 512))
        in_maps.append(m)
    res = run_bass_kernel_spmd(nc, in_maps, core_ids=list(range(8)))
    R = res.results
    yp = np.zeros((2, 4096, D), np.float32)
    ysm = np.zeros((8, TS, D), np.float32)
    kvp_out = [np.zeros((1, 2, 2, w, 4, 128), np.float32) for w in (128, 512, 2048)]
    kvs_out = [np.zeros((1, 8, 2, TS, 4, 128), np.float32) for _ in range(3)]
    gv = np.zeros((1, 8, TS, 512), np.float32)
    for c in range(8):
        b, j = c // 4, c % 4
        yp[b, j * NT:(j + 1) * NT] = R[c]["y"]
        ysm[c] = R[c]["ys"]
        if j == 3:
            kvp_out[0][0, b] = R[c]["kvp0"].reshape(2, 128, 4, 128)
            kvp_out[1][0, b] = R[c]["kvp1"].reshape(2, 512, 4, 128)
        if j >= 2:
            kvp_out[2][0, b, :, (j - 2) * 1024:(j - 1) * 1024] = R[c]["kvp2"].reshape(2, 1024, 4, 128)
        for g in range(3):
            kvs_out[g][0, c] = R[c]["kvs"][g].reshape(2, TS, 4, 128)
        gv[0, c] = R[c]["gvs"]
    return (yp, ysm, kvp_out[0], kvp_out[1], kvp_out[2], kvs_out[0], kvs_out[1], kvs_out[2], gv)
```

```python
import contextlib
import os
import numpy as np
import concourse.bass as bass
import concourse.mybir as mybir
from concourse.bass_utils import run_bass_kernel_spmd

F32 = mybir.dt.float32
BF16 = mybir.dt.bfloat16
AF = mybir.ActivationFunctionType
ALU = mybir.AluOpType
AX = mybir.AxisListType

D = 2048
NT = 1024
NH = 2048
TS = 4
NCOL = NT + TS
DIN = 5632
DFF = 8192
EPS = 1e-6
DILS = (1, 4, 16)
SCALE = 128 ** -0.5
SAME_ENG_SYNC = os.environ.get('MK_SES', '1') == '1'

ENGS = ("pe", "act", "dve", "pool", "sp")


class Op:
    __slots__ = ("eng", "fn", "deps", "dma", "signal", "seq", "target")


class Prog:
    def __init__(self):
        self.ops = []
        self.lw = {}
        self.rd = {}
        self.dcnt = {}
        self.last = {}
        self.auto_r = ()

    def add(self, eng, fn, r=(), w=(), dma=None, extra=()):
        idx = len(self.ops)
        deps = set(extra)
        r = tuple(r) + tuple(self.auto_r)
        pr = tuple(k for k in r if isinstance(k, str) and k[:2] == 'ps' and k[2:].isdigit())
        if pr:
            r = tuple(k for k in r if k not in pr)
            w = tuple(w) + pr
        for k in r:
            if k in self.lw:
                deps.add(self.lw[k])
        for k in w:
            if k in self.lw:
                deps.add(self.lw[k])
            deps.update(self.rd.get(k, ()))
        op = Op()
        op.eng, op.fn, op.deps, op.dma, op.signal, op.seq, op.target = eng, fn, deps, dma, False, 0, 0
        if dma is not None:
            c = self.dcnt.get(dma, 0) + 16
            self.dcnt[dma] = c
            op.target = c
        for k in r:
            self.rd.setdefault(k, []).append(idx)
        for k in w:
            self.lw[k] = idx
            self.rd[k] = []
        self.ops.append(op)
        if fn is not None:
            self.last[eng] = idx
        return idx

    def barrier(self):
        alld = set(self.last.values())
        for k, v in self.lw.items():
            alld.add(v)
        for e in ENGS:
            self.add(e, None, extra=tuple(alld))

    def emit(self, nc, es):
        limit = int(os.environ.get('MK_NOPS', '0'))
        if limit:
            self.ops = self.ops[:limit]
            for e in ENGS:
                op = Op()
                op.eng, op.fn, op.deps, op.dma, op.signal, op.seq, op.target = e, None, set(range(limit)), None, False, 0, 0
                self.ops.append(op)
        ops = self.ops
        if os.environ.get('MK_DUMP'):
            for i, op in enumerate(ops):
                print(i, op.eng, op.dma, sorted(op.deps)[-6:], getattr(op.fn, '__name__', None))
        for op in ops:
            op.deps = set(d for d in op.deps if ops[d].fn is not None)
            for d in op.deps:
                if ops[d].dma is None:
                    ops[d].signal = True
        cnt = {e: 0 for e in ENGS}
        for op in ops:
            if op.dma is None and op.signal:
                cnt[op.eng] += 1
                op.seq = cnt[op.eng]
        esem = {e: es.enter_context(nc.semaphore("sem_" + e)) for e in ENGS}
        dsem = {}
        for i, k in enumerate(self.dcnt):
            dsem[k] = es.enter_context(nc.semaphore("dsem%d" % i))
        block = es.enter_context(nc.Block())
        reg = {"pe": block.tensor, "act": block.scalar, "dve": block.vector,
               "pool": block.gpsimd, "sp": block.sync}
        for e in ENGS:
            mine = [op for op in ops if op.eng == e]

            def body(eng, e=e, mine=mine):
                waited = {}
                for op in mine:
                    need = {}
                    for d in op.deps:
                        p = ops[d]
                        if p.dma is not None:
                            s, v = dsem[p.dma], p.target
                        else:
                            if p.eng == e and (e == "pe" or not SAME_ENG_SYNC):
                                continue
                            s, v = esem[p.eng], p.seq
                        key = id(s)
                        if key not in need or need[key][1] < v:
                            need[key] = (s, v)
                    for key, (s, v) in need.items():
                        if waited.get(key, 0) < v:
                            eng.wait_ge(s, v)
                            waited[key] = v
                    if op.fn is not None:
                        ins = op.fn(eng)
                        if op.dma is not None:
                            ins.then_inc(dsem[op.dma], 16)
                        elif op.signal:
                            ins.then_inc(esem[e], 1)
            reg[e](body)


def bucket(dist):
    if dist < 16:
        return dist
    v = 16 + int(np.float32(np.log(np.float32(dist) / np.float32(16)) / np.float32(np.log(128.0)) * np.float32(16)))
    return min(v, 31)


def t5_bucket_np(dist):
    import math
    d = np.maximum(dist, 1).astype(np.float32)
    rnd = np.rint if os.environ.get('MK_BUCKET_RINT', '0') == '1' else np.trunc
    large = 16 + rnd(np.log(d / np.float32(16)) / np.float32(math.log(2048 / 16)) * np.float32(16)).astype(np.int32)
    large = np.minimum(large, 31)
    return np.where(dist < 16, dist, large)


def host_constants():
    oh = np.zeros((33, 3, 384), np.float32)
    for g, dil in enumerate(DILS):
        sub = np.arange(384) - 128
        ok = (sub >= 0) & (sub <= 128)
        b = t5_bucket_np(np.clip(sub, 0, 128).astype(np.int32) * dil)
        for u in range(384):
            if ok[u]:
                oh[b[u], g, u] = 1.0
            else:
                oh[32, g, u] = 1.0
    ident = np.eye(128, dtype=np.float32)
    jm = np.ascontiguousarray(ident[::-1])
    return oh.reshape(33, 1152), ident, jm


def build_program():
    nc = bass.Bass("TRN2", target_bir_lowering=False)

    def din(name, shape):
        return nc.dram_tensor(name, shape, F32, kind="ExternalInput").ap()

    def dout(name, shape):
        return nc.dram_tensor(name, shape, F32, kind="ExternalOutput").ap()

    xm = din("xm", [NT, D]); xh = din("xh", [NH, D]); xs = din("xs", [TS, D])
    kvalid = din("kvalid", [128, 21])
    ck = [din("ck0", [2, 128, 512]), din("ck1", [2, 512, 512]), din("ck2", [2, 2048, 512])]
    w_in = din("w_in", [D, DIN]); w_out = din("w_out", [1024, D])
    w_up = din("w_up", [D, DFF]); w_down = din("w_down", [DFF, D])
    norm_mix = din("norm_mix", [16, 128]); norm_ffn = din("norm_ffn", [16, 128])
    q_norm = din("q_norm", [1, 128]); k_norm = din("k_norm", [1, 128])
    rel_bias = din("rel_bias", [32, 12]); gvn = din("gmlp_v_norm", [1, 512])
    gw = din("gmlp_w", [4, 128, 128]); gb = din("gmlp_b", [4, 128])
    oh_d = din("oh", [33, 1152]); ident_d = din("ident", [128, 128]); jm_d = din("jm", [128, 128])

    y = dout("y", [NT, D]); ys = dout("ys", [TS, D])
    kvp = [dout("kvp0", [2, 128, 512]), dout("kvp1", [2, 512, 512]), dout("kvp2", [2, 1024, 512])]
    kvs = dout("kvs", [3, 2, TS, 512]); gvs = dout("gvs", [TS, 512])
    escr = dout("escr", [12, 384])
    DBG = os.environ.get('MK_DBG')
    if DBG:
        dbg_rx = dout("dbg_rx", [128, 16384])
        dbg_cat = nc.dram_tensor("dbg_cat", [128, 8 * NCOL], BF16, kind="ExternalOutput").ap()
        dbg_hT = nc.dram_tensor("dbg_hT", [128, 16 * NCOL], BF16, kind="ExternalOutput").ap()

    P = Prog()
    es = contextlib.ExitStack()
    STOP = int(os.environ.get('MK_STOP', '99'))
    SUB = int(os.environ.get('MK_SUB', '255'))

    def sb(name, shape, dt):
        return es.enter_context(nc.sbuf_tensor(name, shape, dt))

    hT = sb("hT", [128, 16, NCOL], BF16)
    ring = [sb("ring%d" % i, [128, 8192], BF16) for i in range(4)]
    RX = sb("RX", [128, 16384], F32)
    catT = sb("catT", [128, 8, NCOL], BF16)
    xt = sb("xt", [128, D], F32)
    xb = sb("xb", [128, D], BF16)
    hTh = sb("hTh", [128, 16, 128], BF16)
    kfK = sb("kfK", [128, 512], F32)
    vf = sb("vf", [128, 512], F32)
    PTb = [sb("PT%d" % i, [128, 2, 2, 128], BF16) for i in range(2)]
    identb = sb("identb", [128, 128], BF16)
    identf = sb("identf", [128, 128], F32)
    onesb = sb("onesb", [128, 128], BF16)
    gmix = sb("gmix", [128, 16], F32)
    gffn = sb("gffn", [128, 16], F32)
    bufA = sb("bufA", [128, 512], F32)
    bufB = sb("bufB", [128, 512], F32)
    WmT = sb("WmT", [128, 4, 128], BF16)
    kval = sb("kval", [128, 21], F32)
    st_ssq = sb("st_ssq", [128, 1], F32)
    junk = sb("junk", [128, 2], BF16)
    st_rs = sb("st_rs", [128, 1], F32)
    st4 = [sb("st4_%d" % i, [128, 4], F32) for i in range(4)]
    accOs = sb("accOs", [128, 4, TS], F32)
    accDs = sb("accDs", [128, 4, TS], F32)
    uTs = sb("uTs", [128, 4, TS], F32)
    ps = [es.enter_context(nc.psum_tensor("ps%d" % i, [128, 512], F32)) for i in range(8)]

    accO = RX[:, 0:4096].rearrange("p (h t) -> p h t", h=4)
    accD = RX[:, 4096:8192].rearrange("p (h t) -> p h t", h=4)
    EB = RX[:, 8192:11264].rearrange("p (g a q) -> p g a q", g=12, a=2)
    EBc2 = RX[:, 11264:11776].rearrange("p (h q) -> p h q", h=4)
    EBp2 = RX[:, 11776:12288].rearrange("p (h q) -> p h q", h=4)
    uT = RX[:, 8192:12288].rearrange("p (h t) -> p h t", h=4)
    rest = RX[:, 12288:16384]
    KTr = [rest[:, i * 256:(i + 1) * 256].bitcast(BF16).rearrange("p (h k) -> p h k", h=4) for i in range(5)]
    Vr = [rest[:, 1280 + i * 256:1280 + (i + 1) * 256].bitcast(BF16) for i in range(5)]
    QTr = [rest[:, 2560 + i * 256:2560 + (i + 1) * 256].bitcast(BF16).rearrange("p (h k) -> p h k", h=4) for i in range(3)]
    Ebuf = rest[:, 3328:3840].rearrange("p (a h q) -> p a h q", a=2, h=2)
    kfQ = RX[:, 12288 + 3840 - 512:12288 + 3840]
    kfQ = sb("kfQ", [128, 512], F32)
    x1 = RX[:, :].rearrange("p (i d) -> p i d", i=8)
    catb = catT[:].rearrange("p c n -> p (c n)")
    xt2 = catb[:, 0:4096].bitcast(F32)
    hTh2 = catb[:, 4096:6144].rearrange("p (c n) -> p c n", c=16)
    kfK2 = catb[:, 6144:7168].bitcast(F32)
    kfQ2 = catb[:, 7168:8192].bitcast(F32)
    XT = [(xt, "xt"), (xt2, "xt2")]
    XT2_OK = [True]
    HTH = [(hTh, "hTh"), (hTh2, "hTh2")]
    KFK = [(kfK, "kfK"), (kfK2, "kfK2")]
    KFQ = [(kfQ, "kfQ"), (kfQ2, "kfQ2")]
    hTf = RX
    oh_sb = hTf[:, 0:1152]
    Eall = hTf[:, 1152:2304].rearrange("p (g u) -> p g u", g=3)
    jm_sb = hTf[:, 2304:2432]
    Hh = hTf[:, 2432:2432 + 3072].rearrange("p (g a q) -> p g a q", g=12, a=2)
    tab33 = hTf[:, 5504:5516]
    wtmp = hTf[:, 5632:5632 + 512].rearrange("p (g s) -> p g s", g=4)
    vtmp = hTf[:, 6144:6144 + 128]
    aT = [catT[:].rearrange("p c n -> p (c n)")[:, i * 4096:(i + 1) * 4096].rearrange("p (f t) -> p f t", f=4) for i in range(2)]
    aTs = sb("aTs", [128, 2, 4, TS], BF16)
    rsc = [xb[:, i * 1024:(i + 1) * 1024].bitcast(F32) for i in range(2)]

    psb = [p[:].bitcast(BF16) for p in ps]

    def dma(q, out, in_, r, w, key):
        return P.add(q, lambda e, out=out, in_=in_: e.dma_start(out=out, in_=in_), r=r, w=w, dma=key)

    def bcast_rows(dram_ap_row, nparts, n):
        return bass.AP(tensor=dram_ap_row.tensor, offset=dram_ap_row.offset, ap=[[0, nparts], [1, n]])

    def bc_free(ap2, n):
        a = ap2.ap
        return bass.AP(tensor=ap2.tensor, offset=ap2.offset, ap=[list(a[0]), list(a[1]), [0, n]])

    wblocks = []

    def wview_rows(w, col0, ncols):
        return w.rearrange("(c p) n -> p c n", p=128)[:, :, col0:col0 + ncols]

    for g in (2, 1, 0):
        wblocks.append(("K%d" % g, wview_rows(w_in, 1536 + g * 512, 512), (16, 512)))
        wblocks.append(("V%d" % g, wview_rows(w_in, 3072 + g * 512, 512), (16, 512)))
        wblocks.append(("Q%d" % g, wview_rows(w_in, g * 512, 512), (16, 512)))
    wblocks.append(("U", wview_rows(w_in, 4608, 512), (16, 512)))
    wblocks.append(("G", wview_rows(w_in, 5120, 512), (16, 512)))
    wblocks.append(("O0", wview_rows(w_out, 0, 1024), (8, 1024)))
    wblocks.append(("O1", wview_rows(w_out, 1024, 1024), (8, 1024)))
    for n in range(16):
        wblocks.append(("UP%d" % n, wview_rows(w_up, n * 512, 512), (16, 512)))
        wblocks.append(("DN%d" % n, w_down[n * 512:(n + 1) * 512, :].rearrange("(c p) n -> p c n", p=128), (4, 2048)))
    wslot = {}
    wnext = [0]

    def wload_next():
        i = wnext[0]
        if i >= len(wblocks):
            return
        name, src, (c, n) = wblocks[i]
        slot = i % 4
        dst = ring[slot][:].rearrange("p (c n) -> p c n", c=c)
        wslot[name] = (slot, dst)
        dma("pool", dst, src, r=(), w=(("ring", slot),), key=("ring", slot))
        wnext[0] += 1

    def W(name):
        return wslot[name]

    def phases():
        P.auto_r = ("accO", "accD")
        dma("sp", identf[:], ident_d[:, :], (), ("identf",), "c0")
        dma("sp", jm_sb, jm_d[:, :], (), ("jm",), "c1")
        dma("sp", oh_sb[0:33, :], oh_d[:, :], (), ("oh",), "c2")
        dma("sp", tab33[0:32, :], rel_bias[:, :], (), ("tab",), "c3")
        dma("sp", kval[:], kvalid[:, :], (), ("kval",), "c4")
        dma("sp", bufA[:], bass.AP(tensor=q_norm.tensor, offset=0, ap=[[0, 128], [0, 4], [1, 128]]), (), ("bufA",), "c5")
        dma("sp", bufB[:], bass.AP(tensor=k_norm.tensor, offset=0, ap=[[0, 128], [0, 4], [1, 128]]), (), ("bufB",), "c6")
        dma("sp", vtmp[0:16, :], norm_mix[:, :], (), ("vtmp",), "c7")
        for _ in range(4):
            wload_next()
        P.add("dve", lambda e: e.tensor_copy(out=identb[:], in_=identf[:]), r=("identf",), w=("identb",))
        P.add("dve", lambda e: e.memset(onesb[:], 1.0), w=("onesb",))
        P.add("dve", lambda e: e.memset(tab33[32:33, :], -30000.0), w=("tab32",))
        if SUB & 1:
            P.add("pe", lambda e: e.transpose(out=ps[7][:, 0:16], in_=vtmp[0:16, :], identity=identf[0:16, 0:16]),
                  r=("vtmp", "identf"), w=("ps7",))
            P.add("act", lambda e: e.copy(out=gmix[:], in_=ps[7][:, 0:16]), r=("ps7",), w=("gmix",))
            dma("sp", vtmp[0:16, :], norm_ffn[:, :], (), ("vtmp",), "c7")
            P.add("pe", lambda e: e.transpose(out=ps[7][:, 0:16], in_=vtmp[0:16, :], identity=identf[0:16, 0:16]),
                  r=("vtmp", "identf"), w=("ps7",))
            P.add("act", lambda e: e.copy(out=gffn[:], in_=ps[7][:, 0:16]), r=("ps7",), w=("gffn",))
        if SUB & 2:
            dma("sp", wtmp, gw.rearrange("g t s -> t g s"), (), ("wtmp",), "c8")
            for gg in range(4):
                P.add("pool", lambda e, gg=gg: e.affine_select(out=wtmp[:, gg, :], in_=wtmp[:, gg, :], pattern=[[-1, 128]],
                                                               compare_op=ALU.is_ge, fill=0.0, base=0, channel_multiplier=1),
                      r=("wtmp",), w=("wtmp",))
            for gg in range(4):
                P.add("pe", lambda e, gg=gg: e.transpose(out=ps[6][:, gg * 128:(gg + 1) * 128], in_=wtmp[:, gg, :], identity=identf[:]),
                      r=("wtmp", "identf"), w=("ps6",))
            P.add("act", lambda e: e.copy(out=WmT[:].rearrange("p g t -> p (g t)"), in_=ps[6][:, :]), r=("ps6",), w=("WmT",))
        if SUB & 4:
            for g in range(3):
                P.add("pe", lambda e, g=g: e.matmul(out=ps[g][0:12, 0:384], lhsT=tab33[0:33, 0:12], rhs=oh_sb[0:33, g * 384:(g + 1) * 384],
                                                    start=True, stop=True),
                      r=("tab", "tab32", "oh"), w=("ps%d" % g,))
                P.add("act", lambda e, g=g: e.activation(out=Eall[0:12, g, :], in_=ps[g][0:12, 0:384], func=AF.Exp),
                      r=("ps%d" % g,), w=("Eall%d" % g,))
                dma("sp", escr[g * 4:(g + 1) * 4, :], Eall[g * 4:(g + 1) * 4, g, :], ("Eall%d" % g,), ("escr%d" % g,), "e%d" % g)
            for gh in range(12):
                src = bass.AP(tensor=escr.tensor, offset=gh * 384 + 1, ap=[[1, 128], [128, 2], [1, 128]])
                dma("sp", Hh[:, gh, :, :], src, ("escr%d" % (gh // 4),), ("Hh%d" % gh,), "h%d" % gh)
            Hf = Hh.rearrange("p g a q -> p (g a q)")
            EBf = EB.rearrange("p g a q -> p (g a q)")
            for i in range(6):
                P.add("pe", lambda e, i=i: e.matmul(out=ps[i][:, :], lhsT=jm_sb, rhs=Hf[:, i * 512:(i + 1) * 512], start=True, stop=True),
                      r=("jm", "Hh%d" % (2 * i), "Hh%d" % (2 * i + 1)), w=("ps%d" % i,))
                P.add("dve" if i % 2 else "act",
                      (lambda e, i=i: e.tensor_copy(out=EBf[:, i * 512:(i + 1) * 512], in_=ps[i][:, :])) if i % 2 else
                      (lambda e, i=i: e.copy(out=EBf[:, i * 512:(i + 1) * 512], in_=ps[i][:, :])),
                      r=("ps%d" % i,), w=("EB",))
        if SUB & 8:
            P.add("dve", lambda e: e.tensor_copy(out=EBc2, in_=EB[:, 8:12, 0, :]), r=("EB",), w=("EBc2",))
            P.add("dve", lambda e: e.memset(EBc2[0:64, :, 64:128], 0.0), w=("EBc2",))
            P.add("dve", lambda e: e.tensor_copy(out=EBp2[:, :, 0:64], in_=EB[:, 8:12, 1, 0:64]), r=("EB",), w=("EBp2",))
            P.add("dve", lambda e: e.tensor_copy(out=EBp2[:, :, 64:128], in_=EB[:, 8:12, 1, 0:64]), r=("EB",), w=("EBp2",))
        P.auto_r = ()
        if STOP <= 1:
            return

        xsel = [0]

        def norm_parts(x_src, npart, gains, dst_fn, dst_keys, load=True, src_sb=None, src_keys=()):
            if load:
                xbuf, xkey = XT[xsel[0] % 2] if XT2_OK[0] else XT[0]
                xsel[0] += 1
                src = xbuf[0:npart, :]
                skeys = (xkey,)
            else:
                src = src_sb
                skeys = tuple(src_keys)

            def n0():
                if load:
                    dma("sp", src, x_src, (), skeys, skeys[0])

            def n1():
                junk_ap = bass.AP(tensor=junk[:].tensor, offset=junk[:].offset, ap=[[junk[:].ap[0][0], npart], [0, D]])
                P.add("act", lambda e: e.activation(out=junk_ap, in_=src, func=AF.Square, accum_out=st_ssq[0:npart, :]),
                      r=skeys, w=("st_ssq",))
                P.add("act", lambda e: e.activation(out=st_ssq[0:npart, :], in_=st_ssq[0:npart, :], func=AF.Ln, scale=1.0 / D, bias=EPS),
                      r=("st_ssq",), w=("st_ssq",))
                P.add("act", lambda e: e.activation(out=st_rs[0:npart, :], in_=st_ssq[0:npart, :], func=AF.Exp, scale=-0.5),
                      r=("st_ssq",), w=("st_rs",))
                P.add("dve", lambda e: e.tensor_scalar(out=xb[0:npart, :], in0=src, scalar1=st_rs[0:npart, 0:1], scalar2=None, op0=ALU.mult),
                      r=skeys + ("st_rs",), w=("xb",))

            def n2():
                for half in range(2):
                    def tp(e, half=half):
                        ins = None
                        for c in range(8):
                            cc = half * 8 + c
                            ins = e.transpose(out=psb[half][:, c * 128:c * 128 + npart], in_=xb[0:npart, cc * 128:(cc + 1) * 128],
                                              identity=identb[0:npart, 0:npart])
                        return ins
                    P.add("pe", tp, r=("xb", "identb"), w=("ps%d" % half,))
                    src_ps = psb[half][:, :].rearrange("p (c n) -> p c n", c=8)[:, :, 0:npart]
                    P.add("dve", lambda e, half=half, src_ps=src_ps: e.tensor_tensor(
                        out=dst_fn(half * 8, half * 8 + 8), in0=src_ps, in1=bc_free(gains[:, half * 8:half * 8 + 8], npart), op=ALU.mult),
                        r=("ps%d" % half, "gmix", "gffn"), w=tuple(dst_keys))
            return n0, n1, n2

        def norm_tile(x_src, npart, gains, dst_fn, dst_keys, load=True, src_sb=None, src_keys=()):
            n0, n1, n2 = norm_parts(x_src, npart, gains, dst_fn, dst_keys, load=load, src_sb=src_sb, src_keys=src_keys)
            n0(); n1(); n2()

        p1 = [norm_parts(xm[i * 128:(i + 1) * 128, :], 128, gmix, (lambda c0, c1, i=i: hT[:, c0:c1, i * 128:(i + 1) * 128]), (("hT", i),))
              for i in range(8)]
        p1[0][0](); p1[1][0]()
        for i in range(8):
            p1[i][1]()
            p1[i][2]()
            if i + 2 < 8:
                p1[i + 2][0]()
        norm_tile(xs[:, :], TS, gmix, lambda c0, c1: hT[:, c0:c1, NT:NT + TS], (("hT", 8),))

        if STOP <= 2:
            return

        def proj(lhs_fn, rkeys, wname, bank, npart):
            slot, wv = W(wname)

            def f(e):
                ins = None
                for c in range(16):
                    ins = e.matmul(out=ps[bank][0:npart, :], lhsT=lhs_fn(c), rhs=wv[:, c, :], start=(c == 0), stop=(c == 15))
                return ins
            P.add("pe", f, r=tuple(rkeys) + (("ring", slot),), w=("ps%d" % bank,))

        def qk_norm(bank, npart, gainbuf, gkey, kf, kfkey, ssq, rs, skey):
            jq = bass.AP(tensor=junk[:].tensor, offset=junk[:].offset, ap=[[junk[:].ap[0][0], npart], [0, 128]])

            def fsq(e):
                ins = None
                for h in range(4):
                    ins = e.activation(out=jq, in_=ps[bank][0:npart, h * 128:(h + 1) * 128], func=AF.Square,
                                       accum_out=ssq[0:npart, h:h + 1])
                return ins
            P.add("act", fsq, r=("ps%d" % bank,), w=(skey,))
            P.add("act", lambda e: e.activation(out=ssq[0:npart, :], in_=ssq[0:npart, :], func=AF.Ln, scale=1.0 / 128, bias=EPS),
                  r=(skey,), w=(skey,))
            P.add("act", lambda e: e.activation(out=rs[0:npart, :], in_=ssq[0:npart, :], func=AF.Exp, scale=-0.5),
                  r=(skey,), w=(skey + "r",))
            for h in range(4):
                P.add("dve", lambda e, h=h: e.scalar_tensor_tensor(
                    out=kf[0:npart, h * 128:(h + 1) * 128], in0=ps[bank][0:npart, h * 128:(h + 1) * 128],
                    scalar=rs[0:npart, h:h + 1], in1=gainbuf[0:npart, h * 128:(h + 1) * 128], op0=ALU.mult, op1=ALU.mult),
                    r=("ps%d" % bank, skey + "r", gkey), w=(kfkey,))

        def tr4(kf, kfkey, npart, dst, dkey):
            def f(e):
                ins = None
                for h in range(4):
                    ins = e.transpose(out=ps[5][:, h * 128:h * 128 + npart], in_=kf[0:npart, h * 128:(h + 1) * 128],
                                      identity=identf[0:npart, 0:npart])
                return ins
            P.add("pe", f, r=(kfkey, "identf"), w=("ps5",))
            P.add("act", lambda e: e.copy(out=dst, in_=ps[5][:, :].rearrange("p (h k) -> p h k", h=4)[:, :, 0:npart]),
                  r=("ps5",), w=(dkey,))

        first_group = [True]

        def run_group(g):
            dil = DILS[g]
            items = []
            if g == 0:
                items.append(dict(kind="H", rows=xh[1920:2048, :], vcol=0))
                for i in range(8):
                    items.append(dict(kind="M", lhs=(lambda c, i=i: hT[:, c, i * 128:(i + 1) * 128]), hkeys=(("hT", i),),
                                      pos=("c", i), out=(7 == i and [(0, 128, kvp[0][:, 0:128, :])] or [])))
            elif g == 1:
                for r in range(4):
                    items.append(dict(kind="H", rows=xh[1536 + r:2048:4, :], vcol=1 + r))
                    for s in range(2):
                        items.append(dict(kind="M", lhs=(lambda c, s=s, r=r: hT[:, c, s * 512 + r:s * 512 + 512:4]),
                                          hkeys=tuple(("hT", 4 * s + k) for k in range(4)), pos=("s4", s, r),
                                          out=(s == 1 and [(0, 128, kvp[1][:, r:512:4, :])] or [])))
            else:
                for T in range(8):
                    items.append(dict(kind="H", rows=xh[2 * T:2048:16, :], vcol=5 + 2 * T))
                    items.append(dict(kind="H", rows=xh[2 * T + 1:2048:16, :], vcol=5 + 2 * T + 1))
                    items.append(dict(kind="M", lhs=(lambda c: hTh[:, c, :]), gather=T,
                                      hkeys=("hTh",), pos=("s16", T),
                                      out=[(0, 64, kvp[2][:, 2 * T:1024:16, :]), (64, 128, kvp[2][:, 2 * T + 1:1024:16, :])]))
            n_items = len(items)
            for idx, it in enumerate(items):
                it["idx"] = idx
                it["kslot"] = idx % 5
            qcount = [0]
            for it in items:
                if it["kind"] == "M":
                    it["qslot"] = qcount[0] % 3
                    it["qbuf"] = qcount[0] % 2
                    qcount[0] += 1
            for idx, it in enumerate(items):
                if it["kind"] != "M":
                    continue
                if g == 2:
                    it["prev"] = [(items[idx - 2], 0, 64), (items[idx - 1], 64, 128)]
                else:
                    it["prev"] = [(items[idx - 1], 0, 128)]

            Kn, Vn, Qn = "K%d" % g, "V%d" % g, "Q%d" % g

            hsel = [0]

            def stageN(it):
                if it["kind"] == "H" or "gather" in it:
                    hb, hkey = HTH[hsel[0] % 2]
                    hsel[0] += 1
                    if it["kind"] == "H":
                        it["_n"] = norm_parts(it["rows"], 128, gmix, (lambda c0, c1, hb=hb: hb[:, c0:c1, :]), (hkey,))
                    else:
                        T = it["gather"]
                        srcv = hT[:, :, 0:NT].rearrange("p c (m r) -> p c r m", r=16)

                        def gat(T=T, srcv=srcv, hb=hb, hkey=hkey):
                            for rr in range(2):
                                P.add("dve", lambda e, rr=rr: e.tensor_copy(out=hb[:, :, rr * 64:(rr + 1) * 64], in_=srcv[:, :, 2 * T + rr, :]),
                                      r=tuple(("hT", k) for k in range(8)), w=(hkey,))
                        it["_n"] = (None, None, gat)
                    it["_lhs"], it["_hk"] = (lambda c, hb=hb: hb[:, c, :]), (hkey,)
                else:
                    it["_lhs"], it["_hk"] = it["lhs"], it["hkeys"]

            def stageA1(it):
                lhs, hk = it["_lhs"], it["_hk"]
                kK, kKkey = KFK[it["idx"] % 2]
                it["_kfK"] = (kK, kKkey)
                if it["kind"] == "M":
                    kQ, kQkey = KFQ[it["qbuf"]]
                    it["_kfQ"] = (kQ, kQkey)
                    proj(lhs, hk, Qn, 4, 128)
                    qk_norm(4, 128, bufA, "bufA", kQ, kQkey, st4[0], st4[1], "sq")
                proj(lhs, hk, Kn, 2, 128)
                qk_norm(2, 128, bufB, "bufB", kK, kKkey, st4[2], st4[3], "sk")
                for (p0, p1, dst) in it.get("out", []):
                    dma("sp", dst[0], kK[p0:p1, :], (kKkey,), (("o", "K", g),), ("stK", g, it["idx"] % 2))

            def stageB(it):
                if it["kind"] == "M":
                    kQ, kQkey = it["_kfQ"]
                    tr4(kQ, kQkey, 128, QTr[it["qslot"]], ("QT", it["qslot"]))
                kK, kKkey = it["_kfK"]
                tr4(kK, kKkey, 128, KTr[it["kslot"]], ("KT", it["kslot"]))

            def stageA2(it):
                proj(it["_lhs"], it["_hk"], Vn, 3, 128)
                ks = it["kslot"]
                P.add("act", lambda e: e.copy(out=Vr[ks], in_=ps[3][:, :]), r=("ps3",), w=(("V", ks),))
                outs = it.get("out", [])
                if outs:
                    P.add("dve", lambda e: e.tensor_copy(out=vf[:], in_=ps[3][:, :]), r=("ps3",), w=("vf",))
                    for (p0, p1, dst) in outs:
                        dma("sp", dst[1], vf[p0:p1, :], ("vf",), (("o", "V", g),), ("stV", g))

            def stageC(it):
                qs, ks = it["qslot"], it["kslot"]
                for hp in range(2):
                    bank = 6 if hp == 0 else 0
                    stv = ps[bank][:, :].rearrange("p (a h q) -> p a h q", a=2, h=2)

                    def f(e, hp=hp, stv=stv):
                        ins = None
                        for hh in range(2):
                            h = hp * 2 + hh
                            for (pit, c0, c1) in it["prev"]:
                                ins = e.matmul(out=stv[:, 0, hh, c0:c1], lhsT=KTr[pit["kslot"]][:, h, :], rhs=QTr[qs][:, h, c0:c1],
                                               start=True, stop=True)
                            ins = e.matmul(out=stv[:, 1, hh, :], lhsT=KTr[ks][:, h, :], rhs=QTr[qs][:, h, :], start=True, stop=True)
                        return ins
                    rk = [("QT", qs), ("KT", ks)] + [("KT", pit["kslot"]) for (pit, _, _) in it["prev"]]
                    P.add("pe", f, r=tuple(rk), w=("ps%d" % bank,))
                    P.add("act", lambda e, bank=bank: e.activation(out=Ebuf.rearrange("p a h q -> p (a h q)"), in_=ps[bank][:, :],
                                                                   func=AF.Exp, scale=SCALE), r=("ps%d" % bank,), w=("Ebuf",))
                    pt = PTb[hp]
                    if g == 2:
                        ebc = EBc2[:, hp * 2:hp * 2 + 2, :]
                        ebp = EBp2[:, hp * 2:hp * 2 + 2, :]
                        ekeys = ("EBc2", "EBp2")
                    else:
                        ebc = EB[:, g * 4 + hp * 2:g * 4 + hp * 2 + 2, 0, :]
                        ebp = EB[:, g * 4 + hp * 2:g * 4 + hp * 2 + 2, 1, :]
                        ekeys = ("EB",)
                    P.add("dve", lambda e, pt=pt, ebc=ebc: e.tensor_tensor(out=pt[:, 1, :, :], in0=Ebuf[:, 1, :, :], in1=ebc, op=ALU.mult),
                          r=("Ebuf",) + ekeys, w=(("PT", hp),))
                    for (pit, c0, c1) in it["prev"]:
                        if pit["kind"] == "H":
                            vc = pit["vcol"]
                            P.add("dve", lambda e, pt=pt, ebp=ebp, c0=c0, c1=c1, vc=vc: e.scalar_tensor_tensor(
                                out=pt[:, 0, :, c0:c1], in0=Ebuf[:, 0, :, c0:c1], scalar=kval[:, vc:vc + 1], in1=ebp[:, :, c0:c1],
                                op0=ALU.mult, op1=ALU.mult), r=("Ebuf", "kval") + ekeys, w=(("PT", hp),))
                        else:
                            P.add("dve", lambda e, pt=pt, ebp=ebp, c0=c0, c1=c1: e.tensor_tensor(
                                out=pt[:, 0, :, c0:c1], in0=Ebuf[:, 0, :, c0:c1], in1=ebp[:, :, c0:c1], op=ALU.mult),
                                r=("Ebuf",) + ekeys, w=(("PT", hp),))

            def stageD(it):
                ks = it["kslot"]
                kind, *pp = it["pos"]
                for hp in range(2):
                    bank = 7 if hp == 0 else 1
                    od = ps[bank][:, :].rearrange("p (a h q) -> p a h q", a=2, h=2)
                    pt = PTb[hp]

                    def f(e, hp=hp, od=od, pt=pt):
                        ins = None
                        for hh in range(2):
                            h = hp * 2 + hh
                            ins = e.matmul(out=od[:, 0, hh, :], lhsT=Vr[ks][:, h * 128:(h + 1) * 128], rhs=pt[:, 1, hh, :],
                                           start=True, stop=False)
                            np_ = len(it["prev"])
                            for j, (pit, c0, c1) in enumerate(it["prev"]):
                                ins = e.matmul(out=od[:, 0, hh, c0:c1], lhsT=Vr[pit["kslot"]][:, h * 128:(h + 1) * 128],
                                               rhs=pt[:, 0, hh, c0:c1], start=False, stop=(j == np_ - 1))
                        ins = e.matmul(out=od[:, 1, :, :], lhsT=onesb[:], rhs=pt[:, 1, :, :], start=True, stop=False)
                        ins = e.matmul(out=od[:, 1, :, :], lhsT=onesb[:], rhs=pt[:, 0, :, :], start=False, stop=True)
                        return ins
                    rk = [("PT", hp), ("V", ks), "onesb"] + [("V", pit["kslot"]) for (pit, _, _) in it["prev"]]
                    P.add("pe", f, r=tuple(rk), w=("ps%d" % bank,))
                    for a, acc, akey in ((0, accO, "accO"), (1, accD, "accD")):
                        hs = slice(hp * 2, hp * 2 + 2)
                        if kind == "c":
                            i = pp[0]
                            dst = acc[:, hs, i * 128:(i + 1) * 128]
                            src = od[:, a, :, :]
                        elif kind == "s4":
                            s, r = pp
                            dst = acc[:, hs, s * 512 + r:s * 512 + 512:4]
                            src = od[:, a, :, :]
                        else:
                            T = pp[0]
                            dst = acc[:, hs, :].rearrange("p h (m r) -> p h r m", r=16)[:, :, 2 * T:2 * T + 2, :]
                            src = od[:, a, :, :].rearrange("p h (r m) -> p h r m", r=2)
                        if first_group[0]:
                            P.add("act", lambda e, dst=dst, src=src: e.copy(out=dst, in_=src),
                                  r=("ps%d" % bank,), w=(akey,))
                        else:
                            P.add("dve", lambda e, dst=dst, src=src: e.tensor_tensor(out=dst, in0=dst, in1=src, op=ALU.add),
                                  r=("ps%d" % bank, akey), w=(akey,))

            def npart_(k, j):
                if 0 <= k < n_items:
                    fn = items[k].get("_n", (None, None, None))[j]
                    if fn is not None:
                        fn()
            samp = {}

            def sample_proj():
                sl = lambda c: hT[:, c, NT:NT + TS]
                last_q = [it for it in items if it["kind"] == "M"][-1]["qbuf"]
                kQs, kQskey = KFQ[(last_q + 1) % 2]
                kKs, kKskey = KFK[n_items % 2]
                samp["kQ"], samp["kK"] = (kQs, kQskey), (kKs, kKskey)
                proj(sl, (("hT", 8),), Qn, 4, TS)
                qk_norm(4, TS, bufA, "bufA", kQs, kQskey, st4[0], st4[1], "sq")
                proj(sl, (("hT", 8),), Kn, 2, TS)
                qk_norm(2, TS, bufB, "bufB", kKs, kKskey, st4[2], st4[3], "sk")
                dma("sp", kvs[g, 0], kKs[0:TS, :], (kKskey,), (("o", "Ks", g),), ("stKs", g))
                proj(sl, (("hT", 8),), Vn, 3, TS)
                P.add("dve", lambda e: e.tensor_copy(out=vf[0:TS, :], in_=ps[3][0:TS, :]), r=("ps3",), w=("vf",))
                dma("sp", kvs[g, 1], vf[0:TS, :], ("vf",), (("o", "Vs", g),), ("stVs", g))
                for _ in range(3):
                    wload_next()

            for it in items:
                stageN(it)
            npart_(0, 0); npart_(1, 0); npart_(0, 1); npart_(0, 2); npart_(1, 1)
            for n in range(n_items + 2):
                if n == n_items:
                    sample_proj()
                npart_(n + 2, 0)
                if 0 <= n - 2 < n_items and items[n - 2]["kind"] == "M":
                    stageC(items[n - 2])
                npart_(n + 1, 2)
                npart_(n + 2, 1)
                if n < n_items:
                    stageA1(items[n])
                if 0 <= n - 1 < n_items:
                    stageB(items[n - 1])
                if n < n_items:
                    stageA2(items[n])
                if 0 <= n - 2 < n_items and items[n - 2]["kind"] == "M":
                    stageD(items[n - 2])
            kQs, kQskey = samp["kQ"]
            kKs, kKskey = samp["kK"]
            tr4(kQs, kQskey, TS, QTr[0][:, :, 0:TS], ("QT", 0))
            tr4(kKs, kKskey, TS, KTr[4][:, :, 0:TS], ("KT", 4))
            P.add("act", lambda e: e.copy(out=Vr[4][0:TS, :], in_=vf[0:TS, :]), r=("vf",), w=(("V", 4),))
            ntile = 1 if g == 0 else 4
            for t in range(ntile):
                rows = slice(0, 128) if g == 0 else slice(t, dil * 128, dil)
                dma("sp", xt[:, t * 512:(t + 1) * 512], ck[g][0, rows, :], (), ("xt",), "xt")
                tr4(xt[:, t * 512:(t + 1) * 512], "xt", 128, KTr[t], ("KT", t))
                dma("sp", kfQ[:, :], ck[g][1, rows, :], (), ("kfQ",), "ckv")
                P.add("dve", lambda e, t=t: e.tensor_copy(out=Vr[t], in_=kfQ[:, :]), r=("kfQ",), w=(("V", t),))
            sc = ps[6]
            od = ps[7]

            def fsc(e):
                ins = None
                for h in range(4):
                    if g == 0:
                        ins = e.matmul(out=sc[:, h * 4:(h + 1) * 4], lhsT=KTr[0][:, h, :], rhs=QTr[0][:, h, 0:TS], start=True, stop=True)
                    else:
                        for t in range(4):
                            ins = e.matmul(out=sc[:, h * 4 + t:h * 4 + t + 1], lhsT=KTr[t][:, h, :], rhs=QTr[0][:, h, t:t + 1],
                                           start=True, stop=True)
                    ins = e.matmul(out=sc[0:TS, 16 + h * 4:16 + (h + 1) * 4], lhsT=KTr[4][:, h, 0:TS], rhs=QTr[0][:, h, 0:TS],
                                   start=True, stop=True)
                return ins
            P.add("pe", fsc, r=(("QT", 0), ("KT", 4)) + tuple(("KT", t) for t in range(ntile)), w=("ps6",))
            Es = kfK[:, 0:32]
            PTs = PTb[0][:].rearrange("p a h q -> p (a h q)")[:, 0:32]
            P.add("act", lambda e: e.activation(out=Es[:, 0:16], in_=sc[:, 0:16], func=AF.Exp, scale=SCALE), r=("ps6",), w=("kfK",))
            P.add("act", lambda e: e.activation(out=Es[0:TS, 16:32], in_=sc[0:TS, 16:32], func=AF.Exp, scale=SCALE), r=("ps6",), w=("kfK",))
            E3 = Es[:, 0:16].rearrange("p (h t) -> p h t", h=4)
            P3 = PTs[:, 0:16].rearrange("p (h t) -> p h t", h=4)
            En = Es[0:TS, 16:32].rearrange("p (h t) -> p h t", h=4)
            Pn = PTs[0:TS, 16:32].rearrange("p (h t) -> p h t", h=4)
            if g == 0:
                ebc_s = EB[:, 0:4, 1, 0:TS]
            else:
                ebc_s = bc_free(EB[:, g * 4:(g + 1) * 4, 1, 0], TS)
            P.add("dve", lambda e: e.tensor_tensor(out=P3, in0=E3, in1=ebc_s, op=ALU.mult), r=("kfK", "EB"), w=(("PT", 0),))
            ebn_s = EB[0:TS, g * 4:(g + 1) * 4, 0, 0:TS]
            if g == 0:
                P.add("dve", lambda e: e.tensor_tensor(out=Pn, in0=En, in1=ebn_s, op=ALU.mult), r=("kfK", "EB"), w=(("PT", 0),))
            else:
                P.add("dve", lambda e: e.tensor_tensor(out=En, in0=En, in1=ebn_s, op=ALU.mult), r=("kfK", "EB"), w=("kfK",))
                idb = bass.AP(tensor=identf[:].tensor, offset=identf[:].offset, ap=[list(identf[:].ap[0][:1]) + [TS], [0, 4], [1, TS]])
                P.add("dve", lambda e: e.tensor_tensor(out=Pn, in0=En, in1=idb, op=ALU.mult), r=("kfK", "identf"), w=(("PT", 0),))

            def fpv(e):
                ins = None
                for h in range(4):
                    hc = slice(h * 128, (h + 1) * 128)
                    ins = e.matmul(out=od[:, h * 4:(h + 1) * 4], lhsT=Vr[4][0:TS, hc], rhs=PTs[0:TS, 16 + h * 4:16 + (h + 1) * 4],
                                   start=True, stop=False)
                    if g == 0:
                        ins = e.matmul(out=od[:, h * 4:(h + 1) * 4], lhsT=Vr[0][:, hc], rhs=PTs[:, h * 4:(h + 1) * 4], start=False, stop=True)
                    else:
                        for t in range(4):
                            ins = e.matmul(out=od[:, h * 4 + t:h * 4 + t + 1], lhsT=Vr[t][:, hc], rhs=PTs[:, h * 4 + t:h * 4 + t + 1],
                                           start=False, stop=(t == 3))
                ins = e.matmul(out=od[:, 16:32], lhsT=onesb[0:TS, :], rhs=PTs[0:TS, 16:32], start=True, stop=False)
                ins = e.matmul(out=od[:, 16:32], lhsT=onesb[:, :], rhs=PTs[:, 0:16], start=False, stop=True)
                return ins
            P.add("pe", fpv, r=(("PT", 0), ("V", 4), "onesb") + tuple(("V", t) for t in range(ntile)), w=("ps7",))
            for a, acc, akey in ((0, accOs, "accOs"), (1, accDs, "accDs")):
                dst = acc[:].rearrange("p h t -> p (h t)")
                src = od[:, a * 16:(a + 1) * 16]
                if first_group[0]:
                    P.add("dve", lambda e, dst=dst, src=src: e.tensor_copy(out=dst, in_=src), r=("ps7",), w=(akey,))
                else:
                    P.add("dve", lambda e, dst=dst, src=src: e.tensor_tensor(out=dst, in0=dst, in1=src, op=ALU.add),
                          r=("ps7", akey), w=(akey,))
            first_group[0] = False
            return items

        group_items = {}
        for g in (2, 1, 0):
            if STOP <= 3 + (2 - g):
                return
            group_items[g] = run_group(g)

        if STOP <= 6:
            return

        XT2_OK[0] = False
        P.barrier()
        for h in range(4):
            P.add("dve", lambda e, h=h: e.reciprocal(out=accD[:, h, :], in_=accD[:, h, :]), r=("accD",), w=("accD",))
            P.add("dve", lambda e, h=h: e.tensor_tensor(out=catT[:, h, 0:NT], in0=accO[:, h, :], in1=accD[:, h, :], op=ALU.mult),
                  r=("accO", "accD"), w=tuple(("cat", i) for i in range(8)))
        P.add("dve", lambda e: e.reciprocal(out=accDs[:], in_=accDs[:]), r=("accDs",), w=("accDs",))
        P.add("dve", lambda e: e.tensor_tensor(out=catT[:, 0:4, NT:NT + TS], in0=accOs[:], in1=accDs[:], op=ALU.mult),
              r=("accOs", "accDs"), w=(("cat", 8),))
        dma("sp", bufA[:], bcast_rows(gvn, 128, 512), (), ("bufA",), "c5")
        dma("sp", bufB[:], bass.AP(tensor=gb.tensor, offset=0, ap=[[0, 128], [1, 512]]), (), ("bufB",), "c6")
        slotU, wU = W("U")
        bk = [2, 3, 4]
        bi = 0
        for gg in range(4):
            for th in range(3):
                bank = bk[bi % 3]; bi += 1
                if th < 2:
                    c0, c1, n = th * 512, th * 512 + 512, 512
                    rk = tuple(("hT", 4 * th + k) for k in range(4))
                    dst = uT[:, gg, c0:c1]
                    dkey = "uT"
                else:
                    c0, c1, n = NT, NT + TS, TS
                    rk = (("hT", 8),)
                    dst = uTs[:, gg, :]
                    dkey = "uTs"

                def f(e, gg=gg, c0=c0, c1=c1, n=n, bank=bank):
                    ins = None
                    for c in range(16):
                        ins = e.matmul(out=ps[bank][:, 0:n], lhsT=wU[:, c, gg * 128:(gg + 1) * 128], rhs=hT[:, c, c0:c1],
                                       start=(c == 0), stop=(c == 15))
                    return ins
                P.add("pe", f, r=rk + (("ring", slotU),), w=("ps%d" % bank,))
                P.add("act", lambda e, dst=dst, n=n, bank=bank: e.activation(out=dst, in_=ps[bank][:, 0:n], func=AF.Gelu),
                      r=("ps%d" % bank,), w=(dkey, "EB", "EBc2", "EBp2") if th < 2 else (dkey,))
        gtile = Vr[0]
        for i in range(9):
            npart = 128 if i < 8 else TS
            cols = slice(i * 128, (i + 1) * 128) if i < 8 else slice(NT, NT + TS)
            proj(lambda c, cols=cols: hT[:, c, cols], (("hT", i),), "G", 2, npart)
            P.add("act", lambda e, npart=npart: e.activation(out=kfQ[0:npart, :], in_=ps[2][0:npart, :], func=AF.Gelu),
                  r=("ps2",), w=("kfQ",))
            jk = bass.AP(tensor=junk[:].tensor, offset=junk[:].offset, ap=[[junk[:].ap[0][0], npart], [0, 512]])
            P.add("act", lambda e, npart=npart, jk=jk: e.activation(out=jk, in_=kfQ[0:npart, :], func=AF.Square,
                                                                    accum_out=st_ssq[0:npart, :]), r=("kfQ",), w=("st_ssq",))
            P.add("act", lambda e, npart=npart: e.activation(out=st_ssq[0:npart, :], in_=st_ssq[0:npart, :], func=AF.Ln,
                                                             scale=1.0 / 512, bias=EPS), r=("st_ssq",), w=("st_ssq",))
            P.add("act", lambda e, npart=npart: e.activation(out=st_rs[0:npart, :], in_=st_ssq[0:npart, :], func=AF.Exp, scale=-0.5),
                  r=("st_ssq",), w=("st_rs",))
            if i < 8:
                P.add("dve", lambda e: e.scalar_tensor_tensor(out=gtile, in0=kfQ[:, :], scalar=st_rs[:, 0:1], in1=bufA[:, :],
                                                              op0=ALU.mult, op1=ALU.mult), r=("kfQ", "st_rs", "bufA"), w=(("V", 0),))
            else:
                P.add("dve", lambda e: e.scalar_tensor_tensor(out=vf[0:TS, :], in0=kfQ[0:TS, :], scalar=st_rs[0:TS, 0:1], in1=bufA[0:TS, :],
                                                              op0=ALU.mult, op1=ALU.mult), r=("kfQ", "st_rs", "bufA"), w=("vf",))
                dma("sp", gvs[:, :], vf[0:TS, :], ("vf",), (("o", "G"),), "stG")
                P.add("dve", lambda e: e.tensor_copy(out=gtile[0:TS, :], in_=vf[0:TS, :]), r=("vf",), w=(("V", 0),))
            nq = npart

            def fm(e, npart=npart):
                ins = None
                for gg in range(4):
                    ins = e.matmul(out=ps[5][:, gg * 128:gg * 128 + npart], lhsT=gtile[0:npart, gg * 128:(gg + 1) * 128],
                                   rhs=WmT[0:npart, gg, 0:npart], start=True, stop=True)
                return ins
            P.add("pe", fm, r=(("V", 0), "WmT"), w=("ps5",))
            mixv = ps[5][:, :].rearrange("p (g t) -> p g t", g=4)[:, :, 0:npart]
            bbv = bufB[:, :].rearrange("p (g t) -> p g t", g=4)[:, :, 0:npart]
            tmpv = kfK[:, :].rearrange("p (g t) -> p g t", g=4)[:, :, 0:npart]
            P.add("dve", lambda e, mixv=mixv, bbv=bbv, tmpv=tmpv: e.tensor_tensor(out=tmpv, in0=mixv, in1=bbv, op=ALU.add),
                  r=("ps5", "bufB"), w=("kfK",))
            uv = uT[:, :, cols] if i < 8 else uTs[:, :, :]
            P.add("dve", lambda e, tmpv=tmpv, uv=uv, cols=cols: e.tensor_tensor(out=catT[:, 4:8, cols], in0=tmpv, in1=uv, op=ALU.mult),
                  r=("kfK", "uT", "uTs"), w=(("cat", i),))
        wload_next(); wload_next()

        if STOP <= 7:
            return

        P.barrier()

        if STOP <= 8:
            return

        for i in range(8):
            dma("sp", x1[:, i, :], xm[i * 128:(i + 1) * 128, :], (), (("x1", i),), ("x1l", i))
        dma("sp", xt[0:TS, :], xs[:, :], (), ("xt",), "xt")

        def outproj(i):
            npart = 128 if i < 8 else TS
            cols = slice(i * 128, (i + 1) * 128) if i < 8 else slice(NT, NT + TS)
            for cb in range(4):
                slot, wv = W("O%d" % (cb // 2))
                bank = 2 + cb

                def f(e, wv=wv, cb=cb, bank=bank):
                    ins = None
                    for c in range(8):
                        ins = e.matmul(out=ps[bank][0:npart, :], lhsT=catT[:, c, cols], rhs=wv[:, c, (cb % 2) * 512:(cb % 2) * 512 + 512],
                                       start=(c == 0), stop=(c == 7))
                    return ins
                P.add("pe", f, r=(("cat", i), ("ring", slot)), w=("ps%d" % bank,))
                dst = x1[:, i, cb * 512:(cb + 1) * 512] if i < 8 else xt[0:TS, cb * 512:(cb + 1) * 512]
                key = ("x1", i) if i < 8 else "xt"
                P.add("dve", lambda e, dst=dst, bank=bank: e.tensor_tensor(out=dst, in0=ps[bank][0:npart, :], in1=dst, op=ALU.add),
                      r=("ps%d" % bank, key), w=(key,))

        def norm2(i):
            cols = slice(i * 128, (i + 1) * 128) if i < 8 else slice(NT, NT + TS)
            if i < 8:
                norm_tile(None, 128, gffn, lambda c0, c1: hT[:, c0:c1, cols], (("hT", i),), load=False,
                          src_sb=x1[:, i, :], src_keys=(("x1", i),))
            else:
                norm_tile(None, TS, gffn, lambda c0, c1: hT[:, c0:c1, cols], (("hT", 8),), load=False,
                          src_sb=xt[0:TS, :], src_keys=("xt",))

        outproj(0)
        for i in range(1, 9):
            outproj(i)
            norm2(i - 1)
        norm2(8)
        P.barrier()
        wload_next(); wload_next()

        if STOP <= 9:
            return

        upb = [0]
        dnb = [0]

        def up(n):
            slot, wv = W("UP%d" % n)
            for fc in range(4):
                for th in range(3):
                    bank = upb[0] % 4; upb[0] += 1
                    if th < 2:
                        c0, c1, nn = th * 512, th * 512 + 512, 512
                        rk = tuple(("hT", 4 * th + k) for k in range(4))
                        dst = aT[n % 2][:, fc, c0:c1]
                        dkey = ("aT", n % 2)
                    else:
                        c0, c1, nn = NT, NT + TS, TS
                        rk = (("hT", 8),)
                        dst = aTs[:, n % 2, fc, :]
                        dkey = ("aTs", n % 2)

                    def f(e, fc=fc, c0=c0, c1=c1, nn=nn, bank=bank):
                        ins = None
                        for c in range(16):
                            ins = e.matmul(out=ps[bank][:, 0:nn], lhsT=wv[:, c, fc * 128:(fc + 1) * 128], rhs=hT[:, c, c0:c1],
                                           start=(c == 0), stop=(c == 15))
                        return ins
                    P.add("pe", f, r=rk + (("ring", slot),), w=("ps%d" % bank,))
                    rs_ = rsc[bank % 2]
                    P.add("act", lambda e, nn=nn, bank=bank, rs_=rs_: e.activation(out=rs_[:, 0:nn], in_=ps[bank][:, 0:nn], func=AF.Relu),
                          r=("ps%d" % bank,), w=(("rsc", bank % 2),))
                    P.add("act", lambda e, nn=nn, rs_=rs_, dst=dst: e.activation(out=dst, in_=rs_[:, 0:nn], func=AF.Square),
                          r=(("rsc", bank % 2),), w=(dkey,))

        def down(n):
            slot, wv = W("DN%d" % n)
            for i in range(9):
                npart = 128 if i < 8 else TS
                for cb in range(4):
                    bank = 4 + dnb[0] % 4; dnb[0] += 1

                    def f(e, i=i, cb=cb, bank=bank, npart=npart):
                        ins = None
                        for fc in range(4):
                            lhsT = aT[n % 2][:, fc, i * 128:(i + 1) * 128] if i < 8 else aTs[:, n % 2, fc, :]
                            ins = e.matmul(out=ps[bank][0:npart, :], lhsT=lhsT, rhs=wv[:, fc, cb * 512:(cb + 1) * 512],
                                           start=(fc == 0), stop=(fc == 3))
                        return ins
                    rk = (("aT", n % 2), ("ring", slot)) if i < 8 else (("aTs", n % 2), ("ring", slot))
                    P.add("pe", f, r=rk, w=("ps%d" % bank,))
                    dst = x1[:, i, cb * 512:(cb + 1) * 512] if i < 8 else xt[0:TS, cb * 512:(cb + 1) * 512]
                    key = ("x1", i) if i < 8 else "xt"
                    P.add("dve", lambda e, dst=dst, bank=bank, npart=npart: e.tensor_tensor(out=dst, in0=ps[bank][0:npart, :], in1=dst, op=ALU.add),
                          r=("ps%d" % bank, key), w=(key,))

        for n in range(17):
            if n < 16:
                up(n)
                if n >= 1:
                    pass
            if n >= 1:
                down(n - 1)
                wload_next(); wload_next()
        for i in range(8):
            dma("sp", y[i * 128:(i + 1) * 128, :], x1[:, i, :], (("x1", i),), (("o", "y", i),), ("sty", i))
        dma("sp", ys[:, :], xt[0:TS, :], ("xt",), (("o", "ys"),), "stys")

    phases()
    if DBG:
        P.barrier()
        dma("sp", dbg_rx[:, :], RX[:, :], (), (("o", "dbgrx"),), "dbg0")
        dma("sp", dbg_cat[:, :], catT[:].rearrange("p c n -> p (c n)"), (), (("o", "dbgcat"),), "dbg1")
        dma("sp", dbg_hT[:, :], hT[:].rearrange("p c n -> p (c n)"), (), (("o", "dbghT"),), "dbg2")
    P.barrier()
    P.emit(nc, es)
    es.close()
    return nc


_CACHE = {}


def kernel(x_prompt, x_sample, cache_kv_w128, cache_kv_w512, cache_kv_w2048, norm_mix, w_in,
           q_norm, k_norm, rel_bias, gmlp_v_norm, gmlp_w, gmlp_b, w_out, norm_ffn, w_up, w_down):
    f = lambda a: np.ascontiguousarray(np.asarray(a, dtype=np.float32))
    x_prompt = f(x_prompt); x_sample = f(x_sample)
    caches = [f(cache_kv_w128), f(cache_kv_w512), f(cache_kv_w2048)]
    if "nc" not in _CACHE:
        _CACHE["nc"] = build_program()
    nc = _CACHE["nc"]
    oh, ident, jm = host_constants()
    shared = {
        "w_in": f(w_in)[0], "w_out": f(w_out)[0], "w_up": f(w_up)[0], "w_down": f(w_down)[0],
        "norm_mix": f(norm_mix).reshape(16, 128), "norm_ffn": f(norm_ffn).reshape(16, 128),
        "q_norm": f(q_norm).reshape(1, 128), "k_norm": f(k_norm).reshape(1, 128),
        "rel_bias": f(rel_bias), "gmlp_v_norm": f(gmlp_v_norm).reshape(1, 512),
        "gmlp_w": f(gmlp_w)[0], "gmlp_b": f(gmlp_b)[0], "oh": oh, "ident": ident, "jm": jm,
    }
    in_maps = []
    for c in range(8):
        b, j = c // 4, c % 4
        q0 = j * NT
        xh = np.zeros((NH, D), np.float32)
        valid = np.zeros((NH,), np.float32)
        lo = q0 - NH
        s = max(lo, 0)
        if q0 > 0:
            xh[s - lo:] = x_prompt[b, s:q0]
            valid[s - lo:] = 1.0
        kv = np.zeros((128, 21), np.float32)
        kv[:, 0] = valid[1920:2048]
        for r in range(4):
            kv[:, 1 + r] = valid[1536 + r:2048:4]
        for r in range(16):
            kv[:, 5 + r] = valid[r:2048:16]
        m = dict(shared)
        m["xm"] = np.ascontiguousarray(x_prompt[b, q0:q0 + NT])
        m["xh"] = xh
        m["xs"] = np.ascontiguousarray(x_sample[c])
        m["kvalid"] = kv
        for g in range(3):
            m["ck%d" % g] = np.ascontiguousarray(caches[g][0, c].reshape(2, -1, 512))
        in_maps.append(m)
    res = run_bass_kernel_spmd(nc, in_maps, core_ids=list(range(8)))
    R = res.results
    yp = np.zeros((2, 4096, D), np.float32)
    ysm = np.zeros((8, TS, D), np.float32)
    kvp_out = [np.zeros((1, 2, 2, w, 4, 128), np.float32) for w in (128, 512, 2048)]
    kvs_out = [np.zeros((1, 8, 2, TS, 4, 128), np.float32) for _ in range(3)]
    gv = np.zeros((1, 8, TS, 512), np.float32)
    for c in range(8):
        b, j = c // 4, c % 4
        yp[b, j * NT:(j + 1) * NT] = R[c]["y"]
        ysm[c] = R[c]["ys"]
        if j == 3:
            kvp_out[0][0, b] = R[c]["kvp0"].reshape(2, 128, 4, 128)
            kvp_out[1][0, b] = R[c]["kvp1"].reshape(2, 512, 4, 128)
        if j >= 2:
            kvp_out[2][0, b, :, (j - 2) * 1024:(j - 1) * 1024] = R[c]["kvp2"].reshape(2, 1024, 4, 128)
        for g in range(3):
            kvs_out[g][0, c] = R[c]["kvs"][g].reshape(2, TS, 4, 128)
        gv[0, c] = R[c]["gvs"]
    return (yp, ysm, kvp_out[0], kvp_out[1], kvp_out[2], kvs_out[0], kvs_out[1], kvs_out[2], gv)
```
